# Optimizing a Trainium2 kernel written in Bass

```python
import jax, jax.numpy as jnp
from jax import lax
import numpy as np

D_MODEL = 1024
BATCH = 2
SEQ = 8192
DEPTH = 4
DEC_BATCH = 8
DEC_SEQ = 2048
PAST_LEN = 128

N_HEADS_A = 8
QK_NOPE = 64
QK_ROPE = 32
QK_DIM = QK_NOPE + QK_ROPE
V_DIM = 64
Q_LORA = 384
KV_LORA = 256
ROPE_BASE = 10000.0
Q_BLOCK = 128
CHUNK = 128
SGU_GROUPS = 8
GMLP_HALF = D_MODEL
SGU_CH = GMLP_HALF // SGU_GROUPS
D_FF = -(-8 * D_MODEL // (3 * 256)) * 256
IN_SPLITS = (Q_LORA, KV_LORA, QK_ROPE, GMLP_HALF, GMLP_HALF, D_MODEL, D_MODEL)
IN_COLS = sum(IN_SPLITS)
IN_OFFSETS = tuple(int(o) for o in np.cumsum(IN_SPLITS)[:-1])
N_MOD = 6
EPS = 1e-6

kernel_name = "hybrid_mla_sgu_adaln_encoder"


def rms_norm(x, g):
    x32 = x.astype(jnp.float32)
    y = x32 * lax.rsqrt(jnp.mean(x32 * x32, axis=-1, keepdims=True) + EPS)
    return y.astype(x.dtype) * g


def rope_tables(seq_len, dtype):
    pos = jnp.arange(seq_len, dtype=jnp.float32)
    inv = ROPE_BASE ** (-jnp.arange(0, QK_ROPE, 2, dtype=jnp.float32) / QK_ROPE)
    ang = pos[:, None] * inv[None, :]
    return jnp.cos(ang).astype(dtype), jnp.sin(ang).astype(dtype)


def apply_rope(x, cos, sin):
    x1, x2 = jnp.split(x, 2, axis=-1)
    c = cos[None, :, None, :]
    s = sin[None, :, None, :]
    return jnp.concatenate([x1 * c - x2 * s, x2 * c + x1 * s], axis=-1)


def mla_branch(q_lat, kv_lat, k_rope_raw, p, cos, sin):
    B, S, _ = q_lat.shape
    q = (rms_norm(q_lat, p["q_a_norm_g"]) @ p["w_q_b"]).reshape(B, S, N_HEADS_A, QK_DIM)
    kv = (rms_norm(kv_lat, p["kv_a_norm_g"]) @ p["w_kv_b"]).reshape(B, S, N_HEADS_A, QK_NOPE + V_DIM)
    k_nope, v = kv[..., :QK_NOPE], kv[..., QK_NOPE:]
    k_rope = jnp.broadcast_to(k_rope_raw[:, :, None, :], (B, S, N_HEADS_A, QK_ROPE))
    k = jnp.concatenate([k_nope, k_rope], axis=-1)
    q = rms_norm(q, p["q_norm_g"])
    k = rms_norm(k, p["k_norm_g"])
    q = jnp.concatenate([q[..., :QK_NOPE], apply_rope(q[..., QK_NOPE:], cos, sin)], axis=-1)
    k = jnp.concatenate([k[..., :QK_NOPE], apply_rope(k[..., QK_NOPE:], cos, sin)], axis=-1)
    nb = S // Q_BLOCK
    qb = q.reshape(B, nb, Q_BLOCK, N_HEADS_A, QK_DIM).transpose(1, 0, 2, 3, 4)
    scale = QK_DIM ** -0.5

    def attend(q_blk):
        s = jnp.einsum('bqhd,bkhd->bhqk', q_blk, k).astype(jnp.float32) * scale
        pr = jax.nn.softmax(s, axis=-1).astype(v.dtype)
        return jnp.einsum('bhqk,bkhd->bqhd', pr, v)

    o = lax.map(attend, qb)
    o = o.transpose(1, 0, 2, 3, 4).reshape(B, S, N_HEADS_A * V_DIM)
    return o @ p["w_o_a"]


def sgu_branch(zu, zv, p):
    B, S, _ = zu.shape
    u = jax.nn.gelu(zu)
    v = rms_norm(jax.nn.gelu(zv), p["sgu_norm_g"])
    v5 = v.reshape(B, S // CHUNK, CHUNK, SGU_GROUPS, SGU_CH)
    mixed = jnp.einsum('gpq,bnqgc->bnpgc', p["w_s"], v5) + p["b_s"].T[None, None, :, :, None]
    return (u * mixed.reshape(B, S, GMLP_HALF)) @ p["w_o_b"]


def encoder_layer(x, c, p, cos, sin):
    mod = jax.nn.silu(c) @ p["w_ada"] + p["b_ada"]
    sh1, sc1, g1, sh2, sc2, g2 = [m[:, None, :] for m in jnp.split(mod, N_MOD, axis=-1)]
    h = rms_norm(x, p["norm1_g"]) * (1 + sc1) + sh1
    proj = h @ p["w_in"]
    q_lat, kv_lat, k_rope, zu, zv, ga, gb = jnp.split(proj, IN_OFFSETS, axis=-1)
    out_a = mla_branch(q_lat, kv_lat, k_rope, p, cos, sin)
    out_b = sgu_branch(zu, zv, p)
    merged = jax.nn.sigmoid(ga) * out_a + jax.nn.sigmoid(gb) * out_b
    x = x + g1 * (merged @ p["w_out"])
    h2 = rms_norm(x, p["norm2_g"]) * (1 + sc2) + sh2
    up, gate = jnp.split(h2 @ p["w_ffn_in"], 2, axis=-1)
    x = x + g2 * ((jax.nn.silu(gate) * up) @ p["w_ffn_out"])
    return x


def setup_inputs(seed: int = 0) -> dict:
    key = jax.random.key(seed)
    ks = jax.random.split(key, 24)
    f32 = jnp.float32

    def nrm(k, shape, s):
        return jax.random.normal(k, shape, f32) * s

    def gain(k, shape):
        return 1.0 + 0.05 * jax.random.normal(k, shape, f32)

    L = DEPTH
    return {
        "x_prompt": nrm(ks[0], (BATCH, SEQ, D_MODEL), 1.0),
        "x_sample": nrm(ks[1], (DEC_BATCH, DEC_SEQ, D_MODEL), 1.0),
        "c_prompt": nrm(ks[2], (BATCH, D_MODEL), 1.0),
        "c_sample": nrm(ks[3], (DEC_BATCH, D_MODEL), 1.0),
        "w_ada": nrm(ks[4], (L, D_MODEL, N_MOD * D_MODEL), 0.5 * D_MODEL ** -0.5),
        "b_ada": nrm(ks[5], (L, N_MOD * D_MODEL), 0.02),
        "norm1_g": gain(ks[6], (L, D_MODEL)),
        "w_in": nrm(ks[7], (L, D_MODEL, IN_COLS), D_MODEL ** -0.5),
        "q_a_norm_g": gain(ks[8], (L, Q_LORA)),
        "kv_a_norm_g": gain(ks[9], (L, KV_LORA)),
        "w_q_b": nrm(ks[10], (L, Q_LORA, N_HEADS_A * QK_DIM), Q_LORA ** -0.5),
        "w_kv_b": nrm(ks[11], (L, KV_LORA, N_HEADS_A * (QK_NOPE + V_DIM)), KV_LORA ** -0.5),
        "q_norm_g": gain(ks[12], (L, QK_DIM)),
        "k_norm_g": gain(ks[13], (L, QK_DIM)),
        "w_o_a": nrm(ks[14], (L, N_HEADS_A * V_DIM, D_MODEL), (N_HEADS_A * V_DIM) ** -0.5),
        "sgu_norm_g": gain(ks[15], (L, GMLP_HALF)),
        "w_s": nrm(ks[16], (L, SGU_GROUPS, CHUNK, CHUNK), CHUNK ** -0.5),
        "b_s": 1.0 + nrm(ks[17], (L, SGU_GROUPS, CHUNK), 0.1),
        "w_o_b": nrm(ks[18], (L, GMLP_HALF, D_MODEL), GMLP_HALF ** -0.5),
        "w_out": nrm(ks[19], (L, D_MODEL, D_MODEL), D_MODEL ** -0.5),
        "norm2_g": gain(ks[20], (L, D_MODEL)),
        "w_ffn_in": nrm(ks[21], (L, D_MODEL, 2 * D_FF), D_MODEL ** -0.5),
        "w_ffn_out": nrm(ks[22], (L, D_FF, D_MODEL), D_FF ** -0.5),
    }


def reference(x_prompt, x_sample, c_prompt, c_sample, w_ada, b_ada, norm1_g, w_in,
              q_a_norm_g, kv_a_norm_g, w_q_b, w_kv_b, q_norm_g, k_norm_g, w_o_a,
              sgu_norm_g, w_s, b_s, w_o_b, w_out, norm2_g, w_ffn_in, w_ffn_out):
    cos_p, sin_p = rope_tables(x_prompt.shape[1], x_prompt.dtype)
    cos_s, sin_s = rope_tables(x_sample.shape[1], x_sample.dtype)
    y_prompt, y_sample = x_prompt, x_sample
    for l in range(DEPTH):
        p = {
            "w_ada": w_ada[l], "b_ada": b_ada[l], "norm1_g": norm1_g[l], "w_in": w_in[l],
            "q_a_norm_g": q_a_norm_g[l], "kv_a_norm_g": kv_a_norm_g[l],
            "w_q_b": w_q_b[l], "w_kv_b": w_kv_b[l], "q_norm_g": q_norm_g[l], "k_norm_g": k_norm_g[l],
            "w_o_a": w_o_a[l], "sgu_norm_g": sgu_norm_g[l], "w_s": w_s[l], "b_s": b_s[l],
            "w_o_b": w_o_b[l], "w_out": w_out[l], "norm2_g": norm2_g[l],
            "w_ffn_in": w_ffn_in[l], "w_ffn_out": w_ffn_out[l],
        }
        y_prompt = encoder_layer(y_prompt, c_prompt, p, cos_p, sin_p)
        y_sample = encoder_layer(y_sample, c_sample, p, cos_s, sin_s)
    return (y_prompt, y_sample)
```

```python
import numpy as np
from contextlib import ExitStack
import concourse.bass as bass
import concourse.mybir as mybir
from concourse.bass_utils import run_bass_kernel_spmd

F32 = mybir.dt.float32
BF16 = mybir.dt.bfloat16
AF = mybir.ActivationFunctionType
ALU = mybir.AluOpType

D = 1024
NH = 8
QL = 384
KVL = 256
DFF = 2816
INC = 4768
EPS = 1e-6
TB = 512
NCORES = 8
SAME_SYNC = True

FULL_CFG = dict(Ls=2048, Lq=2048, L=4)


class Tracker:
    COMPUTE = ("pe", "act", "dve", "pool")

    def __init__(self, nc, es):
        self.nc = nc
        self.es = es
        self.ops = {e: [] for e in ("pe", "act", "dve", "pool", "sp")}
        self.cnt = {e: 0 for e in self.COMPUTE}
        self.esem = {e: es.enter_context(nc.semaphore("c_" + e)) for e in self.COMPUTE}
        self.dsem = {}
        self.seen = {e: {} for e in self.ops}
        self.lastw = {}
        self.readers = {}
        self.nops = 0
        self.no_pool = False

    def _dsem(self, key):
        if key not in self.dsem:
            self.dsem[key] = [self.es.enter_context(self.nc.semaphore("d%d" % len(self.dsem))), 0]
        return self.dsem[key]

    def _resolve(self, tok):
        if tok[0] == "c":
            return self.esem[tok[1]], tok[2], ("c", tok[1])
        d = self._dsem(tok[1])
        return d[0], d[1], ("d", tok[1])

    def op(self, eng, fn, reads=(), writes=(), dma=None, inc=None):
        if eng == "pool" and dma is None and self.no_pool:
            eng = "dve"
        deps = set()
        for r in reads:
            if r in self.lastw:
                deps.add(self.lastw[r])
            if isinstance(r, tuple) and r[0] == "ps":
                for k, v in self.readers.get(r, {}).items():
                    if k != ("c", eng):
                        deps.add(k + (v,))
        for w in writes:
            if w in self.lastw:
                deps.add(self.lastw[w])
            for k, v in self.readers.get(w, {}).items():
                deps.add(k + (v,))
        waits = []
        for tok in deps:
            if tok[0] == "c" and tok[1] == eng and dma is None and (eng == "pe" or not SAME_SYNC):
                continue
            sem, val, sid = self._resolve(tok)
            if self.seen[eng].get(sid, 0) >= val:
                continue
            self.seen[eng][sid] = val
            waits.append((sem, val))
        if dma is not None:
            d = self._dsem(dma)
            step = 16 if inc is None else inc
            d[1] += step
            tok = ("d", dma, d[1])
            incr = (d[0], step)
        else:
            self.cnt[eng] += 1
            tok = ("c", eng, self.cnt[eng])
            incr = (self.esem[eng], 1)
        self.ops[eng].append((waits, fn, incr))
        self.nops += 1
        for r in reads:
            self.readers.setdefault(r, {})[tok[:2]] = tok[2]
        for w in writes:
            self.lastw[w] = tok
            self.readers[w] = {}
        return tok

    def barrier(self):
        for eng in self.ops:
            waits = []
            for f in self.COMPUTE:
                if f == eng or self.cnt[f] == 0:
                    continue
                if self.seen[eng].get(("c", f), 0) < self.cnt[f]:
                    self.seen[eng][("c", f)] = self.cnt[f]
                    waits.append((self.esem[f], self.cnt[f]))
            for key, d in self.dsem.items():
                if d[1] and self.seen[eng].get(("d", key), 0) < d[1]:
                    self.seen[eng][("d", key)] = d[1]
                    waits.append((d[0], d[1]))
            if waits:
                self.ops[eng].append((waits, None, None))
        self.lastw = {}
        self.readers = {}

    def replay(self, name, e):
        for waits, fn, incr in self.ops[name]:
            for sem, val in waits:
                e.wait_ge(sem, val)
            if fn is not None:
                ins = fn(e)
                ins.then_inc(incr[0], incr[1])


class SbAlloc:
    BASE = 16512
    TOP = 229344

    def __init__(self, nc):
        self.nc = nc
        self.persist = self.BASE
        self.cur = self.BASE
        self.n = 0

    def _al(self, shape, dt, off):
        self.n += 1
        return self.nc.alloc_sbuf_tensor_at("sb%d" % self.n, list(shape), dt, offset=off)

    def alloc(self, shape, dt, persistent=False):
        esz = 4 if dt == F32 else 2
        nbytes = int(np.prod(shape[1:])) * esz
        nbytes = (nbytes + 31) // 32 * 32
        if persistent:
            assert self.cur == self.persist, "persistent allocs must come first"
            off = self.persist
            self.persist += nbytes
            self.cur = self.persist
        else:
            off = self.cur
            self.cur += nbytes
        assert self.cur <= self.TOP, "SBUF overflow %d" % self.cur
        return self._al(shape, dt, off)

    def phase(self):
        self.cur = self.persist


W_SPECS = [
    ("w_in", 1024, INC, "A"), ("w_kr4", 1024, 128, "A"), ("w_krrot4", 1024, 128, "A"),
    ("wq_nope", QL, 512, "A"), ("wq_rope", QL, 256, "A"), ("wq_rot", QL, 256, "A"),
    ("wkv_nope", KVL, 512, "A"), ("wkv_v", KVL, 512, "A"),
    ("w_o_a", 512, D, "B"), ("w_o_b", D, D, "B"), ("w_out", D, D, "B"),
    ("w_ffn_in", D, 2 * DFF, "B"), ("w_ffn_out", DFF, D, "B"),
]
V_EPS, V_N1G, V_N2G, V_SGUG, V_QAG, V_KVAG, V_GQN, V_GQR, V_GQRP, V_GKN, V_GKR, V_GKRP, V_BADA = range(13)


def vec_layout(L):
    off = {}
    o = 0
    for name, n in [(V_EPS, 1), (V_N1G, 8 * L), (V_N2G, 8 * L), (V_SGUG, 8 * L), (V_QAG, 3 * L), (V_KVAG, 2 * L),
                    (V_GQN, L), (V_GQR, L), (V_GQRP, L), (V_GKN, L), (V_GKR, L), (V_GKRP, L), (V_BADA, 48 * L)]:
        off[name] = o
        o += n
    return off, o


def build_program(cfg):
    Ls, Lq, L = cfg["Ls"], cfg["Lq"], cfg["L"]
    Lp = 4 * Lq
    NTOK = Ls + Lq
    nS, nP = Ls // TB, Lq // TB
    NB = nS + nP
    voff, NV = vec_layout(L)

    nc = bass.Bass("TRN2", target_bir_lowering=False)

    def din(name, shape, dt=F32):
        return nc.dram_tensor(name, list(shape), dt, kind="ExternalInput").ap()

    xT_in = din("xT", [D, NTOK])
    cT_in = din("cT", [128, 8, 2])
    cos_in = din("cos4", [128, NTOK])
    sin_in = din("sin4", [128, NTOK])
    vecs_in = din("vecs", [128, NV])
    consts_in = din("consts", [128, 7, 128])
    w_ada_in = din("w_ada", [L * D, 6 * D])
    wsT_in = din("w_sT", [L, 128, 8, 128])
    bsb_in = din("bsb", [L, 128, 8, 128])
    w_ext = {name: din(name, [L * K, N]) for name, K, N, _ in W_SPECS}
    yT_out = nc.dram_tensor("yT", [D, NTOK], F32, kind="ExternalOutput").ap()

    def dscr(name, shape, dt):
        return nc.dram_tensor(name, list(shape), dt)

    w_bf = {name: dscr(name + "_b", [L * K, N], BF16) for name, K, N, _ in W_SPECS}
    WK = {name: K for name, K, N, _ in W_SPECS}
    xT_d = dscr("xT_d", [D, NTOK], F32)
    hT_d = dscr("hT_d", [D, NTOK], BF16)
    qT_d = dscr("qT_d", [NH * 96, NTOK], BF16)
    oT_d = dscr("oT_d", [512, NTOK], BF16)
    kT_S = dscr("kT_S", [NH * 96, Ls], BF16)
    v_S = dscr("v_S", [Ls, 512], BF16)
    kT_Pin = [dscr("kT_Pin%d" % b, [NH * 96, TB], BF16) for b in range(nP)]
    kT_Pall = [dscr("kT_Pall%d" % b, [4 * NH * 96, TB], BF16) for b in range(nP)]
    v_Pin = [dscr("v_Pin%d" % b, [TB, 512], BF16) for b in range(nP)]
    v_Pall = [dscr("v_Pall%d" % b, [4 * TB, 512], BF16) for b in range(nP)]

    es = ExitStack()
    T = Tracker(nc, es)
    T.no_pool = bool(cfg.get("no_pool"))
    sb = SbAlloc(nc)
    PS = [es.enter_context(nc.psum_tensor("ps%d" % i, [128, 512], F32)) for i in range(8)]
    ps_rr = [0]

    def ps_next():
        i = ps_rr[0]
        ps_rr[0] = (i + 1) % 8
        return i

    def dma(eng, out, in_, reads, writes, key):
        T.op(eng, lambda e, o=out, i=in_: e.dma_start(out=o, in_=i), reads=reads, writes=writes, dma=key)

    def mm(ps_ap, pairs, reads, writes):
        def fn(e, ps_ap=ps_ap, pairs=pairs):
            n = len(pairs)
            ins = None
            for i, (l, r) in enumerate(pairs):
                ins = e.matmul(ps_ap, l, r, start=(i == 0), stop=(i == n - 1))
            return ins
        T.op("pe", fn, reads=reads, writes=writes)

    def act(out, in_, func, reads, writes, **kw):
        T.op("act", lambda e, o=out, i=in_, f=func, kw=kw: e.activation(out=o, in_=i, func=f, **kw), reads=reads, writes=writes)

    def tt(out, in0, in1, op, reads, writes, eng="dve"):
        T.op(eng, lambda e, o=out, a=in0, b=in1, op=op: e.tensor_tensor(out=o, in0=a, in1=b, op=op), reads=reads, writes=writes)

    def stt(out, in0, scalar, in1, op0, op1, reads, writes, eng="dve"):
        T.op(eng, lambda e, o=out, a=in0, s=scalar, b=in1, p0=op0, p1=op1:
             e.scalar_tensor_tensor(out=o, in0=a, scalar=s, in1=b, op0=p0, op1=p1), reads=reads, writes=writes)

    def ts(out, in0, s1, s2, op0, op1, reads, writes, eng="dve"):
        if s2 is None:
            T.op(eng, lambda e, o=out, a=in0, s1=s1, p0=op0: e.tensor_scalar(out=o, in0=a, scalar1=s1, scalar2=None, op0=p0),
                 reads=reads, writes=writes)
        else:
            T.op(eng, lambda e, o=out, a=in0, s1=s1, s2=s2, p0=op0, p1=op1:
                 e.tensor_scalar(out=o, in0=a, scalar1=s1, scalar2=s2, op0=p0, op1=p1), reads=reads, writes=writes)

    def recip(out, in_, reads, writes):
        T.op("dve", lambda e, o=out, i=in_: e.reciprocal(out=o, in_=i), reads=reads, writes=writes)

    def memset(ap, val, writes, eng="dve"):
        T.op(eng, lambda e, a=ap, v=val: e.memset(a, v), writes=writes)

    vecs = sb.alloc([128, NV], F32, True)
    cst = sb.alloc([128, 7, 128], BF16, True)
    ones_f = sb.alloc([128, 64], F32, True)
    mods = sb.alloc([128, L, 48, 2], F32, True)
    A1 = sb.alloc([128, L, 2, 8], F32, True)
    A2 = sb.alloc([128, L, 2, 8], F32, True)
    silc = sb.alloc([128, 8, 2], BF16, True)
    eps_ap = vecs[:, voff[V_EPS]:voff[V_EPS] + 1]

    def vcol(kind, idx):
        o = voff[kind] + idx
        return vecs[:, o:o + 1]

    ONES, NN, RN0, RN1, NR0, NR1, RR = [cst[:, i, :] for i in range(7)]
    RN = [RN0, RN1]
    NR = [NR0, NR1]

    dma("sp", vecs[:], vecs_in, [], ["vecs"], "ld_misc")
    dma("pool", cst[:], consts_in, [], ["cst"], "ld_cst")
    memset(ones_f[:], 1.0, ["ones_f"])

    def cast_weights(l, grp):
        for name, K, N, g in W_SPECS:
            if g != grp:
                continue
            for rb in range(0, K, 128):
                r0 = l * K + rb
                dma("pool", w_bf[name][r0:r0 + 128, :], w_ext[name][r0:r0 + 128, :], [], [("wb", name, l, rb // 128)],
                    "cast%s%d" % (grp, l))

    def wkeys(name, l):
        return [("wb", name, l, i) for i in range(WK[name] // 128)]

    def wsrc(name, l, c0, n):
        K = WK[name]
        return w_bf[name][l * K:(l + 1) * K, c0:c0 + n].rearrange("(kc p) n -> p kc n", p=128)

    cast_weights(0, "A")

    cTt = sb.alloc([128, 8, 2], F32)
    WA = [sb.alloc([128, 8, 512], BF16) for _ in range(2)]
    dma("sp", cTt[:], cT_in, [], ["cTt"], "ld_misc")
    act(silc[:], cTt[:], AF.Silu, ["cTt"], ["silc"])
    gi = 0
    for l in range(L):
        for jg in range(12):
            s = gi % 2
            gi += 1
            src = w_ada_in[l * D:(l + 1) * D, jg * 512:(jg + 1) * 512].rearrange("(kc p) n -> p kc n", p=128)
            dma("pool", WA[s][:], src, [], [("WA", s)], "ld_wa%d" % s)
            for jj in range(4):
                j = jg * 4 + jj
                pi = ps_next()
                mm(PS[pi][:, 0:2], [(WA[s][:, kc, jj * 128:(jj + 1) * 128], silc[:, kc, :]) for kc in range(8)],
                   [("WA", s), "silc"], [("ps", pi)])
                ts(mods[:, l, j, :], PS[pi][:, 0:2], vcol(V_BADA, l * 48 + j), None, ALU.add, None,
                   [("ps", pi), "vecs"], [("mods", l, j)])
        for s in range(2):
            for (Ax, gk, m0) in ((A1, V_N1G, 8), (A2, V_N2G, 32)):
                ts(Ax[:, l, s, :], mods[:, l, m0:m0 + 8, s], 1.0, None, ALU.add, None,
                   [("mods", l, j) for j in range(m0, m0 + 8)], [("A", id(Ax), l, s)])
                tt(Ax[:, l, s, :], Ax[:, l, s, :], vecs[:, voff[gk] + 8 * l:voff[gk] + 8 * l + 8], ALU.mult,
                   [("A", id(Ax), l, s), "vecs"], [("A", id(Ax), l, s)])
        if l == 0:
            cast_weights(0, "B")
    for l in range(1, L):
        cast_weights(l, "A")
        cast_weights(l, "B")
    T.barrier()
    STOP = cfg.get("stop")

    def mod_ap(l, m, c, s):
        return mods[:, l, m * 8 + c, s:s + 1]

    def tok0(b):
        return b * TB

    def seg(b):
        return 0 if b < nS else 1

    def rstd_from_psum(rs_ap, pi, n, rs_key):
        act(rs_ap, PS[pi][:], AF.Ln, [("ps", pi), "vecs"], [rs_key], bias=eps_ap, scale=1.0 / n)
        act(rs_ap, rs_ap, AF.Exp, [rs_key], [rs_key], scale=-0.5)

    def rms_stats(sq_list, lhs_list, n, rs_ap, rs_key, reads):
        pi = ps_next()
        mm(PS[pi][:], list(zip(lhs_list, sq_list)), reads + ["cst"], [("ps", pi)])
        rstd_from_psum(rs_ap, pi, n, rs_key)

    def norm_block(Xb, xkey, Hb, hkey, SQ, RS, Ax, l, s, shm, NT):
        for c in range(8):
            tt(SQ[:, c, :], Xb[:, c, :], Xb[:, c, :], ALU.mult, [(xkey, c)], [("SQ", c)], eng="pool")
        rms_stats([SQ[:, c, :] for c in range(8)], [ONES] * 8, D, RS[:], "RS", [("SQ", c) for c in range(8)])
        for c in range(8):
            i = c % 2
            tt(NT[i][:], Xb[:, c, :], RS[:], ALU.mult, [(xkey, c), "RS"], [("NT", i)])
            act(Hb[:, c, :], NT[i][:], AF.Identity, [("NT", i), ("A", id(Ax), l, s), ("mods", l, shm * 8 + c)], [(hkey, c)],
                scale=Ax[:, l, s, c:c + 1], bias=mod_ap(l, shm, c, s))

    PHASES = {}
    for l in range(L if STOP != "prologue" else 0):
        x_src = xT_in if l == 0 else xT_d
        x_dst = yT_out if l == L - 1 else xT_d
        last = (l == L - 1)

        if "ph1" not in PHASES:
            sb.phase()
            ns = {}
            ns["X"] = [sb.alloc([128, 8, TB], F32) for _ in range(2)]
            ns["H"] = [sb.alloc([128, 8, TB], BF16) for _ in range(2)]
            ns["SQ"] = sb.alloc([128, 8, TB], BF16)
            ns["RS"] = sb.alloc([128, TB], F32)
            ns["W1in"] = sb.alloc([128, 8, 640], BF16)
            ns["Wkr"] = sb.alloc([128, 8, 128], BF16)
            ns["Wkrr"] = sb.alloc([128, 8, 128], BF16)
            ns["Wqn"] = sb.alloc([128, 3, 512], BF16)
            ns["Wqr"] = sb.alloc([128, 3, 256], BF16)
            ns["Wqt"] = sb.alloc([128, 3, 256], BF16)
            ns["Wkn"] = sb.alloc([128, 2, 512], BF16)
            ns["Wv"] = sb.alloc([128, 2, 512], BF16)
            ns["QN"] = sb.alloc([128, 3, TB], BF16)
            ns["KVN"] = sb.alloc([128, 2, TB], BF16)
            ns["QO"] = sb.alloc([128, 6, TB], BF16)
            ns["KO"] = sb.alloc([128, 6, TB], BF16)
            ns["VO"] = sb.alloc([128, 4, 512], BF16)
            ns["TK"] = sb.alloc([128, TB], F32)
            ns["SQRK"] = sb.alloc([128, TB], BF16)
            ns["COS"] = [sb.alloc([128, TB], F32) for _ in range(2)]
            ns["SIN"] = [sb.alloc([128, TB], F32) for _ in range(2)]
            ns["CQ"] = sb.alloc([128, TB], F32)
            ns["SQT"] = sb.alloc([128, TB], F32)
            ns["CK"] = sb.alloc([128, TB], F32)
            ns["SK"] = sb.alloc([128, TB], F32)
            ns["SQH"] = [sb.alloc([128, TB], BF16) for _ in range(3)]
            ns["RSH"] = [sb.alloc([128, TB], F32) for _ in range(3)]
            ns["T1"] = sb.alloc([128, TB], F32)
            ns["T2"] = sb.alloc([128, TB], F32)
            ns["NTT"] = [sb.alloc([128, TB], F32) for _ in range(2)]
            PHASES["ph1"] = ns
        ns = PHASES["ph1"]
        X = ns["X"]
        H = ns["H"]
        SQ = ns["SQ"]
        RS = ns["RS"]
        W1in = ns["W1in"]
        Wkr = ns["Wkr"]
        Wkrr = ns["Wkrr"]
        Wqn = ns["Wqn"]
        Wqr = ns["Wqr"]
        Wqt = ns["Wqt"]
        Wkn = ns["Wkn"]
        Wv = ns["Wv"]
        QN = ns["QN"]
        KVN = ns["KVN"]
        QO = ns["QO"]
        KO = ns["KO"]
        VO = ns["VO"]
        TK = ns["TK"]
        SQRK = ns["SQRK"]
        COS = ns["COS"]
        SIN = ns["SIN"]
        CQ = ns["CQ"]
        SQT = ns["SQT"]
        CK = ns["CK"]
        SK = ns["SK"]
        SQH = ns["SQH"]
        RSH = ns["RSH"]
        T1 = ns["T1"]
        T2 = ns["T2"]
        NTT = ns["NTT"]

        for (buf, name, c0, n) in ((W1in, "w_in", 0, 640), (Wkr, "w_kr4", 0, 128), (Wkrr, "w_krrot4", 0, 128),
                                   (Wqn, "wq_nope", 0, 512), (Wqr, "wq_rope", 0, 256), (Wqt, "wq_rot", 0, 256),
                                   (Wkn, "wkv_nope", 0, 512), (Wv, "wkv_v", 0, 512)):
            dma("sp", buf[:], wsrc(name, l, c0, n), wkeys(name, l), [("W1", name)], "ld_w1")
        W1R = [("W1", n) for n in ("w_in", "w_kr4", "w_krrot4", "wq_nope", "wq_rope", "wq_rot", "wkv_nope", "wkv_v")]

        order = list(range(nS, NB)) + list(range(nS))

        def p1_loads(i):
            b = order[i]
            xs = i % 2
            t0 = tok0(b)
            dma("sp", X[xs][:], x_src[:, t0:t0 + TB].rearrange("(c p) t -> p c t", p=128), [("xd", b)],
                [(("X", xs), c) for c in range(8)], "ld_x%d" % xs)
            dma("sp", COS[xs][:], cos_in[:, t0:t0 + TB], [], [("COS", xs)], "ld_cs%d" % xs)
            dma("sp", SIN[xs][:], sin_in[:, t0:t0 + TB], [], [("SIN", xs)], "ld_cs%d" % xs)

        def head_half(hh, PSn, PSr_sq_key, sqr_ap, rope_src_fn, OUT, okey, gn, kdst_fn, rope_keys):
            for i in range(2):
                act(SQH[i][:], PS[PSn[i]][:], AF.Square, [("ps", PSn[i])], [("SQH", i)])
            for i in range(2):
                pi = ps_next()
                mm(PS[pi][:], [(NN, SQH[i][:]), (RN[i], sqr_ap)], [("SQH", i), PSr_sq_key, "cst"], [("ps", pi)])
                rstd_from_psum(RSH[i][:], pi, 96, ("RSH", i))
            pi = ps_next()
            mm(PS[pi][:], [(NR[0], SQH[0][:]), (NR[1], SQH[1][:]), (RR, sqr_ap)],
               [("SQH", 0), ("SQH", 1), PSr_sq_key, "cst"], [("ps", pi)])
            rstd_from_psum(RSH[2][:], pi, 96, ("RSH", 2))
            for i in range(2):
                jn = 2 * hh + i
                stt(OUT[:, jn, :], PS[PSn[i]][:], vcol(gn, l), RSH[i][:], ALU.mult, ALU.mult,
                    [("ps", PSn[i]), ("RSH", i), "vecs"], [(okey, jn)])
            rap, rkeys = rope_src_fn(hh)
            tt(OUT[:, 4 + hh, :], rap, RSH[2][:], ALU.mult, rkeys + [("RSH", 2)] + rope_keys, [(okey, 4 + hh)])
            kdst_fn(hh)

        for i, b in enumerate(order):
            if i == 0:
                p1_loads(0)
            if i + 1 < NB:
                p1_loads(i + 1)
            xs = i % 2
            s = seg(b)
            t0 = tok0(b)
            Xb, Hb = X[xs], H[xs]
            xkey, hkey = ("X", xs), ("H", xs)
            norm_block(Xb, xkey, Hb, hkey, SQ, RS, A1, l, s, 0, NTT)
            dma("pool", hT_d[:, t0:t0 + TB].rearrange("(c p) t -> p c t", p=128), Hb[:],
                [(hkey, c) for c in range(8)], [("hd", b)], "st_h%d" % xs)
            hreads = [(hkey, c) for c in range(8)] + W1R
            ts(CQ[:], COS[xs][:], vcol(V_GQR, l), None, ALU.mult, None, [("COS", xs), "vecs"], ["CQ"])
            ts(SQT[:], SIN[xs][:], vcol(V_GQRP, l), None, ALU.mult, None, [("SIN", xs), "vecs"], ["SQT"])
            ts(CK[:], COS[xs][:], vcol(V_GKR, l), None, ALU.mult, None, [("COS", xs), "vecs"], ["CK"])
            ts(SK[:], SIN[xs][:], vcol(V_GKRP, l), None, ALU.mult, None, [("SIN", xs), "vecs"], ["SK"])

            def latent(c0, nch, n, gk, OUTN, okey):
                pis = []
                for j in range(nch):
                    pi = ps_next()
                    pis.append(pi)
                    mm(PS[pi][:], [(W1in[:, kc, c0 + j * 128:c0 + (j + 1) * 128], Hb[:, kc, :]) for kc in range(8)],
                       hreads, [("ps", pi)])
                    act(SQ[:, j, :], PS[pi][:], AF.Square, [("ps", pi)], [("SQ", j)])
                rms_stats([SQ[:, j, :] for j in range(nch)], [ONES] * nch, n, RS[:], "RS", [("SQ", j) for j in range(nch)])
                for j in range(nch):
                    stt(OUTN[:, j, :], PS[pis[j]][:], vcol(gk, l * nch + j), RS[:], ALU.mult, ALU.mult,
                        [("ps", pis[j]), "RS", "vecs"], [(okey, j)])

            latent(0, 3, QL, V_QAG, QN, "QN")
            latent(QL, 2, KVL, V_KVAG, KVN, "KVN")
            qn_r = [("QN", j) for j in range(3)] + W1R
            kvn_r = [("KVN", j) for j in range(2)] + W1R

            for t4 in range(4):
                pi = ps_next()
                mm(PS[pi][:], [(KVN[:, kc, t4 * 128:(t4 + 1) * 128], Wv[:, kc, :]) for kc in range(2)], kvn_r, [("ps", pi)])
                act(VO[:, t4, :], PS[pi][:], AF.Copy, [("ps", pi)], [("VO", t4)])
            bp = b - nS
            vdst = (v_S if s == 0 else v_Pin[bp])
            tl = t0 if s == 0 else 0
            dma("pool", vdst[tl:tl + TB, :].rearrange("(a p) c -> p a c", p=128), VO[:],
                [("VO", a) for a in range(4)], [("vdst", s)], "st_v")

            pk = ps_next()
            mm(PS[pk][:], [(Wkr[:, kc, :], Hb[:, kc, :]) for kc in range(8)], hreads, [("ps", pk)])
            pkr = ps_next()
            mm(PS[pkr][:], [(Wkrr[:, kc, :], Hb[:, kc, :]) for kc in range(8)], hreads, [("ps", pkr)])
            act(SQRK[:], PS[pk][:], AF.Square, [("ps", pk)], ["SQRK"])
            tt(TK[:], PS[pk][:], CK[:], ALU.mult, [("ps", pk), "CK"], ["TK"])
            tt(T2[:], PS[pkr][:], SK[:], ALU.mult, [("ps", pkr), "SK"], ["T2"])
            tt(TK[:], TK[:], T2[:], ALU.add, ["TK", "T2"], ["TK"], eng="pool")

            def store_heads(OUT, okey, hh, dst, tl_, stkey):
                if cfg.get("no_qkstore"):
                    return
                for i2 in range(2):
                    jn = 2 * hh + i2
                    for hl in range(2):
                        h = 2 * jn + hl
                        dma("pool", dst[h * 96:h * 96 + 64, tl_:tl_ + TB], OUT[64 * hl:64 * hl + 64, jn, :],
                            [(okey, jn)], [("qkdst", stkey)], "st_" + stkey)
                for hl in range(4):
                    h = 4 * hh + hl
                    dma("pool", dst[h * 96 + 64:h * 96 + 96, tl_:tl_ + TB], OUT[32 * hl:32 * hl + 32, 4 + hh, :],
                        [(okey, 4 + hh)], [("qkdst", stkey)], "st_" + stkey)

            kdst = kT_S if s == 0 else kT_Pin[bp]
            for hh in range(2):
                pn = []
                for i2 in range(2):
                    jn = 2 * hh + i2
                    pi = ps_next()
                    pn.append(pi)
                    mm(PS[pi][:], [(Wqn[:, kc, jn * 128:(jn + 1) * 128], QN[:, kc, :]) for kc in range(3)], qn_r, [("ps", pi)])
                pr = ps_next()
                mm(PS[pr][:], [(Wqr[:, kc, hh * 128:(hh + 1) * 128], QN[:, kc, :]) for kc in range(3)], qn_r, [("ps", pr)])
                pt = ps_next()
                mm(PS[pt][:], [(Wqt[:, kc, hh * 128:(hh + 1) * 128], QN[:, kc, :]) for kc in range(3)], qn_r, [("ps", pt)])
                act(SQH[2][:], PS[pr][:], AF.Square, [("ps", pr)], [("SQH", 2)])

                def q_rope(hh_, pr=pr, pt=pt):
                    tt(T1[:], PS[pr][:], CQ[:], ALU.mult, [("ps", pr), "CQ"], ["T1"])
                    tt(T2[:], PS[pt][:], SQT[:], ALU.mult, [("ps", pt), "SQT"], ["T2"])
                    tt(T1[:], T1[:], T2[:], ALU.add, ["T1", "T2"], ["T1"], eng="pool")
                    return T1[:], ["T1"]

                head_half(hh, pn, ("SQH", 2), SQH[2][:], q_rope, QO, "QO", V_GQN,
                          lambda hh_: store_heads(QO, "QO", hh_, qT_d, t0, "q"), [])
                pn = []
                for i2 in range(2):
                    jn = 2 * hh + i2
                    pi = ps_next()
                    pn.append(pi)
                    mm(PS[pi][:], [(Wkn[:, kc, jn * 128:(jn + 1) * 128], KVN[:, kc, :]) for kc in range(2)], kvn_r, [("ps", pi)])

                def k_rope(hh_):
                    return TK[:], ["TK"]

                head_half(hh, pn, "SQRK", SQRK[:], k_rope, KO, "KO", V_GKN,
                          lambda hh_: store_heads(KO, "KO", hh_, kdst, tl, "k%d" % s), [])

            if s == 1 and not cfg.get("no_cc"):
                for ci, (src, dst, rk) in enumerate(((kT_Pin[bp], kT_Pall[bp], ("qkdst", "k1")), (v_Pin[bp], v_Pall[bp], ("vdst", 1)))):
                    T.op("pool", lambda e, src=src, dst=dst: e.collective_compute(
                        "AllGather", ALU.bypass, replica_groups=[[0, 1, 2, 3], [4, 5, 6, 7]],
                        ins=[src.ap().opt()], outs=[dst.ap().opt()]),
                        reads=[rk], writes=[("gath", ci, bp)], dma="cc%d_%d_%d" % (l, ci, bp), inc=1)
        T.barrier()
        if STOP == "pass1":
            break

        LkMax = max(Ls, Lp)
        if "ph2" not in PHASES:
            sb.phase()
            ns = {}
            ns["KT"] = [sb.alloc([96, LkMax], BF16) for _ in range(2)]
            ns["VP"] = [sb.alloc([128, LkMax // 128, 65], BF16) for _ in range(2)]
            ns["QT"] = [sb.alloc([96, max(Ls, Lq)], BF16) for _ in range(2)]
            ns["PT"] = [sb.alloc([128, TB], BF16) for _ in range(4)]
            ns["OSB"] = [sb.alloc([128, TB], F32) for _ in range(2)]
            ns["OTH"] = [sb.alloc([64, TB], BF16) for _ in range(2)]
            PHASES["ph2"] = ns
        ns = PHASES["ph2"]
        KT = ns["KT"]
        VP = ns["VP"]
        QT = ns["QT"]
        PT = ns["PT"]
        OSB = ns["OSB"]
        OTH = ns["OTH"]
        for sl in range(2):
            memset(VP[sl][:, :, 64:65], 1.0, [("VP", sl)])
        scale = 96 ** -0.5
        hcount = 0
        for s in range(2):
            Lk = Ls if s == 0 else Lp
            Lseg = Ls if s == 0 else Lq
            tbase = 0 if s == 0 else Ls
            nkb = Lk // 128
            nqb = Lseg // TB

            def att_loads(h, sl, s=s, Lk=Lk, Lseg=Lseg, tbase=tbase, nkb=nkb):
                if s == 0:
                    dma("sp", KT[sl][:, 0:Lk], kT_S[h * 96:(h + 1) * 96, :], [("qkdst", "k0")], [("KT", sl)], "ld_kt%d" % sl)
                    vsrc = v_S[:, h * 64:(h + 1) * 64].rearrange("(kb p) c -> p kb c", p=128)
                    dma("sp", VP[sl][:, 0:nkb, 0:64], vsrc, [("vdst", 0)], [("VP", sl)], "ld_vp%d" % sl)
                else:
                    for bq in range(nP):
                        ksrc = kT_Pall[bq].ap().rearrange("(r c) t -> c r t", r=4)[h * 96:(h + 1) * 96, :, :]
                        dma("sp", KT[sl][:, bq * 4 * TB:(bq + 1) * 4 * TB].rearrange("d (r t) -> d r t", r=4), ksrc,
                            [("gath", 0, bq)], [("KT", sl)], "ld_kt%d" % sl)
                        vsrc = v_Pall[bq][:, h * 64:(h + 1) * 64].rearrange("(kb p) c -> p kb c", p=128)
                        dma("sp", VP[sl][:, bq * 16:(bq + 1) * 16, 0:64], vsrc, [("gath", 1, bq)], [("VP", sl)], "ld_vp%d" % sl)
                dma("sp", QT[sl][:, 0:Lseg], qT_d[h * 96:(h + 1) * 96, tbase:tbase + Lseg], [("qkdst", "q")], [("QT", sl)], "ld_qt%d" % sl)

            att_loads(0, hcount % 2)
            for h in range(NH):
                sl = hcount % 2
                hcount += 1
                if h + 1 < NH:
                    att_loads(h + 1, hcount % 2)
                for qb in range(nqb):
                    po = 5 + (qb % 2)
                    steps = list(range(nkb))
                    spi = {}

                    def qk(kb, sl=sl, qb=qb):
                        pi = kb % 5
                        spi[kb] = pi
                        mm(PS[pi][:], [(KT[sl][:, kb * 128:(kb + 1) * 128], QT[sl][:, qb * TB:(qb + 1) * TB])],
                           [("KT", sl), ("QT", sl)], [("ps", pi)])

                    qk(0)
                    if nkb > 1:
                        qk(1)
                    for kb in steps:
                        pi = spi[kb]
                        pt_i = kb % 4
                        act(PT[pt_i][:], PS[pi][:], AF.Exp, [("ps", pi)], [("PT", pt_i)], scale=scale)
                        if kb + 2 < nkb:
                            qk(kb + 2)
                        T.op("pe", lambda e, po=po, sl=sl, kb=kb, pt_i=pt_i, nkb=nkb:
                             e.matmul(PS[po][0:65, :], VP[sl][:, kb, :], PT[pt_i][:], start=(kb == 0), stop=(kb == nkb - 1)),
                             reads=[("VP", sl), ("PT", pt_i)], writes=[("ps", po)])
                    ob = qb % 2
                    T.op("dve", lambda e, ob=ob, po=po: e.tensor_copy(out=OSB[ob][0:65, :], in_=PS[po][0:65, :]),
                         reads=[("ps", po)], writes=[("OSB", ob)])
                    recip(OSB[ob][64:65, :], OSB[ob][64:65, :], [("OSB", ob)], [("OSB", ob)])
                    mm(PS[7][0:64, :], [(ones_f[64:65, 0:64], OSB[ob][64:65, :])], [("OSB", ob), "ones_f"], [("ps", 7)])
                    tt(OTH[ob][:], OSB[ob][0:64, :], PS[7][0:64, :], ALU.mult, [("OSB", ob), ("ps", 7)], [("OTH", ob)])
                    tq = tbase + qb * TB
                    dma("pool", oT_d[h * 64:(h + 1) * 64, tq:tq + TB], OTH[ob][:], [("OTH", ob)], [("od",)], "st_o%d" % ob)
        T.barrier()
        if STOP == "attn":
            break

        if "ph3" not in PHASES:
            sb.phase()
            ns = {}
            ns["X"] = [sb.alloc([128, 8, TB], F32) for _ in range(2)]
            ns["H"] = [sb.alloc([128, 8, TB], BF16) for _ in range(2)]
            ns["OT"] = [sb.alloc([128, 4, TB], BF16) for _ in range(2)]
            ns["US"] = sb.alloc([128, 8, TB], BF16)
            ns["BIGF"] = sb.alloc([128, 4096], F32)
            ns["VM"] = sb.alloc([128, 4096], BF16)
            ns["SH"] = sb.alloc([128, 8, TB], BF16)
            ns["SGT"] = sb.alloc([128, 4, TB], F32)
            ns["ACTT"] = sb.alloc([128, 22, TB], BF16)
            ns["SQ"] = sb.alloc([128, 8, TB], BF16)
            ns["RS"] = sb.alloc([128, TB], F32)
            ns["WS"] = [sb.alloc([128, 4096], BF16) for _ in range(3)]
            ns["BSB"] = sb.alloc([128, 8, 128], F32)
            ns["WST"] = sb.alloc([128, 8, 128], BF16)
            ns["SSV"] = sb.alloc([128, 4], F32)
            ns["JUNK"] = sb.alloc([128, 1024], BF16)
            ns["TMP"] = [sb.alloc([128, TB], F32) for _ in range(2)]
            PHASES["ph3"] = ns
        ns = PHASES["ph3"]
        X = ns["X"]
        H = ns["H"]
        OT = ns["OT"]
        US = ns["US"]
        BIGF = ns["BIGF"]
        VM = ns["VM"]
        SH = ns["SH"]
        SGT = ns["SGT"]
        ACTT = ns["ACTT"]
        SQ = ns["SQ"]
        RS = ns["RS"]
        WS = ns["WS"]
        BSB = ns["BSB"]
        WST = ns["WST"]
        SSV = ns["SSV"]
        JUNK = ns["JUNK"]
        TMP = ns["TMP"]
        GV = BIGF[:].rearrange("p (a b) -> p a b", a=4)
        MT = BIGF[:].rearrange("p (a b) -> p a b", a=8)
        VH = VM[:].rearrange("p (a b) -> p a b", a=4)
        MG = VM[:].rearrange("p (a b) -> p a b", a=8)
        dma("sp", BSB[:], bsb_in[l], [], ["BSB"], "ld_misc")
        dma("pool", WST[:], wsT_in[l], [], ["WST"], "ld_cst")
        tmp_rr = [0]

        def tmp_next():
            i = tmp_rr[0]
            tmp_rr[0] = 1 - i
            return i

        ws_rr = [0]
        pending = []

        def p2_loads(i):
            b = i
            xs = i % 2
            t0 = tok0(b)
            dma("sp", X[xs][:], x_src[:, t0:t0 + TB].rearrange("(c p) t -> p c t", p=128), [("xd", b)],
                [(("X", xs), c) for c in range(8)], "ld_x%d" % xs)
            dma("sp", H[xs][:], hT_d[:, t0:t0 + TB].rearrange("(c p) t -> p c t", p=128), [("hd", b)],
                [(("H", xs), c) for c in range(8)], "ld_h%d" % xs)
            dma("sp", OT[xs][:], oT_d[:, t0:t0 + TB].rearrange("(c p) t -> p c t", p=128), [("od",)],
                [("OT", xs)], "ld_o%d" % xs)

        for b in range(NB):
            if b == 0:
                p2_loads(0)
            xs = b % 2
            s = seg(b)
            t0 = tok0(b)
            Xb, Hb, OTb = X[xs], H[xs], OT[xs]
            xkey, hkey = ("X", xs), ("H", xs)
            hreads = [(hkey, c) for c in range(8)]

            steps = []

            def add(name, c0, n, fn):
                steps.append((name, c0, n, WK[name] // 128, fn))

            def f_zu(W, wkey, g):
                for jj in range(4):
                    c = g * 4 + jj
                    pi = ps_next()
                    mm(PS[pi][:], [(W[:, kc, jj * 128:(jj + 1) * 128], Hb[:, kc, :]) for kc in range(8)], hreads + [wkey], [("ps", pi)])
                    act(US[:, c, :], PS[pi][:], AF.Gelu_apprx_tanh, [("ps", pi)], [("US", c)])
            for g in range(2):
                add("w_in", 672 + g * 512, 512, lambda W, wkey, g=g: f_zu(W, wkey, g))

            def f_zv(W, wkey, half):
                for t4 in range(4):
                    pi = ps_next()
                    mm(PS[pi][:], [(Hb[:, kc, t4 * 128:(t4 + 1) * 128], W[:, kc, :]) for kc in range(8)], hreads + [wkey], [("ps", pi)])
                    act(GV[:, t4, half * 512:(half + 1) * 512], PS[pi][:], AF.Gelu_apprx_tanh, [("ps", pi)], [("BIGF", 2 * t4 + half)])
                if half == 1:
                    memset(SSV[:], 0.0, [("SSV", t4) for t4 in range(4)])
                    for t4 in range(4):
                        act(JUNK[:], GV[:, t4, :], AF.Square, [("BIGF", 2 * t4), ("BIGF", 2 * t4 + 1)], ["JUNK", ("SSV", t4)],
                            accum_out=SSV[:, t4:t4 + 1])
                    act(SSV[:], SSV[:], AF.Sqrt, [("SSV", t4) for t4 in range(4)] + ["vecs"], [("SSV", t4) for t4 in range(4)],
                        bias=eps_ap, scale=1.0 / D)
                    recip(SSV[:], SSV[:], [("SSV", t4) for t4 in range(4)], [("SSV", t4) for t4 in range(4)])
                    for t4 in range(4):
                        ts(VH[:, t4, :], GV[:, t4, :], SSV[:, t4:t4 + 1], None, ALU.mult, None,
                           [("BIGF", 2 * t4), ("BIGF", 2 * t4 + 1), ("SSV", t4)], [("VM", 2 * t4), ("VM", 2 * t4 + 1)])
                    for g in range(8):
                        pi = ps_next()

                        def fmix(e, pi=pi, g=g):
                            ins = None
                            for t4 in range(4):
                                ins = e.matmul(PS[pi][:, t4 * 128:(t4 + 1) * 128], VH[:, t4, g * 128:(g + 1) * 128], WST[:, g, :],
                                               start=True, stop=True)
                            return ins
                        T.op("pe", fmix, reads=[("VM", k) for k in range(8)] + ["WST"], writes=[("ps", pi)])
                        ti = tmp_next()
                        for t4 in range(4):
                            stt(TMP[ti][:, t4 * 128:(t4 + 1) * 128], PS[pi][:, t4 * 128:(t4 + 1) * 128], vcol(V_SGUG, l * 8 + g),
                                BSB[:, g, :], ALU.mult, ALU.add, [("ps", pi), "BSB", "vecs"], [("TMP", ti)])
                        tt(SH[:, g, :], TMP[ti][:], US[:, g, :], ALU.mult, [("TMP", ti), ("US", g)], [("SH", g)], eng="pool")
            for half in range(2):
                add("w_in", 1696 + half * 512, 512, lambda W, wkey, half=half: f_zv(W, wkey, half))

            def f_gate(W, wkey, g):
                for jj in range(4):
                    c = g * 4 + jj
                    pi = ps_next()
                    mm(PS[pi][:], [(W[:, kc, jj * 128:(jj + 1) * 128], Hb[:, kc, :]) for kc in range(8)], hreads + [wkey], [("ps", pi)])
                    act(US[:, c, :], PS[pi][:], AF.Sigmoid, [("ps", pi)], [("US", c)])

            def f_ob(W, wkey, g):
                for jj in range(4):
                    c = g * 4 + jj
                    pi = ps_next()
                    mm(PS[pi][:], [(W[:, kc, jj * 128:(jj + 1) * 128], SH[:, kc, :]) for kc in range(8)],
                       [("SH", k) for k in range(8)] + [wkey], [("ps", pi)])
                    tt(MT[:, c, :], PS[pi][:], US[:, c, :], ALU.mult, [("ps", pi), ("US", c)], [("BIGF", c)])

            def f_oa(W, wkey):
                for c in range(8):
                    pi = ps_next()
                    mm(PS[pi][:], [(W[:, kc, c * 128:(c + 1) * 128], OTb[:, kc, :]) for kc in range(4)], [("OT", xs), wkey], [("ps", pi)])
                    ti = tmp_next()
                    tt(TMP[ti][:], PS[pi][:], US[:, c, :], ALU.mult, [("ps", pi), ("US", c)], [("TMP", ti)])
                    tt(MG[:, c, :], TMP[ti][:], MT[:, c, :], ALU.add, [("TMP", ti), ("BIGF", c)], [("VM", c)], eng="pool")

            def f_out(W, wkey, g):
                for jj in range(4):
                    c = g * 4 + jj
                    pi = ps_next()
                    mm(PS[pi][:], [(W[:, kc, jj * 128:(jj + 1) * 128], MG[:, kc, :]) for kc in range(8)],
                       [("VM", k) for k in range(8)] + [wkey], [("ps", pi)])
                    stt(Xb[:, c, :], PS[pi][:], mod_ap(l, 2, c, s), Xb[:, c, :], ALU.mult, ALU.add,
                        [("ps", pi), (xkey, c), ("mods", l, 16 + c)], [(xkey, c)])
                if g == 1:
                    norm_block(Xb, xkey, SH, "SH", SQ, RS, A2, l, s, 3, TMP)

            for g in range(2):
                add("w_in", 3744 + g * 512, 512, lambda W, wkey, g=g: f_gate(W, wkey, g))
            for g in range(2):
                add("w_o_b", g * 512, 512, lambda W, wkey, g=g: f_ob(W, wkey, g))
            for g in range(2):
                add("w_in", 2720 + g * 512, 512, lambda W, wkey, g=g: f_gate(W, wkey, g))
            add("w_o_a", 0, 1024, lambda W, wkey: f_oa(W, wkey))
            for g in range(2):
                add("w_out", g * 512, 512, lambda W, wkey, g=g: f_out(W, wkey, g))

            h2reads = [("SH", k) for k in range(8)]

            def f_gatef(W, wkey, j0, nch):
                for jj in range(nch):
                    pi = ps_next()
                    mm(PS[pi][:], [(W[:, kc, jj * 128:(jj + 1) * 128], SH[:, kc, :]) for kc in range(8)], h2reads + [wkey], [("ps", pi)])
                    act(SGT[:, jj, :], PS[pi][:], AF.Silu, [("ps", pi)], [("SGT", jj)])

            def f_upf(W, wkey, j0, nch):
                for jj in range(nch):
                    pi = ps_next()
                    mm(PS[pi][:], [(W[:, kc, jj * 128:(jj + 1) * 128], SH[:, kc, :]) for kc in range(8)], h2reads + [wkey], [("ps", pi)])
                    tt(ACTT[:, j0 + jj, :], PS[pi][:], SGT[:, jj, :], ALU.mult, [("ps", pi), ("SGT", jj)], [("ACTT", j0 + jj)])

            for j0 in range(0, 22, 4):
                nch = min(4, 22 - j0)
                add("w_ffn_in", DFF + j0 * 128, nch * 128, lambda W, wkey, j0=j0, nch=nch: f_gatef(W, wkey, j0, nch))
                add("w_ffn_in", j0 * 128, nch * 128, lambda W, wkey, j0=j0, nch=nch: f_upf(W, wkey, j0, nch))

            def f_fo(W, wkey, c):
                pi = ps_next()
                mm(PS[pi][:], [(W[:, kc, :], ACTT[:, kc, :]) for kc in range(22)], [("ACTT", k) for k in range(22)] + [wkey], [("ps", pi)])
                stt(Xb[:, c, :], PS[pi][:], mod_ap(l, 5, c, s), Xb[:, c, :], ALU.mult, ALU.add,
                    [("ps", pi), (xkey, c), ("mods", l, 40 + c)], [(xkey, c)])
            for c in range(8):
                add("w_ffn_out", c * 128, 128, lambda W, wkey, c=c: f_fo(W, wkey, c))

            nst = len(steps)
            views = {}

            def issue_load(k):
                name, c0, n, kcn, fn = steps[k]
                sl = ws_rr[0]
                ws_rr[0] = (sl + 1) % 3
                view = WS[sl][:, 0:kcn * n].rearrange("p (k n) -> p k n", k=kcn)
                dma("sp", view, wsrc(name, l, c0, n), wkeys(name, l), [("WS", sl)], "ld_ws%d" % sl)
                views[k] = (view, ("WS", sl))

            issue_load(0)
            issue_load(1)
            for k in range(nst):
                if k + 2 < nst:
                    issue_load(k + 2)
                if k == 4 and b + 1 < NB:
                    p2_loads(b + 1)
                view, wkey = views[k]
                steps[k][4](view, wkey)
            dma("pool", x_dst[:, t0:t0 + TB].rearrange("(c p) t -> p c t", p=128), Xb[:],
                [(xkey, c) for c in range(8)], [("xd", b)], "st_x%d" % xs)
        T.barrier()

    with nc.Block() as block:
        @block.tensor
        def _(e):
            T.replay("pe", e)

        @block.scalar
        def _(e):
            T.replay("act", e)

        @block.vector
        def _(e):
            T.replay("dve", e)

        @block.gpsimd
        def _(e):
            T.replay("pool", e)

        @block.sync
        def _(e):
            T.replay("sp", e)
    es.close()
    return nc


def rope_tables(pos):
    inv = (10000.0 ** (-np.arange(0, 32, 2, dtype=np.float32) / np.float32(32))).astype(np.float32)
    ang = pos.astype(np.float32)[:, None] * inv[None, :]
    return np.cos(ang).astype(np.float32), np.sin(ang).astype(np.float32)


def make_consts():
    k = np.arange(128)[:, None]
    m = np.arange(128)[None, :]
    ones = np.ones((128, 128), np.float32)
    nn = (k // 64 == m // 64)
    rn = [(k // 32 == 2 * i + m // 64) for i in range(2)]
    nr = [(2 * i + k // 64 == m // 32) for i in range(2)]
    rr = (k // 32 == m // 32)
    mats = [ones, nn, rn[0], rn[1], nr[0], nr[1], rr]
    return np.ascontiguousarray(np.stack([np.asarray(x, np.float32) for x in mats], axis=1))


def prepare_inputs(cfg, inp):
    Ls, Lq, L = cfg["Ls"], cfg["Lq"], cfg["L"]
    voff, NV = vec_layout(L)
    f = lambda a: np.ascontiguousarray(np.asarray(a, dtype=np.float32))
    w_in = f(inp["w_in"])[:L]
    w_q_b = f(inp["w_q_b"])[:L]
    w_kv_b = f(inp["w_kv_b"])[:L]
    shared = {}
    shared["w_in"] = w_in.reshape(L * D, INC)
    kr = w_in[:, :, 640:672]
    krrot = np.concatenate([kr[:, :, 16:32], kr[:, :, 0:16]], axis=2)
    shared["w_kr4"] = f(np.tile(kr, (1, 1, 4))).reshape(L * D, 128)
    shared["w_krrot4"] = f(np.tile(krrot, (1, 1, 4))).reshape(L * D, 128)
    wq = w_q_b.reshape(L, QL, NH, 96)
    shared["wq_nope"] = f(wq[:, :, :, 0:64]).reshape(L * QL, 512)
    shared["wq_rope"] = f(wq[:, :, :, 64:96]).reshape(L * QL, 256)
    shared["wq_rot"] = f(np.concatenate([wq[:, :, :, 80:96], wq[:, :, :, 64:80]], axis=3)).reshape(L * QL, 256)
    wkv = w_kv_b.reshape(L, KVL, NH, 128)
    shared["wkv_nope"] = f(wkv[:, :, :, 0:64]).reshape(L * KVL, 512)
    shared["wkv_v"] = f(wkv[:, :, :, 64:128]).reshape(L * KVL, 512)
    shared["w_o_a"] = f(inp["w_o_a"])[:L].reshape(L * 512, D)
    shared["w_o_b"] = f(inp["w_o_b"])[:L].reshape(L * D, D)
    shared["w_out"] = f(inp["w_out"])[:L].reshape(L * D, D)
    shared["w_ffn_in"] = f(inp["w_ffn_in"])[:L].reshape(L * D, 2 * DFF)
    shared["w_ffn_out"] = f(inp["w_ffn_out"])[:L].reshape(L * DFF, D)
    shared["w_ada"] = f(inp["w_ada"])[:L].reshape(L * D, 6 * D)
    shared["w_sT"] = f(np.transpose(f(inp["w_s"])[:L], (0, 3, 1, 2)))
    shared["bsb"] = f(np.broadcast_to(f(inp["b_s"])[:L][:, None, :, :], (L, 128, 8, 128)))
    shared["consts"] = make_consts()
    vecs = np.zeros((128, NV), np.float32)
    vecs[:, voff[V_EPS]] = EPS
    p = np.arange(128)

    def fm(a, n):
        return f(a)[:L].reshape(L, n, 128).transpose(2, 0, 1).reshape(128, L * n)
    vecs[:, voff[V_N1G]:voff[V_N1G] + 8 * L] = fm(inp["norm1_g"], 8)
    vecs[:, voff[V_N2G]:voff[V_N2G] + 8 * L] = fm(inp["norm2_g"], 8)
    vecs[:, voff[V_SGUG]:voff[V_SGUG] + 8 * L] = fm(inp["sgu_norm_g"], 8)
    vecs[:, voff[V_QAG]:voff[V_QAG] + 3 * L] = fm(inp["q_a_norm_g"], 3)
    vecs[:, voff[V_KVAG]:voff[V_KVAG] + 2 * L] = fm(inp["kv_a_norm_g"], 2)
    vecs[:, voff[V_BADA]:voff[V_BADA] + 48 * L] = fm(inp["b_ada"], 48)
    qg = f(inp["q_norm_g"])[:L]
    kg = f(inp["k_norm_g"])[:L]
    for (g, vn, vr, vrp) in ((qg, V_GQN, V_GQR, V_GQRP), (kg, V_GKN, V_GKR, V_GKRP)):
        vecs[:, voff[vn]:voff[vn] + L] = g[:, p % 64].T
        vecs[:, voff[vr]:voff[vr] + L] = g[:, 64 + p % 32].T
        vecs[:, voff[vrp]:voff[vrp] + L] = g[:, 64 + (p % 32 + 16) % 32].T
    shared["vecs"] = vecs
    xs = f(inp["x_sample"])
    xp = f(inp["x_prompt"])
    cs = f(inp["c_sample"])
    cp = f(inp["c_prompt"])
    fidx = (p % 32) % 16
    sgn = np.where((p % 32) < 16, -1.0, 1.0).astype(np.float32)
    in_maps = []
    for c in range(NCORES):
        ps_, r = c // 4, c % 4
        m = dict(shared)
        m["xT"] = np.ascontiguousarray(np.concatenate([xs[c, :Ls].T, xp[ps_, r * Lq:(r + 1) * Lq].T], axis=1))
        m["cT"] = np.ascontiguousarray(np.stack([cs[c].reshape(8, 128).T, cp[ps_].reshape(8, 128).T], axis=2))
        pos = np.concatenate([np.arange(Ls), r * Lq + np.arange(Lq)])
        cos, sin = rope_tables(pos)
        m["cos4"] = np.ascontiguousarray(cos.T[fidx, :])
        m["sin4"] = np.ascontiguousarray(sin.T[fidx, :] * sgn[:, None])
        in_maps.append(m)
    return in_maps


_PROG_CACHE = {}


def run(cfg, inp):
    key = (cfg["Ls"], cfg["Lq"], cfg["L"], cfg.get("stop"), cfg.get("no_cc"), cfg.get("no_pool"), cfg.get("no_qkstore"))
    if key not in _PROG_CACHE:
        _PROG_CACHE[key] = build_program(cfg)
    nc = _PROG_CACHE[key]
    in_maps = prepare_inputs(cfg, inp)
    res = run_bass_kernel_spmd(nc, in_maps, core_ids=list(range(NCORES)))
    Ls, Lq = cfg["Ls"], cfg["Lq"]
    ys = np.stack([res.results[c]["yT"][:, :Ls].T for c in range(NCORES)], axis=0)
    yp = np.stack([np.concatenate([res.results[4 * s_ + r]["yT"][:, Ls:].T for r in range(4)], axis=0) for s_ in range(2)], axis=0)
    return np.ascontiguousarray(yp, dtype=np.float32), np.ascontiguousarray(ys, dtype=np.float32)


def kernel(**inputs):
    return run(FULL_CFG, inputs)
```

```python
import numpy as np
from contextlib import ExitStack
import concourse.bass as bass
import concourse.mybir as mybir
from concourse.bass_utils import run_bass_kernel_spmd

F32 = mybir.dt.float32
BF16 = mybir.dt.bfloat16
AF = mybir.ActivationFunctionType
ALU = mybir.AluOpType

D = 1024
NH = 8
QL = 384
KVL = 256
DFF = 2816
INC = 4768
EPS = 1e-6
TB = 512
NCORES = 8
SAME_SYNC = True

FULL_CFG = dict(Ls=2048, Lq=2048, L=4)


class Tracker:
    COMPUTE = ("pe", "act", "dve", "pool")

    def __init__(self, nc, es):
        self.nc = nc
        self.es = es
        self.ops = {e: [] for e in ("pe", "act", "dve", "pool", "sp")}
        self.cnt = {e: 0 for e in self.COMPUTE}
        self.esem = {e: es.enter_context(nc.semaphore("c_" + e)) for e in self.COMPUTE}
        self.dsem = {}
        self.seen = {e: {} for e in self.ops}
        self.lastw = {}
        self.readers = {}
        self.nops = 0
        self.no_pool = False

    def _dsem(self, key):
        if key not in self.dsem:
            self.dsem[key] = [self.es.enter_context(self.nc.semaphore("d%d" % len(self.dsem))), 0]
        return self.dsem[key]

    def _resolve(self, tok):
        if tok[0] == "c":
            return self.esem[tok[1]], tok[2], ("c", tok[1])
        d = self._dsem(tok[1])
        return d[0], d[1], ("d", tok[1])

    def op(self, eng, fn, reads=(), writes=(), dma=None, inc=None):
        if eng == "pool" and dma is None and self.no_pool:
            eng = "dve"
        deps = set()
        for r in reads:
            if r in self.lastw:
                deps.add(self.lastw[r])
            if isinstance(r, tuple) and r[0] == "ps":
                for k, v in self.readers.get(r, {}).items():
                    if k != ("c", eng):
                        deps.add(k + (v,))
        for w in writes:
            if w in self.lastw:
                deps.add(self.lastw[w])
            for k, v in self.readers.get(w, {}).items():
                deps.add(k + (v,))
        waits = []
        for tok in deps:
            if tok[0] == "c" and tok[1] == eng and dma is None and (eng == "pe" or not SAME_SYNC):
                continue
            sem, val, sid = self._resolve(tok)
            if self.seen[eng].get(sid, 0) >= val:
                continue
            self.seen[eng][sid] = val
            waits.append((sem, val))
        if dma is not None:
            d = self._dsem(dma)
            step = 16 if inc is None else inc
            d[1] += step
            tok = ("d", dma, d[1])
            incr = (d[0], step)
        else:
            self.cnt[eng] += 1
            tok = ("c", eng, self.cnt[eng])
            incr = (self.esem[eng], 1)
        self.ops[eng].append((waits, fn, incr))
        self.nops += 1
        for r in reads:
            self.readers.setdefault(r, {})[tok[:2]] = tok[2]
        for w in writes:
            self.lastw[w] = tok
            self.readers[w] = {}
        return tok

    def barrier(self):
        for eng in self.ops:
            waits = []
            for f in self.COMPUTE:
                if f == eng or self.cnt[f] == 0:
                    continue
                if self.seen[eng].get(("c", f), 0) < self.cnt[f]:
                    self.seen[eng][("c", f)] = self.cnt[f]
                    waits.append((self.esem[f], self.cnt[f]))
            for key, d in self.dsem.items():
                if d[1] and self.seen[eng].get(("d", key), 0) < d[1]:
                    self.seen[eng][("d", key)] = d[1]
                    waits.append((d[0], d[1]))
            if waits:
                self.ops[eng].append((waits, None, None))
        self.lastw = {}
        self.readers = {}

    def replay(self, name, e):
        for waits, fn, incr in self.ops[name]:
            for sem, val in waits:
                e.wait_ge(sem, val)
            if fn is not None:
                ins = fn(e)
                ins.then_inc(incr[0], incr[1])


class SbAlloc:
    BASE = 16512
    TOP = 229344

    def __init__(self, nc):
        self.nc = nc
        self.persist = self.BASE
        self.cur = self.BASE
        self.n = 0

    def _al(self, shape, dt, off):
        self.n += 1
        return self.nc.alloc_sbuf_tensor_at("sb%d" % self.n, list(shape), dt, offset=off)

    def alloc(self, shape, dt, persistent=False):
        esz = 4 if dt == F32 else 2
        nbytes = int(np.prod(shape[1:])) * esz
        nbytes = (nbytes + 31) // 32 * 32
        if persistent:
            assert self.cur == self.persist, "persistent allocs must come first"
            off = self.persist
            self.persist += nbytes
            self.cur = self.persist
        else:
            off = self.cur
            self.cur += nbytes
        assert self.cur <= self.TOP, "SBUF overflow %d" % self.cur
        return self._al(shape, dt, off)

    def phase(self):
        self.cur = self.persist


W_SPECS = [
    ("w_in", 1024, INC, "A"), ("w_kr4", 1024, 128, "A"), ("w_krrot4", 1024, 128, "A"),
    ("wq_nope", QL, 512, "A"), ("wq_rope", QL, 256, "A"), ("wq_rot", QL, 256, "A"),
    ("wkv_nope", KVL, 512, "A"), ("wkv_v", KVL, 512, "A"),
    ("w_o_a", 512, D, "B"), ("w_o_b", D, D, "B"), ("w_out", D, D, "B"),
    ("w_ffn_in", D, 2 * DFF, "B"), ("w_ffn_out", DFF, D, "B"),
]
V_EPS, V_N1G, V_N2G, V_SGUG, V_QAG, V_KVAG, V_GQN, V_GQR, V_GQRP, V_GKN, V_GKR, V_GKRP, V_BADA = range(13)


def vec_layout(L):
    off = {}
    o = 0
    for name, n in [(V_EPS, 1), (V_N1G, 8 * L), (V_N2G, 8 * L), (V_SGUG, 8 * L), (V_QAG, 3 * L), (V_KVAG, 2 * L),
                    (V_GQN, L), (V_GQR, L), (V_GQRP, L), (V_GKN, L), (V_GKR, L), (V_GKRP, L), (V_BADA, 48 * L)]:
        off[name] = o
        o += n
    return off, o


def build_program(cfg):
    Ls, Lq, L = cfg["Ls"], cfg["Lq"], cfg["L"]
    Lp = 4 * Lq
    NTOK = Ls + Lq
    nS, nP = Ls // TB, Lq // TB
    NB = nS + nP
    voff, NV = vec_layout(L)

    nc = bass.Bass("TRN2", target_bir_lowering=False)

    def din(name, shape, dt=F32):
        return nc.dram_tensor(name, list(shape), dt, kind="ExternalInput").ap()

    xT_in = din("xT", [D, NTOK])
    cT_in = din("cT", [128, 8, 2])
    cos_in = din("cos4", [128, NTOK])
    sin_in = din("sin4", [128, NTOK])
    vecs_in = din("vecs", [128, NV])
    consts_in = din("consts", [128, 7, 128])
    w_ada_in = din("w_ada", [L * D, 6 * D])
    wsT_in = din("w_sT", [L, 128, 8, 128])
    bsb_in = din("bsb", [L, 128, 8, 128])
    w_ext = {name: din(name, [L * K, N]) for name, K, N, _ in W_SPECS}
    yT_out = nc.dram_tensor("yT", [D, NTOK], F32, kind="ExternalOutput").ap()

    def dscr(name, shape, dt):
        return nc.dram_tensor(name, list(shape), dt)

    w_bf = {name: dscr(name + "_b", [L * K, N], BF16) for name, K, N, _ in W_SPECS}
    WK = {name: K for name, K, N, _ in W_SPECS}
    xT_d = dscr("xT_d", [D, NTOK], F32)
    hT_d = dscr("hT_d", [D, NTOK], BF16)
    qT_d = dscr("qT_d", [NB * 768, TB], BF16)
    oT_d = dscr("oT_d", [512, NTOK], BF16)
    kT_S = dscr("kT_S", [nS * 768, TB], BF16)
    v_S = dscr("v_S", [Ls, 512], BF16)
    kT_Pin = [dscr("kT_Pin%d" % b, [NH * 96, TB], BF16) for b in range(nP)]
    kT_Pall = [dscr("kT_Pall%d" % b, [4 * NH * 96, TB], BF16) for b in range(nP)]
    v_Pin = [dscr("v_Pin%d" % b, [TB, 512], BF16) for b in range(nP)]
    v_Pall = [dscr("v_Pall%d" % b, [4 * TB, 512], BF16) for b in range(nP)]

    es = ExitStack()
    T = Tracker(nc, es)
    T.no_pool = bool(cfg.get("no_pool"))
    sb = SbAlloc(nc)
    PS = [es.enter_context(nc.psum_tensor("ps%d" % i, [128, 512], F32)) for i in range(8)]
    ps_rr = [0]

    def ps_next():
        i = ps_rr[0]
        ps_rr[0] = (i + 1) % 8
        return i

    def dma(eng, out, in_, reads, writes, key):
        T.op(eng, lambda e, o=out, i=in_: e.dma_start(out=o, in_=i), reads=reads, writes=writes, dma=key)

    def mm(ps_ap, pairs, reads, writes):
        def fn(e, ps_ap=ps_ap, pairs=pairs):
            n = len(pairs)
            ins = None
            for i, (l, r) in enumerate(pairs):
                ins = e.matmul(ps_ap, l, r, start=(i == 0), stop=(i == n - 1))
            return ins
        T.op("pe", fn, reads=reads, writes=writes)

    def act(out, in_, func, reads, writes, **kw):
        T.op("act", lambda e, o=out, i=in_, f=func, kw=kw: e.activation(out=o, in_=i, func=f, **kw), reads=reads, writes=writes)

    def tt(out, in0, in1, op, reads, writes, eng="dve"):
        T.op(eng, lambda e, o=out, a=in0, b=in1, op=op: e.tensor_tensor(out=o, in0=a, in1=b, op=op), reads=reads, writes=writes)

    def stt(out, in0, scalar, in1, op0, op1, reads, writes, eng="dve"):
        T.op(eng, lambda e, o=out, a=in0, s=scalar, b=in1, p0=op0, p1=op1:
             e.scalar_tensor_tensor(out=o, in0=a, scalar=s, in1=b, op0=p0, op1=p1), reads=reads, writes=writes)

    def ts(out, in0, s1, s2, op0, op1, reads, writes, eng="dve"):
        if s2 is None:
            T.op(eng, lambda e, o=out, a=in0, s1=s1, p0=op0: e.tensor_scalar(out=o, in0=a, scalar1=s1, scalar2=None, op0=p0),
                 reads=reads, writes=writes)
        else:
            T.op(eng, lambda e, o=out, a=in0, s1=s1, s2=s2, p0=op0, p1=op1:
                 e.tensor_scalar(out=o, in0=a, scalar1=s1, scalar2=s2, op0=p0, op1=p1), reads=reads, writes=writes)

    def recip(out, in_, reads, writes):
        T.op("dve", lambda e, o=out, i=in_: e.reciprocal(out=o, in_=i), reads=reads, writes=writes)

    def memset(ap, val, writes, eng="dve"):
        T.op(eng, lambda e, a=ap, v=val: e.memset(a, v), writes=writes)

    vecs = sb.alloc([128, NV], F32, True)
    cst = sb.alloc([128, 7, 128], BF16, True)
    ones_f = sb.alloc([128, 64], F32, True)
    mods = sb.alloc([128, L, 48, 2], F32, True)
    A1 = sb.alloc([128, L, 2, 8], F32, True)
    A2 = sb.alloc([128, L, 2, 8], F32, True)
    silc = sb.alloc([128, 8, 2], BF16, True)
    eps_ap = vecs[:, voff[V_EPS]:voff[V_EPS] + 1]

    def vcol(kind, idx):
        o = voff[kind] + idx
        return vecs[:, o:o + 1]

    ONES, NN, RN0, RN1, NR0, NR1, RR = [cst[:, i, :] for i in range(7)]
    RN = [RN0, RN1]
    NR = [NR0, NR1]

    dma("sp", vecs[:], vecs_in, [], ["vecs"], "ld_misc")
    dma("pool", cst[:], consts_in, [], ["cst"], "ld_cst")
    memset(ones_f[:], 1.0, ["ones_f"])

    def cast_weights(l, grp):
        for name, K, N, g in W_SPECS:
            if g != grp:
                continue
            for rb in range(0, K, 128):
                r0 = l * K + rb
                dma("pool", w_bf[name][r0:r0 + 128, :], w_ext[name][r0:r0 + 128, :], [], [("wb", name, l, rb // 128)],
                    "cast%s%d" % (grp, l))

    def wkeys(name, l):
        return [("wb", name, l, i) for i in range(WK[name] // 128)]

    def wsrc(name, l, c0, n):
        K = WK[name]
        return w_bf[name][l * K:(l + 1) * K, c0:c0 + n].rearrange("(kc p) n -> p kc n", p=128)

    cast_weights(0, "A")

    cTt = sb.alloc([128, 8, 2], F32)
    WA = [sb.alloc([128, 8, 512], BF16) for _ in range(2)]
    dma("sp", cTt[:], cT_in, [], ["cTt"], "ld_misc")
    act(silc[:], cTt[:], AF.Silu, ["cTt"], ["silc"])
    gi = 0
    for l in range(L):
        for jg in range(12):
            s = gi % 2
            gi += 1
            src = w_ada_in[l * D:(l + 1) * D, jg * 512:(jg + 1) * 512].rearrange("(kc p) n -> p kc n", p=128)
            dma("pool", WA[s][:], src, [], [("WA", s)], "ld_wa%d" % s)
            for jj in range(4):
                j = jg * 4 + jj
                pi = ps_next()
                mm(PS[pi][:, 0:2], [(WA[s][:, kc, jj * 128:(jj + 1) * 128], silc[:, kc, :]) for kc in range(8)],
                   [("WA", s), "silc"], [("ps", pi)])
                ts(mods[:, l, j, :], PS[pi][:, 0:2], vcol(V_BADA, l * 48 + j), None, ALU.add, None,
                   [("ps", pi), "vecs"], [("mods", l, j)])
        for s in range(2):
            for (Ax, gk, m0) in ((A1, V_N1G, 8), (A2, V_N2G, 32)):
                ts(Ax[:, l, s, :], mods[:, l, m0:m0 + 8, s], 1.0, None, ALU.add, None,
                   [("mods", l, j) for j in range(m0, m0 + 8)], [("A", id(Ax), l, s)])
                tt(Ax[:, l, s, :], Ax[:, l, s, :], vecs[:, voff[gk] + 8 * l:voff[gk] + 8 * l + 8], ALU.mult,
                   [("A", id(Ax), l, s), "vecs"], [("A", id(Ax), l, s)])
        if l == 0:
            cast_weights(0, "B")
    for l in range(1, L):
        cast_weights(l, "A")
        cast_weights(l, "B")
    T.barrier()
    STOP = cfg.get("stop")

    def mod_ap(l, m, c, s):
        return mods[:, l, m * 8 + c, s:s + 1]

    def tok0(b):
        return b * TB

    def seg(b):
        return 0 if b < nS else 1

    def rstd_from_psum(rs_ap, pi, n, rs_key):
        act(rs_ap, PS[pi][:], AF.Ln, [("ps", pi), "vecs"], [rs_key], bias=eps_ap, scale=1.0 / n)
        act(rs_ap, rs_ap, AF.Exp, [rs_key], [rs_key], scale=-0.5)

    def rms_stats(sq_list, lhs_list, n, rs_ap, rs_key, reads):
        pi = ps_next()
        mm(PS[pi][:], list(zip(lhs_list, sq_list)), reads + ["cst"], [("ps", pi)])
        rstd_from_psum(rs_ap, pi, n, rs_key)

    def norm_block(Xb, xkey, Hb, hkey, SQ, RS, Ax, l, s, shm, NT):
        for c in range(8):
            tt(SQ[:, c, :], Xb[:, c, :], Xb[:, c, :], ALU.mult, [(xkey, c)], [("SQ", c)], eng="pool")
        rms_stats([SQ[:, c, :] for c in range(8)], [ONES] * 8, D, RS[:], "RS", [("SQ", c) for c in range(8)])
        for c in range(8):
            i = c % 2
            tt(NT[i][:], Xb[:, c, :], RS[:], ALU.mult, [(xkey, c), "RS"], [("NT", i)])
            act(Hb[:, c, :], NT[i][:], AF.Identity, [("NT", i), ("A", id(Ax), l, s), ("mods", l, shm * 8 + c)], [(hkey, c)],
                scale=Ax[:, l, s, c:c + 1], bias=mod_ap(l, shm, c, s))

    PHASES = {}
    for l in range(L if STOP != "prologue" else 0):
        x_src = xT_in if l == 0 else xT_d
        x_dst = yT_out if l == L - 1 else xT_d
        last = (l == L - 1)

        if "ph1" not in PHASES:
            sb.phase()
            ns = {}
            ns["X"] = [sb.alloc([128, 8, TB], F32) for _ in range(2)]
            ns["H"] = [sb.alloc([128, 8, TB], BF16) for _ in range(2)]
            ns["SQ"] = sb.alloc([128, 8, TB], BF16)
            ns["RS"] = sb.alloc([128, TB], F32)
            ns["W1in"] = sb.alloc([128, 8, 640], BF16)
            ns["Wkr"] = sb.alloc([128, 8, 128], BF16)
            ns["Wkrr"] = sb.alloc([128, 8, 128], BF16)
            ns["Wqn"] = sb.alloc([128, 3, 512], BF16)
            ns["Wqr"] = sb.alloc([128, 3, 256], BF16)
            ns["Wqt"] = sb.alloc([128, 3, 256], BF16)
            ns["Wkn"] = sb.alloc([128, 2, 512], BF16)
            ns["Wv"] = sb.alloc([128, 2, 512], BF16)
            ns["QN"] = sb.alloc([128, 3, TB], BF16)
            ns["KVN"] = sb.alloc([128, 2, TB], BF16)
            ns["QO"] = [sb.alloc([128, 6, TB], BF16) for _ in range(2)]
            ns["KO"] = [sb.alloc([128, 6, TB], BF16) for _ in range(2)]
            ns["VO"] = [sb.alloc([128, 4, 512], BF16) for _ in range(2)]
            ns["TK"] = sb.alloc([128, TB], F32)
            ns["SQRK"] = sb.alloc([128, TB], BF16)
            ns["COS"] = [sb.alloc([128, TB], F32) for _ in range(2)]
            ns["SIN"] = [sb.alloc([128, TB], F32) for _ in range(2)]
            ns["CQ"] = sb.alloc([128, TB], F32)
            ns["SQT"] = sb.alloc([128, TB], F32)
            ns["CK"] = sb.alloc([128, TB], F32)
            ns["SK"] = sb.alloc([128, TB], F32)
            ns["SQH"] = [sb.alloc([128, TB], BF16) for _ in range(3)]
            ns["RSH"] = [sb.alloc([128, TB], F32) for _ in range(3)]
            ns["T1"] = sb.alloc([128, TB], F32)
            ns["T2"] = sb.alloc([128, TB], F32)
            ns["NTT"] = [sb.alloc([128, TB], F32) for _ in range(2)]
            PHASES["ph1"] = ns
        ns = PHASES["ph1"]
        X = ns["X"]
        H = ns["H"]
        SQ = ns["SQ"]
        RS = ns["RS"]
        W1in = ns["W1in"]
        Wkr = ns["Wkr"]
        Wkrr = ns["Wkrr"]
        Wqn = ns["Wqn"]
        Wqr = ns["Wqr"]
        Wqt = ns["Wqt"]
        Wkn = ns["Wkn"]
        Wv = ns["Wv"]
        QN = ns["QN"]
        KVN = ns["KVN"]
        QO = ns["QO"]
        KO = ns["KO"]
        VO = ns["VO"]
        TK = ns["TK"]
        SQRK = ns["SQRK"]
        COS = ns["COS"]
        SIN = ns["SIN"]
        CQ = ns["CQ"]
        SQT = ns["SQT"]
        CK = ns["CK"]
        SK = ns["SK"]
        SQH = ns["SQH"]
        RSH = ns["RSH"]
        T1 = ns["T1"]
        T2 = ns["T2"]
        NTT = ns["NTT"]

        for (buf, name, c0, n) in ((W1in, "w_in", 0, 640), (Wkr, "w_kr4", 0, 128), (Wkrr, "w_krrot4", 0, 128),
                                   (Wqn, "wq_nope", 0, 512), (Wqr, "wq_rope", 0, 256), (Wqt, "wq_rot", 0, 256),
                                   (Wkn, "wkv_nope", 0, 512), (Wv, "wkv_v", 0, 512)):
            dma("sp", buf[:], wsrc(name, l, c0, n), wkeys(name, l), [("W1", name)], "ld_w1")
        W1R = [("W1", n) for n in ("w_in", "w_kr4", "w_krrot4", "wq_nope", "wq_rope", "wq_rot", "wkv_nope", "wkv_v")]

        order = list(range(nS, NB)) + list(range(nS))

        def p1_loads(i):
            b = order[i]
            xs = i % 2
            t0 = tok0(b)
            dma("sp", X[xs][:], x_src[:, t0:t0 + TB].rearrange("(c p) t -> p c t", p=128), [("xd", b)],
                [(("X", xs), c) for c in range(8)], "ld_x%d" % xs)
            dma("sp", COS[xs][:], cos_in[:, t0:t0 + TB], [], [("COS", xs)], "ld_cs%d" % xs)
            dma("sp", SIN[xs][:], sin_in[:, t0:t0 + TB], [], [("SIN", xs)], "ld_cs%d" % xs)

        def head_half(hh, PSn, PSr_sq_key, sqr_ap, rope_src_fn, OUT, okey, gn, kdst_fn, rope_keys):
            for i in range(2):
                act(SQH[i][:], PS[PSn[i]][:], AF.Square, [("ps", PSn[i])], [("SQH", i)])
            for i in range(2):
                pi = ps_next()
                mm(PS[pi][:], [(NN, SQH[i][:]), (RN[i], sqr_ap)], [("SQH", i), PSr_sq_key, "cst"], [("ps", pi)])
                rstd_from_psum(RSH[i][:], pi, 96, ("RSH", i))
            pi = ps_next()
            mm(PS[pi][:], [(NR[0], SQH[0][:]), (NR[1], SQH[1][:]), (RR, sqr_ap)],
               [("SQH", 0), ("SQH", 1), PSr_sq_key, "cst"], [("ps", pi)])
            rstd_from_psum(RSH[2][:], pi, 96, ("RSH", 2))
            for i in range(2):
                jn = 2 * hh + i
                stt(OUT[:, jn, :], PS[PSn[i]][:], vcol(gn, l), RSH[i][:], ALU.mult, ALU.mult,
                    [("ps", PSn[i]), ("RSH", i), "vecs"], [(okey, jn)])
            rap, rkeys = rope_src_fn(hh)
            tt(OUT[:, 4 + hh, :], rap, RSH[2][:], ALU.mult, rkeys + [("RSH", 2)] + rope_keys, [(okey, 4 + hh)])
            kdst_fn(hh)

        for i, b in enumerate(order):
            if i == 0:
                p1_loads(0)
            if i + 1 < NB:
                p1_loads(i + 1)
            xs = i % 2
            s = seg(b)
            t0 = tok0(b)
            Xb, Hb = X[xs], H[xs]
            QOb, KOb, VOb = QO[xs], KO[xs], VO[xs]
            qok, kok, vok = "QO%d" % xs, "KO%d" % xs, "VO%d" % xs
            xkey, hkey = ("X", xs), ("H", xs)
            norm_block(Xb, xkey, Hb, hkey, SQ, RS, A1, l, s, 0, NTT)
            dma("pool", hT_d[:, t0:t0 + TB].rearrange("(c p) t -> p c t", p=128), Hb[:],
                [(hkey, c) for c in range(8)], [("hd", b)], "st_h%d" % xs)
            hreads = [(hkey, c) for c in range(8)] + W1R
            ts(CQ[:], COS[xs][:], vcol(V_GQR, l), None, ALU.mult, None, [("COS", xs), "vecs"], ["CQ"])
            ts(SQT[:], SIN[xs][:], vcol(V_GQRP, l), None, ALU.mult, None, [("SIN", xs), "vecs"], ["SQT"])
            ts(CK[:], COS[xs][:], vcol(V_GKR, l), None, ALU.mult, None, [("COS", xs), "vecs"], ["CK"])
            ts(SK[:], SIN[xs][:], vcol(V_GKRP, l), None, ALU.mult, None, [("SIN", xs), "vecs"], ["SK"])

            def latent(c0, nch, n, gk, OUTN, okey):
                pis = []
                for j in range(nch):
                    pi = ps_next()
                    pis.append(pi)
                    mm(PS[pi][:], [(W1in[:, kc, c0 + j * 128:c0 + (j + 1) * 128], Hb[:, kc, :]) for kc in range(8)],
                       hreads, [("ps", pi)])
                    act(SQ[:, j, :], PS[pi][:], AF.Square, [("ps", pi)], [("SQ", j)])
                rms_stats([SQ[:, j, :] for j in range(nch)], [ONES] * nch, n, RS[:], "RS", [("SQ", j) for j in range(nch)])
                for j in range(nch):
                    stt(OUTN[:, j, :], PS[pis[j]][:], vcol(gk, l * nch + j), RS[:], ALU.mult, ALU.mult,
                        [("ps", pis[j]), "RS", "vecs"], [(okey, j)])

            latent(0, 3, QL, V_QAG, QN, "QN")
            latent(QL, 2, KVL, V_KVAG, KVN, "KVN")
            qn_r = [("QN", j) for j in range(3)] + W1R
            kvn_r = [("KVN", j) for j in range(2)] + W1R

            for t4 in range(4):
                pi = ps_next()
                mm(PS[pi][:], [(KVN[:, kc, t4 * 128:(t4 + 1) * 128], Wv[:, kc, :]) for kc in range(2)], kvn_r, [("ps", pi)])
                act(VOb[:, t4, :], PS[pi][:], AF.Copy, [("ps", pi)], [(vok, t4)])
            bp = b - nS
            vdst = (v_S if s == 0 else v_Pin[bp])
            tl = t0 if s == 0 else 0
            dma("pool", vdst[tl:tl + TB, :].rearrange("(a p) c -> p a c", p=128), VOb[:],
                [(vok, a) for a in range(4)], [("vdst", s)], "st_v%d" % xs)

            pk = ps_next()
            mm(PS[pk][:], [(Wkr[:, kc, :], Hb[:, kc, :]) for kc in range(8)], hreads, [("ps", pk)])
            pkr = ps_next()
            mm(PS[pkr][:], [(Wkrr[:, kc, :], Hb[:, kc, :]) for kc in range(8)], hreads, [("ps", pkr)])
            act(SQRK[:], PS[pk][:], AF.Square, [("ps", pk)], ["SQRK"])
            tt(TK[:], PS[pk][:], CK[:], ALU.mult, [("ps", pk), "CK"], ["TK"])
            tt(T2[:], PS[pkr][:], SK[:], ALU.mult, [("ps", pkr), "SK"], ["T2"])
            tt(TK[:], TK[:], T2[:], ALU.add, ["TK", "T2"], ["TK"], eng="pool")

            kdst = kT_S if s == 0 else kT_Pin[bp]
            for hh in range(2):
                pn = []
                for i2 in range(2):
                    jn = 2 * hh + i2
                    pi = ps_next()
                    pn.append(pi)
                    mm(PS[pi][:], [(Wqn[:, kc, jn * 128:(jn + 1) * 128], QN[:, kc, :]) for kc in range(3)], qn_r, [("ps", pi)])
                pr = ps_next()
                mm(PS[pr][:], [(Wqr[:, kc, hh * 128:(hh + 1) * 128], QN[:, kc, :]) for kc in range(3)], qn_r, [("ps", pr)])
                pt = ps_next()
                mm(PS[pt][:], [(Wqt[:, kc, hh * 128:(hh + 1) * 128], QN[:, kc, :]) for kc in range(3)], qn_r, [("ps", pt)])
                act(SQH[2][:], PS[pr][:], AF.Square, [("ps", pr)], [("SQH", 2)])

                def q_rope(hh_, pr=pr, pt=pt):
                    tt(T1[:], PS[pr][:], CQ[:], ALU.mult, [("ps", pr), "CQ"], ["T1"])
                    tt(T2[:], PS[pt][:], SQT[:], ALU.mult, [("ps", pt), "SQT"], ["T2"])
                    tt(T1[:], T1[:], T2[:], ALU.add, ["T1", "T2"], ["T1"], eng="pool")
                    return T1[:], ["T1"]

                head_half(hh, pn, ("SQH", 2), SQH[2][:], q_rope, QOb, qok, V_GQN, lambda hh_: None, [])
                pn = []
                for i2 in range(2):
                    jn = 2 * hh + i2
                    pi = ps_next()
                    pn.append(pi)
                    mm(PS[pi][:], [(Wkn[:, kc, jn * 128:(jn + 1) * 128], KVN[:, kc, :]) for kc in range(2)], kvn_r, [("ps", pi)])

                def k_rope(hh_):
                    return TK[:], ["TK"]

                head_half(hh, pn, "SQRK", SQRK[:], k_rope, KOb, kok, V_GKN, lambda hh_: None, [])
            if not cfg.get("no_qkstore"):
                dma("pool", qT_d[b * 768:(b + 1) * 768, :].rearrange("(j p) t -> p j t", p=128), QOb[:],
                    [(qok, j) for j in range(6)], [("qkdst", "q")], "st_q%d" % xs)
                krows = kT_S[b * 768:(b + 1) * 768, :] if s == 0 else kT_Pin[bp][:, :]
                dma("pool", krows.rearrange("(j p) t -> p j t", p=128), KOb[:],
                    [(kok, j) for j in range(6)], [("qkdst", "k%d" % s)], "st_k%d" % xs)

            if s == 1 and not cfg.get("no_cc"):
                for ci, (src, dst, rk) in enumerate(((kT_Pin[bp], kT_Pall[bp], ("qkdst", "k1")), (v_Pin[bp], v_Pall[bp], ("vdst", 1)))):
                    T.op("pool", lambda e, src=src, dst=dst: e.collective_compute(
                        "AllGather", ALU.bypass, replica_groups=[[0, 1, 2, 3], [4, 5, 6, 7]],
                        ins=[src.ap().opt()], outs=[dst.ap().opt()]),
                        reads=[rk], writes=[("gath", ci, bp)], dma="cc%d_%d_%d" % (l, ci, bp), inc=1)
        T.barrier()
        if STOP == "pass1":
            break

        LkMax = max(Ls, Lp)
        if "ph2" not in PHASES:
            sb.phase()
            ns = {}
            ns["KT"] = [sb.alloc([96, LkMax], BF16) for _ in range(2)]
            ns["VP"] = [sb.alloc([128, LkMax // 128, 65], BF16) for _ in range(2)]
            ns["QT"] = [sb.alloc([96, max(Ls, Lq)], BF16) for _ in range(2)]
            ns["PT"] = [sb.alloc([128, TB], BF16) for _ in range(4)]
            ns["OSB"] = [sb.alloc([128, TB], F32) for _ in range(2)]
            ns["OTH"] = [sb.alloc([64, TB], BF16) for _ in range(2)]
            PHASES["ph2"] = ns
        ns = PHASES["ph2"]
        KT = ns["KT"]
        VP = ns["VP"]
        QT = ns["QT"]
        PT = ns["PT"]
        OSB = ns["OSB"]
        OTH = ns["OTH"]
        for sl in range(2):
            memset(VP[sl][:, :, 64:65], 1.0, [("VP", sl)])
        scale = 96 ** -0.5
        hcount = 0
        for s in range(2):
            Lk = Ls if s == 0 else Lp
            Lseg = Ls if s == 0 else Lq
            tbase = 0 if s == 0 else Ls
            nkb = Lk // 128
            nqb = Lseg // TB

            def att_loads(h, sl, s=s, Lk=Lk, Lseg=Lseg, tbase=tbase, nkb=nkb):
                rn = (h // 2) * 128 + 64 * (h % 2)
                rr = (4 + h // 4) * 128 + 32 * (h % 4)
                parts = ((0, 64, rn), (64, 32, rr))
                if s == 0:
                    kv = kT_S.ap().rearrange("(b r) t -> r b t", r=768)
                    for (p0, np_, r0) in parts:
                        dma("sp", KT[sl][p0:p0 + np_, 0:Lk].rearrange("d (b t) -> d b t", t=TB), kv[r0:r0 + np_, :, :],
                            [("qkdst", "k0")], [("KT", sl)], "ld_kt%d" % sl)
                    vsrc = v_S[:, h * 64:(h + 1) * 64].rearrange("(kb p) c -> p kb c", p=128)
                    dma("sp", VP[sl][:, 0:nkb, 0:64], vsrc, [("vdst", 0)], [("VP", sl)], "ld_vp%d" % sl)
                else:
                    for bq in range(nP):
                        kv = kT_Pall[bq].ap().rearrange("(r c) t -> c r t", r=4)
                        for (p0, np_, r0) in parts:
                            dma("sp", KT[sl][p0:p0 + np_, bq * 4 * TB:(bq + 1) * 4 * TB].rearrange("d (r t) -> d r t", r=4),
                                kv[r0:r0 + np_, :, :], [("gath", 0, bq)], [("KT", sl)], "ld_kt%d" % sl)
                        vsrc = v_Pall[bq][:, h * 64:(h + 1) * 64].rearrange("(kb p) c -> p kb c", p=128)
                        dma("sp", VP[sl][:, bq * 16:(bq + 1) * 16, 0:64], vsrc, [("gath", 1, bq)], [("VP", sl)], "ld_vp%d" % sl)
                blk0 = 0 if s == 0 else nS
                nblk = Lseg // TB
                qv = qT_d.ap().rearrange("(b r) t -> r b t", r=768)
                for (p0, np_, r0) in parts:
                    dma("sp", QT[sl][p0:p0 + np_, 0:Lseg].rearrange("d (b t) -> d b t", t=TB), qv[r0:r0 + np_, blk0:blk0 + nblk, :],
                        [("qkdst", "q")], [("QT", sl)], "ld_qt%d" % sl)

            att_loads(0, hcount % 2)
            for h in range(NH):
                sl = hcount % 2
                hcount += 1
                if h + 1 < NH:
                    att_loads(h + 1, hcount % 2)
                for qb in range(nqb):
                    po = 5 + (qb % 2)
                    steps = list(range(nkb))
                    spi = {}

                    def qk(kb, sl=sl, qb=qb):
                        pi = kb % 5
                        spi[kb] = pi
                        mm(PS[pi][:], [(KT[sl][:, kb * 128:(kb + 1) * 128], QT[sl][:, qb * TB:(qb + 1) * TB])],
                           [("KT", sl), ("QT", sl)], [("ps", pi)])

                    qk(0)
                    if nkb > 1:
                        qk(1)
                    for kb in steps:
                        pi = spi[kb]
                        pt_i = kb % 4
                        act(PT[pt_i][:], PS[pi][:], AF.Exp, [("ps", pi)], [("PT", pt_i)], scale=scale)
                        if kb + 2 < nkb:
                            qk(kb + 2)
                        T.op("pe", lambda e, po=po, sl=sl, kb=kb, pt_i=pt_i, nkb=nkb:
                             e.matmul(PS[po][0:65, :], VP[sl][:, kb, :], PT[pt_i][:], start=(kb == 0), stop=(kb == nkb - 1)),
                             reads=[("VP", sl), ("PT", pt_i)], writes=[("ps", po)])
                    ob = qb % 2
                    T.op("dve", lambda e, ob=ob, po=po: e.tensor_copy(out=OSB[ob][0:65, :], in_=PS[po][0:65, :]),
                         reads=[("ps", po)], writes=[("OSB", ob)])
                    recip(OSB[ob][64:65, :], OSB[ob][64:65, :], [("OSB", ob)], [("OSB", ob)])
                    mm(PS[7][0:64, :], [(ones_f[64:65, 0:64], OSB[ob][64:65, :])], [("OSB", ob), "ones_f"], [("ps", 7)])
                    tt(OTH[ob][:], OSB[ob][0:64, :], PS[7][0:64, :], ALU.mult, [("OSB", ob), ("ps", 7)], [("OTH", ob)])
                    tq = tbase + qb * TB
                    dma("pool", oT_d[h * 64:(h + 1) * 64, tq:tq + TB], OTH[ob][:], [("OTH", ob)], [("od",)], "st_o%d" % ob)
        T.barrier()
        if STOP == "attn":
            break

        if "ph3" not in PHASES:
            sb.phase()
            ns = {}
            ns["X"] = [sb.alloc([128, 8, TB], F32) for _ in range(2)]
            ns["H"] = [sb.alloc([128, 8, TB], BF16) for _ in range(2)]
            ns["OT"] = [sb.alloc([128, 4, TB], BF16) for _ in range(2)]
            ns["US"] = sb.alloc([128, 8, TB], BF16)
            ns["BIGF"] = sb.alloc([128, 4096], F32)
            ns["VM"] = sb.alloc([128, 4096], BF16)
            ns["SH"] = sb.alloc([128, 8, TB], BF16)
            ns["SGT"] = sb.alloc([128, 4, TB], F32)
            ns["ACTT"] = sb.alloc([128, 22, TB], BF16)
            ns["SQ"] = sb.alloc([128, 8, TB], BF16)
            ns["RS"] = sb.alloc([128, TB], F32)
            ns["WS"] = [sb.alloc([128, 4096], BF16) for _ in range(3)]
            ns["BSB"] = sb.alloc([128, 8, 128], F32)
            ns["WST"] = sb.alloc([128, 8, 128], BF16)
            ns["SSV"] = sb.alloc([128, 4], F32)
            ns["JUNK"] = sb.alloc([128, 1024], BF16)
            ns["TMP"] = [sb.alloc([128, TB], F32) for _ in range(2)]
            PHASES["ph3"] = ns
        ns = PHASES["ph3"]
        X = ns["X"]
        H = ns["H"]
        OT = ns["OT"]
        US = ns["US"]
        BIGF = ns["BIGF"]
        VM = ns["VM"]
        SH = ns["SH"]
        SGT = ns["SGT"]
        ACTT = ns["ACTT"]
        SQ = ns["SQ"]
        RS = ns["RS"]
        WS = ns["WS"]
        BSB = ns["BSB"]
        WST = ns["WST"]
        SSV = ns["SSV"]
        JUNK = ns["JUNK"]
        TMP = ns["TMP"]
        GV = BIGF[:].rearrange("p (a b) -> p a b", a=4)
        MT = BIGF[:].rearrange("p (a b) -> p a b", a=8)
        VH = VM[:].rearrange("p (a b) -> p a b", a=4)
        MG = VM[:].rearrange("p (a b) -> p a b", a=8)
        dma("sp", BSB[:], bsb_in[l], [], ["BSB"], "ld_misc")
        dma("pool", WST[:], wsT_in[l], [], ["WST"], "ld_cst")
        tmp_rr = [0]

        def tmp_next():
            i = tmp_rr[0]
            tmp_rr[0] = 1 - i
            return i

        ws_rr = [0]
        pending = []

        def p2_loads(i):
            b = i
            xs = i % 2
            t0 = tok0(b)
            dma("sp", X[xs][:], x_src[:, t0:t0 + TB].rearrange("(c p) t -> p c t", p=128), [("xd", b)],
                [(("X", xs), c) for c in range(8)], "ld_x%d" % xs)
            dma("sp", H[xs][:], hT_d[:, t0:t0 + TB].rearrange("(c p) t -> p c t", p=128), [("hd", b)],
                [(("H", xs), c) for c in range(8)], "ld_h%d" % xs)
            dma("sp", OT[xs][:], oT_d[:, t0:t0 + TB].rearrange("(c p) t -> p c t", p=128), [("od",)],
                [("OT", xs)], "ld_o%d" % xs)

        for b in range(NB):
            if b == 0:
                p2_loads(0)
            xs = b % 2
            s = seg(b)
            t0 = tok0(b)
            Xb, Hb, OTb = X[xs], H[xs], OT[xs]
            xkey, hkey = ("X", xs), ("H", xs)
            hreads = [(hkey, c) for c in range(8)]

            steps = []

            def add(name, c0, n, fn):
                steps.append((name, c0, n, WK[name] // 128, fn))

            def f_zu(W, wkey, g):
                for jj in range(4):
                    c = g * 4 + jj
                    pi = ps_next()
                    mm(PS[pi][:], [(W[:, kc, jj * 128:(jj + 1) * 128], Hb[:, kc, :]) for kc in range(8)], hreads + [wkey], [("ps", pi)])
                    act(US[:, c, :], PS[pi][:], AF.Gelu_apprx_tanh, [("ps", pi)], [("US", c)])
            for g in range(2):
                add("w_in", 672 + g * 512, 512, lambda W, wkey, g=g: f_zu(W, wkey, g))

            def f_zv(W, wkey, half):
                for t4 in range(4):
                    pi = ps_next()
                    mm(PS[pi][:], [(Hb[:, kc, t4 * 128:(t4 + 1) * 128], W[:, kc, :]) for kc in range(8)], hreads + [wkey], [("ps", pi)])
                    act(GV[:, t4, half * 512:(half + 1) * 512], PS[pi][:], AF.Gelu_apprx_tanh, [("ps", pi)], [("BIGF", 2 * t4 + half)])
                if half == 1:
                    memset(SSV[:], 0.0, [("SSV", t4) for t4 in range(4)])
                    for t4 in range(4):
                        act(JUNK[:], GV[:, t4, :], AF.Square, [("BIGF", 2 * t4), ("BIGF", 2 * t4 + 1)], ["JUNK", ("SSV", t4)],
                            accum_out=SSV[:, t4:t4 + 1])
                    act(SSV[:], SSV[:], AF.Sqrt, [("SSV", t4) for t4 in range(4)] + ["vecs"], [("SSV", t4) for t4 in range(4)],
                        bias=eps_ap, scale=1.0 / D)
                    recip(SSV[:], SSV[:], [("SSV", t4) for t4 in range(4)], [("SSV", t4) for t4 in range(4)])
                    for t4 in range(4):
                        ts(VH[:, t4, :], GV[:, t4, :], SSV[:, t4:t4 + 1], None, ALU.mult, None,
                           [("BIGF", 2 * t4), ("BIGF", 2 * t4 + 1), ("SSV", t4)], [("VM", 2 * t4), ("VM", 2 * t4 + 1)])
                    for g in range(8):
                        pi = ps_next()

                        def fmix(e, pi=pi, g=g):
                            ins = None
                            for t4 in range(4):
                                ins = e.matmul(PS[pi][:, t4 * 128:(t4 + 1) * 128], VH[:, t4, g * 128:(g + 1) * 128], WST[:, g, :],
                                               start=True, stop=True)
                            return ins
                        T.op("pe", fmix, reads=[("VM", k) for k in range(8)] + ["WST"], writes=[("ps", pi)])
                        ti = tmp_next()
                        for t4 in range(4):
                            stt(TMP[ti][:, t4 * 128:(t4 + 1) * 128], PS[pi][:, t4 * 128:(t4 + 1) * 128], vcol(V_SGUG, l * 8 + g),
                                BSB[:, g, :], ALU.mult, ALU.add, [("ps", pi), "BSB", "vecs"], [("TMP", ti)])
                        tt(SH[:, g, :], TMP[ti][:], US[:, g, :], ALU.mult, [("TMP", ti), ("US", g)], [("SH", g)], eng="pool")
            for half in range(2):
                add("w_in", 1696 + half * 512, 512, lambda W, wkey, half=half: f_zv(W, wkey, half))

            def f_gate(W, wkey, g):
                for jj in range(4):
                    c = g * 4 + jj
                    pi = ps_next()
                    mm(PS[pi][:], [(W[:, kc, jj * 128:(jj + 1) * 128], Hb[:, kc, :]) for kc in range(8)], hreads + [wkey], [("ps", pi)])
                    act(US[:, c, :], PS[pi][:], AF.Sigmoid, [("ps", pi)], [("US", c)])

            def f_ob(W, wkey, g):
                for jj in range(4):
                    c = g * 4 + jj
                    pi = ps_next()
                    mm(PS[pi][:], [(W[:, kc, jj * 128:(jj + 1) * 128], SH[:, kc, :]) for kc in range(8)],
                       [("SH", k) for k in range(8)] + [wkey], [("ps", pi)])
                    tt(MT[:, c, :], PS[pi][:], US[:, c, :], ALU.mult, [("ps", pi), ("US", c)], [("BIGF", c)])

            def f_oa(W, wkey):
                for c in range(8):
                    pi = ps_next()
                    mm(PS[pi][:], [(W[:, kc, c * 128:(c + 1) * 128], OTb[:, kc, :]) for kc in range(4)], [("OT", xs), wkey], [("ps", pi)])
                    ti = tmp_next()
                    tt(TMP[ti][:], PS[pi][:], US[:, c, :], ALU.mult, [("ps", pi), ("US", c)], [("TMP", ti)])
                    tt(MG[:, c, :], TMP[ti][:], MT[:, c, :], ALU.add, [("TMP", ti), ("BIGF", c)], [("VM", c)], eng="pool")

            def f_out(W, wkey, g):
                for jj in range(4):
                    c = g * 4 + jj
                    pi = ps_next()
                    mm(PS[pi][:], [(W[:, kc, jj * 128:(jj + 1) * 128], MG[:, kc, :]) for kc in range(8)],
                       [("VM", k) for k in range(8)] + [wkey], [("ps", pi)])
                    stt(Xb[:, c, :], PS[pi][:], mod_ap(l, 2, c, s), Xb[:, c, :], ALU.mult, ALU.add,
                        [("ps", pi), (xkey, c), ("mods", l, 16 + c)], [(xkey, c)])
                if g == 1:
                    norm_block(Xb, xkey, SH, "SH", SQ, RS, A2, l, s, 3, TMP)

            for g in range(2):
                add("w_in", 3744 + g * 512, 512, lambda W, wkey, g=g: f_gate(W, wkey, g))
            for g in range(2):
                add("w_o_b", g * 512, 512, lambda W, wkey, g=g: f_ob(W, wkey, g))
            for g in range(2):
                add("w_in", 2720 + g * 512, 512, lambda W, wkey, g=g: f_gate(W, wkey, g))
            add("w_o_a", 0, 1024, lambda W, wkey: f_oa(W, wkey))
            for g in range(2):
                add("w_out", g * 512, 512, lambda W, wkey, g=g: f_out(W, wkey, g))

            h2reads = [("SH", k) for k in range(8)]

            def f_gatef(W, wkey, j0, nch):
                for jj in range(nch):
                    pi = ps_next()
                    mm(PS[pi][:], [(W[:, kc, jj * 128:(jj + 1) * 128], SH[:, kc, :]) for kc in range(8)], h2reads + [wkey], [("ps", pi)])
                    act(SGT[:, jj, :], PS[pi][:], AF.Silu, [("ps", pi)], [("SGT", jj)])

            def f_upf(W, wkey, j0, nch):
                for jj in range(nch):
                    pi = ps_next()
                    mm(PS[pi][:], [(W[:, kc, jj * 128:(jj + 1) * 128], SH[:, kc, :]) for kc in range(8)], h2reads + [wkey], [("ps", pi)])
                    tt(ACTT[:, j0 + jj, :], PS[pi][:], SGT[:, jj, :], ALU.mult, [("ps", pi), ("SGT", jj)], [("ACTT", j0 + jj)])

            for j0 in range(0, 22, 4):
                nch = min(4, 22 - j0)
                add("w_ffn_in", DFF + j0 * 128, nch * 128, lambda W, wkey, j0=j0, nch=nch: f_gatef(W, wkey, j0, nch))
                add("w_ffn_in", j0 * 128, nch * 128, lambda W, wkey, j0=j0, nch=nch: f_upf(W, wkey, j0, nch))

            def f_fo(W, wkey, c):
                pi = ps_next()
                mm(PS[pi][:], [(W[:, kc, :], ACTT[:, kc, :]) for kc in range(22)], [("ACTT", k) for k in range(22)] + [wkey], [("ps", pi)])
                stt(Xb[:, c, :], PS[pi][:], mod_ap(l, 5, c, s), Xb[:, c, :], ALU.mult, ALU.add,
                    [("ps", pi), (xkey, c), ("mods", l, 40 + c)], [(xkey, c)])
            for c in range(8):
                add("w_ffn_out", c * 128, 128, lambda W, wkey, c=c: f_fo(W, wkey, c))

            nst = len(steps)
            views = {}

            def issue_load(k):
                name, c0, n, kcn, fn = steps[k]
                sl = ws_rr[0]
                ws_rr[0] = (sl + 1) % 3
                view = WS[sl][:, 0:kcn * n].rearrange("p (k n) -> p k n", k=kcn)
                dma("sp", view, wsrc(name, l, c0, n), wkeys(name, l), [("WS", sl)], "ld_ws%d" % sl)
                views[k] = (view, ("WS", sl))

            issue_load(0)
            issue_load(1)
            for k in range(nst):
                if k + 2 < nst:
                    issue_load(k + 2)
                if k == 4 and b + 1 < NB:
                    p2_loads(b + 1)
                view, wkey = views[k]
                steps[k][4](view, wkey)
            dma("pool", x_dst[:, t0:t0 + TB].rearrange("(c p) t -> p c t", p=128), Xb[:],
                [(xkey, c) for c in range(8)], [("xd", b)], "st_x%d" % xs)
        T.barrier()

    with nc.Block() as block:
        @block.tensor
        def _(e):
            T.replay("pe", e)

        @block.scalar
        def _(e):
            T.replay("act", e)

        @block.vector
        def _(e):
            T.replay("dve", e)

        @block.gpsimd
        def _(e):
            T.replay("pool", e)

        @block.sync
        def _(e):
            T.replay("sp", e)
    es.close()
    return nc


def rope_tables(pos):
    inv = (10000.0 ** (-np.arange(0, 32, 2, dtype=np.float32) / np.float32(32))).astype(np.float32)
    ang = pos.astype(np.float32)[:, None] * inv[None, :]
    return np.cos(ang).astype(np.float32), np.sin(ang).astype(np.float32)


def make_consts():
    k = np.arange(128)[:, None]
    m = np.arange(128)[None, :]
    ones = np.ones((128, 128), np.float32)
    nn = (k // 64 == m // 64)
    rn = [(k // 32 == 2 * i + m // 64) for i in range(2)]
    nr = [(2 * i + k // 64 == m // 32) for i in range(2)]
    rr = (k // 32 == m // 32)
    mats = [ones, nn, rn[0], rn[1], nr[0], nr[1], rr]
    return np.ascontiguousarray(np.stack([np.asarray(x, np.float32) for x in mats], axis=1))


def prepare_inputs(cfg, inp):
    Ls, Lq, L = cfg["Ls"], cfg["Lq"], cfg["L"]
    voff, NV = vec_layout(L)
    f = lambda a: np.ascontiguousarray(np.asarray(a, dtype=np.float32))
    w_in = f(inp["w_in"])[:L]
    w_q_b = f(inp["w_q_b"])[:L]
    w_kv_b = f(inp["w_kv_b"])[:L]
    shared = {}
    shared["w_in"] = w_in.reshape(L * D, INC)
    kr = w_in[:, :, 640:672]
    krrot = np.concatenate([kr[:, :, 16:32], kr[:, :, 0:16]], axis=2)
    shared["w_kr4"] = f(np.tile(kr, (1, 1, 4))).reshape(L * D, 128)
    shared["w_krrot4"] = f(np.tile(krrot, (1, 1, 4))).reshape(L * D, 128)
    wq = w_q_b.reshape(L, QL, NH, 96)
    shared["wq_nope"] = f(wq[:, :, :, 0:64]).reshape(L * QL, 512)
    shared["wq_rope"] = f(wq[:, :, :, 64:96]).reshape(L * QL, 256)
    shared["wq_rot"] = f(np.concatenate([wq[:, :, :, 80:96], wq[:, :, :, 64:80]], axis=3)).reshape(L * QL, 256)
    wkv = w_kv_b.reshape(L, KVL, NH, 128)
    shared["wkv_nope"] = f(wkv[:, :, :, 0:64]).reshape(L * KVL, 512)
    shared["wkv_v"] = f(wkv[:, :, :, 64:128]).reshape(L * KVL, 512)
    shared["w_o_a"] = f(inp["w_o_a"])[:L].reshape(L * 512, D)
    shared["w_o_b"] = f(inp["w_o_b"])[:L].reshape(L * D, D)
    shared["w_out"] = f(inp["w_out"])[:L].reshape(L * D, D)
    shared["w_ffn_in"] = f(inp["w_ffn_in"])[:L].reshape(L * D, 2 * DFF)
    shared["w_ffn_out"] = f(inp["w_ffn_out"])[:L].reshape(L * DFF, D)
    shared["w_ada"] = f(inp["w_ada"])[:L].reshape(L * D, 6 * D)
    shared["w_sT"] = f(np.transpose(f(inp["w_s"])[:L], (0, 3, 1, 2)))
    shared["bsb"] = f(np.broadcast_to(f(inp["b_s"])[:L][:, None, :, :], (L, 128, 8, 128)))
    shared["consts"] = make_consts()
    vecs = np.zeros((128, NV), np.float32)
    vecs[:, voff[V_EPS]] = EPS
    p = np.arange(128)

    def fm(a, n):
        return f(a)[:L].reshape(L, n, 128).transpose(2, 0, 1).reshape(128, L * n)
    vecs[:, voff[V_N1G]:voff[V_N1G] + 8 * L] = fm(inp["norm1_g"], 8)
    vecs[:, voff[V_N2G]:voff[V_N2G] + 8 * L] = fm(inp["norm2_g"], 8)
    vecs[:, voff[V_SGUG]:voff[V_SGUG] + 8 * L] = fm(inp["sgu_norm_g"], 8)
    vecs[:, voff[V_QAG]:voff[V_QAG] + 3 * L] = fm(inp["q_a_norm_g"], 3)
    vecs[:, voff[V_KVAG]:voff[V_KVAG] + 2 * L] = fm(inp["kv_a_norm_g"], 2)
    vecs[:, voff[V_BADA]:voff[V_BADA] + 48 * L] = fm(inp["b_ada"], 48)
    qg = f(inp["q_norm_g"])[:L]
    kg = f(inp["k_norm_g"])[:L]
    for (g, vn, vr, vrp) in ((qg, V_GQN, V_GQR, V_GQRP), (kg, V_GKN, V_GKR, V_GKRP)):
        vecs[:, voff[vn]:voff[vn] + L] = g[:, p % 64].T
        vecs[:, voff[vr]:voff[vr] + L] = g[:, 64 + p % 32].T
        vecs[:, voff[vrp]:voff[vrp] + L] = g[:, 64 + (p % 32 + 16) % 32].T
    shared["vecs"] = vecs
    xs = f(inp["x_sample"])
    xp = f(inp["x_prompt"])
    cs = f(inp["c_sample"])
    cp = f(inp["c_prompt"])
    fidx = (p % 32) % 16
    sgn = np.where((p % 32) < 16, -1.0, 1.0).astype(np.float32)
    in_maps = []
    for c in range(NCORES):
        ps_, r = c // 4, c % 4
        m = dict(shared)
        m["xT"] = np.ascontiguousarray(np.concatenate([xs[c, :Ls].T, xp[ps_, r * Lq:(r + 1) * Lq].T], axis=1))
        m["cT"] = np.ascontiguousarray(np.stack([cs[c].reshape(8, 128).T, cp[ps_].reshape(8, 128).T], axis=2))
        pos = np.concatenate([np.arange(Ls), r * Lq + np.arange(Lq)])
        cos, sin = rope_tables(pos)
        m["cos4"] = np.ascontiguousarray(cos.T[fidx, :])
        m["sin4"] = np.ascontiguousarray(sin.T[fidx, :] * sgn[:, None])
        in_maps.append(m)
    return in_maps


_PROG_CACHE = {}


def run(cfg, inp):
    key = (cfg["Ls"], cfg["Lq"], cfg["L"], cfg.get("stop"), cfg.get("no_cc"), cfg.get("no_pool"), cfg.get("no_qkstore"))
    if key not in _PROG_CACHE:
        _PROG_CACHE[key] = build_program(cfg)
    nc = _PROG_CACHE[key]
    in_maps = prepare_inputs(cfg, inp)
    res = run_bass_kernel_spmd(nc, in_maps, core_ids=list(range(NCORES)))
    Ls, Lq = cfg["Ls"], cfg["Lq"]
    ys = np.stack([res.results[c]["yT"][:, :Ls].T for c in range(NCORES)], axis=0)
    yp = np.stack([np.concatenate([res.results[4 * s_ + r]["yT"][:, Ls:].T for r in range(4)], axis=0) for s_ in range(2)], axis=0)
    return np.ascontiguousarray(yp, dtype=np.float32), np.ascontiguousarray(ys, dtype=np.float32)


def kernel(**inputs):
    return run(FULL_CFG, inputs)
```

```python
import numpy as np
from contextlib import ExitStack
import concourse.bass as bass
import concourse.mybir as mybir
from concourse.bass_utils import run_bass_kernel_spmd

F32 = mybir.dt.float32
BF16 = mybir.dt.bfloat16
AF = mybir.ActivationFunctionType
ALU = mybir.AluOpType

D = 1024
NH = 8
QL = 384
KVL = 256
DFF = 2816
INC = 4768
EPS = 1e-6
TB = 512
NCORES = 8
SAME_SYNC = True

FULL_CFG = dict(Ls=2048, Lq=2048, L=4)


class Tracker:
    COMPUTE = ("pe", "act", "dve", "pool")

    def __init__(self, nc, es):
        self.nc = nc
        self.es = es
        self.ops = {e: [] for e in ("pe", "act", "dve", "pool", "sp")}
        self.cnt = {e: 0 for e in self.COMPUTE}
        self.esem = {e: es.enter_context(nc.semaphore("c_" + e)) for e in self.COMPUTE}
        self.dsem = {}
        self.seen = {e: {} for e in self.ops}
        self.lastw = {}
        self.readers = {}
        self.nops = 0
        self.no_pool = False

    def _dsem(self, key):
        if key not in self.dsem:
            self.dsem[key] = [self.es.enter_context(self.nc.semaphore("d%d" % len(self.dsem))), 0]
        return self.dsem[key]

    def _resolve(self, tok):
        if tok[0] == "c":
            return self.esem[tok[1]], tok[2], ("c", tok[1])
        d = self._dsem(tok[1])
        return d[0], d[1], ("d", tok[1])

    def op(self, eng, fn, reads=(), writes=(), dma=None, inc=None):
        if eng == "pool" and dma is None and self.no_pool:
            eng = "dve"
        deps = set()
        for r in reads:
            if r in self.lastw:
                deps.add(self.lastw[r])
            if isinstance(r, tuple) and r[0] == "ps":
                for k, v in self.readers.get(r, {}).items():
                    if k != ("c", eng):
                        deps.add(k + (v,))
        for w in writes:
            if w in self.lastw:
                deps.add(self.lastw[w])
            for k, v in self.readers.get(w, {}).items():
                deps.add(k + (v,))
        waits = []
        for tok in deps:
            if tok[0] == "c" and tok[1] == eng and dma is None and (eng == "pe" or not SAME_SYNC):
                continue
            sem, val, sid = self._resolve(tok)
            if self.seen[eng].get(sid, 0) >= val:
                continue
            self.seen[eng][sid] = val
            waits.append((sem, val))
        if dma is not None:
            d = self._dsem(dma)
            step = 16 if inc is None else inc
            d[1] += step
            tok = ("d", dma, d[1])
            incr = (d[0], step)
        else:
            self.cnt[eng] += 1
            tok = ("c", eng, self.cnt[eng])
            incr = (self.esem[eng], 1)
        self.ops[eng].append((waits, fn, incr))
        self.nops += 1
        for r in reads:
            self.readers.setdefault(r, {})[tok[:2]] = tok[2]
        for w in writes:
            self.lastw[w] = tok
            self.readers[w] = {}
        return tok

    @staticmethod
    def _bg(key):
        return isinstance(key, str) and key.startswith("cast")

    def barrier(self):
        for eng in self.ops:
            waits = []
            for f in self.COMPUTE:
                if f == eng or self.cnt[f] == 0:
                    continue
                if self.seen[eng].get(("c", f), 0) < self.cnt[f]:
                    self.seen[eng][("c", f)] = self.cnt[f]
                    waits.append((self.esem[f], self.cnt[f]))
            for key, d in self.dsem.items():
                if self._bg(key):
                    continue
                if d[1] and self.seen[eng].get(("d", key), 0) < d[1]:
                    self.seen[eng][("d", key)] = d[1]
                    waits.append((d[0], d[1]))
            if waits:
                self.ops[eng].append((waits, None, None))
        self.lastw = {r: t for r, t in self.lastw.items() if t[0] == "d" and self._bg(t[1])}
        self.readers = {}

    def replay(self, name, e):
        for waits, fn, incr in self.ops[name]:
            for sem, val in waits:
                e.wait_ge(sem, val)
            if fn is not None:
                ins = fn(e)
                ins.then_inc(incr[0], incr[1])


class SbAlloc:
    BASE = 16512
    TOP = 229344

    def __init__(self, nc):
        self.nc = nc
        self.persist = self.BASE
        self.cur = self.BASE
        self.n = 0

    def _al(self, shape, dt, off):
        self.n += 1
        return self.nc.alloc_sbuf_tensor_at("sb%d" % self.n, list(shape), dt, offset=off)

    def alloc(self, shape, dt, persistent=False):
        esz = 4 if dt == F32 else 2
        nbytes = int(np.prod(shape[1:])) * esz
        nbytes = (nbytes + 31) // 32 * 32
        if persistent:
            assert self.cur == self.persist, "persistent allocs must come first"
            off = self.persist
            self.persist += nbytes
            self.cur = self.persist
        else:
            off = self.cur
            self.cur += nbytes
        assert self.cur <= self.TOP, "SBUF overflow %d" % self.cur
        return self._al(shape, dt, off)

    def phase(self):
        self.cur = self.persist


W_SPECS = [
    ("w_in", 1024, INC, "A"), ("w_kr4", 1024, 128, "A"), ("w_krrot4", 1024, 128, "A"),
    ("wq_nope", QL, 512, "A"), ("wq_rope", QL, 256, "A"), ("wq_rot", QL, 256, "A"),
    ("wkv_nope", KVL, 512, "A"), ("wkv_v", KVL, 512, "A"),
    ("w_o_a", 512, D, "B"), ("w_o_b", D, D, "B"), ("w_out", D, D, "B"),
    ("w_ffn_in", D, 2 * DFF, "B"), ("w_ffn_out", DFF, D, "B"),
]
V_EPS, V_N1G, V_N2G, V_SGUG, V_QAG, V_KVAG, V_GQN, V_GQR, V_GQRP, V_GKN, V_GKR, V_GKRP, V_BADA = range(13)


def vec_layout(L):
    off = {}
    o = 0
    for name, n in [(V_EPS, 1), (V_N1G, 8 * L), (V_N2G, 8 * L), (V_SGUG, 8 * L), (V_QAG, 3 * L), (V_KVAG, 2 * L),
                    (V_GQN, L), (V_GQR, L), (V_GQRP, L), (V_GKN, L), (V_GKR, L), (V_GKRP, L), (V_BADA, 48 * L)]:
        off[name] = o
        o += n
    return off, o


def build_program(cfg):
    Ls, Lq, L = cfg["Ls"], cfg["Lq"], cfg["L"]
    Lp = 4 * Lq
    NTOK = Ls + Lq
    nS, nP = Ls // TB, Lq // TB
    NB = nS + nP
    voff, NV = vec_layout(L)

    nc = bass.Bass("TRN2", target_bir_lowering=False)

    def din(name, shape, dt=F32):
        return nc.dram_tensor(name, list(shape), dt, kind="ExternalInput").ap()

    xT_in = din("xT", [D, NTOK])
    cT_in = din("cT", [128, 8, 2])
    cos_in = din("cos4", [128, NTOK])
    sin_in = din("sin4", [128, NTOK])
    vecs_in = din("vecs", [128, NV])
    consts_in = din("consts", [128, 7, 128])
    w_ada_in = din("w_ada", [L * D, 6 * D])
    wsT_in = din("w_sT", [L, 128, 8, 128])
    bsb_in = din("bsb", [L, 128, 8, 128])
    w_ext = {name: din(name, [L * K, N]) for name, K, N, _ in W_SPECS}
    yT_out = nc.dram_tensor("yT", [D, NTOK], F32, kind="ExternalOutput").ap()

    def dscr(name, shape, dt):
        return nc.dram_tensor(name, list(shape), dt)

    w_bf = {name: dscr(name + "_b", [L * K, N], BF16) for name, K, N, _ in W_SPECS}
    WK = {name: K for name, K, N, _ in W_SPECS}
    xT_d = dscr("xT_d", [D, NTOK], F32)
    hT_d = dscr("hT_d", [D, NTOK], BF16)
    qT_d = dscr("qT_d", [NB * 768, TB], BF16)
    oT_d = dscr("oT_d", [512, NTOK], BF16)
    kT_S = dscr("kT_S", [nS * 768, TB], BF16)
    v_S = dscr("v_S", [Ls, 512], BF16)
    kT_Pin = [dscr("kT_Pin%d" % b, [NH * 96, TB], BF16) for b in range(nP)]
    kT_Pall = [dscr("kT_Pall%d" % b, [4 * NH * 96, TB], BF16) for b in range(nP)]
    v_Pin = [dscr("v_Pin%d" % b, [TB, 512], BF16) for b in range(nP)]
    v_Pall = [dscr("v_Pall%d" % b, [4 * TB, 512], BF16) for b in range(nP)]

    es = ExitStack()
    T = Tracker(nc, es)
    T.no_pool = bool(cfg.get("no_pool"))
    sb = SbAlloc(nc)
    PS = [es.enter_context(nc.psum_tensor("ps%d" % i, [128, 512], F32)) for i in range(8)]
    ps_rr = [0]

    def ps_next():
        i = ps_rr[0]
        ps_rr[0] = (i + 1) % 8
        return i

    def dma(eng, out, in_, reads, writes, key):
        T.op(eng, lambda e, o=out, i=in_: e.dma_start(out=o, in_=i), reads=reads, writes=writes, dma=key)

    def mm(ps_ap, pairs, reads, writes):
        def fn(e, ps_ap=ps_ap, pairs=pairs):
            n = len(pairs)
            ins = None
            for i, (l, r) in enumerate(pairs):
                ins = e.matmul(ps_ap, l, r, start=(i == 0), stop=(i == n - 1))
            return ins
        T.op("pe", fn, reads=reads, writes=writes)

    def act(out, in_, func, reads, writes, **kw):
        T.op("act", lambda e, o=out, i=in_, f=func, kw=kw: e.activation(out=o, in_=i, func=f, **kw), reads=reads, writes=writes)

    def tt(out, in0, in1, op, reads, writes, eng="dve"):
        T.op(eng, lambda e, o=out, a=in0, b=in1, op=op: e.tensor_tensor(out=o, in0=a, in1=b, op=op), reads=reads, writes=writes)

    def stt(out, in0, scalar, in1, op0, op1, reads, writes, eng="dve"):
        T.op(eng, lambda e, o=out, a=in0, s=scalar, b=in1, p0=op0, p1=op1:
             e.scalar_tensor_tensor(out=o, in0=a, scalar=s, in1=b, op0=p0, op1=p1), reads=reads, writes=writes)

    def ts(out, in0, s1, s2, op0, op1, reads, writes, eng="dve"):
        if s2 is None:
            T.op(eng, lambda e, o=out, a=in0, s1=s1, p0=op0: e.tensor_scalar(out=o, in0=a, scalar1=s1, scalar2=None, op0=p0),
                 reads=reads, writes=writes)
        else:
            T.op(eng, lambda e, o=out, a=in0, s1=s1, s2=s2, p0=op0, p1=op1:
                 e.tensor_scalar(out=o, in0=a, scalar1=s1, scalar2=s2, op0=p0, op1=p1), reads=reads, writes=writes)

    def recip(out, in_, reads, writes):
        T.op("dve", lambda e, o=out, i=in_: e.reciprocal(out=o, in_=i), reads=reads, writes=writes)

    def memset(ap, val, writes, eng="dve"):
        T.op(eng, lambda e, a=ap, v=val: e.memset(a, v), writes=writes)

    vecs = sb.alloc([128, NV], F32, True)
    cst = sb.alloc([128, 7, 128], BF16, True)
    ones_f = sb.alloc([128, 64], F32, True)
    mods = sb.alloc([128, L, 48, 2], F32, True)
    A1 = sb.alloc([128, L, 2, 8], F32, True)
    A2 = sb.alloc([128, L, 2, 8], F32, True)
    silc = sb.alloc([128, 8, 2], BF16, True)
    eps_ap = vecs[:, voff[V_EPS]:voff[V_EPS] + 1]

    def vcol(kind, idx):
        o = voff[kind] + idx
        return vecs[:, o:o + 1]

    ONES, NN, RN0, RN1, NR0, NR1, RR = [cst[:, i, :] for i in range(7)]
    RN = [RN0, RN1]
    NR = [NR0, NR1]

    dma("sp", vecs[:], vecs_in, [], ["vecs"], "ld_misc")
    dma("pool", cst[:], consts_in, [], ["cst"], "ld_cst")
    memset(ones_f[:], 1.0, ["ones_f"])

    def cast_weights(l, grp):
        for name, K, N, g in W_SPECS:
            if g != grp:
                continue
            for rb in range(0, K, 128):
                r0 = l * K + rb
                dma("pool", w_bf[name][r0:r0 + 128, :], w_ext[name][r0:r0 + 128, :], [], [("wb", name, l, rb // 128)],
                    "cast%s%d" % (grp, l))

    def wkeys(name, l):
        return [("wb", name, l, i) for i in range(WK[name] // 128)]

    def wsrc(name, l, c0, n):
        K = WK[name]
        return w_bf[name][l * K:(l + 1) * K, c0:c0 + n].rearrange("(kc p) n -> p kc n", p=128)

    cast_weights(0, "A")

    cTt = sb.alloc([128, 8, 2], F32)
    WA = [sb.alloc([128, 8, 512], BF16) for _ in range(2)]
    dma("sp", cTt[:], cT_in, [], ["cTt"], "ld_misc")
    act(silc[:], cTt[:], AF.Silu, ["cTt"], ["silc"])
    gi = 0
    for l in range(L):
        for jg in range(12):
            s = gi % 2
            gi += 1
            src = w_ada_in[l * D:(l + 1) * D, jg * 512:(jg + 1) * 512].rearrange("(kc p) n -> p kc n", p=128)
            dma("pool", WA[s][:], src, [], [("WA", s)], "ld_wa%d" % s)
            for jj in range(4):
                j = jg * 4 + jj
                pi = ps_next()
                mm(PS[pi][:, 0:2], [(WA[s][:, kc, jj * 128:(jj + 1) * 128], silc[:, kc, :]) for kc in range(8)],
                   [("WA", s), "silc"], [("ps", pi)])
                ts(mods[:, l, j, :], PS[pi][:, 0:2], vcol(V_BADA, l * 48 + j), None, ALU.add, None,
                   [("ps", pi), "vecs"], [("mods", l, j)])
        for s in range(2):
            for (Ax, gk, m0) in ((A1, V_N1G, 8), (A2, V_N2G, 32)):
                ts(Ax[:, l, s, :], mods[:, l, m0:m0 + 8, s], 1.0, None, ALU.add, None,
                   [("mods", l, j) for j in range(m0, m0 + 8)], [("A", id(Ax), l, s)])
                tt(Ax[:, l, s, :], Ax[:, l, s, :], vecs[:, voff[gk] + 8 * l:voff[gk] + 8 * l + 8], ALU.mult,
                   [("A", id(Ax), l, s), "vecs"], [("A", id(Ax), l, s)])
        if l == 0:
            cast_weights(0, "B")
    for l in range(1, L):
        cast_weights(l, "A")
        cast_weights(l, "B")
    T.barrier()
    STOP = cfg.get("stop")

    def mod_ap(l, m, c, s):
        return mods[:, l, m * 8 + c, s:s + 1]

    def tok0(b):
        return b * TB

    def seg(b):
        return 0 if b < nS else 1

    def rstd_from_psum(rs_ap, pi, n, rs_key):
        act(rs_ap, PS[pi][:], AF.Ln, [("ps", pi), "vecs"], [rs_key], bias=eps_ap, scale=1.0 / n)
        act(rs_ap, rs_ap, AF.Exp, [rs_key], [rs_key], scale=-0.5)

    def rms_stats(sq_list, lhs_list, n, rs_ap, rs_key, reads):
        pi = ps_next()
        mm(PS[pi][:], list(zip(lhs_list, sq_list)), reads + ["cst"], [("ps", pi)])
        rstd_from_psum(rs_ap, pi, n, rs_key)

    def norm_block(Xb, xkey, Hb, hkey, SQ, RS, Ax, l, s, shm, NT):
        for c in range(8):
            tt(SQ[:, c, :], Xb[:, c, :], Xb[:, c, :], ALU.mult, [(xkey, c)], [("SQ", c)], eng="pool")
        rms_stats([SQ[:, c, :] for c in range(8)], [ONES] * 8, D, RS[:], "RS", [("SQ", c) for c in range(8)])
        for c in range(8):
            i = c % 2
            tt(NT[i][:], Xb[:, c, :], RS[:], ALU.mult, [(xkey, c), "RS"], [("NT", i)])
            act(Hb[:, c, :], NT[i][:], AF.Identity, [("NT", i), ("A", id(Ax), l, s), ("mods", l, shm * 8 + c)], [(hkey, c)],
                scale=Ax[:, l, s, c:c + 1], bias=mod_ap(l, shm, c, s))

    PHASES = {}
    for l in range(L if STOP != "prologue" else 0):
        x_src = xT_in if l == 0 else xT_d
        x_dst = yT_out if l == L - 1 else xT_d
        last = (l == L - 1)

        if "ph1" not in PHASES:
            sb.phase()
            ns = {}
            ns["X"] = [sb.alloc([128, 8, TB], F32) for _ in range(2)]
            ns["H"] = [sb.alloc([128, 8, TB], BF16) for _ in range(2)]
            ns["SQ"] = sb.alloc([128, 8, TB], BF16)
            ns["RS"] = sb.alloc([128, TB], F32)
            ns["W1in"] = sb.alloc([128, 8, 640], BF16)
            ns["Wkr"] = sb.alloc([128, 8, 128], BF16)
            ns["Wkrr"] = sb.alloc([128, 8, 128], BF16)
            ns["Wqn"] = sb.alloc([128, 3, 512], BF16)
            ns["Wqr"] = sb.alloc([128, 3, 256], BF16)
            ns["Wqt"] = sb.alloc([128, 3, 256], BF16)
            ns["Wkn"] = sb.alloc([128, 2, 512], BF16)
            ns["Wv"] = sb.alloc([128, 2, 512], BF16)
            ns["QN"] = sb.alloc([128, 3, TB], BF16)
            ns["KVN"] = sb.alloc([128, 2, TB], BF16)
            ns["QO"] = [sb.alloc([128, 6, TB], BF16) for _ in range(2)]
            ns["KO"] = [sb.alloc([128, 6, TB], BF16) for _ in range(2)]
            ns["VO"] = [sb.alloc([128, 4, 512], BF16) for _ in range(2)]
            ns["TK"] = sb.alloc([128, TB], F32)
            ns["SQRK"] = sb.alloc([128, TB], BF16)
            ns["COS"] = [sb.alloc([128, TB], F32) for _ in range(2)]
            ns["SIN"] = [sb.alloc([128, TB], F32) for _ in range(2)]
            ns["CQ"] = sb.alloc([128, TB], F32)
            ns["SQT"] = sb.alloc([128, TB], F32)
            ns["CK"] = sb.alloc([128, TB], F32)
            ns["SK"] = sb.alloc([128, TB], F32)
            ns["SQH"] = [sb.alloc([128, TB], BF16) for _ in range(3)]
            ns["RSH"] = [sb.alloc([128, TB], F32) for _ in range(3)]
            ns["T1"] = sb.alloc([128, TB], F32)
            ns["T2"] = sb.alloc([128, TB], F32)
            ns["NTT"] = [sb.alloc([128, TB], F32) for _ in range(2)]
            PHASES["ph1"] = ns
        ns = PHASES["ph1"]
        X = ns["X"]
        H = ns["H"]
        SQ = ns["SQ"]
        RS = ns["RS"]
        W1in = ns["W1in"]
        Wkr = ns["Wkr"]
        Wkrr = ns["Wkrr"]
        Wqn = ns["Wqn"]
        Wqr = ns["Wqr"]
        Wqt = ns["Wqt"]
        Wkn = ns["Wkn"]
        Wv = ns["Wv"]
        QN = ns["QN"]
        KVN = ns["KVN"]
        QO = ns["QO"]
        KO = ns["KO"]
        VO = ns["VO"]
        TK = ns["TK"]
        SQRK = ns["SQRK"]
        COS = ns["COS"]
        SIN = ns["SIN"]
        CQ = ns["CQ"]
        SQT = ns["SQT"]
        CK = ns["CK"]
        SK = ns["SK"]
        SQH = ns["SQH"]
        RSH = ns["RSH"]
        T1 = ns["T1"]
        T2 = ns["T2"]
        NTT = ns["NTT"]

        for (buf, name, c0, n) in ((W1in, "w_in", 0, 640), (Wkr, "w_kr4", 0, 128), (Wkrr, "w_krrot4", 0, 128),
                                   (Wqn, "wq_nope", 0, 512), (Wqr, "wq_rope", 0, 256), (Wqt, "wq_rot", 0, 256),
                                   (Wkn, "wkv_nope", 0, 512), (Wv, "wkv_v", 0, 512)):
            dma("sp", buf[:], wsrc(name, l, c0, n), wkeys(name, l), [("W1", name)], "ld_w1")
        W1R = [("W1", n) for n in ("w_in", "w_kr4", "w_krrot4", "wq_nope", "wq_rope", "wq_rot", "wkv_nope", "wkv_v")]

        order = list(range(nS, NB)) + list(range(nS))

        def p1_loads(i):
            b = order[i]
            xs = i % 2
            t0 = tok0(b)
            dma("sp", X[xs][:], x_src[:, t0:t0 + TB].rearrange("(c p) t -> p c t", p=128), [("xd", b)],
                [(("X", xs), c) for c in range(8)], "ld_x%d" % xs)
            dma("sp", COS[xs][:], cos_in[:, t0:t0 + TB], [], [("COS", xs)], "ld_cs%d" % xs)
            dma("sp", SIN[xs][:], sin_in[:, t0:t0 + TB], [], [("SIN", xs)], "ld_cs%d" % xs)

        def head_half(hh, PSn, PSr_sq_key, sqr_ap, rope_src_fn, OUT, okey, gn, kdst_fn, rope_keys):
            for i in range(2):
                act(SQH[i][:], PS[PSn[i]][:], AF.Square, [("ps", PSn[i])], [("SQH", i)])
            for i in range(2):
                pi = ps_next()
                mm(PS[pi][:], [(NN, SQH[i][:]), (RN[i], sqr_ap)], [("SQH", i), PSr_sq_key, "cst"], [("ps", pi)])
                rstd_from_psum(RSH[i][:], pi, 96, ("RSH", i))
            pi = ps_next()
            mm(PS[pi][:], [(NR[0], SQH[0][:]), (NR[1], SQH[1][:]), (RR, sqr_ap)],
               [("SQH", 0), ("SQH", 1), PSr_sq_key, "cst"], [("ps", pi)])
            rstd_from_psum(RSH[2][:], pi, 96, ("RSH", 2))
            for i in range(2):
                jn = 2 * hh + i
                stt(OUT[:, jn, :], PS[PSn[i]][:], vcol(gn, l), RSH[i][:], ALU.mult, ALU.mult,
                    [("ps", PSn[i]), ("RSH", i), "vecs"], [(okey, jn)])
            rap, rkeys = rope_src_fn(hh)
            tt(OUT[:, 4 + hh, :], rap, RSH[2][:], ALU.mult, rkeys + [("RSH", 2)] + rope_keys, [(okey, 4 + hh)])
            kdst_fn(hh)

        for i, b in enumerate(order):
            if i == 0:
                p1_loads(0)
            if i + 1 < NB:
                p1_loads(i + 1)
            xs = i % 2
            s = seg(b)
            t0 = tok0(b)
            Xb, Hb = X[xs], H[xs]
            QOb, KOb, VOb = QO[xs], KO[xs], VO[xs]
            qok, kok, vok = "QO%d" % xs, "KO%d" % xs, "VO%d" % xs
            xkey, hkey = ("X", xs), ("H", xs)
            norm_block(Xb, xkey, Hb, hkey, SQ, RS, A1, l, s, 0, NTT)
            dma("pool", hT_d[:, t0:t0 + TB].rearrange("(c p) t -> p c t", p=128), Hb[:],
                [(hkey, c) for c in range(8)], [("hd", b)], "st_h%d" % xs)
            hreads = [(hkey, c) for c in range(8)] + W1R
            ts(CQ[:], COS[xs][:], vcol(V_GQR, l), None, ALU.mult, None, [("COS", xs), "vecs"], ["CQ"])
            ts(SQT[:], SIN[xs][:], vcol(V_GQRP, l), None, ALU.mult, None, [("SIN", xs), "vecs"], ["SQT"])
            ts(CK[:], COS[xs][:], vcol(V_GKR, l), None, ALU.mult, None, [("COS", xs), "vecs"], ["CK"])
            ts(SK[:], SIN[xs][:], vcol(V_GKRP, l), None, ALU.mult, None, [("SIN", xs), "vecs"], ["SK"])

            def latent(c0, nch, n, gk, OUTN, okey):
                pis = []
                for j in range(nch):
                    pi = ps_next()
                    pis.append(pi)
                    mm(PS[pi][:], [(W1in[:, kc, c0 + j * 128:c0 + (j + 1) * 128], Hb[:, kc, :]) for kc in range(8)],
                       hreads, [("ps", pi)])
                    act(SQ[:, j, :], PS[pi][:], AF.Square, [("ps", pi)], [("SQ", j)])
                rms_stats([SQ[:, j, :] for j in range(nch)], [ONES] * nch, n, RS[:], "RS", [("SQ", j) for j in range(nch)])
                for j in range(nch):
                    stt(OUTN[:, j, :], PS[pis[j]][:], vcol(gk, l * nch + j), RS[:], ALU.mult, ALU.mult,
                        [("ps", pis[j]), "RS", "vecs"], [(okey, j)])

            latent(0, 3, QL, V_QAG, QN, "QN")
            latent(QL, 2, KVL, V_KVAG, KVN, "KVN")
            qn_r = [("QN", j) for j in range(3)] + W1R
            kvn_r = [("KVN", j) for j in range(2)] + W1R

            for t4 in range(4):
                pi = ps_next()
                mm(PS[pi][:], [(KVN[:, kc, t4 * 128:(t4 + 1) * 128], Wv[:, kc, :]) for kc in range(2)], kvn_r, [("ps", pi)])
                act(VOb[:, t4, :], PS[pi][:], AF.Copy, [("ps", pi)], [(vok, t4)])
            bp = b - nS
            vdst = (v_S if s == 0 else v_Pin[bp])
            tl = t0 if s == 0 else 0
            dma("pool", vdst[tl:tl + TB, :].rearrange("(a p) c -> p a c", p=128), VOb[:],
                [(vok, a) for a in range(4)], [("vdst", s)], "st_v%d" % xs)

            pk = ps_next()
            mm(PS[pk][:], [(Wkr[:, kc, :], Hb[:, kc, :]) for kc in range(8)], hreads, [("ps", pk)])
            pkr = ps_next()
            mm(PS[pkr][:], [(Wkrr[:, kc, :], Hb[:, kc, :]) for kc in range(8)], hreads, [("ps", pkr)])
            act(SQRK[:], PS[pk][:], AF.Square, [("ps", pk)], ["SQRK"])
            tt(TK[:], PS[pk][:], CK[:], ALU.mult, [("ps", pk), "CK"], ["TK"])
            tt(T2[:], PS[pkr][:], SK[:], ALU.mult, [("ps", pkr), "SK"], ["T2"])
            tt(TK[:], TK[:], T2[:], ALU.add, ["TK", "T2"], ["TK"], eng="pool")

            kdst = kT_S if s == 0 else kT_Pin[bp]
            for hh in range(2):
                pn = []
                for i2 in range(2):
                    jn = 2 * hh + i2
                    pi = ps_next()
                    pn.append(pi)
                    mm(PS[pi][:], [(Wqn[:, kc, jn * 128:(jn + 1) * 128], QN[:, kc, :]) for kc in range(3)], qn_r, [("ps", pi)])
                pr = ps_next()
                mm(PS[pr][:], [(Wqr[:, kc, hh * 128:(hh + 1) * 128], QN[:, kc, :]) for kc in range(3)], qn_r, [("ps", pr)])
                pt = ps_next()
                mm(PS[pt][:], [(Wqt[:, kc, hh * 128:(hh + 1) * 128], QN[:, kc, :]) for kc in range(3)], qn_r, [("ps", pt)])
                act(SQH[2][:], PS[pr][:], AF.Square, [("ps", pr)], [("SQH", 2)])

                def q_rope(hh_, pr=pr, pt=pt):
                    tt(T1[:], PS[pr][:], CQ[:], ALU.mult, [("ps", pr), "CQ"], ["T1"])
                    tt(T2[:], PS[pt][:], SQT[:], ALU.mult, [("ps", pt), "SQT"], ["T2"])
                    tt(T1[:], T1[:], T2[:], ALU.add, ["T1", "T2"], ["T1"], eng="pool")
                    return T1[:], ["T1"]

                head_half(hh, pn, ("SQH", 2), SQH[2][:], q_rope, QOb, qok, V_GQN, lambda hh_: None, [])
                pn = []
                for i2 in range(2):
                    jn = 2 * hh + i2
                    pi = ps_next()
                    pn.append(pi)
                    mm(PS[pi][:], [(Wkn[:, kc, jn * 128:(jn + 1) * 128], KVN[:, kc, :]) for kc in range(2)], kvn_r, [("ps", pi)])

                def k_rope(hh_):
                    return TK[:], ["TK"]

                head_half(hh, pn, "SQRK", SQRK[:], k_rope, KOb, kok, V_GKN, lambda hh_: None, [])
            if not cfg.get("no_qkstore"):
                dma("pool", qT_d[b * 768:(b + 1) * 768, :].rearrange("(j p) t -> p j t", p=128), QOb[:],
                    [(qok, j) for j in range(6)], [("qkdst", "q")], "st_q%d" % xs)
                krows = kT_S[b * 768:(b + 1) * 768, :] if s == 0 else kT_Pin[bp][:, :]
                dma("pool", krows.rearrange("(j p) t -> p j t", p=128), KOb[:],
                    [(kok, j) for j in range(6)], [("qkdst", "k%d" % s)], "st_k%d" % xs)

            if s == 1 and not cfg.get("no_cc"):
                for ci, (src, dst, rk) in enumerate(((kT_Pin[bp], kT_Pall[bp], ("qkdst", "k1")), (v_Pin[bp], v_Pall[bp], ("vdst", 1)))):
                    T.op("pool", lambda e, src=src, dst=dst: e.collective_compute(
                        "AllGather", ALU.bypass, replica_groups=[[0, 1, 2, 3], [4, 5, 6, 7]],
                        ins=[src.ap().opt()], outs=[dst.ap().opt()]),
                        reads=[rk], writes=[("gath", ci, bp)], dma="cc%d_%d_%d" % (l, ci, bp), inc=1)
        T.barrier()
        if STOP == "pass1":
            break

        LkMax = max(Ls, Lp)
        if "ph2" not in PHASES:
            sb.phase()
            ns = {}
            ns["KT"] = [sb.alloc([96, LkMax], BF16) for _ in range(2)]
            ns["VP"] = [sb.alloc([128, LkMax // 128, 65], BF16) for _ in range(2)]
            ns["QT"] = [sb.alloc([96, max(Ls, Lq)], BF16) for _ in range(2)]
            ns["PT"] = [sb.alloc([128, TB], BF16) for _ in range(4)]
            ns["OSB"] = [sb.alloc([128, TB], F32) for _ in range(2)]
            ns["OTH"] = [sb.alloc([64, TB], BF16) for _ in range(2)]
            PHASES["ph2"] = ns
        ns = PHASES["ph2"]
        KT = ns["KT"]
        VP = ns["VP"]
        QT = ns["QT"]
        PT = ns["PT"]
        OSB = ns["OSB"]
        OTH = ns["OTH"]
        for sl in range(2):
            memset(VP[sl][:, :, 64:65], 1.0, [("VP", sl)])
        scale = 96 ** -0.5
        hcount = 0
        for s in range(2):
            Lk = Ls if s == 0 else Lp
            Lseg = Ls if s == 0 else Lq
            tbase = 0 if s == 0 else Ls
            nkb = Lk // 128
            nqb = Lseg // TB

            def att_loads(h, sl, s=s, Lk=Lk, Lseg=Lseg, tbase=tbase, nkb=nkb):
                rn = (h // 2) * 128 + 64 * (h % 2)
                rr = (4 + h // 4) * 128 + 32 * (h % 4)
                parts = ((0, 64, rn), (64, 32, rr))
                if s == 0:
                    kv = kT_S.ap().rearrange("(b r) t -> r b t", r=768)
                    for (p0, np_, r0) in parts:
                        dma("sp", KT[sl][p0:p0 + np_, 0:Lk].rearrange("d (b t) -> d b t", t=TB), kv[r0:r0 + np_, :, :],
                            [("qkdst", "k0")], [("KT", sl)], "ld_kt%d" % sl)
                    vsrc = v_S[:, h * 64:(h + 1) * 64].rearrange("(kb p) c -> p kb c", p=128)
                    dma("sp", VP[sl][:, 0:nkb, 0:64], vsrc, [("vdst", 0)], [("VP", sl)], "ld_vp%d" % sl)
                else:
                    for bq in range(nP):
                        kv = kT_Pall[bq].ap().rearrange("(r c) t -> c r t", r=4)
                        for (p0, np_, r0) in parts:
                            dma("sp", KT[sl][p0:p0 + np_, bq * 4 * TB:(bq + 1) * 4 * TB].rearrange("d (r t) -> d r t", r=4),
                                kv[r0:r0 + np_, :, :], [("gath", 0, bq)], [("KT", sl)], "ld_kt%d" % sl)
                        vsrc = v_Pall[bq][:, h * 64:(h + 1) * 64].rearrange("(kb p) c -> p kb c", p=128)
                        dma("sp", VP[sl][:, bq * 16:(bq + 1) * 16, 0:64], vsrc, [("gath", 1, bq)], [("VP", sl)], "ld_vp%d" % sl)
                blk0 = 0 if s == 0 else nS
                nblk = Lseg // TB
                qv = qT_d.ap().rearrange("(b r) t -> r b t", r=768)
                for (p0, np_, r0) in parts:
                    dma("sp", QT[sl][p0:p0 + np_, 0:Lseg].rearrange("d (b t) -> d b t", t=TB), qv[r0:r0 + np_, blk0:blk0 + nblk, :],
                        [("qkdst", "q")], [("QT", sl)], "ld_qt%d" % sl)

            att_loads(0, hcount % 2)
            for h in range(NH):
                sl = hcount % 2
                hcount += 1
                if h + 1 < NH:
                    att_loads(h + 1, hcount % 2)
                for qb in range(nqb):
                    po = 5 + (qb % 2)
                    steps = list(range(nkb))
                    spi = {}

                    def qk(kb, sl=sl, qb=qb):
                        pi = kb % 5
                        spi[kb] = pi
                        mm(PS[pi][:], [(KT[sl][:, kb * 128:(kb + 1) * 128], QT[sl][:, qb * TB:(qb + 1) * TB])],
                           [("KT", sl), ("QT", sl)], [("ps", pi)])

                    qk(0)
                    if nkb > 1:
                        qk(1)
                    for kb in steps:
                        pi = spi[kb]
                        pt_i = kb % 4
                        act(PT[pt_i][:], PS[pi][:], AF.Exp, [("ps", pi)], [("PT", pt_i)], scale=scale)
                        if kb + 2 < nkb:
                            qk(kb + 2)
                        T.op("pe", lambda e, po=po, sl=sl, kb=kb, pt_i=pt_i, nkb=nkb:
                             e.matmul(PS[po][0:65, :], VP[sl][:, kb, :], PT[pt_i][:], start=(kb == 0), stop=(kb == nkb - 1)),
                             reads=[("VP", sl), ("PT", pt_i)], writes=[("ps", po)])
                    ob = qb % 2
                    T.op("dve", lambda e, ob=ob, po=po: e.tensor_copy(out=OSB[ob][0:65, :], in_=PS[po][0:65, :]),
                         reads=[("ps", po)], writes=[("OSB", ob)])
                    recip(OSB[ob][64:65, :], OSB[ob][64:65, :], [("OSB", ob)], [("OSB", ob)])
                    mm(PS[7][0:64, :], [(ones_f[64:65, 0:64], OSB[ob][64:65, :])], [("OSB", ob), "ones_f"], [("ps", 7)])
                    tt(OTH[ob][:], OSB[ob][0:64, :], PS[7][0:64, :], ALU.mult, [("OSB", ob), ("ps", 7)], [("OTH", ob)])
                    tq = tbase + qb * TB
                    dma("pool", oT_d[h * 64:(h + 1) * 64, tq:tq + TB], OTH[ob][:], [("OTH", ob)], [("od",)], "st_o%d" % ob)
        T.barrier()
        if STOP == "attn":
            break

        if "ph3" not in PHASES:
            sb.phase()
            ns = {}
            ns["X"] = [sb.alloc([128, 8, TB], F32) for _ in range(2)]
            ns["H"] = [sb.alloc([128, 8, TB], BF16) for _ in range(2)]
            ns["OT"] = [sb.alloc([128, 4, TB], BF16) for _ in range(2)]
            ns["US"] = sb.alloc([128, 8, TB], BF16)
            ns["BIGF"] = sb.alloc([128, 4096], F32)
            ns["VM"] = sb.alloc([128, 4096], BF16)
            ns["SH"] = sb.alloc([128, 8, TB], BF16)
            ns["SGT"] = sb.alloc([128, 4, TB], F32)
            ns["ACTT"] = sb.alloc([128, 22, TB], BF16)
            ns["SQ"] = sb.alloc([128, 8, TB], BF16)
            ns["RS"] = sb.alloc([128, TB], F32)
            ns["WS"] = [sb.alloc([128, 4096], BF16) for _ in range(3)]
            ns["BSB"] = sb.alloc([128, 8, 128], F32)
            ns["WST"] = sb.alloc([128, 8, 128], BF16)
            ns["SSV"] = sb.alloc([128, 4], F32)
            ns["JUNK"] = sb.alloc([128, 1024], BF16)
            ns["TMP"] = [sb.alloc([128, TB], F32) for _ in range(2)]
            PHASES["ph3"] = ns
        ns = PHASES["ph3"]
        X = ns["X"]
        H = ns["H"]
        OT = ns["OT"]
        US = ns["US"]
        BIGF = ns["BIGF"]
        VM = ns["VM"]
        SH = ns["SH"]
        SGT = ns["SGT"]
        ACTT = ns["ACTT"]
        SQ = ns["SQ"]
        RS = ns["RS"]
        WS = ns["WS"]
        BSB = ns["BSB"]
        WST = ns["WST"]
        SSV = ns["SSV"]
        JUNK = ns["JUNK"]
        TMP = ns["TMP"]
        GV = BIGF[:].rearrange("p (a b) -> p a b", a=4)
        MT = BIGF[:].rearrange("p (a b) -> p a b", a=8)
        VH = VM[:].rearrange("p (a b) -> p a b", a=4)
        MG = VM[:].rearrange("p (a b) -> p a b", a=8)
        dma("sp", BSB[:], bsb_in[l], [], ["BSB"], "ld_misc")
        dma("pool", WST[:], wsT_in[l], [], ["WST"], "ld_cst")
        tmp_rr = [0]

        def tmp_next():
            i = tmp_rr[0]
            tmp_rr[0] = 1 - i
            return i

        ws_rr = [0]
        pending = []

        def p2_loads(i):
            b = i
            xs = i % 2
            t0 = tok0(b)
            dma("sp", X[xs][:], x_src[:, t0:t0 + TB].rearrange("(c p) t -> p c t", p=128), [("xd", b)],
                [(("X", xs), c) for c in range(8)], "ld_x%d" % xs)
            dma("sp", H[xs][:], hT_d[:, t0:t0 + TB].rearrange("(c p) t -> p c t", p=128), [("hd", b)],
                [(("H", xs), c) for c in range(8)], "ld_h%d" % xs)
            dma("sp", OT[xs][:], oT_d[:, t0:t0 + TB].rearrange("(c p) t -> p c t", p=128), [("od",)],
                [("OT", xs)], "ld_o%d" % xs)

        for b in range(NB):
            if b == 0:
                p2_loads(0)
            xs = b % 2
            s = seg(b)
            t0 = tok0(b)
            Xb, Hb, OTb = X[xs], H[xs], OT[xs]
            xkey, hkey = ("X", xs), ("H", xs)
            hreads = [(hkey, c) for c in range(8)]

            steps = []

            def add(name, c0, n, fn):
                steps.append((name, c0, n, WK[name] // 128, fn))

            def f_zu(W, wkey, g):
                for jj in range(4):
                    c = g * 4 + jj
                    pi = ps_next()
                    mm(PS[pi][:], [(W[:, kc, jj * 128:(jj + 1) * 128], Hb[:, kc, :]) for kc in range(8)], hreads + [wkey], [("ps", pi)])
                    act(US[:, c, :], PS[pi][:], AF.Gelu_apprx_tanh, [("ps", pi)], [("US", c)])
            for g in range(2):
                add("w_in", 672 + g * 512, 512, lambda W, wkey, g=g: f_zu(W, wkey, g))

            def f_zv(W, wkey, half):
                for t4 in range(4):
                    pi = ps_next()
                    mm(PS[pi][:], [(Hb[:, kc, t4 * 128:(t4 + 1) * 128], W[:, kc, :]) for kc in range(8)], hreads + [wkey], [("ps", pi)])
                    act(GV[:, t4, half * 512:(half + 1) * 512], PS[pi][:], AF.Gelu_apprx_tanh, [("ps", pi)], [("BIGF", 2 * t4 + half)])
                if half == 1:
                    memset(SSV[:], 0.0, [("SSV", t4) for t4 in range(4)])
                    for t4 in range(4):
                        act(JUNK[:], GV[:, t4, :], AF.Square, [("BIGF", 2 * t4), ("BIGF", 2 * t4 + 1)], ["JUNK", ("SSV", t4)],
                            accum_out=SSV[:, t4:t4 + 1])
                    act(SSV[:], SSV[:], AF.Sqrt, [("SSV", t4) for t4 in range(4)] + ["vecs"], [("SSV", t4) for t4 in range(4)],
                        bias=eps_ap, scale=1.0 / D)
                    recip(SSV[:], SSV[:], [("SSV", t4) for t4 in range(4)], [("SSV", t4) for t4 in range(4)])
                    for t4 in range(4):
                        ts(VH[:, t4, :], GV[:, t4, :], SSV[:, t4:t4 + 1], None, ALU.mult, None,
                           [("BIGF", 2 * t4), ("BIGF", 2 * t4 + 1), ("SSV", t4)], [("VM", 2 * t4), ("VM", 2 * t4 + 1)])
                    for g in range(8):
                        pi = ps_next()

                        def fmix(e, pi=pi, g=g):
                            ins = None
                            for t4 in range(4):
                                ins = e.matmul(PS[pi][:, t4 * 128:(t4 + 1) * 128], VH[:, t4, g * 128:(g + 1) * 128], WST[:, g, :],
                                               start=True, stop=True)
                            return ins
                        T.op("pe", fmix, reads=[("VM", k) for k in range(8)] + ["WST"], writes=[("ps", pi)])
                        ti = tmp_next()
                        for t4 in range(4):
                            stt(TMP[ti][:, t4 * 128:(t4 + 1) * 128], PS[pi][:, t4 * 128:(t4 + 1) * 128], vcol(V_SGUG, l * 8 + g),
                                BSB[:, g, :], ALU.mult, ALU.add, [("ps", pi), "BSB", "vecs"], [("TMP", ti)])
                        tt(SH[:, g, :], TMP[ti][:], US[:, g, :], ALU.mult, [("TMP", ti), ("US", g)], [("SH", g)], eng="pool")
            for half in range(2):
                add("w_in", 1696 + half * 512, 512, lambda W, wkey, half=half: f_zv(W, wkey, half))

            def f_gate(W, wkey, g):
                for jj in range(4):
                    c = g * 4 + jj
                    pi = ps_next()
                    mm(PS[pi][:], [(W[:, kc, jj * 128:(jj + 1) * 128], Hb[:, kc, :]) for kc in range(8)], hreads + [wkey], [("ps", pi)])
                    act(US[:, c, :], PS[pi][:], AF.Sigmoid, [("ps", pi)], [("US", c)])

            def f_ob(W, wkey, g):
                for jj in range(4):
                    c = g * 4 + jj
                    pi = ps_next()
                    mm(PS[pi][:], [(W[:, kc, jj * 128:(jj + 1) * 128], SH[:, kc, :]) for kc in range(8)],
                       [("SH", k) for k in range(8)] + [wkey], [("ps", pi)])
                    tt(MT[:, c, :], PS[pi][:], US[:, c, :], ALU.mult, [("ps", pi), ("US", c)], [("BIGF", c)])

            def f_oa(W, wkey):
                for c in range(8):
                    pi = ps_next()
                    mm(PS[pi][:], [(W[:, kc, c * 128:(c + 1) * 128], OTb[:, kc, :]) for kc in range(4)], [("OT", xs), wkey], [("ps", pi)])
                    ti = tmp_next()
                    tt(TMP[ti][:], PS[pi][:], US[:, c, :], ALU.mult, [("ps", pi), ("US", c)], [("TMP", ti)])
                    tt(MG[:, c, :], TMP[ti][:], MT[:, c, :], ALU.add, [("TMP", ti), ("BIGF", c)], [("VM", c)], eng="pool")

            def f_out(W, wkey, g):
                for jj in range(4):
                    c = g * 4 + jj
                    pi = ps_next()
                    mm(PS[pi][:], [(W[:, kc, jj * 128:(jj + 1) * 128], MG[:, kc, :]) for kc in range(8)],
                       [("VM", k) for k in range(8)] + [wkey], [("ps", pi)])
                    stt(Xb[:, c, :], PS[pi][:], mod_ap(l, 2, c, s), Xb[:, c, :], ALU.mult, ALU.add,
                        [("ps", pi), (xkey, c), ("mods", l, 16 + c)], [(xkey, c)])
                if g == 1:
                    norm_block(Xb, xkey, SH, "SH", SQ, RS, A2, l, s, 3, TMP)

            for g in range(2):
                add("w_in", 3744 + g * 512, 512, lambda W, wkey, g=g: f_gate(W, wkey, g))
            for g in range(2):
                add("w_o_b", g * 512, 512, lambda W, wkey, g=g: f_ob(W, wkey, g))
            for g in range(2):
                add("w_in", 2720 + g * 512, 512, lambda W, wkey, g=g: f_gate(W, wkey, g))
            add("w_o_a", 0, 1024, lambda W, wkey: f_oa(W, wkey))
            for g in range(2):
                add("w_out", g * 512, 512, lambda W, wkey, g=g: f_out(W, wkey, g))

            h2reads = [("SH", k) for k in range(8)]

            def f_gatef(W, wkey, j0, nch):
                for jj in range(nch):
                    pi = ps_next()
                    mm(PS[pi][:], [(W[:, kc, jj * 128:(jj + 1) * 128], SH[:, kc, :]) for kc in range(8)], h2reads + [wkey], [("ps", pi)])
                    act(SGT[:, jj, :], PS[pi][:], AF.Silu, [("ps", pi)], [("SGT", jj)])

            def f_upf(W, wkey, j0, nch):
                for jj in range(nch):
                    pi = ps_next()
                    mm(PS[pi][:], [(W[:, kc, jj * 128:(jj + 1) * 128], SH[:, kc, :]) for kc in range(8)], h2reads + [wkey], [("ps", pi)])
                    tt(ACTT[:, j0 + jj, :], PS[pi][:], SGT[:, jj, :], ALU.mult, [("ps", pi), ("SGT", jj)], [("ACTT", j0 + jj)])

            for j0 in range(0, 22, 4):
                nch = min(4, 22 - j0)
                add("w_ffn_in", DFF + j0 * 128, nch * 128, lambda W, wkey, j0=j0, nch=nch: f_gatef(W, wkey, j0, nch))
                add("w_ffn_in", j0 * 128, nch * 128, lambda W, wkey, j0=j0, nch=nch: f_upf(W, wkey, j0, nch))

            def f_fo(W, wkey, c):
                pi = ps_next()
                mm(PS[pi][:], [(W[:, kc, :], ACTT[:, kc, :]) for kc in range(22)], [("ACTT", k) for k in range(22)] + [wkey], [("ps", pi)])
                stt(Xb[:, c, :], PS[pi][:], mod_ap(l, 5, c, s), Xb[:, c, :], ALU.mult, ALU.add,
                    [("ps", pi), (xkey, c), ("mods", l, 40 + c)], [(xkey, c)])
            for c in range(8):
                add("w_ffn_out", c * 128, 128, lambda W, wkey, c=c: f_fo(W, wkey, c))

            nst = len(steps)
            views = {}

            def issue_load(k):
                name, c0, n, kcn, fn = steps[k]
                sl = ws_rr[0]
                ws_rr[0] = (sl + 1) % 3
                view = WS[sl][:, 0:kcn * n].rearrange("p (k n) -> p k n", k=kcn)
                dma("sp", view, wsrc(name, l, c0, n), wkeys(name, l), [("WS", sl)], "ld_ws%d" % sl)
                views[k] = (view, ("WS", sl))

            issue_load(0)
            issue_load(1)
            for k in range(nst):
                if k + 2 < nst:
                    issue_load(k + 2)
                if k == 4 and b + 1 < NB:
                    p2_loads(b + 1)
                view, wkey = views[k]
                steps[k][4](view, wkey)
            dma("pool", x_dst[:, t0:t0 + TB].rearrange("(c p) t -> p c t", p=128), Xb[:],
                [(xkey, c) for c in range(8)], [("xd", b)], "st_x%d" % xs)
        T.barrier()

    with nc.Block() as block:
        @block.tensor
        def _(e):
            T.replay("pe", e)

        @block.scalar
        def _(e):
            T.replay("act", e)

        @block.vector
        def _(e):
            T.replay("dve", e)

        @block.gpsimd
        def _(e):
            T.replay("pool", e)

        @block.sync
        def _(e):
            T.replay("sp", e)
    es.close()
    return nc


def rope_tables(pos):
    inv = (10000.0 ** (-np.arange(0, 32, 2, dtype=np.float32) / np.float32(32))).astype(np.float32)
    ang = pos.astype(np.float32)[:, None] * inv[None, :]
    return np.cos(ang).astype(np.float32), np.sin(ang).astype(np.float32)


def make_consts():
    k = np.arange(128)[:, None]
    m = np.arange(128)[None, :]
    ones = np.ones((128, 128), np.float32)
    nn = (k // 64 == m // 64)
    rn = [(k // 32 == 2 * i + m // 64) for i in range(2)]
    nr = [(2 * i + k // 64 == m // 32) for i in range(2)]
    rr = (k // 32 == m // 32)
    mats = [ones, nn, rn[0], rn[1], nr[0], nr[1], rr]
    return np.ascontiguousarray(np.stack([np.asarray(x, np.float32) for x in mats], axis=1))


def prepare_inputs(cfg, inp):
    Ls, Lq, L = cfg["Ls"], cfg["Lq"], cfg["L"]
    voff, NV = vec_layout(L)
    f = lambda a: np.ascontiguousarray(np.asarray(a, dtype=np.float32))
    w_in = f(inp["w_in"])[:L]
    w_q_b = f(inp["w_q_b"])[:L]
    w_kv_b = f(inp["w_kv_b"])[:L]
    shared = {}
    shared["w_in"] = w_in.reshape(L * D, INC)
    kr = w_in[:, :, 640:672]
    krrot = np.concatenate([kr[:, :, 16:32], kr[:, :, 0:16]], axis=2)
    shared["w_kr4"] = f(np.tile(kr, (1, 1, 4))).reshape(L * D, 128)
    shared["w_krrot4"] = f(np.tile(krrot, (1, 1, 4))).reshape(L * D, 128)
    wq = w_q_b.reshape(L, QL, NH, 96)
    shared["wq_nope"] = f(wq[:, :, :, 0:64]).reshape(L * QL, 512)
    shared["wq_rope"] = f(wq[:, :, :, 64:96]).reshape(L * QL, 256)
    shared["wq_rot"] = f(np.concatenate([wq[:, :, :, 80:96], wq[:, :, :, 64:80]], axis=3)).reshape(L * QL, 256)
    wkv = w_kv_b.reshape(L, KVL, NH, 128)
    shared["wkv_nope"] = f(wkv[:, :, :, 0:64]).reshape(L * KVL, 512)
    shared["wkv_v"] = f(wkv[:, :, :, 64:128]).reshape(L * KVL, 512)
    shared["w_o_a"] = f(inp["w_o_a"])[:L].reshape(L * 512, D)
    shared["w_o_b"] = f(inp["w_o_b"])[:L].reshape(L * D, D)
    shared["w_out"] = f(inp["w_out"])[:L].reshape(L * D, D)
    shared["w_ffn_in"] = f(inp["w_ffn_in"])[:L].reshape(L * D, 2 * DFF)
    shared["w_ffn_out"] = f(inp["w_ffn_out"])[:L].reshape(L * DFF, D)
    shared["w_ada"] = f(inp["w_ada"])[:L].reshape(L * D, 6 * D)
    shared["w_sT"] = f(np.transpose(f(inp["w_s"])[:L], (0, 3, 1, 2)))
    shared["bsb"] = f(np.broadcast_to(f(inp["b_s"])[:L][:, None, :, :], (L, 128, 8, 128)))
    shared["consts"] = make_consts()
    vecs = np.zeros((128, NV), np.float32)
    vecs[:, voff[V_EPS]] = EPS
    p = np.arange(128)

    def fm(a, n):
        return f(a)[:L].reshape(L, n, 128).transpose(2, 0, 1).reshape(128, L * n)
    vecs[:, voff[V_N1G]:voff[V_N1G] + 8 * L] = fm(inp["norm1_g"], 8)
    vecs[:, voff[V_N2G]:voff[V_N2G] + 8 * L] = fm(inp["norm2_g"], 8)
    vecs[:, voff[V_SGUG]:voff[V_SGUG] + 8 * L] = fm(inp["sgu_norm_g"], 8)
    vecs[:, voff[V_QAG]:voff[V_QAG] + 3 * L] = fm(inp["q_a_norm_g"], 3)
    vecs[:, voff[V_KVAG]:voff[V_KVAG] + 2 * L] = fm(inp["kv_a_norm_g"], 2)
    vecs[:, voff[V_BADA]:voff[V_BADA] + 48 * L] = fm(inp["b_ada"], 48)
    qg = f(inp["q_norm_g"])[:L]
    kg = f(inp["k_norm_g"])[:L]
    for (g, vn, vr, vrp) in ((qg, V_GQN, V_GQR, V_GQRP), (kg, V_GKN, V_GKR, V_GKRP)):
        vecs[:, voff[vn]:voff[vn] + L] = g[:, p % 64].T
        vecs[:, voff[vr]:voff[vr] + L] = g[:, 64 + p % 32].T
        vecs[:, voff[vrp]:voff[vrp] + L] = g[:, 64 + (p % 32 + 16) % 32].T
    shared["vecs"] = vecs
    xs = f(inp["x_sample"])
    xp = f(inp["x_prompt"])
    cs = f(inp["c_sample"])
    cp = f(inp["c_prompt"])
    fidx = (p % 32) % 16
    sgn = np.where((p % 32) < 16, -1.0, 1.0).astype(np.float32)
    in_maps = []
    for c in range(NCORES):
        ps_, r = c // 4, c % 4
        m = dict(shared)
        m["xT"] = np.ascontiguousarray(np.concatenate([xs[c, :Ls].T, xp[ps_, r * Lq:(r + 1) * Lq].T], axis=1))
        m["cT"] = np.ascontiguousarray(np.stack([cs[c].reshape(8, 128).T, cp[ps_].reshape(8, 128).T], axis=2))
        pos = np.concatenate([np.arange(Ls), r * Lq + np.arange(Lq)])
        cos, sin = rope_tables(pos)
        m["cos4"] = np.ascontiguousarray(cos.T[fidx, :])
        m["sin4"] = np.ascontiguousarray(sin.T[fidx, :] * sgn[:, None])
        in_maps.append(m)
    return in_maps


_PROG_CACHE = {}


def run(cfg, inp):
    key = (cfg["Ls"], cfg["Lq"], cfg["L"], cfg.get("stop"), cfg.get("no_cc"), cfg.get("no_pool"), cfg.get("no_qkstore"))
    if key not in _PROG_CACHE:
        _PROG_CACHE[key] = build_program(cfg)
    nc = _PROG_CACHE[key]
    in_maps = prepare_inputs(cfg, inp)
    res = run_bass_kernel_spmd(nc, in_maps, core_ids=list(range(NCORES)))
    Ls, Lq = cfg["Ls"], cfg["Lq"]
    ys = np.stack([res.results[c]["yT"][:, :Ls].T for c in range(NCORES)], axis=0)
    yp = np.stack([np.concatenate([res.results[4 * s_ + r]["yT"][:, Ls:].T for r in range(4)], axis=0) for s_ in range(2)], axis=0)
    return np.ascontiguousarray(yp, dtype=np.float32), np.ascontiguousarray(ys, dtype=np.float32)


def kernel(**inputs):
    return run(FULL_CFG, inputs)
```

```python
import numpy as np
from contextlib import ExitStack
import concourse.bass as bass
import concourse.mybir as mybir
from concourse.bass_utils import run_bass_kernel_spmd

F32 = mybir.dt.float32
BF16 = mybir.dt.bfloat16
AF = mybir.ActivationFunctionType
ALU = mybir.AluOpType

D = 1024
NH = 8
QL = 384
KVL = 256
DFF = 2816
INC = 4768
EPS = 1e-6
TB = 512
NCORES = 8
SAME_SYNC = True

FULL_CFG = dict(Ls=2048, Lq=2048, L=4)


class Tracker:
    COMPUTE = ("pe", "act", "dve", "pool")

    def __init__(self, nc, es):
        self.nc = nc
        self.es = es
        self.ops = {e: [] for e in ("pe", "act", "dve", "pool", "sp")}
        self.cnt = {e: 0 for e in self.COMPUTE}
        self.esem = {e: es.enter_context(nc.semaphore("c_" + e)) for e in self.COMPUTE}
        self.dsem = {}
        self.seen = {e: {} for e in self.ops}
        self.lastw = {}
        self.readers = {}
        self.nops = 0
        self.no_pool = False

    def _dsem(self, key):
        if key not in self.dsem:
            self.dsem[key] = [self.es.enter_context(self.nc.semaphore("d%d" % len(self.dsem))), 0]
        return self.dsem[key]

    def _resolve(self, tok):
        if tok[0] == "c":
            return self.esem[tok[1]], tok[2], ("c", tok[1])
        d = self._dsem(tok[1])
        return d[0], d[1], ("d", tok[1])

    def op(self, eng, fn, reads=(), writes=(), dma=None, inc=None):
        if eng == "pool" and dma is None and self.no_pool:
            eng = "dve"
        deps = set()
        for r in reads:
            if r in self.lastw:
                deps.add(self.lastw[r])
            if isinstance(r, tuple) and r[0] == "ps":
                for k, v in self.readers.get(r, {}).items():
                    if k != ("c", eng):
                        deps.add(k + (v,))
        for w in writes:
            if w in self.lastw:
                deps.add(self.lastw[w])
            for k, v in self.readers.get(w, {}).items():
                deps.add(k + (v,))
        waits = []
        for tok in deps:
            if tok[0] == "c" and tok[1] == eng and dma is None and (eng == "pe" or not SAME_SYNC):
                continue
            sem, val, sid = self._resolve(tok)
            if self.seen[eng].get(sid, 0) >= val:
                continue
            self.seen[eng][sid] = val
            waits.append((sem, val))
        if dma is not None:
            d = self._dsem(dma)
            step = 16 if inc is None else inc
            d[1] += step
            tok = ("d", dma, d[1])
            incr = (d[0], step)
        else:
            self.cnt[eng] += 1
            tok = ("c", eng, self.cnt[eng])
            incr = (self.esem[eng], 1)
        self.ops[eng].append((waits, fn, incr))
        self.nops += 1
        for r in reads:
            self.readers.setdefault(r, {})[tok[:2]] = tok[2]
        for w in writes:
            self.lastw[w] = tok
            self.readers[w] = {}
        return tok

    @staticmethod
    def _bg(key):
        return isinstance(key, str) and key.startswith("cast")

    def barrier(self):
        for eng in self.ops:
            waits = []
            for f in self.COMPUTE:
                if f == eng or self.cnt[f] == 0:
                    continue
                if self.seen[eng].get(("c", f), 0) < self.cnt[f]:
                    self.seen[eng][("c", f)] = self.cnt[f]
                    waits.append((self.esem[f], self.cnt[f]))
            for key, d in self.dsem.items():
                if self._bg(key):
                    continue
                if d[1] and self.seen[eng].get(("d", key), 0) < d[1]:
                    self.seen[eng][("d", key)] = d[1]
                    waits.append((d[0], d[1]))
            if waits:
                self.ops[eng].append((waits, None, None))
        self.lastw = {r: t for r, t in self.lastw.items() if t[0] == "d" and self._bg(t[1])}
        self.readers = {}

    def replay(self, name, e):
        for waits, fn, incr in self.ops[name]:
            for sem, val in waits:
                e.wait_ge(sem, val)
            if fn is not None:
                ins = fn(e)
                ins.then_inc(incr[0], incr[1])


class SbAlloc:
    BASE = 16512
    TOP = 229344

    def __init__(self, nc):
        self.nc = nc
        self.persist = self.BASE
        self.cur = self.BASE
        self.n = 0

    def _al(self, shape, dt, off):
        self.n += 1
        return self.nc.alloc_sbuf_tensor_at("sb%d" % self.n, list(shape), dt, offset=off)

    def alloc(self, shape, dt, persistent=False):
        esz = 4 if dt == F32 else 2
        nbytes = int(np.prod(shape[1:])) * esz
        nbytes = (nbytes + 31) // 32 * 32
        if persistent:
            assert self.cur == self.persist, "persistent allocs must come first"
            off = self.persist
            self.persist += nbytes
            self.cur = self.persist
        else:
            off = self.cur
            self.cur += nbytes
        assert self.cur <= self.TOP, "SBUF overflow %d" % self.cur
        return self._al(shape, dt, off)

    def phase(self):
        self.cur = self.persist


W_SPECS = [
    ("w_in", 1024, INC, "A"), ("w_kr4", 1024, 128, "A"), ("w_krrot4", 1024, 128, "A"),
    ("wq_nope", QL, 512, "A"), ("wq_rope", QL, 256, "A"), ("wq_rot", QL, 256, "A"),
    ("wkv_nope", KVL, 512, "A"), ("wkv_v", KVL, 512, "A"),
    ("w_o_a", 512, D, "B"), ("w_o_b", D, D, "B"), ("w_out", D, D, "B"),
    ("w_ffn_in", D, 2 * DFF, "B"), ("w_ffn_out", DFF, D, "B"),
]
V_EPS, V_N1G, V_N2G, V_SGUG, V_QAG, V_KVAG, V_GQN, V_GQR, V_GQRP, V_GKN, V_GKR, V_GKRP, V_BADA = range(13)


def vec_layout(L):
    off = {}
    o = 0
    for name, n in [(V_EPS, 1), (V_N1G, 8 * L), (V_N2G, 8 * L), (V_SGUG, 8 * L), (V_QAG, 3 * L), (V_KVAG, 2 * L),
                    (V_GQN, L), (V_GQR, L), (V_GQRP, L), (V_GKN, L), (V_GKR, L), (V_GKRP, L), (V_BADA, 48 * L)]:
        off[name] = o
        o += n
    return off, o


def build_program(cfg):
    Ls, Lq, L = cfg["Ls"], cfg["Lq"], cfg["L"]
    Lp = 4 * Lq
    NTOK = Ls + Lq
    nS, nP = Ls // TB, Lq // TB
    NB = nS + nP
    voff, NV = vec_layout(L)

    nc = bass.Bass("TRN2", target_bir_lowering=False)

    def din(name, shape, dt=F32):
        return nc.dram_tensor(name, list(shape), dt, kind="ExternalInput").ap()

    xT_in = din("xT", [D, NTOK])
    cT_in = din("cT", [128, 8, 2])
    cos_in = din("cos4", [128, NTOK])
    sin_in = din("sin4", [128, NTOK])
    vecs_in = din("vecs", [128, NV])
    consts_in = din("consts", [128, 7, 128])
    w_ada_in = din("w_ada", [L * D, 6 * D])
    wsT_in = din("w_sT", [L, 128, 8, 128])
    bsb_in = din("bsb", [L, 128, 8, 128])
    w_ext = {name: din(name, [L * K, N]) for name, K, N, _ in W_SPECS}
    yT_out = nc.dram_tensor("yT", [D, NTOK], F32, kind="ExternalOutput").ap()

    def dscr(name, shape, dt):
        return nc.dram_tensor(name, list(shape), dt)

    w_bf = {name: dscr(name + "_b", [L * K, N], BF16) for name, K, N, _ in W_SPECS}
    WK = {name: K for name, K, N, _ in W_SPECS}
    xT_d = dscr("xT_d", [D, NTOK], F32)
    hT_d = dscr("hT_d", [D, NTOK], BF16)
    qT_d = dscr("qT_d", [NB * 768, TB], BF16)
    oT_d = dscr("oT_d", [512, NTOK], BF16)
    kT_S = dscr("kT_S", [nS * 768, TB], BF16)
    v_S = dscr("v_S", [Ls, 512], BF16)
    kT_Pin = [dscr("kT_Pin%d" % b, [NH * 96, TB], BF16) for b in range(nP)]
    kT_Pall = [dscr("kT_Pall%d" % b, [4 * NH * 96, TB], BF16) for b in range(nP)]
    v_Pin = [dscr("v_Pin%d" % b, [TB, 512], BF16) for b in range(nP)]
    v_Pall = [dscr("v_Pall%d" % b, [4 * TB, 512], BF16) for b in range(nP)]

    es = ExitStack()
    T = Tracker(nc, es)
    T.no_pool = bool(cfg.get("no_pool"))
    sb = SbAlloc(nc)
    PS = [es.enter_context(nc.psum_tensor("ps%d" % i, [128, 512], F32)) for i in range(8)]
    ps_rr = [0]

    def ps_next():
        i = ps_rr[0]
        ps_rr[0] = (i + 1) % 8
        return i

    def dma(eng, out, in_, reads, writes, key):
        T.op(eng, lambda e, o=out, i=in_: e.dma_start(out=o, in_=i), reads=reads, writes=writes, dma=key)

    def mm(ps_ap, pairs, reads, writes):
        def fn(e, ps_ap=ps_ap, pairs=pairs):
            n = len(pairs)
            ins = None
            for i, (l, r) in enumerate(pairs):
                ins = e.matmul(ps_ap, l, r, start=(i == 0), stop=(i == n - 1))
            return ins
        T.op("pe", fn, reads=reads, writes=writes)

    def act(out, in_, func, reads, writes, **kw):
        T.op("act", lambda e, o=out, i=in_, f=func, kw=kw: e.activation(out=o, in_=i, func=f, **kw), reads=reads, writes=writes)

    def tt(out, in0, in1, op, reads, writes, eng="dve"):
        T.op(eng, lambda e, o=out, a=in0, b=in1, op=op: e.tensor_tensor(out=o, in0=a, in1=b, op=op), reads=reads, writes=writes)

    def stt(out, in0, scalar, in1, op0, op1, reads, writes, eng="dve"):
        T.op(eng, lambda e, o=out, a=in0, s=scalar, b=in1, p0=op0, p1=op1:
             e.scalar_tensor_tensor(out=o, in0=a, scalar=s, in1=b, op0=p0, op1=p1), reads=reads, writes=writes)

    def ts(out, in0, s1, s2, op0, op1, reads, writes, eng="dve"):
        if s2 is None:
            T.op(eng, lambda e, o=out, a=in0, s1=s1, p0=op0: e.tensor_scalar(out=o, in0=a, scalar1=s1, scalar2=None, op0=p0),
                 reads=reads, writes=writes)
        else:
            T.op(eng, lambda e, o=out, a=in0, s1=s1, s2=s2, p0=op0, p1=op1:
                 e.tensor_scalar(out=o, in0=a, scalar1=s1, scalar2=s2, op0=p0, op1=p1), reads=reads, writes=writes)

    def recip(out, in_, reads, writes):
        T.op("dve", lambda e, o=out, i=in_: e.reciprocal(out=o, in_=i), reads=reads, writes=writes)

    def memset(ap, val, writes, eng="dve"):
        T.op(eng, lambda e, a=ap, v=val: e.memset(a, v), writes=writes)

    vecs = sb.alloc([128, NV], F32, True)
    cst = sb.alloc([128, 7, 128], BF16, True)
    ones_f = sb.alloc([128, 64], F32, True)
    mods = sb.alloc([128, L, 48, 2], F32, True)
    A1 = sb.alloc([128, L, 2, 8], F32, True)
    A2 = sb.alloc([128, L, 2, 8], F32, True)
    silc = sb.alloc([128, 8, 2], BF16, True)
    eps_ap = vecs[:, voff[V_EPS]:voff[V_EPS] + 1]

    def vcol(kind, idx):
        o = voff[kind] + idx
        return vecs[:, o:o + 1]

    ONES, NN, RN0, RN1, NR0, NR1, RR = [cst[:, i, :] for i in range(7)]
    RN = [RN0, RN1]
    NR = [NR0, NR1]

    dma("sp", vecs[:], vecs_in, [], ["vecs"], "ld_misc")
    dma("pool", cst[:], consts_in, [], ["cst"], "ld_cst")
    memset(ones_f[:], 1.0, ["ones_f"])

    def cast_weights(l, grp):
        for name, K, N, g in W_SPECS:
            if g != grp:
                continue
            for rb in range(0, K, 128):
                r0 = l * K + rb
                dma("pool", w_bf[name][r0:r0 + 128, :], w_ext[name][r0:r0 + 128, :], [], [("wb", name, l, rb // 128)],
                    "cast%s%d" % (grp, l))

    def wkeys(name, l):
        return [("wb", name, l, i) for i in range(WK[name] // 128)]

    def wsrc(name, l, c0, n):
        K = WK[name]
        return w_bf[name][l * K:(l + 1) * K, c0:c0 + n].rearrange("(kc p) n -> p kc n", p=128)

    cast_weights(0, "A")

    cTt = sb.alloc([128, 8, 2], F32)
    WA0 = [sb.alloc([128, 8, 512], BF16) for _ in range(2)]
    dma("sp", cTt[:], cT_in, [], ["cTt"], "ld_misc")
    act(silc[:], cTt[:], AF.Silu, ["cTt"], ["silc"])

    def mods_load(l, jg, WA):
        s_ = jg % 2
        src = w_ada_in[l * D:(l + 1) * D, jg * 512:(jg + 1) * 512].rearrange("(kc p) n -> p kc n", p=128)
        dma("pool", WA[s_][:], src, [], [("WA", s_)], "ld_wa%d" % s_)

    def mods_compute(l, jg, WA, banks=None):
        s_ = jg % 2
        for jj in range(4):
            j = jg * 4 + jj
            pi = ps_next() if banks is None else banks[jj % len(banks)]
            mm(PS[pi][:, 0:2], [(WA[s_][:, kc, jj * 128:(jj + 1) * 128], silc[:, kc, :]) for kc in range(8)],
               [("WA", s_), "silc"], [("ps", pi)])
            ts(mods[:, l, j, :], PS[pi][:, 0:2], vcol(V_BADA, l * 48 + j), None, ALU.add, None,
               [("ps", pi), "vecs"], [("mods", l, j)])

    def mods_finish(l):
        for s_ in range(2):
            for (Ax, gk, m0) in ((A1, V_N1G, 8), (A2, V_N2G, 32)):
                ts(Ax[:, l, s_, :], mods[:, l, m0:m0 + 8, s_], 1.0, None, ALU.add, None,
                   [("mods", l, j) for j in range(m0, m0 + 8)], [("A", id(Ax), l, s_)])
                tt(Ax[:, l, s_, :], Ax[:, l, s_, :], vecs[:, voff[gk] + 8 * l:voff[gk] + 8 * l + 8], ALU.mult,
                   [("A", id(Ax), l, s_), "vecs"], [("A", id(Ax), l, s_)])

    mods_load(0, 0, WA0)
    for jg in range(12):
        if jg + 1 < 12:
            mods_load(0, jg + 1, WA0)
        mods_compute(0, jg, WA0)
    mods_finish(0)
    cast_weights(0, "B")
    T.barrier()
    STOP = cfg.get("stop")

    def mod_ap(l, m, c, s):
        return mods[:, l, m * 8 + c, s:s + 1]

    def tok0(b):
        return b * TB

    def seg(b):
        return 0 if b < nS else 1

    def rstd_from_psum(rs_ap, pi, n, rs_key):
        act(rs_ap, PS[pi][:], AF.Ln, [("ps", pi), "vecs"], [rs_key], bias=eps_ap, scale=1.0 / n)
        act(rs_ap, rs_ap, AF.Exp, [rs_key], [rs_key], scale=-0.5)

    def rms_stats(sq_list, lhs_list, n, rs_ap, rs_key, reads):
        pi = ps_next()
        mm(PS[pi][:], list(zip(lhs_list, sq_list)), reads + ["cst"], [("ps", pi)])
        rstd_from_psum(rs_ap, pi, n, rs_key)

    def norm_block(Xb, xkey, Hb, hkey, SQ, RS, Ax, l, s, shm, NT):
        for c in range(8):
            tt(SQ[:, c, :], Xb[:, c, :], Xb[:, c, :], ALU.mult, [(xkey, c)], [("SQ", c)], eng="pool")
        rms_stats([SQ[:, c, :] for c in range(8)], [ONES] * 8, D, RS[:], "RS", [("SQ", c) for c in range(8)])
        for c in range(8):
            i = c % 2
            tt(NT[i][:], Xb[:, c, :], RS[:], ALU.mult, [(xkey, c), "RS"], [("NT", i)])
            act(Hb[:, c, :], NT[i][:], AF.Identity, [("NT", i), ("A", id(Ax), l, s), ("mods", l, shm * 8 + c)], [(hkey, c)],
                scale=Ax[:, l, s, c:c + 1], bias=mod_ap(l, shm, c, s))

    PHASES = {}
    for l in range(L if STOP != "prologue" else 0):
        x_src = xT_in if l == 0 else xT_d
        x_dst = yT_out if l == L - 1 else xT_d
        last = (l == L - 1)

        if "ph1" not in PHASES:
            sb.phase()
            ns = {}
            ns["X"] = [sb.alloc([128, 8, TB], F32) for _ in range(2)]
            ns["H"] = [sb.alloc([128, 8, TB], BF16) for _ in range(2)]
            ns["SQ"] = sb.alloc([128, 8, TB], BF16)
            ns["RS"] = sb.alloc([128, TB], F32)
            ns["W1in"] = sb.alloc([128, 8, 640], BF16)
            ns["Wkr"] = sb.alloc([128, 8, 128], BF16)
            ns["Wkrr"] = sb.alloc([128, 8, 128], BF16)
            ns["Wqn"] = sb.alloc([128, 3, 512], BF16)
            ns["Wqr"] = sb.alloc([128, 3, 256], BF16)
            ns["Wqt"] = sb.alloc([128, 3, 256], BF16)
            ns["Wkn"] = sb.alloc([128, 2, 512], BF16)
            ns["Wv"] = sb.alloc([128, 2, 512], BF16)
            ns["QN"] = sb.alloc([128, 3, TB], BF16)
            ns["KVN"] = sb.alloc([128, 2, TB], BF16)
            ns["QO"] = [sb.alloc([128, 6, TB], BF16) for _ in range(2)]
            ns["KO"] = [sb.alloc([128, 6, TB], BF16) for _ in range(2)]
            ns["VO"] = [sb.alloc([128, 4, 512], BF16) for _ in range(2)]
            ns["TK"] = sb.alloc([128, TB], F32)
            ns["SQRK"] = sb.alloc([128, TB], BF16)
            ns["COS"] = [sb.alloc([128, TB], F32) for _ in range(2)]
            ns["SIN"] = [sb.alloc([128, TB], F32) for _ in range(2)]
            ns["CQ"] = sb.alloc([128, TB], F32)
            ns["SQT"] = sb.alloc([128, TB], F32)
            ns["CK"] = sb.alloc([128, TB], F32)
            ns["SK"] = sb.alloc([128, TB], F32)
            ns["SQH"] = [sb.alloc([128, TB], BF16) for _ in range(3)]
            ns["RSH"] = [sb.alloc([128, TB], F32) for _ in range(3)]
            ns["T1"] = sb.alloc([128, TB], F32)
            ns["T2"] = sb.alloc([128, TB], F32)
            ns["NTT"] = [sb.alloc([128, TB], F32) for _ in range(2)]
            PHASES["ph1"] = ns
        ns = PHASES["ph1"]
        X = ns["X"]
        H = ns["H"]
        SQ = ns["SQ"]
        RS = ns["RS"]
        W1in = ns["W1in"]
        Wkr = ns["Wkr"]
        Wkrr = ns["Wkrr"]
        Wqn = ns["Wqn"]
        Wqr = ns["Wqr"]
        Wqt = ns["Wqt"]
        Wkn = ns["Wkn"]
        Wv = ns["Wv"]
        QN = ns["QN"]
        KVN = ns["KVN"]
        QO = ns["QO"]
        KO = ns["KO"]
        VO = ns["VO"]
        TK = ns["TK"]
        SQRK = ns["SQRK"]
        COS = ns["COS"]
        SIN = ns["SIN"]
        CQ = ns["CQ"]
        SQT = ns["SQT"]
        CK = ns["CK"]
        SK = ns["SK"]
        SQH = ns["SQH"]
        RSH = ns["RSH"]
        T1 = ns["T1"]
        T2 = ns["T2"]
        NTT = ns["NTT"]

        for (buf, name, c0, n) in ((W1in, "w_in", 0, 640), (Wkr, "w_kr4", 0, 128), (Wkrr, "w_krrot4", 0, 128),
                                   (Wqn, "wq_nope", 0, 512), (Wqr, "wq_rope", 0, 256), (Wqt, "wq_rot", 0, 256),
                                   (Wkn, "wkv_nope", 0, 512), (Wv, "wkv_v", 0, 512)):
            dma("sp", buf[:], wsrc(name, l, c0, n), wkeys(name, l), [("W1", name)], "ld_w1")
        W1R = [("W1", n) for n in ("w_in", "w_kr4", "w_krrot4", "wq_nope", "wq_rope", "wq_rot", "wkv_nope", "wkv_v")]

        order = list(range(nS, NB)) + list(range(nS))

        def p1_loads(i):
            b = order[i]
            xs = i % 2
            t0 = tok0(b)
            dma("sp", X[xs][:], x_src[:, t0:t0 + TB].rearrange("(c p) t -> p c t", p=128), [("xd", b)],
                [(("X", xs), c) for c in range(8)], "ld_x%d" % xs)
            dma("sp", COS[xs][:], cos_in[:, t0:t0 + TB], [], [("COS", xs)], "ld_cs%d" % xs)
            dma("sp", SIN[xs][:], sin_in[:, t0:t0 + TB], [], [("SIN", xs)], "ld_cs%d" % xs)

        def head_half(hh, PSn, PSr_sq_key, sqr_ap, rope_src_fn, OUT, okey, gn, kdst_fn, rope_keys):
            for i in range(2):
                act(SQH[i][:], PS[PSn[i]][:], AF.Square, [("ps", PSn[i])], [("SQH", i)])
            for i in range(2):
                pi = ps_next()
                mm(PS[pi][:], [(NN, SQH[i][:]), (RN[i], sqr_ap)], [("SQH", i), PSr_sq_key, "cst"], [("ps", pi)])
                rstd_from_psum(RSH[i][:], pi, 96, ("RSH", i))
            pi = ps_next()
            mm(PS[pi][:], [(NR[0], SQH[0][:]), (NR[1], SQH[1][:]), (RR, sqr_ap)],
               [("SQH", 0), ("SQH", 1), PSr_sq_key, "cst"], [("ps", pi)])
            rstd_from_psum(RSH[2][:], pi, 96, ("RSH", 2))
            for i in range(2):
                jn = 2 * hh + i
                stt(OUT[:, jn, :], PS[PSn[i]][:], vcol(gn, l), RSH[i][:], ALU.mult, ALU.mult,
                    [("ps", PSn[i]), ("RSH", i), "vecs"], [(okey, jn)])
            rap, rkeys = rope_src_fn(hh)
            tt(OUT[:, 4 + hh, :], rap, RSH[2][:], ALU.mult, rkeys + [("RSH", 2)] + rope_keys, [(okey, 4 + hh)])
            kdst_fn(hh)

        for i, b in enumerate(order):
            if i == 0:
                p1_loads(0)
            if i + 1 < NB:
                p1_loads(i + 1)
            xs = i % 2
            s = seg(b)
            t0 = tok0(b)
            Xb, Hb = X[xs], H[xs]
            QOb, KOb, VOb = QO[xs], KO[xs], VO[xs]
            qok, kok, vok = "QO%d" % xs, "KO%d" % xs, "VO%d" % xs
            xkey, hkey = ("X", xs), ("H", xs)
            norm_block(Xb, xkey, Hb, hkey, SQ, RS, A1, l, s, 0, NTT)
            dma("pool", hT_d[:, t0:t0 + TB].rearrange("(c p) t -> p c t", p=128), Hb[:],
                [(hkey, c) for c in range(8)], [("hd", b)], "st_h%d" % xs)
            hreads = [(hkey, c) for c in range(8)] + W1R
            ts(CQ[:], COS[xs][:], vcol(V_GQR, l), None, ALU.mult, None, [("COS", xs), "vecs"], ["CQ"])
            ts(SQT[:], SIN[xs][:], vcol(V_GQRP, l), None, ALU.mult, None, [("SIN", xs), "vecs"], ["SQT"])
            ts(CK[:], COS[xs][:], vcol(V_GKR, l), None, ALU.mult, None, [("COS", xs), "vecs"], ["CK"])
            ts(SK[:], SIN[xs][:], vcol(V_GKRP, l), None, ALU.mult, None, [("SIN", xs), "vecs"], ["SK"])

            def latent(c0, nch, n, gk, OUTN, okey):
                pis = []
                for j in range(nch):
                    pi = ps_next()
                    pis.append(pi)
                    mm(PS[pi][:], [(W1in[:, kc, c0 + j * 128:c0 + (j + 1) * 128], Hb[:, kc, :]) for kc in range(8)],
                       hreads, [("ps", pi)])
                    act(SQ[:, j, :], PS[pi][:], AF.Square, [("ps", pi)], [("SQ", j)])
                rms_stats([SQ[:, j, :] for j in range(nch)], [ONES] * nch, n, RS[:], "RS", [("SQ", j) for j in range(nch)])
                for j in range(nch):
                    stt(OUTN[:, j, :], PS[pis[j]][:], vcol(gk, l * nch + j), RS[:], ALU.mult, ALU.mult,
                        [("ps", pis[j]), "RS", "vecs"], [(okey, j)])

            latent(0, 3, QL, V_QAG, QN, "QN")
            latent(QL, 2, KVL, V_KVAG, KVN, "KVN")
            qn_r = [("QN", j) for j in range(3)] + W1R
            kvn_r = [("KVN", j) for j in range(2)] + W1R

            for t4 in range(4):
                pi = ps_next()
                mm(PS[pi][:], [(KVN[:, kc, t4 * 128:(t4 + 1) * 128], Wv[:, kc, :]) for kc in range(2)], kvn_r, [("ps", pi)])
                act(VOb[:, t4, :], PS[pi][:], AF.Copy, [("ps", pi)], [(vok, t4)])
            bp = b - nS
            vdst = (v_S if s == 0 else v_Pin[bp])
            tl = t0 if s == 0 else 0
            dma("pool", vdst[tl:tl + TB, :].rearrange("(a p) c -> p a c", p=128), VOb[:],
                [(vok, a) for a in range(4)], [("vdst", s)], "st_v%d" % xs)

            pk = ps_next()
            mm(PS[pk][:], [(Wkr[:, kc, :], Hb[:, kc, :]) for kc in range(8)], hreads, [("ps", pk)])
            pkr = ps_next()
            mm(PS[pkr][:], [(Wkrr[:, kc, :], Hb[:, kc, :]) for kc in range(8)], hreads, [("ps", pkr)])
            act(SQRK[:], PS[pk][:], AF.Square, [("ps", pk)], ["SQRK"])
            tt(TK[:], PS[pk][:], CK[:], ALU.mult, [("ps", pk), "CK"], ["TK"])
            tt(T2[:], PS[pkr][:], SK[:], ALU.mult, [("ps", pkr), "SK"], ["T2"])
            tt(TK[:], TK[:], T2[:], ALU.add, ["TK", "T2"], ["TK"], eng="pool")

            kdst = kT_S if s == 0 else kT_Pin[bp]
            for hh in range(2):
                pn = []
                for i2 in range(2):
                    jn = 2 * hh + i2
                    pi = ps_next()
                    pn.append(pi)
                    mm(PS[pi][:], [(Wqn[:, kc, jn * 128:(jn + 1) * 128], QN[:, kc, :]) for kc in range(3)], qn_r, [("ps", pi)])
                pr = ps_next()
                mm(PS[pr][:], [(Wqr[:, kc, hh * 128:(hh + 1) * 128], QN[:, kc, :]) for kc in range(3)], qn_r, [("ps", pr)])
                pt = ps_next()
                mm(PS[pt][:], [(Wqt[:, kc, hh * 128:(hh + 1) * 128], QN[:, kc, :]) for kc in range(3)], qn_r, [("ps", pt)])
                act(SQH[2][:], PS[pr][:], AF.Square, [("ps", pr)], [("SQH", 2)])

                def q_rope(hh_, pr=pr, pt=pt):
                    tt(T1[:], PS[pr][:], CQ[:], ALU.mult, [("ps", pr), "CQ"], ["T1"])
                    tt(T2[:], PS[pt][:], SQT[:], ALU.mult, [("ps", pt), "SQT"], ["T2"])
                    tt(T1[:], T1[:], T2[:], ALU.add, ["T1", "T2"], ["T1"], eng="pool")
                    return T1[:], ["T1"]

                head_half(hh, pn, ("SQH", 2), SQH[2][:], q_rope, QOb, qok, V_GQN, lambda hh_: None, [])
                pn = []
                for i2 in range(2):
                    jn = 2 * hh + i2
                    pi = ps_next()
                    pn.append(pi)
                    mm(PS[pi][:], [(Wkn[:, kc, jn * 128:(jn + 1) * 128], KVN[:, kc, :]) for kc in range(2)], kvn_r, [("ps", pi)])

                def k_rope(hh_):
                    return TK[:], ["TK"]

                head_half(hh, pn, "SQRK", SQRK[:], k_rope, KOb, kok, V_GKN, lambda hh_: None, [])
            if not cfg.get("no_qkstore"):
                dma("pool", qT_d[b * 768:(b + 1) * 768, :].rearrange("(j p) t -> p j t", p=128), QOb[:],
                    [(qok, j) for j in range(6)], [("qkdst", "q")], "st_q%d" % xs)
                krows = kT_S[b * 768:(b + 1) * 768, :] if s == 0 else kT_Pin[bp][:, :]
                dma("pool", krows.rearrange("(j p) t -> p j t", p=128), KOb[:],
                    [(kok, j) for j in range(6)], [("qkdst", "k%d" % s)], "st_k%d" % xs)

            if s == 1 and not cfg.get("no_cc"):
                for ci, (src, dst, rk) in enumerate(((kT_Pin[bp], kT_Pall[bp], ("qkdst", "k1")), (v_Pin[bp], v_Pall[bp], ("vdst", 1)))):
                    T.op("pool", lambda e, src=src, dst=dst: e.collective_compute(
                        "AllGather", ALU.bypass, replica_groups=[[0, 1, 2, 3], [4, 5, 6, 7]],
                        ins=[src.ap().opt()], outs=[dst.ap().opt()]),
                        reads=[rk], writes=[("gath", ci, bp)], dma="cc%d_%d_%d" % (l, ci, bp), inc=1)
        T.barrier()
        if STOP == "pass1":
            break

        LkMax = max(Ls, Lp)
        if "ph2" not in PHASES:
            sb.phase()
            ns = {}
            ns["KT"] = [sb.alloc([96, LkMax], BF16) for _ in range(2)]
            ns["VP"] = [sb.alloc([128, LkMax // 128, 65], BF16) for _ in range(2)]
            ns["QT"] = [sb.alloc([96, max(Ls, Lq)], BF16) for _ in range(2)]
            ns["PT"] = [sb.alloc([128, TB], BF16) for _ in range(4)]
            ns["OSB"] = [sb.alloc([128, TB], F32) for _ in range(2)]
            ns["OTH"] = [sb.alloc([64, TB], BF16) for _ in range(2)]
            ns["WAa"] = [sb.alloc([128, 8, 512], BF16) for _ in range(2)]
            PHASES["ph2"] = ns
        ns = PHASES["ph2"]
        KT = ns["KT"]
        VP = ns["VP"]
        QT = ns["QT"]
        PT = ns["PT"]
        OSB = ns["OSB"]
        OTH = ns["OTH"]
        WAa = ns["WAa"]
        bg = []
        if l + 1 < L:
            cast_weights(l + 1, "A")
            cast_weights(l + 1, "B")
            for k in range(13):
                def step(k=k):
                    if k < 12:
                        mods_load(l + 1, k, WAa)
                    if k >= 1:
                        mods_compute(l + 1, k - 1, WAa, banks=[7])
                bg.append(step)
            bg.append(lambda: mods_finish(l + 1))
        for sl in range(2):
            memset(VP[sl][:, :, 64:65], 1.0, [("VP", sl)])
        scale = 96 ** -0.5
        hcount = 0
        for s in range(2):
            Lk = Ls if s == 0 else Lp
            Lseg = Ls if s == 0 else Lq
            tbase = 0 if s == 0 else Ls
            nkb = Lk // 128
            nqb = Lseg // TB

            def att_loads(h, sl, s=s, Lk=Lk, Lseg=Lseg, tbase=tbase, nkb=nkb):
                rn = (h // 2) * 128 + 64 * (h % 2)
                rr = (4 + h // 4) * 128 + 32 * (h % 4)
                parts = ((0, 64, rn), (64, 32, rr))
                if s == 0:
                    kv = kT_S.ap().rearrange("(b r) t -> r b t", r=768)
                    for (p0, np_, r0) in parts:
                        dma("sp", KT[sl][p0:p0 + np_, 0:Lk].rearrange("d (b t) -> d b t", t=TB), kv[r0:r0 + np_, :, :],
                            [("qkdst", "k0")], [("KT", sl)], "ld_kt%d" % sl)
                    vsrc = v_S[:, h * 64:(h + 1) * 64].rearrange("(kb p) c -> p kb c", p=128)
                    dma("sp", VP[sl][:, 0:nkb, 0:64], vsrc, [("vdst", 0)], [("VP", sl)], "ld_vp%d" % sl)
                else:
                    for bq in range(nP):
                        kv = kT_Pall[bq].ap().rearrange("(r c) t -> c r t", r=4)
                        for (p0, np_, r0) in parts:
                            dma("sp", KT[sl][p0:p0 + np_, bq * 4 * TB:(bq + 1) * 4 * TB].rearrange("d (r t) -> d r t", r=4),
                                kv[r0:r0 + np_, :, :], [("gath", 0, bq)], [("KT", sl)], "ld_kt%d" % sl)
                        vsrc = v_Pall[bq][:, h * 64:(h + 1) * 64].rearrange("(kb p) c -> p kb c", p=128)
                        dma("sp", VP[sl][:, bq * 16:(bq + 1) * 16, 0:64], vsrc, [("gath", 1, bq)], [("VP", sl)], "ld_vp%d" % sl)
                blk0 = 0 if s == 0 else nS
                nblk = Lseg // TB
                qv = qT_d.ap().rearrange("(b r) t -> r b t", r=768)
                for (p0, np_, r0) in parts:
                    dma("sp", QT[sl][p0:p0 + np_, 0:Lseg].rearrange("d (b t) -> d b t", t=TB), qv[r0:r0 + np_, blk0:blk0 + nblk, :],
                        [("qkdst", "q")], [("QT", sl)], "ld_qt%d" % sl)

            att_loads(0, hcount % 2)
            for h in range(NH):
                sl = hcount % 2
                hcount += 1
                if h + 1 < NH:
                    att_loads(h + 1, hcount % 2)
                if bg:
                    bg.pop(0)()
                for qb in range(nqb):
                    po = 5 + (qb % 2)
                    steps = list(range(nkb))
                    spi = {}

                    def qk(kb, sl=sl, qb=qb):
                        pi = kb % 5
                        spi[kb] = pi
                        mm(PS[pi][:], [(KT[sl][:, kb * 128:(kb + 1) * 128], QT[sl][:, qb * TB:(qb + 1) * TB])],
                           [("KT", sl), ("QT", sl)], [("ps", pi)])

                    qk(0)
                    if nkb > 1:
                        qk(1)
                    for kb in steps:
                        pi = spi[kb]
                        pt_i = kb % 4
                        act(PT[pt_i][:], PS[pi][:], AF.Exp, [("ps", pi)], [("PT", pt_i)], scale=scale)
                        if kb + 2 < nkb:
                            qk(kb + 2)
                        T.op("pe", lambda e, po=po, sl=sl, kb=kb, pt_i=pt_i, nkb=nkb:
                             e.matmul(PS[po][0:65, :], VP[sl][:, kb, :], PT[pt_i][:], start=(kb == 0), stop=(kb == nkb - 1)),
                             reads=[("VP", sl), ("PT", pt_i)], writes=[("ps", po)])
                    ob = qb % 2
                    T.op("dve", lambda e, ob=ob, po=po: e.tensor_copy(out=OSB[ob][0:65, :], in_=PS[po][0:65, :]),
                         reads=[("ps", po)], writes=[("OSB", ob)])
                    recip(OSB[ob][64:65, :], OSB[ob][64:65, :], [("OSB", ob)], [("OSB", ob)])
                    mm(PS[7][0:64, :], [(ones_f[64:65, 0:64], OSB[ob][64:65, :])], [("OSB", ob), "ones_f"], [("ps", 7)])
                    tt(OTH[ob][:], OSB[ob][0:64, :], PS[7][0:64, :], ALU.mult, [("OSB", ob), ("ps", 7)], [("OTH", ob)])
                    tq = tbase + qb * TB
                    dma("pool", oT_d[h * 64:(h + 1) * 64, tq:tq + TB], OTH[ob][:], [("OTH", ob)], [("od",)], "st_o%d" % ob)
        while bg:
            bg.pop(0)()
        T.barrier()
        if STOP == "attn":
            break

        if "ph3" not in PHASES:
            sb.phase()
            ns = {}
            ns["X"] = [sb.alloc([128, 8, TB], F32) for _ in range(2)]
            ns["H"] = [sb.alloc([128, 8, TB], BF16) for _ in range(2)]
            ns["OT"] = [sb.alloc([128, 4, TB], BF16) for _ in range(2)]
            ns["US"] = sb.alloc([128, 8, TB], BF16)
            ns["BIGF"] = sb.alloc([128, 4096], F32)
            ns["VM"] = sb.alloc([128, 4096], BF16)
            ns["SH"] = sb.alloc([128, 8, TB], BF16)
            ns["SGT"] = sb.alloc([128, 4, TB], F32)
            ns["ACTT"] = sb.alloc([128, 22, TB], BF16)
            ns["SQ"] = sb.alloc([128, 8, TB], BF16)
            ns["RS"] = sb.alloc([128, TB], F32)
            ns["WS"] = [sb.alloc([128, 4096], BF16) for _ in range(3)]
            ns["BSB"] = sb.alloc([128, 8, 128], F32)
            ns["WST"] = sb.alloc([128, 8, 128], BF16)
            ns["SSV"] = sb.alloc([128, 4], F32)
            ns["JUNK"] = sb.alloc([128, 1024], BF16)
            ns["TMP"] = [sb.alloc([128, TB], F32) for _ in range(2)]
            PHASES["ph3"] = ns
        ns = PHASES["ph3"]
        X = ns["X"]
        H = ns["H"]
        OT = ns["OT"]
        US = ns["US"]
        BIGF = ns["BIGF"]
        VM = ns["VM"]
        SH = ns["SH"]
        SGT = ns["SGT"]
        ACTT = ns["ACTT"]
        SQ = ns["SQ"]
        RS = ns["RS"]
        WS = ns["WS"]
        BSB = ns["BSB"]
        WST = ns["WST"]
        SSV = ns["SSV"]
        JUNK = ns["JUNK"]
        TMP = ns["TMP"]
        GV = BIGF[:].rearrange("p (a b) -> p a b", a=4)
        MT = BIGF[:].rearrange("p (a b) -> p a b", a=8)
        VH = VM[:].rearrange("p (a b) -> p a b", a=4)
        MG = VM[:].rearrange("p (a b) -> p a b", a=8)
        dma("sp", BSB[:], bsb_in[l], [], ["BSB"], "ld_misc")
        dma("pool", WST[:], wsT_in[l], [], ["WST"], "ld_cst")
        tmp_rr = [0]

        def tmp_next():
            i = tmp_rr[0]
            tmp_rr[0] = 1 - i
            return i

        ws_rr = [0]
        pending = []

        def p2_loads(i):
            b = i
            xs = i % 2
            t0 = tok0(b)
            dma("sp", X[xs][:], x_src[:, t0:t0 + TB].rearrange("(c p) t -> p c t", p=128), [("xd", b)],
                [(("X", xs), c) for c in range(8)], "ld_x%d" % xs)
            dma("sp", H[xs][:], hT_d[:, t0:t0 + TB].rearrange("(c p) t -> p c t", p=128), [("hd", b)],
                [(("H", xs), c) for c in range(8)], "ld_h%d" % xs)
            dma("sp", OT[xs][:], oT_d[:, t0:t0 + TB].rearrange("(c p) t -> p c t", p=128), [("od",)],
                [("OT", xs)], "ld_o%d" % xs)

        for b in range(NB):
            if b == 0:
                p2_loads(0)
            xs = b % 2
            s = seg(b)
            t0 = tok0(b)
            Xb, Hb, OTb = X[xs], H[xs], OT[xs]
            xkey, hkey = ("X", xs), ("H", xs)
            hreads = [(hkey, c) for c in range(8)]

            steps = []

            def add(name, c0, n, fn):
                steps.append((name, c0, n, WK[name] // 128, fn))

            def f_zu(W, wkey, g):
                for jj in range(4):
                    c = g * 4 + jj
                    pi = ps_next()
                    mm(PS[pi][:], [(W[:, kc, jj * 128:(jj + 1) * 128], Hb[:, kc, :]) for kc in range(8)], hreads + [wkey], [("ps", pi)])
                    act(US[:, c, :], PS[pi][:], AF.Gelu_apprx_tanh, [("ps", pi)], [("US", c)])
            for g in range(2):
                add("w_in", 672 + g * 512, 512, lambda W, wkey, g=g: f_zu(W, wkey, g))

            def f_zv(W, wkey, half):
                for t4 in range(4):
                    pi = ps_next()
                    mm(PS[pi][:], [(Hb[:, kc, t4 * 128:(t4 + 1) * 128], W[:, kc, :]) for kc in range(8)], hreads + [wkey], [("ps", pi)])
                    act(GV[:, t4, half * 512:(half + 1) * 512], PS[pi][:], AF.Gelu_apprx_tanh, [("ps", pi)], [("BIGF", 2 * t4 + half)])
                if half == 1:
                    memset(SSV[:], 0.0, [("SSV", t4) for t4 in range(4)])
                    for t4 in range(4):
                        act(JUNK[:], GV[:, t4, :], AF.Square, [("BIGF", 2 * t4), ("BIGF", 2 * t4 + 1)], ["JUNK", ("SSV", t4)],
                            accum_out=SSV[:, t4:t4 + 1])
                    act(SSV[:], SSV[:], AF.Sqrt, [("SSV", t4) for t4 in range(4)] + ["vecs"], [("SSV", t4) for t4 in range(4)],
                        bias=eps_ap, scale=1.0 / D)
                    recip(SSV[:], SSV[:], [("SSV", t4) for t4 in range(4)], [("SSV", t4) for t4 in range(4)])
                    for t4 in range(4):
                        ts(VH[:, t4, :], GV[:, t4, :], SSV[:, t4:t4 + 1], None, ALU.mult, None,
                           [("BIGF", 2 * t4), ("BIGF", 2 * t4 + 1), ("SSV", t4)], [("VM", 2 * t4), ("VM", 2 * t4 + 1)])
                    for g in range(8):
                        pi = ps_next()

                        def fmix(e, pi=pi, g=g):
                            ins = None
                            for t4 in range(4):
                                ins = e.matmul(PS[pi][:, t4 * 128:(t4 + 1) * 128], VH[:, t4, g * 128:(g + 1) * 128], WST[:, g, :],
                                               start=True, stop=True)
                            return ins
                        T.op("pe", fmix, reads=[("VM", k) for k in range(8)] + ["WST"], writes=[("ps", pi)])
                        ti = tmp_next()
                        for t4 in range(4):
                            stt(TMP[ti][:, t4 * 128:(t4 + 1) * 128], PS[pi][:, t4 * 128:(t4 + 1) * 128], vcol(V_SGUG, l * 8 + g),
                                BSB[:, g, :], ALU.mult, ALU.add, [("ps", pi), "BSB", "vecs"], [("TMP", ti)])
                        tt(SH[:, g, :], TMP[ti][:], US[:, g, :], ALU.mult, [("TMP", ti), ("US", g)], [("SH", g)], eng="pool")
            for half in range(2):
                add("w_in", 1696 + half * 512, 512, lambda W, wkey, half=half: f_zv(W, wkey, half))

            def f_gate(W, wkey, g):
                for jj in range(4):
                    c = g * 4 + jj
                    pi = ps_next()
                    mm(PS[pi][:], [(W[:, kc, jj * 128:(jj + 1) * 128], Hb[:, kc, :]) for kc in range(8)], hreads + [wkey], [("ps", pi)])
                    act(US[:, c, :], PS[pi][:], AF.Sigmoid, [("ps", pi)], [("US", c)])

            def f_ob(W, wkey, g):
                for jj in range(4):
                    c = g * 4 + jj
                    pi = ps_next()
                    mm(PS[pi][:], [(W[:, kc, jj * 128:(jj + 1) * 128], SH[:, kc, :]) for kc in range(8)],
                       [("SH", k) for k in range(8)] + [wkey], [("ps", pi)])
                    tt(MT[:, c, :], PS[pi][:], US[:, c, :], ALU.mult, [("ps", pi), ("US", c)], [("BIGF", c)])

            def f_oa(W, wkey):
                for c in range(8):
                    pi = ps_next()
                    mm(PS[pi][:], [(W[:, kc, c * 128:(c + 1) * 128], OTb[:, kc, :]) for kc in range(4)], [("OT", xs), wkey], [("ps", pi)])
                    ti = tmp_next()
                    tt(TMP[ti][:], PS[pi][:], US[:, c, :], ALU.mult, [("ps", pi), ("US", c)], [("TMP", ti)])
                    tt(MG[:, c, :], TMP[ti][:], MT[:, c, :], ALU.add, [("TMP", ti), ("BIGF", c)], [("VM", c)], eng="pool")

            def f_out(W, wkey, g):
                for jj in range(4):
                    c = g * 4 + jj
                    pi = ps_next()
                    mm(PS[pi][:], [(W[:, kc, jj * 128:(jj + 1) * 128], MG[:, kc, :]) for kc in range(8)],
                       [("VM", k) for k in range(8)] + [wkey], [("ps", pi)])
                    stt(Xb[:, c, :], PS[pi][:], mod_ap(l, 2, c, s), Xb[:, c, :], ALU.mult, ALU.add,
                        [("ps", pi), (xkey, c), ("mods", l, 16 + c)], [(xkey, c)])
                if g == 1:
                    norm_block(Xb, xkey, SH, "SH", SQ, RS, A2, l, s, 3, TMP)

            for g in range(2):
                add("w_in", 3744 + g * 512, 512, lambda W, wkey, g=g: f_gate(W, wkey, g))
            for g in range(2):
                add("w_o_b", g * 512, 512, lambda W, wkey, g=g: f_ob(W, wkey, g))
            for g in range(2):
                add("w_in", 2720 + g * 512, 512, lambda W, wkey, g=g: f_gate(W, wkey, g))
            add("w_o_a", 0, 1024, lambda W, wkey: f_oa(W, wkey))
            for g in range(2):
                add("w_out", g * 512, 512, lambda W, wkey, g=g: f_out(W, wkey, g))

            h2reads = [("SH", k) for k in range(8)]

            def f_gatef(W, wkey, j0, nch):
                for jj in range(nch):
                    pi = ps_next()
                    mm(PS[pi][:], [(W[:, kc, jj * 128:(jj + 1) * 128], SH[:, kc, :]) for kc in range(8)], h2reads + [wkey], [("ps", pi)])
                    act(SGT[:, jj, :], PS[pi][:], AF.Silu, [("ps", pi)], [("SGT", jj)])

            def f_upf(W, wkey, j0, nch):
                for jj in range(nch):
                    pi = ps_next()
                    mm(PS[pi][:], [(W[:, kc, jj * 128:(jj + 1) * 128], SH[:, kc, :]) for kc in range(8)], h2reads + [wkey], [("ps", pi)])
                    tt(ACTT[:, j0 + jj, :], PS[pi][:], SGT[:, jj, :], ALU.mult, [("ps", pi), ("SGT", jj)], [("ACTT", j0 + jj)])

            for j0 in range(0, 22, 4):
                nch = min(4, 22 - j0)
                add("w_ffn_in", DFF + j0 * 128, nch * 128, lambda W, wkey, j0=j0, nch=nch: f_gatef(W, wkey, j0, nch))
                add("w_ffn_in", j0 * 128, nch * 128, lambda W, wkey, j0=j0, nch=nch: f_upf(W, wkey, j0, nch))

            def f_fo(W, wkey, c):
                pi = ps_next()
                mm(PS[pi][:], [(W[:, kc, :], ACTT[:, kc, :]) for kc in range(22)], [("ACTT", k) for k in range(22)] + [wkey], [("ps", pi)])
                stt(Xb[:, c, :], PS[pi][:], mod_ap(l, 5, c, s), Xb[:, c, :], ALU.mult, ALU.add,
                    [("ps", pi), (xkey, c), ("mods", l, 40 + c)], [(xkey, c)])
            for c in range(8):
                add("w_ffn_out", c * 128, 128, lambda W, wkey, c=c: f_fo(W, wkey, c))

            nst = len(steps)
            views = {}

            def issue_load(k):
                name, c0, n, kcn, fn = steps[k]
                sl = ws_rr[0]
                ws_rr[0] = (sl + 1) % 3
                view = WS[sl][:, 0:kcn * n].rearrange("p (k n) -> p k n", k=kcn)
                dma("sp", view, wsrc(name, l, c0, n), wkeys(name, l), [("WS", sl)], "ld_ws%d" % sl)
                views[k] = (view, ("WS", sl))

            issue_load(0)
            issue_load(1)
            for k in range(nst):
                if k + 2 < nst:
                    issue_load(k + 2)
                if k == 4 and b + 1 < NB:
                    p2_loads(b + 1)
                view, wkey = views[k]
                steps[k][4](view, wkey)
            dma("pool", x_dst[:, t0:t0 + TB].rearrange("(c p) t -> p c t", p=128), Xb[:],
                [(xkey, c) for c in range(8)], [("xd", b)], "st_x%d" % xs)
        T.barrier()

    with nc.Block() as block:
        @block.tensor
        def _(e):
            T.replay("pe", e)

        @block.scalar
        def _(e):
            T.replay("act", e)

        @block.vector
        def _(e):
            T.replay("dve", e)

        @block.gpsimd
        def _(e):
            T.replay("pool", e)

        @block.sync
        def _(e):
            T.replay("sp", e)
    es.close()
    return nc


def rope_tables(pos):
    inv = (10000.0 ** (-np.arange(0, 32, 2, dtype=np.float32) / np.float32(32))).astype(np.float32)
    ang = pos.astype(np.float32)[:, None] * inv[None, :]
    return np.cos(ang).astype(np.float32), np.sin(ang).astype(np.float32)


def make_consts():
    k = np.arange(128)[:, None]
    m = np.arange(128)[None, :]
    ones = np.ones((128, 128), np.float32)
    nn = (k // 64 == m // 64)
    rn = [(k // 32 == 2 * i + m // 64) for i in range(2)]
    nr = [(2 * i + k // 64 == m // 32) for i in range(2)]
    rr = (k // 32 == m // 32)
    mats = [ones, nn, rn[0], rn[1], nr[0], nr[1], rr]
    return np.ascontiguousarray(np.stack([np.asarray(x, np.float32) for x in mats], axis=1))


def prepare_inputs(cfg, inp):
    Ls, Lq, L = cfg["Ls"], cfg["Lq"], cfg["L"]
    voff, NV = vec_layout(L)
    f = lambda a: np.ascontiguousarray(np.asarray(a, dtype=np.float32))
    w_in = f(inp["w_in"])[:L]
    w_q_b = f(inp["w_q_b"])[:L]
    w_kv_b = f(inp["w_kv_b"])[:L]
    shared = {}
    shared["w_in"] = w_in.reshape(L * D, INC)
    kr = w_in[:, :, 640:672]
    krrot = np.concatenate([kr[:, :, 16:32], kr[:, :, 0:16]], axis=2)
    shared["w_kr4"] = f(np.tile(kr, (1, 1, 4))).reshape(L * D, 128)
    shared["w_krrot4"] = f(np.tile(krrot, (1, 1, 4))).reshape(L * D, 128)
    wq = w_q_b.reshape(L, QL, NH, 96)
    shared["wq_nope"] = f(wq[:, :, :, 0:64]).reshape(L * QL, 512)
    shared["wq_rope"] = f(wq[:, :, :, 64:96]).reshape(L * QL, 256)
    shared["wq_rot"] = f(np.concatenate([wq[:, :, :, 80:96], wq[:, :, :, 64:80]], axis=3)).reshape(L * QL, 256)
    wkv = w_kv_b.reshape(L, KVL, NH, 128)
    shared["wkv_nope"] = f(wkv[:, :, :, 0:64]).reshape(L * KVL, 512)
    shared["wkv_v"] = f(wkv[:, :, :, 64:128]).reshape(L * KVL, 512)
    shared["w_o_a"] = f(inp["w_o_a"])[:L].reshape(L * 512, D)
    shared["w_o_b"] = f(inp["w_o_b"])[:L].reshape(L * D, D)
    shared["w_out"] = f(inp["w_out"])[:L].reshape(L * D, D)
    shared["w_ffn_in"] = f(inp["w_ffn_in"])[:L].reshape(L * D, 2 * DFF)
    shared["w_ffn_out"] = f(inp["w_ffn_out"])[:L].reshape(L * DFF, D)
    shared["w_ada"] = f(inp["w_ada"])[:L].reshape(L * D, 6 * D)
    shared["w_sT"] = f(np.transpose(f(inp["w_s"])[:L], (0, 3, 1, 2)))
    shared["bsb"] = f(np.broadcast_to(f(inp["b_s"])[:L][:, None, :, :], (L, 128, 8, 128)))
    shared["consts"] = make_consts()
    vecs = np.zeros((128, NV), np.float32)
    vecs[:, voff[V_EPS]] = EPS
    p = np.arange(128)

    def fm(a, n):
        return f(a)[:L].reshape(L, n, 128).transpose(2, 0, 1).reshape(128, L * n)
    vecs[:, voff[V_N1G]:voff[V_N1G] + 8 * L] = fm(inp["norm1_g"], 8)
    vecs[:, voff[V_N2G]:voff[V_N2G] + 8 * L] = fm(inp["norm2_g"], 8)
    vecs[:, voff[V_SGUG]:voff[V_SGUG] + 8 * L] = fm(inp["sgu_norm_g"], 8)
    vecs[:, voff[V_QAG]:voff[V_QAG] + 3 * L] = fm(inp["q_a_norm_g"], 3)
    vecs[:, voff[V_KVAG]:voff[V_KVAG] + 2 * L] = fm(inp["kv_a_norm_g"], 2)
    vecs[:, voff[V_BADA]:voff[V_BADA] + 48 * L] = fm(inp["b_ada"], 48)
    qg = f(inp["q_norm_g"])[:L]
    kg = f(inp["k_norm_g"])[:L]
    for (g, vn, vr, vrp) in ((qg, V_GQN, V_GQR, V_GQRP), (kg, V_GKN, V_GKR, V_GKRP)):
        vecs[:, voff[vn]:voff[vn] + L] = g[:, p % 64].T
        vecs[:, voff[vr]:voff[vr] + L] = g[:, 64 + p % 32].T
        vecs[:, voff[vrp]:voff[vrp] + L] = g[:, 64 + (p % 32 + 16) % 32].T
    shared["vecs"] = vecs
    xs = f(inp["x_sample"])
    xp = f(inp["x_prompt"])
    cs = f(inp["c_sample"])
    cp = f(inp["c_prompt"])
    fidx = (p % 32) % 16
    sgn = np.where((p % 32) < 16, -1.0, 1.0).astype(np.float32)
    in_maps = []
    for c in range(NCORES):
        ps_, r = c // 4, c % 4
        m = dict(shared)
        m["xT"] = np.ascontiguousarray(np.concatenate([xs[c, :Ls].T, xp[ps_, r * Lq:(r + 1) * Lq].T], axis=1))
        m["cT"] = np.ascontiguousarray(np.stack([cs[c].reshape(8, 128).T, cp[ps_].reshape(8, 128).T], axis=2))
        pos = np.concatenate([np.arange(Ls), r * Lq + np.arange(Lq)])
        cos, sin = rope_tables(pos)
        m["cos4"] = np.ascontiguousarray(cos.T[fidx, :])
        m["sin4"] = np.ascontiguousarray(sin.T[fidx, :] * sgn[:, None])
        in_maps.append(m)
    return in_maps


_PROG_CACHE = {}


def run(cfg, inp):
    key = (cfg["Ls"], cfg["Lq"], cfg["L"], cfg.get("stop"), cfg.get("no_cc"), cfg.get("no_pool"), cfg.get("no_qkstore"))
    if key not in _PROG_CACHE:
        _PROG_CACHE[key] = build_program(cfg)
    nc = _PROG_CACHE[key]
    in_maps = prepare_inputs(cfg, inp)
    res = run_bass_kernel_spmd(nc, in_maps, core_ids=list(range(NCORES)))
    Ls, Lq = cfg["Ls"], cfg["Lq"]
    ys = np.stack([res.results[c]["yT"][:, :Ls].T for c in range(NCORES)], axis=0)
    yp = np.stack([np.concatenate([res.results[4 * s_ + r]["yT"][:, Ls:].T for r in range(4)], axis=0) for s_ in range(2)], axis=0)
    return np.ascontiguousarray(yp, dtype=np.float32), np.ascontiguousarray(ys, dtype=np.float32)


def kernel(**inputs):
    return run(FULL_CFG, inputs)
```

```python
import numpy as np
from contextlib import ExitStack
import concourse.bass as bass
import concourse.mybir as mybir
from concourse.bass_utils import run_bass_kernel_spmd

F32 = mybir.dt.float32
BF16 = mybir.dt.bfloat16
AF = mybir.ActivationFunctionType
ALU = mybir.AluOpType

D = 1024
NH = 8
QL = 384
KVL = 256
DFF = 2816
INC = 4768
EPS = 1e-6
TB = 512
NCORES = 8
SAME_SYNC = True

FULL_CFG = dict(Ls=2048, Lq=2048, L=4)


class Tracker:
    COMPUTE = ("pe", "act", "dve", "pool")

    def __init__(self, nc, es):
        self.nc = nc
        self.es = es
        self.ops = {e: [] for e in ("pe", "act", "dve", "pool", "sp")}
        self.cnt = {e: 0 for e in self.COMPUTE}
        self.esem = {e: es.enter_context(nc.semaphore("c_" + e)) for e in self.COMPUTE}
        self.dsem = {}
        self.seen = {e: {} for e in self.ops}
        self.lastw = {}
        self.readers = {}
        self.nops = 0
        self.no_pool = False

    def _dsem(self, key):
        if key not in self.dsem:
            self.dsem[key] = [self.es.enter_context(self.nc.semaphore("d%d" % len(self.dsem))), 0]
        return self.dsem[key]

    def _resolve(self, tok):
        if tok[0] == "c":
            return self.esem[tok[1]], tok[2], ("c", tok[1])
        d = self._dsem(tok[1])
        return d[0], d[1], ("d", tok[1])

    def op(self, eng, fn, reads=(), writes=(), dma=None, inc=None):
        if eng == "pool" and dma is None and self.no_pool:
            eng = "dve"
        deps = set()
        for r in reads:
            if r in self.lastw:
                deps.add(self.lastw[r])
            if isinstance(r, tuple) and r[0] == "ps":
                for k, v in self.readers.get(r, {}).items():
                    if k != ("c", eng):
                        deps.add(k + (v,))
        for w in writes:
            if w in self.lastw:
                deps.add(self.lastw[w])
            for k, v in self.readers.get(w, {}).items():
                deps.add(k + (v,))
        waits = []
        for tok in deps:
            if tok[0] == "c" and tok[1] == eng and dma is None and (eng == "pe" or not SAME_SYNC):
                continue
            sem, val, sid = self._resolve(tok)
            if self.seen[eng].get(sid, 0) >= val:
                continue
            self.seen[eng][sid] = val
            waits.append((sem, val))
        if dma is not None:
            d = self._dsem(dma)
            step = 16 if inc is None else inc
            d[1] += step
            tok = ("d", dma, d[1])
            incr = (d[0], step)
        else:
            self.cnt[eng] += 1
            tok = ("c", eng, self.cnt[eng])
            incr = (self.esem[eng], 1)
        self.ops[eng].append((waits, fn, incr))
        self.nops += 1
        for r in reads:
            self.readers.setdefault(r, {})[tok[:2]] = tok[2]
        for w in writes:
            self.lastw[w] = tok
            self.readers[w] = {}
        return tok

    @staticmethod
    def _bg(key):
        return isinstance(key, str) and key.startswith("cast")

    def barrier(self):
        for eng in self.ops:
            waits = []
            for f in self.COMPUTE:
                if f == eng or self.cnt[f] == 0:
                    continue
                if self.seen[eng].get(("c", f), 0) < self.cnt[f]:
                    self.seen[eng][("c", f)] = self.cnt[f]
                    waits.append((self.esem[f], self.cnt[f]))
            for key, d in self.dsem.items():
                if self._bg(key):
                    continue
                if d[1] and self.seen[eng].get(("d", key), 0) < d[1]:
                    self.seen[eng][("d", key)] = d[1]
                    waits.append((d[0], d[1]))
            if waits:
                self.ops[eng].append((waits, None, None))
        self.lastw = {r: t for r, t in self.lastw.items() if t[0] == "d" and self._bg(t[1])}
        self.readers = {}

    def replay(self, name, e):
        for waits, fn, incr in self.ops[name]:
            for sem, val in waits:
                e.wait_ge(sem, val)
            if fn is not None:
                ins = fn(e)
                ins.then_inc(incr[0], incr[1])


class SbAlloc:
    BASE = 16512
    TOP = 229344

    def __init__(self, nc):
        self.nc = nc
        self.persist = self.BASE
        self.cur = self.BASE
        self.n = 0

    def _al(self, shape, dt, off):
        self.n += 1
        return self.nc.alloc_sbuf_tensor_at("sb%d" % self.n, list(shape), dt, offset=off)

    def alloc(self, shape, dt, persistent=False):
        esz = 4 if dt == F32 else 2
        nbytes = int(np.prod(shape[1:])) * esz
        nbytes = (nbytes + 31) // 32 * 32
        if persistent:
            assert self.cur == self.persist, "persistent allocs must come first"
            off = self.persist
            self.persist += nbytes
            self.cur = self.persist
        else:
            off = self.cur
            self.cur += nbytes
        assert self.cur <= self.TOP, "SBUF overflow %d" % self.cur
        return self._al(shape, dt, off)

    def phase(self):
        self.cur = self.persist


W_SPECS = [
    ("w_in", 1024, INC, "A"), ("w_kr4", 1024, 128, "A"), ("w_krrot4", 1024, 128, "A"),
    ("wq_nope", QL, 512, "A"), ("wq_rope", QL, 256, "A"), ("wq_rot", QL, 256, "A"),
    ("wkv_nope", KVL, 512, "A"), ("wkv_v", KVL, 512, "A"),
    ("w_o_a", 512, D, "B"), ("w_o_b", D, D, "B"), ("w_out", D, D, "B"),
    ("w_ffn_in", D, 2 * DFF, "B"), ("w_ffn_out", DFF, D, "B"),
]
V_EPS, V_N1G, V_N2G, V_SGUG, V_QAG, V_KVAG, V_GQN, V_GQR, V_GQRP, V_GKN, V_GKR, V_GKRP, V_BADA = range(13)


def vec_layout(L):
    off = {}
    o = 0
    for name, n in [(V_EPS, 1), (V_N1G, 8 * L), (V_N2G, 8 * L), (V_SGUG, 8 * L), (V_QAG, 3 * L), (V_KVAG, 2 * L),
                    (V_GQN, L), (V_GQR, L), (V_GQRP, L), (V_GKN, L), (V_GKR, L), (V_GKRP, L), (V_BADA, 48 * L)]:
        off[name] = o
        o += n
    return off, o


def build_program(cfg):
    Ls, Lq, L = cfg["Ls"], cfg["Lq"], cfg["L"]
    Lp = 4 * Lq
    NTOK = Ls + Lq
    nS, nP = Ls // TB, Lq // TB
    NB = nS + nP
    voff, NV = vec_layout(L)

    nc = bass.Bass("TRN2", target_bir_lowering=False)

    def din(name, shape, dt=F32):
        return nc.dram_tensor(name, list(shape), dt, kind="ExternalInput").ap()

    xT_in = din("xT", [D, NTOK])
    cT_in = din("cT", [128, 8, 2])
    cos_in = din("cos4", [128, NTOK])
    sin_in = din("sin4", [128, NTOK])
    vecs_in = din("vecs", [128, NV])
    consts_in = din("consts", [128, 7, 128])
    w_ada_in = din("w_ada", [L * D, 6 * D])
    wsT_in = din("w_sT", [L, 128, 8, 128])
    bsb_in = din("bsb", [L, 128, 8, 128])
    w_ext = {name: din(name, [L * K, N]) for name, K, N, _ in W_SPECS}
    yT_out = nc.dram_tensor("yT", [D, NTOK], F32, kind="ExternalOutput").ap()

    def dscr(name, shape, dt):
        return nc.dram_tensor(name, list(shape), dt)

    w_bf = {name: dscr(name + "_b", [L * K, N], BF16) for name, K, N, _ in W_SPECS}
    WK = {name: K for name, K, N, _ in W_SPECS}
    xT_d = dscr("xT_d", [D, NTOK], F32)
    hT_d = dscr("hT_d", [D, NTOK], BF16)
    qT_d = dscr("qT_d", [NB * 768, TB], BF16)
    oT_d = dscr("oT_d", [512, NTOK], BF16)
    kT_S = dscr("kT_S", [nS * 768, TB], BF16)
    v_S = dscr("v_S", [Ls, 512], BF16)
    kT_Pin = [dscr("kT_Pin%d" % b, [NH * 96, TB], BF16) for b in range(nP)]
    kT_Pall = [dscr("kT_Pall%d" % b, [4 * NH * 96, TB], BF16) for b in range(nP)]
    v_Pin = [dscr("v_Pin%d" % b, [TB, 512], BF16) for b in range(nP)]
    v_Pall = [dscr("v_Pall%d" % b, [4 * TB, 512], BF16) for b in range(nP)]

    es = ExitStack()
    T = Tracker(nc, es)
    T.no_pool = bool(cfg.get("no_pool"))
    sb = SbAlloc(nc)
    PS = [es.enter_context(nc.psum_tensor("ps%d" % i, [128, 512], F32)) for i in range(8)]
    ps_rr = [0]

    def ps_next():
        i = ps_rr[0]
        ps_rr[0] = (i + 1) % 8
        return i

    def dma(eng, out, in_, reads, writes, key):
        T.op(eng, lambda e, o=out, i=in_: e.dma_start(out=o, in_=i), reads=reads, writes=writes, dma=key)

    def mm(ps_ap, pairs, reads, writes):
        def fn(e, ps_ap=ps_ap, pairs=pairs):
            n = len(pairs)
            ins = None
            for i, (l, r) in enumerate(pairs):
                ins = e.matmul(ps_ap, l, r, start=(i == 0), stop=(i == n - 1))
            return ins
        T.op("pe", fn, reads=reads, writes=writes)

    def act(out, in_, func, reads, writes, **kw):
        T.op("act", lambda e, o=out, i=in_, f=func, kw=kw: e.activation(out=o, in_=i, func=f, **kw), reads=reads, writes=writes)

    def tt(out, in0, in1, op, reads, writes, eng="dve"):
        T.op(eng, lambda e, o=out, a=in0, b=in1, op=op: e.tensor_tensor(out=o, in0=a, in1=b, op=op), reads=reads, writes=writes)

    def stt(out, in0, scalar, in1, op0, op1, reads, writes, eng="dve"):
        T.op(eng, lambda e, o=out, a=in0, s=scalar, b=in1, p0=op0, p1=op1:
             e.scalar_tensor_tensor(out=o, in0=a, scalar=s, in1=b, op0=p0, op1=p1), reads=reads, writes=writes)

    def ts(out, in0, s1, s2, op0, op1, reads, writes, eng="dve"):
        if s2 is None:
            T.op(eng, lambda e, o=out, a=in0, s1=s1, p0=op0: e.tensor_scalar(out=o, in0=a, scalar1=s1, scalar2=None, op0=p0),
                 reads=reads, writes=writes)
        else:
            T.op(eng, lambda e, o=out, a=in0, s1=s1, s2=s2, p0=op0, p1=op1:
                 e.tensor_scalar(out=o, in0=a, scalar1=s1, scalar2=s2, op0=p0, op1=p1), reads=reads, writes=writes)

    def recip(out, in_, reads, writes):
        T.op("dve", lambda e, o=out, i=in_: e.reciprocal(out=o, in_=i), reads=reads, writes=writes)

    def memset(ap, val, writes, eng="dve"):
        T.op(eng, lambda e, a=ap, v=val: e.memset(a, v), writes=writes)

    vecs = sb.alloc([128, NV], F32, True)
    cst = sb.alloc([128, 7, 128], BF16, True)
    ones_f = sb.alloc([128, 64], F32, True)
    mods = sb.alloc([128, L, 48, 2], F32, True)
    A1 = sb.alloc([128, L, 2, 8], F32, True)
    A2 = sb.alloc([128, L, 2, 8], F32, True)
    silc = sb.alloc([128, 8, 2], BF16, True)
    eps_ap = vecs[:, voff[V_EPS]:voff[V_EPS] + 1]

    def vcol(kind, idx):
        o = voff[kind] + idx
        return vecs[:, o:o + 1]

    ONES, NN, RN0, RN1, NR0, NR1, RR = [cst[:, i, :] for i in range(7)]
    RN = [RN0, RN1]
    NR = [NR0, NR1]

    dma("sp", vecs[:], vecs_in, [], ["vecs"], "ld_misc")
    dma("pool", cst[:], consts_in, [], ["cst"], "ld_cst")
    memset(ones_f[:], 1.0, ["ones_f"])

    def cast_weights(l, grp):
        for name, K, N, g in W_SPECS:
            if g != grp:
                continue
            for rb in range(0, K, 128):
                r0 = l * K + rb
                dma("pool", w_bf[name][r0:r0 + 128, :], w_ext[name][r0:r0 + 128, :], [], [("wb", name, l, rb // 128)],
                    "cast%s%d" % (grp, l))

    def wkeys(name, l):
        return [("wb", name, l, i) for i in range(WK[name] // 128)]

    def wsrc(name, l, c0, n):
        K = WK[name]
        return w_bf[name][l * K:(l + 1) * K, c0:c0 + n].rearrange("(kc p) n -> p kc n", p=128)

    cast_weights(0, "A")

    cTt = sb.alloc([128, 8, 2], F32)
    WA0 = [sb.alloc([128, 8, 512], BF16) for _ in range(2)]
    dma("sp", cTt[:], cT_in, [], ["cTt"], "ld_misc")
    act(silc[:], cTt[:], AF.Silu, ["cTt"], ["silc"])

    def mods_load(l, jg, WA):
        s_ = jg % 2
        src = w_ada_in[l * D:(l + 1) * D, jg * 512:(jg + 1) * 512].rearrange("(kc p) n -> p kc n", p=128)
        dma("pool", WA[s_][:], src, [], [("WA", s_)], "ld_wa%d" % s_)

    def mods_compute(l, jg, WA, banks=None):
        s_ = jg % 2
        for jj in range(4):
            j = jg * 4 + jj
            pi = ps_next() if banks is None else banks[jj % len(banks)]
            mm(PS[pi][:, 0:2], [(WA[s_][:, kc, jj * 128:(jj + 1) * 128], silc[:, kc, :]) for kc in range(8)],
               [("WA", s_), "silc"], [("ps", pi)])
            ts(mods[:, l, j, :], PS[pi][:, 0:2], vcol(V_BADA, l * 48 + j), None, ALU.add, None,
               [("ps", pi), "vecs"], [("mods", l, j)])

    def mods_finish(l):
        for s_ in range(2):
            for (Ax, gk, m0) in ((A1, V_N1G, 8), (A2, V_N2G, 32)):
                ts(Ax[:, l, s_, :], mods[:, l, m0:m0 + 8, s_], 1.0, None, ALU.add, None,
                   [("mods", l, j) for j in range(m0, m0 + 8)], [("A", id(Ax), l, s_)])
                tt(Ax[:, l, s_, :], Ax[:, l, s_, :], vecs[:, voff[gk] + 8 * l:voff[gk] + 8 * l + 8], ALU.mult,
                   [("A", id(Ax), l, s_), "vecs"], [("A", id(Ax), l, s_)])

    mods_load(0, 0, WA0)
    for jg in range(12):
        if jg + 1 < 12:
            mods_load(0, jg + 1, WA0)
        mods_compute(0, jg, WA0)
    mods_finish(0)
    cast_weights(0, "B")
    T.barrier()
    STOP = cfg.get("stop")

    def mod_ap(l, m, c, s):
        return mods[:, l, m * 8 + c, s:s + 1]

    def tok0(b):
        return b * TB

    def seg(b):
        return 0 if b < nS else 1

    def rstd_from_psum(rs_ap, pi, n, rs_key):
        act(rs_ap, PS[pi][:], AF.Ln, [("ps", pi), "vecs"], [rs_key], bias=eps_ap, scale=1.0 / n)
        act(rs_ap, rs_ap, AF.Exp, [rs_key], [rs_key], scale=-0.5)

    def rms_stats(sq_list, lhs_list, n, rs_ap, rs_key, reads):
        pi = ps_next()
        mm(PS[pi][:], list(zip(lhs_list, sq_list)), reads + ["cst"], [("ps", pi)])
        rstd_from_psum(rs_ap, pi, n, rs_key)

    def norm_block(Xb, xkey, Hb, hkey, SQ, RS, Ax, l, s, shm, NT):
        for c in range(8):
            tt(SQ[:, c, :], Xb[:, c, :], Xb[:, c, :], ALU.mult, [(xkey, c)], [("SQ", c)], eng="pool")
        rms_stats([SQ[:, c, :] for c in range(8)], [ONES] * 8, D, RS[:], "RS", [("SQ", c) for c in range(8)])
        for c in range(8):
            i = c % 2
            tt(NT[i][:], Xb[:, c, :], RS[:], ALU.mult, [(xkey, c), "RS"], [("NT", i)])
            act(Hb[:, c, :], NT[i][:], AF.Identity, [("NT", i), ("A", id(Ax), l, s), ("mods", l, shm * 8 + c)], [(hkey, c)],
                scale=Ax[:, l, s, c:c + 1], bias=mod_ap(l, shm, c, s))

    PHASES = {}
    for l in range(L if STOP != "prologue" else 0):
        x_src = xT_in if l == 0 else xT_d
        x_dst = yT_out if l == L - 1 else xT_d
        last = (l == L - 1)

        if "ph1" not in PHASES:
            sb.phase()
            ns = {}
            ns["X"] = [sb.alloc([128, 8, TB], F32) for _ in range(2)]
            ns["H"] = [sb.alloc([128, 8, TB], BF16) for _ in range(2)]
            ns["SQ"] = sb.alloc([128, 8, TB], BF16)
            ns["RS"] = sb.alloc([128, TB], F32)
            ns["W1in"] = sb.alloc([128, 8, 640], BF16)
            ns["Wkr"] = sb.alloc([128, 8, 128], BF16)
            ns["Wkrr"] = sb.alloc([128, 8, 128], BF16)
            ns["Wqn"] = sb.alloc([128, 3, 512], BF16)
            ns["Wqr"] = sb.alloc([128, 3, 256], BF16)
            ns["Wqt"] = sb.alloc([128, 3, 256], BF16)
            ns["Wkn"] = sb.alloc([128, 2, 512], BF16)
            ns["Wv"] = sb.alloc([128, 2, 512], BF16)
            ns["QN"] = sb.alloc([128, 3, TB], BF16)
            ns["KVN"] = sb.alloc([128, 2, TB], BF16)
            ns["QO"] = [sb.alloc([128, 6, TB], BF16) for _ in range(2)]
            ns["KO"] = [sb.alloc([128, 6, TB], BF16) for _ in range(2)]
            ns["VO"] = [sb.alloc([128, 4, 512], BF16) for _ in range(2)]
            ns["TK"] = sb.alloc([128, TB], F32)
            ns["SQRK"] = sb.alloc([128, TB], BF16)
            ns["COS"] = [sb.alloc([128, TB], F32) for _ in range(2)]
            ns["SIN"] = [sb.alloc([128, TB], F32) for _ in range(2)]
            ns["CQ"] = sb.alloc([128, TB], F32)
            ns["SQT"] = sb.alloc([128, TB], F32)
            ns["CK"] = sb.alloc([128, TB], F32)
            ns["SK"] = sb.alloc([128, TB], F32)
            ns["SQH"] = [sb.alloc([128, TB], BF16) for _ in range(3)]
            ns["RSH"] = [sb.alloc([128, TB], F32) for _ in range(3)]
            ns["T1"] = sb.alloc([128, TB], F32)
            ns["T2"] = sb.alloc([128, TB], F32)
            ns["NTT"] = [sb.alloc([128, TB], F32) for _ in range(2)]
            PHASES["ph1"] = ns
        ns = PHASES["ph1"]
        X = ns["X"]
        H = ns["H"]
        SQ = ns["SQ"]
        RS = ns["RS"]
        W1in = ns["W1in"]
        Wkr = ns["Wkr"]
        Wkrr = ns["Wkrr"]
        Wqn = ns["Wqn"]
        Wqr = ns["Wqr"]
        Wqt = ns["Wqt"]
        Wkn = ns["Wkn"]
        Wv = ns["Wv"]
        QN = ns["QN"]
        KVN = ns["KVN"]
        QO = ns["QO"]
        KO = ns["KO"]
        VO = ns["VO"]
        TK = ns["TK"]
        SQRK = ns["SQRK"]
        COS = ns["COS"]
        SIN = ns["SIN"]
        CQ = ns["CQ"]
        SQT = ns["SQT"]
        CK = ns["CK"]
        SK = ns["SK"]
        SQH = ns["SQH"]
        RSH = ns["RSH"]
        T1 = ns["T1"]
        T2 = ns["T2"]
        NTT = ns["NTT"]

        for (buf, name, c0, n) in ((W1in, "w_in", 0, 640), (Wkr, "w_kr4", 0, 128), (Wkrr, "w_krrot4", 0, 128),
                                   (Wqn, "wq_nope", 0, 512), (Wqr, "wq_rope", 0, 256), (Wqt, "wq_rot", 0, 256),
                                   (Wkn, "wkv_nope", 0, 512), (Wv, "wkv_v", 0, 512)):
            dma("sp", buf[:], wsrc(name, l, c0, n), wkeys(name, l), [("W1", name)], "ld_w1")
        W1R = [("W1", n) for n in ("w_in", "w_kr4", "w_krrot4", "wq_nope", "wq_rope", "wq_rot", "wkv_nope", "wkv_v")]

        order = list(range(nS, NB)) + list(range(nS))

        def p1_loads(i):
            b = order[i]
            xs = i % 2
            t0 = tok0(b)
            dma("sp", X[xs][:], x_src[:, t0:t0 + TB].rearrange("(c p) t -> p c t", p=128), [("xd", b)],
                [(("X", xs), c) for c in range(8)], "ld_x%d" % xs)
            dma("sp", COS[xs][:], cos_in[:, t0:t0 + TB], [], [("COS", xs)], "ld_cs%d" % xs)
            dma("sp", SIN[xs][:], sin_in[:, t0:t0 + TB], [], [("SIN", xs)], "ld_cs%d" % xs)

        def head_half(hh, PSn, PSr_sq_key, sqr_ap, rope_src_fn, OUT, okey, gn, kdst_fn, rope_keys):
            for i in range(2):
                act(SQH[i][:], PS[PSn[i]][:], AF.Square, [("ps", PSn[i])], [("SQH", i)])
            for i in range(2):
                pi = ps_next()
                mm(PS[pi][:], [(NN, SQH[i][:]), (RN[i], sqr_ap)], [("SQH", i), PSr_sq_key, "cst"], [("ps", pi)])
                rstd_from_psum(RSH[i][:], pi, 96, ("RSH", i))
            pi = ps_next()
            mm(PS[pi][:], [(NR[0], SQH[0][:]), (NR[1], SQH[1][:]), (RR, sqr_ap)],
               [("SQH", 0), ("SQH", 1), PSr_sq_key, "cst"], [("ps", pi)])
            rstd_from_psum(RSH[2][:], pi, 96, ("RSH", 2))
            for i in range(2):
                jn = 2 * hh + i
                stt(OUT[:, jn, :], PS[PSn[i]][:], vcol(gn, l), RSH[i][:], ALU.mult, ALU.mult,
                    [("ps", PSn[i]), ("RSH", i), "vecs"], [(okey, jn)])
            rap, rkeys = rope_src_fn(hh)
            tt(OUT[:, 4 + hh, :], rap, RSH[2][:], ALU.mult, rkeys + [("RSH", 2)] + rope_keys, [(okey, 4 + hh)])
            kdst_fn(hh)

        for i, b in enumerate(order):
            if i == 0:
                p1_loads(0)
            if i + 1 < NB:
                p1_loads(i + 1)
            xs = i % 2
            s = seg(b)
            t0 = tok0(b)
            Xb, Hb = X[xs], H[xs]
            QOb, KOb, VOb = QO[xs], KO[xs], VO[xs]
            qok, kok, vok = "QO%d" % xs, "KO%d" % xs, "VO%d" % xs
            xkey, hkey = ("X", xs), ("H", xs)
            norm_block(Xb, xkey, Hb, hkey, SQ, RS, A1, l, s, 0, NTT)
            dma("pool", hT_d[:, t0:t0 + TB].rearrange("(c p) t -> p c t", p=128), Hb[:],
                [(hkey, c) for c in range(8)], [("hd", b)], "st_h%d" % xs)
            hreads = [(hkey, c) for c in range(8)] + W1R
            ts(CQ[:], COS[xs][:], vcol(V_GQR, l), None, ALU.mult, None, [("COS", xs), "vecs"], ["CQ"])
            ts(SQT[:], SIN[xs][:], vcol(V_GQRP, l), None, ALU.mult, None, [("SIN", xs), "vecs"], ["SQT"])
            ts(CK[:], COS[xs][:], vcol(V_GKR, l), None, ALU.mult, None, [("COS", xs), "vecs"], ["CK"])
            ts(SK[:], SIN[xs][:], vcol(V_GKRP, l), None, ALU.mult, None, [("SIN", xs), "vecs"], ["SK"])

            def latent(c0, nch, n, gk, OUTN, okey):
                pis = []
                for j in range(nch):
                    pi = ps_next()
                    pis.append(pi)
                    mm(PS[pi][:], [(W1in[:, kc, c0 + j * 128:c0 + (j + 1) * 128], Hb[:, kc, :]) for kc in range(8)],
                       hreads, [("ps", pi)])
                    act(SQ[:, j, :], PS[pi][:], AF.Square, [("ps", pi)], [("SQ", j)])
                rms_stats([SQ[:, j, :] for j in range(nch)], [ONES] * nch, n, RS[:], "RS", [("SQ", j) for j in range(nch)])
                for j in range(nch):
                    stt(OUTN[:, j, :], PS[pis[j]][:], vcol(gk, l * nch + j), RS[:], ALU.mult, ALU.mult,
                        [("ps", pis[j]), "RS", "vecs"], [(okey, j)])

            latent(0, 3, QL, V_QAG, QN, "QN")
            latent(QL, 2, KVL, V_KVAG, KVN, "KVN")
            qn_r = [("QN", j) for j in range(3)] + W1R
            kvn_r = [("KVN", j) for j in range(2)] + W1R

            for t4 in range(4):
                pi = ps_next()
                mm(PS[pi][:], [(KVN[:, kc, t4 * 128:(t4 + 1) * 128], Wv[:, kc, :]) for kc in range(2)], kvn_r, [("ps", pi)])
                act(VOb[:, t4, :], PS[pi][:], AF.Copy, [("ps", pi)], [(vok, t4)])
            bp = b - nS
            vdst = (v_S if s == 0 else v_Pin[bp])
            tl = t0 if s == 0 else 0
            dma("pool", vdst[tl:tl + TB, :].rearrange("(a p) c -> p a c", p=128), VOb[:],
                [(vok, a) for a in range(4)], [("vdst", s)], "st_v%d" % xs)

            pk = ps_next()
            mm(PS[pk][:], [(Wkr[:, kc, :], Hb[:, kc, :]) for kc in range(8)], hreads, [("ps", pk)])
            pkr = ps_next()
            mm(PS[pkr][:], [(Wkrr[:, kc, :], Hb[:, kc, :]) for kc in range(8)], hreads, [("ps", pkr)])
            act(SQRK[:], PS[pk][:], AF.Square, [("ps", pk)], ["SQRK"])
            tt(TK[:], PS[pk][:], CK[:], ALU.mult, [("ps", pk), "CK"], ["TK"])
            tt(T2[:], PS[pkr][:], SK[:], ALU.mult, [("ps", pkr), "SK"], ["T2"])
            tt(TK[:], TK[:], T2[:], ALU.add, ["TK", "T2"], ["TK"], eng="pool")

            kdst = kT_S if s == 0 else kT_Pin[bp]
            for hh in range(2):
                pn = []
                for i2 in range(2):
                    jn = 2 * hh + i2
                    pi = ps_next()
                    pn.append(pi)
                    mm(PS[pi][:], [(Wqn[:, kc, jn * 128:(jn + 1) * 128], QN[:, kc, :]) for kc in range(3)], qn_r, [("ps", pi)])
                pr = ps_next()
                mm(PS[pr][:], [(Wqr[:, kc, hh * 128:(hh + 1) * 128], QN[:, kc, :]) for kc in range(3)], qn_r, [("ps", pr)])
                pt = ps_next()
                mm(PS[pt][:], [(Wqt[:, kc, hh * 128:(hh + 1) * 128], QN[:, kc, :]) for kc in range(3)], qn_r, [("ps", pt)])
                act(SQH[2][:], PS[pr][:], AF.Square, [("ps", pr)], [("SQH", 2)])

                def q_rope(hh_, pr=pr, pt=pt):
                    tt(T1[:], PS[pr][:], CQ[:], ALU.mult, [("ps", pr), "CQ"], ["T1"])
                    tt(T2[:], PS[pt][:], SQT[:], ALU.mult, [("ps", pt), "SQT"], ["T2"])
                    tt(T1[:], T1[:], T2[:], ALU.add, ["T1", "T2"], ["T1"], eng="pool")
                    return T1[:], ["T1"]

                head_half(hh, pn, ("SQH", 2), SQH[2][:], q_rope, QOb, qok, V_GQN, lambda hh_: None, [])
                pn = []
                for i2 in range(2):
                    jn = 2 * hh + i2
                    pi = ps_next()
                    pn.append(pi)
                    mm(PS[pi][:], [(Wkn[:, kc, jn * 128:(jn + 1) * 128], KVN[:, kc, :]) for kc in range(2)], kvn_r, [("ps", pi)])

                def k_rope(hh_):
                    return TK[:], ["TK"]

                head_half(hh, pn, "SQRK", SQRK[:], k_rope, KOb, kok, V_GKN, lambda hh_: None, [])
            if not cfg.get("no_qkstore"):
                dma("pool", qT_d[b * 768:(b + 1) * 768, :].rearrange("(j p) t -> p j t", p=128), QOb[:],
                    [(qok, j) for j in range(6)], [("qkdst", "q")], "st_q%d" % xs)
                krows = kT_S[b * 768:(b + 1) * 768, :] if s == 0 else kT_Pin[bp][:, :]
                dma("pool", krows.rearrange("(j p) t -> p j t", p=128), KOb[:],
                    [(kok, j) for j in range(6)], [("qkdst", "k%d" % s)], "st_k%d" % xs)

            if s == 1 and not cfg.get("no_cc"):
                for ci, (src, dst, rk) in enumerate(((kT_Pin[bp], kT_Pall[bp], ("qkdst", "k1")), (v_Pin[bp], v_Pall[bp], ("vdst", 1)))):
                    T.op("pool", lambda e, src=src, dst=dst: e.collective_compute(
                        "AllGather", ALU.bypass, replica_groups=[[0, 1, 2, 3], [4, 5, 6, 7]],
                        ins=[src.ap().opt()], outs=[dst.ap().opt()]),
                        reads=[rk], writes=[("gath", ci, bp)], dma="cc%d_%d_%d" % (l, ci, bp), inc=1)
        T.barrier()
        if STOP == "pass1":
            break

        LkMax = max(Ls, Lp)
        if "ph2" not in PHASES:
            sb.phase()
            ns = {}
            ns["KT"] = [sb.alloc([96, LkMax], BF16) for _ in range(2)]
            ns["VP"] = [sb.alloc([128, LkMax // 128, 65], BF16) for _ in range(2)]
            ns["QT"] = [sb.alloc([96, max(Ls, Lq)], BF16) for _ in range(2)]
            ns["PT"] = [sb.alloc([128, TB], BF16) for _ in range(4)]
            ns["OSB"] = [sb.alloc([128, TB], F32) for _ in range(2)]
            ns["OTH"] = [sb.alloc([64, TB], BF16) for _ in range(2)]
            ns["WAa"] = [sb.alloc([128, 8, 512], BF16) for _ in range(2)]
            PHASES["ph2"] = ns
        ns = PHASES["ph2"]
        KT = ns["KT"]
        VP = ns["VP"]
        QT = ns["QT"]
        PT = ns["PT"]
        OSB = ns["OSB"]
        OTH = ns["OTH"]
        WAa = ns["WAa"]
        bg = []
        if l + 1 < L:
            cast_weights(l + 1, "A")
            cast_weights(l + 1, "B")
            for i8 in range(8):
                def step(i8=i8):
                    for k in (2 * i8 - 2, 2 * i8 - 1):
                        if 0 <= k < 12:
                            mods_compute(l + 1, k, WAa, banks=[7])
                    for k in (2 * i8, 2 * i8 + 1):
                        if k < 12:
                            mods_load(l + 1, k, WAa)
                    if i8 == 7:
                        mods_finish(l + 1)
                bg.append(step)
        for sl in range(2):
            memset(VP[sl][:, :, 64:65], 1.0, [("VP", sl)])
        scale = 96 ** -0.5
        hcount = 0
        for s in range(2):
            Lk = Ls if s == 0 else Lp
            Lseg = Ls if s == 0 else Lq
            tbase = 0 if s == 0 else Ls
            nkb = Lk // 128
            nqb = Lseg // TB

            def att_loads(h, sl, s=s, Lk=Lk, Lseg=Lseg, tbase=tbase, nkb=nkb):
                rn = (h // 2) * 128 + 64 * (h % 2)
                rr = (4 + h // 4) * 128 + 32 * (h % 4)
                parts = ((0, 64, rn), (64, 32, rr))
                if s == 0:
                    kv = kT_S.ap().rearrange("(b r) t -> r b t", r=768)
                    for (p0, np_, r0) in parts:
                        dma("sp", KT[sl][p0:p0 + np_, 0:Lk].rearrange("d (b t) -> d b t", t=TB), kv[r0:r0 + np_, :, :],
                            [("qkdst", "k0")], [("KT", sl)], "ld_kt%d" % sl)
                    vsrc = v_S[:, h * 64:(h + 1) * 64].rearrange("(kb p) c -> p kb c", p=128)
                    dma("sp", VP[sl][:, 0:nkb, 0:64], vsrc, [("vdst", 0)], [("VP", sl)], "ld_vp%d" % sl)
                else:
                    for bq in range(nP):
                        kv = kT_Pall[bq].ap().rearrange("(r c) t -> c r t", r=4)
                        for (p0, np_, r0) in parts:
                            dma("sp", KT[sl][p0:p0 + np_, bq * 4 * TB:(bq + 1) * 4 * TB].rearrange("d (r t) -> d r t", r=4),
                                kv[r0:r0 + np_, :, :], [("gath", 0, bq)], [("KT", sl)], "ld_kt%d" % sl)
                        vsrc = v_Pall[bq][:, h * 64:(h + 1) * 64].rearrange("(kb p) c -> p kb c", p=128)
                        dma("sp", VP[sl][:, bq * 16:(bq + 1) * 16, 0:64], vsrc, [("gath", 1, bq)], [("VP", sl)], "ld_vp%d" % sl)
                blk0 = 0 if s == 0 else nS
                nblk = Lseg // TB
                qv = qT_d.ap().rearrange("(b r) t -> r b t", r=768)
                for (p0, np_, r0) in parts:
                    dma("sp", QT[sl][p0:p0 + np_, 0:Lseg].rearrange("d (b t) -> d b t", t=TB), qv[r0:r0 + np_, blk0:blk0 + nblk, :],
                        [("qkdst", "q")], [("QT", sl)], "ld_qt%d" % sl)

            att_loads(0, hcount % 2)
            for h in range(NH):
                sl = hcount % 2
                hcount += 1
                if h + 1 < NH:
                    att_loads(h + 1, hcount % 2)
                if bg and s == 1:
                    bg.pop(0)()
                for qb in range(nqb):
                    po = 5 + (qb % 2)
                    steps = list(range(nkb))
                    spi = {}

                    def qk(kb, sl=sl, qb=qb):
                        pi = kb % 5
                        spi[kb] = pi
                        mm(PS[pi][:], [(KT[sl][:, kb * 128:(kb + 1) * 128], QT[sl][:, qb * TB:(qb + 1) * TB])],
                           [("KT", sl), ("QT", sl)], [("ps", pi)])

                    qk(0)
                    if nkb > 1:
                        qk(1)
                    for kb in steps:
                        pi = spi[kb]
                        pt_i = kb % 4
                        act(PT[pt_i][:], PS[pi][:], AF.Exp, [("ps", pi)], [("PT", pt_i)], scale=scale)
                        if kb + 2 < nkb:
                            qk(kb + 2)
                        T.op("pe", lambda e, po=po, sl=sl, kb=kb, pt_i=pt_i, nkb=nkb:
                             e.matmul(PS[po][0:65, :], VP[sl][:, kb, :], PT[pt_i][:], start=(kb == 0), stop=(kb == nkb - 1)),
                             reads=[("VP", sl), ("PT", pt_i)], writes=[("ps", po)])
                    ob = qb % 2
                    T.op("dve", lambda e, ob=ob, po=po: e.tensor_copy(out=OSB[ob][0:65, :], in_=PS[po][0:65, :]),
                         reads=[("ps", po)], writes=[("OSB", ob)])
                    recip(OSB[ob][64:65, :], OSB[ob][64:65, :], [("OSB", ob)], [("OSB", ob)])
                    mm(PS[7][0:64, :], [(ones_f[64:65, 0:64], OSB[ob][64:65, :])], [("OSB", ob), "ones_f"], [("ps", 7)])
                    tt(OTH[ob][:], OSB[ob][0:64, :], PS[7][0:64, :], ALU.mult, [("OSB", ob), ("ps", 7)], [("OTH", ob)])
                    tq = tbase + qb * TB
                    dma("pool", oT_d[h * 64:(h + 1) * 64, tq:tq + TB], OTH[ob][:], [("OTH", ob)], [("od",)], "st_o%d" % ob)
        while bg:
            bg.pop(0)()
        T.barrier()
        if STOP == "attn":
            break

        if "ph3" not in PHASES:
            sb.phase()
            ns = {}
            ns["X"] = [sb.alloc([128, 8, TB], F32) for _ in range(2)]
            ns["H"] = [sb.alloc([128, 8, TB], BF16) for _ in range(2)]
            ns["OT"] = [sb.alloc([128, 4, TB], BF16) for _ in range(2)]
            ns["US"] = sb.alloc([128, 8, TB], BF16)
            ns["BIGF"] = sb.alloc([128, 4096], F32)
            ns["VM"] = sb.alloc([128, 4096], BF16)
            ns["SH"] = sb.alloc([128, 8, TB], BF16)
            ns["SGT"] = sb.alloc([128, 4, TB], F32)
            ns["ACTT"] = sb.alloc([128, 22, TB], BF16)
            ns["SQ"] = sb.alloc([128, 8, TB], BF16)
            ns["RS"] = sb.alloc([128, TB], F32)
            ns["WS"] = [sb.alloc([128, 4096], BF16) for _ in range(3)]
            ns["BSB"] = sb.alloc([128, 8, 128], F32)
            ns["WST"] = sb.alloc([128, 8, 128], BF16)
            ns["SSV"] = sb.alloc([128, 4], F32)
            ns["JUNK"] = sb.alloc([128, 1024], BF16)
            ns["TMP"] = [sb.alloc([128, TB], F32) for _ in range(2)]
            PHASES["ph3"] = ns
        ns = PHASES["ph3"]
        X = ns["X"]
        H = ns["H"]
        OT = ns["OT"]
        US = ns["US"]
        BIGF = ns["BIGF"]
        VM = ns["VM"]
        SH = ns["SH"]
        SGT = ns["SGT"]
        ACTT = ns["ACTT"]
        SQ = ns["SQ"]
        RS = ns["RS"]
        WS = ns["WS"]
        BSB = ns["BSB"]
        WST = ns["WST"]
        SSV = ns["SSV"]
        JUNK = ns["JUNK"]
        TMP = ns["TMP"]
        GV = BIGF[:].rearrange("p (a b) -> p a b", a=4)
        MT = BIGF[:].rearrange("p (a b) -> p a b", a=8)
        VH = VM[:].rearrange("p (a b) -> p a b", a=4)
        MG = VM[:].rearrange("p (a b) -> p a b", a=8)
        dma("sp", BSB[:], bsb_in[l], [], ["BSB"], "ld_misc")
        dma("pool", WST[:], wsT_in[l], [], ["WST"], "ld_cst")
        tmp_rr = [0]

        def tmp_next():
            i = tmp_rr[0]
            tmp_rr[0] = 1 - i
            return i

        ws_rr = [0]
        pending = []

        def p2_loads(i):
            b = i
            xs = i % 2
            t0 = tok0(b)
            dma("sp", X[xs][:], x_src[:, t0:t0 + TB].rearrange("(c p) t -> p c t", p=128), [("xd", b)],
                [(("X", xs), c) for c in range(8)], "ld_x%d" % xs)
            dma("sp", H[xs][:], hT_d[:, t0:t0 + TB].rearrange("(c p) t -> p c t", p=128), [("hd", b)],
                [(("H", xs), c) for c in range(8)], "ld_h%d" % xs)
            dma("sp", OT[xs][:], oT_d[:, t0:t0 + TB].rearrange("(c p) t -> p c t", p=128), [("od",)],
                [("OT", xs)], "ld_o%d" % xs)

        for b in range(NB):
            if b == 0:
                p2_loads(0)
            xs = b % 2
            s = seg(b)
            t0 = tok0(b)
            Xb, Hb, OTb = X[xs], H[xs], OT[xs]
            xkey, hkey = ("X", xs), ("H", xs)
            hreads = [(hkey, c) for c in range(8)]

            steps = []

            def add(name, c0, n, fn):
                steps.append((name, c0, n, WK[name] // 128, fn))

            def f_zu(W, wkey, g):
                for jj in range(4):
                    c = g * 4 + jj
                    pi = ps_next()
                    mm(PS[pi][:], [(W[:, kc, jj * 128:(jj + 1) * 128], Hb[:, kc, :]) for kc in range(8)], hreads + [wkey], [("ps", pi)])
                    act(US[:, c, :], PS[pi][:], AF.Gelu_apprx_tanh, [("ps", pi)], [("US", c)])
            for g in range(2):
                add("w_in", 672 + g * 512, 512, lambda W, wkey, g=g: f_zu(W, wkey, g))

            def f_zv(W, wkey, half):
                for t4 in range(4):
                    pi = ps_next()
                    mm(PS[pi][:], [(Hb[:, kc, t4 * 128:(t4 + 1) * 128], W[:, kc, :]) for kc in range(8)], hreads + [wkey], [("ps", pi)])
                    act(GV[:, t4, half * 512:(half + 1) * 512], PS[pi][:], AF.Gelu_apprx_tanh, [("ps", pi)], [("BIGF", 2 * t4 + half)])
                if half == 1:
                    memset(SSV[:], 0.0, [("SSV", t4) for t4 in range(4)])
                    for t4 in range(4):
                        act(JUNK[:], GV[:, t4, :], AF.Square, [("BIGF", 2 * t4), ("BIGF", 2 * t4 + 1)], ["JUNK", ("SSV", t4)],
                            accum_out=SSV[:, t4:t4 + 1])
                    act(SSV[:], SSV[:], AF.Sqrt, [("SSV", t4) for t4 in range(4)] + ["vecs"], [("SSV", t4) for t4 in range(4)],
                        bias=eps_ap, scale=1.0 / D)
                    recip(SSV[:], SSV[:], [("SSV", t4) for t4 in range(4)], [("SSV", t4) for t4 in range(4)])
                    for t4 in range(4):
                        ts(VH[:, t4, :], GV[:, t4, :], SSV[:, t4:t4 + 1], None, ALU.mult, None,
                           [("BIGF", 2 * t4), ("BIGF", 2 * t4 + 1), ("SSV", t4)], [("VM", 2 * t4), ("VM", 2 * t4 + 1)])
                    for g in range(8):
                        pi = ps_next()

                        def fmix(e, pi=pi, g=g):
                            ins = None
                            for t4 in range(4):
                                ins = e.matmul(PS[pi][:, t4 * 128:(t4 + 1) * 128], VH[:, t4, g * 128:(g + 1) * 128], WST[:, g, :],
                                               start=True, stop=True)
                            return ins
                        T.op("pe", fmix, reads=[("VM", k) for k in range(8)] + ["WST"], writes=[("ps", pi)])
                        ti = tmp_next()
                        for t4 in range(4):
                            stt(TMP[ti][:, t4 * 128:(t4 + 1) * 128], PS[pi][:, t4 * 128:(t4 + 1) * 128], vcol(V_SGUG, l * 8 + g),
                                BSB[:, g, :], ALU.mult, ALU.add, [("ps", pi), "BSB", "vecs"], [("TMP", ti)])
                        tt(SH[:, g, :], TMP[ti][:], US[:, g, :], ALU.mult, [("TMP", ti), ("US", g)], [("SH", g)], eng="pool")
            for half in range(2):
                add("w_in", 1696 + half * 512, 512, lambda W, wkey, half=half: f_zv(W, wkey, half))

            def f_gate(W, wkey, g):
                for jj in range(4):
                    c = g * 4 + jj
                    pi = ps_next()
                    mm(PS[pi][:], [(W[:, kc, jj * 128:(jj + 1) * 128], Hb[:, kc, :]) for kc in range(8)], hreads + [wkey], [("ps", pi)])
                    act(US[:, c, :], PS[pi][:], AF.Sigmoid, [("ps", pi)], [("US", c)])

            def f_ob(W, wkey, g):
                for jj in range(4):
                    c = g * 4 + jj
                    pi = ps_next()
                    mm(PS[pi][:], [(W[:, kc, jj * 128:(jj + 1) * 128], SH[:, kc, :]) for kc in range(8)],
                       [("SH", k) for k in range(8)] + [wkey], [("ps", pi)])
                    tt(MT[:, c, :], PS[pi][:], US[:, c, :], ALU.mult, [("ps", pi), ("US", c)], [("BIGF", c)])

            def f_oa(W, wkey):
                for c in range(8):
                    pi = ps_next()
                    mm(PS[pi][:], [(W[:, kc, c * 128:(c + 1) * 128], OTb[:, kc, :]) for kc in range(4)], [("OT", xs), wkey], [("ps", pi)])
                    ti = tmp_next()
                    tt(TMP[ti][:], PS[pi][:], US[:, c, :], ALU.mult, [("ps", pi), ("US", c)], [("TMP", ti)])
                    tt(MG[:, c, :], TMP[ti][:], MT[:, c, :], ALU.add, [("TMP", ti), ("BIGF", c)], [("VM", c)], eng="pool")

            def f_out(W, wkey, g):
                for jj in range(4):
                    c = g * 4 + jj
                    pi = ps_next()
                    mm(PS[pi][:], [(W[:, kc, jj * 128:(jj + 1) * 128], MG[:, kc, :]) for kc in range(8)],
                       [("VM", k) for k in range(8)] + [wkey], [("ps", pi)])
                    stt(Xb[:, c, :], PS[pi][:], mod_ap(l, 2, c, s), Xb[:, c, :], ALU.mult, ALU.add,
                        [("ps", pi), (xkey, c), ("mods", l, 16 + c)], [(xkey, c)])
                if g == 1:
                    norm_block(Xb, xkey, SH, "SH", SQ, RS, A2, l, s, 3, TMP)

            for g in range(2):
                add("w_in", 3744 + g * 512, 512, lambda W, wkey, g=g: f_gate(W, wkey, g))
            for g in range(2):
                add("w_o_b", g * 512, 512, lambda W, wkey, g=g: f_ob(W, wkey, g))
            for g in range(2):
                add("w_in", 2720 + g * 512, 512, lambda W, wkey, g=g: f_gate(W, wkey, g))
            add("w_o_a", 0, 1024, lambda W, wkey: f_oa(W, wkey))
            for g in range(2):
                add("w_out", g * 512, 512, lambda W, wkey, g=g: f_out(W, wkey, g))

            h2reads = [("SH", k) for k in range(8)]

            def f_gatef(W, wkey, j0, nch):
                for jj in range(nch):
                    pi = ps_next()
                    mm(PS[pi][:], [(W[:, kc, jj * 128:(jj + 1) * 128], SH[:, kc, :]) for kc in range(8)], h2reads + [wkey], [("ps", pi)])
                    act(SGT[:, jj, :], PS[pi][:], AF.Silu, [("ps", pi)], [("SGT", jj)])

            def f_upf(W, wkey, j0, nch):
                for jj in range(nch):
                    pi = ps_next()
                    mm(PS[pi][:], [(W[:, kc, jj * 128:(jj + 1) * 128], SH[:, kc, :]) for kc in range(8)], h2reads + [wkey], [("ps", pi)])
                    tt(ACTT[:, j0 + jj, :], PS[pi][:], SGT[:, jj, :], ALU.mult, [("ps", pi), ("SGT", jj)], [("ACTT", j0 + jj)])

            for j0 in range(0, 22, 4):
                nch = min(4, 22 - j0)
                add("w_ffn_in", DFF + j0 * 128, nch * 128, lambda W, wkey, j0=j0, nch=nch: f_gatef(W, wkey, j0, nch))
                add("w_ffn_in", j0 * 128, nch * 128, lambda W, wkey, j0=j0, nch=nch: f_upf(W, wkey, j0, nch))

            def f_fo(W, wkey, c):
                pi = ps_next()
                mm(PS[pi][:], [(W[:, kc, :], ACTT[:, kc, :]) for kc in range(22)], [("ACTT", k) for k in range(22)] + [wkey], [("ps", pi)])
                stt(Xb[:, c, :], PS[pi][:], mod_ap(l, 5, c, s), Xb[:, c, :], ALU.mult, ALU.add,
                    [("ps", pi), (xkey, c), ("mods", l, 40 + c)], [(xkey, c)])
            for c in range(8):
                add("w_ffn_out", c * 128, 128, lambda W, wkey, c=c: f_fo(W, wkey, c))

            nst = len(steps)
            views = {}

            def issue_load(k):
                name, c0, n, kcn, fn = steps[k]
                sl = ws_rr[0]
                ws_rr[0] = (sl + 1) % 3
                view = WS[sl][:, 0:kcn * n].rearrange("p (k n) -> p k n", k=kcn)
                dma("sp", view, wsrc(name, l, c0, n), wkeys(name, l), [("WS", sl)], "ld_ws%d" % sl)
                views[k] = (view, ("WS", sl))

            issue_load(0)
            issue_load(1)
            for k in range(nst):
                if k + 2 < nst:
                    issue_load(k + 2)
                if k == 4 and b + 1 < NB:
                    p2_loads(b + 1)
                view, wkey = views[k]
                steps[k][4](view, wkey)
            dma("pool", x_dst[:, t0:t0 + TB].rearrange("(c p) t -> p c t", p=128), Xb[:],
                [(xkey, c) for c in range(8)], [("xd", b)], "st_x%d" % xs)
        T.barrier()

    with nc.Block() as block:
        @block.tensor
        def _(e):
            T.replay("pe", e)

        @block.scalar
        def _(e):
            T.replay("act", e)

        @block.vector
        def _(e):
            T.replay("dve", e)

        @block.gpsimd
        def _(e):
            T.replay("pool", e)

        @block.sync
        def _(e):
            T.replay("sp", e)
    es.close()
    return nc


def rope_tables(pos):
    inv = (10000.0 ** (-np.arange(0, 32, 2, dtype=np.float32) / np.float32(32))).astype(np.float32)
    ang = pos.astype(np.float32)[:, None] * inv[None, :]
    return np.cos(ang).astype(np.float32), np.sin(ang).astype(np.float32)


def make_consts():
    k = np.arange(128)[:, None]
    m = np.arange(128)[None, :]
    ones = np.ones((128, 128), np.float32)
    nn = (k // 64 == m // 64)
    rn = [(k // 32 == 2 * i + m // 64) for i in range(2)]
    nr = [(2 * i + k // 64 == m // 32) for i in range(2)]
    rr = (k // 32 == m // 32)
    mats = [ones, nn, rn[0], rn[1], nr[0], nr[1], rr]
    return np.ascontiguousarray(np.stack([np.asarray(x, np.float32) for x in mats], axis=1))


def prepare_inputs(cfg, inp):
    Ls, Lq, L = cfg["Ls"], cfg["Lq"], cfg["L"]
    voff, NV = vec_layout(L)
    f = lambda a: np.ascontiguousarray(np.asarray(a, dtype=np.float32))
    w_in = f(inp["w_in"])[:L]
    w_q_b = f(inp["w_q_b"])[:L]
    w_kv_b = f(inp["w_kv_b"])[:L]
    shared = {}
    shared["w_in"] = w_in.reshape(L * D, INC)
    kr = w_in[:, :, 640:672]
    krrot = np.concatenate([kr[:, :, 16:32], kr[:, :, 0:16]], axis=2)
    shared["w_kr4"] = f(np.tile(kr, (1, 1, 4))).reshape(L * D, 128)
    shared["w_krrot4"] = f(np.tile(krrot, (1, 1, 4))).reshape(L * D, 128)
    wq = w_q_b.reshape(L, QL, NH, 96)
    shared["wq_nope"] = f(wq[:, :, :, 0:64]).reshape(L * QL, 512)
    shared["wq_rope"] = f(wq[:, :, :, 64:96]).reshape(L * QL, 256)
    shared["wq_rot"] = f(np.concatenate([wq[:, :, :, 80:96], wq[:, :, :, 64:80]], axis=3)).reshape(L * QL, 256)
    wkv = w_kv_b.reshape(L, KVL, NH, 128)
    shared["wkv_nope"] = f(wkv[:, :, :, 0:64]).reshape(L * KVL, 512)
    shared["wkv_v"] = f(wkv[:, :, :, 64:128]).reshape(L * KVL, 512)
    shared["w_o_a"] = f(inp["w_o_a"])[:L].reshape(L * 512, D)
    shared["w_o_b"] = f(inp["w_o_b"])[:L].reshape(L * D, D)
    shared["w_out"] = f(inp["w_out"])[:L].reshape(L * D, D)
    shared["w_ffn_in"] = f(inp["w_ffn_in"])[:L].reshape(L * D, 2 * DFF)
    shared["w_ffn_out"] = f(inp["w_ffn_out"])[:L].reshape(L * DFF, D)
    shared["w_ada"] = f(inp["w_ada"])[:L].reshape(L * D, 6 * D)
    shared["w_sT"] = f(np.transpose(f(inp["w_s"])[:L], (0, 3, 1, 2)))
    shared["bsb"] = f(np.broadcast_to(f(inp["b_s"])[:L][:, None, :, :], (L, 128, 8, 128)))
    shared["consts"] = make_consts()
    vecs = np.zeros((128, NV), np.float32)
    vecs[:, voff[V_EPS]] = EPS
    p = np.arange(128)

    def fm(a, n):
        return f(a)[:L].reshape(L, n, 128).transpose(2, 0, 1).reshape(128, L * n)
    vecs[:, voff[V_N1G]:voff[V_N1G] + 8 * L] = fm(inp["norm1_g"], 8)
    vecs[:, voff[V_N2G]:voff[V_N2G] + 8 * L] = fm(inp["norm2_g"], 8)
    vecs[:, voff[V_SGUG]:voff[V_SGUG] + 8 * L] = fm(inp["sgu_norm_g"], 8)
    vecs[:, voff[V_QAG]:voff[V_QAG] + 3 * L] = fm(inp["q_a_norm_g"], 3)
    vecs[:, voff[V_KVAG]:voff[V_KVAG] + 2 * L] = fm(inp["kv_a_norm_g"], 2)
    vecs[:, voff[V_BADA]:voff[V_BADA] + 48 * L] = fm(inp["b_ada"], 48)
    qg = f(inp["q_norm_g"])[:L]
    kg = f(inp["k_norm_g"])[:L]
    for (g, vn, vr, vrp) in ((qg, V_GQN, V_GQR, V_GQRP), (kg, V_GKN, V_GKR, V_GKRP)):
        vecs[:, voff[vn]:voff[vn] + L] = g[:, p % 64].T
        vecs[:, voff[vr]:voff[vr] + L] = g[:, 64 + p % 32].T
        vecs[:, voff[vrp]:voff[vrp] + L] = g[:, 64 + (p % 32 + 16) % 32].T
    shared["vecs"] = vecs
    xs = f(inp["x_sample"])
    xp = f(inp["x_prompt"])
    cs = f(inp["c_sample"])
    cp = f(inp["c_prompt"])
    fidx = (p % 32) % 16
    sgn = np.where((p % 32) < 16, -1.0, 1.0).astype(np.float32)
    in_maps = []
    for c in range(NCORES):
        ps_, r = c // 4, c % 4
        m = dict(shared)
        m["xT"] = np.ascontiguousarray(np.concatenate([xs[c, :Ls].T, xp[ps_, r * Lq:(r + 1) * Lq].T], axis=1))
        m["cT"] = np.ascontiguousarray(np.stack([cs[c].reshape(8, 128).T, cp[ps_].reshape(8, 128).T], axis=2))
        pos = np.concatenate([np.arange(Ls), r * Lq + np.arange(Lq)])
        cos, sin = rope_tables(pos)
        m["cos4"] = np.ascontiguousarray(cos.T[fidx, :])
        m["sin4"] = np.ascontiguousarray(sin.T[fidx, :] * sgn[:, None])
        in_maps.append(m)
    return in_maps


_PROG_CACHE = {}


def run(cfg, inp):
    key = (cfg["Ls"], cfg["Lq"], cfg["L"], cfg.get("stop"), cfg.get("no_cc"), cfg.get("no_pool"), cfg.get("no_qkstore"))
    if key not in _PROG_CACHE:
        _PROG_CACHE[key] = build_program(cfg)
    nc = _PROG_CACHE[key]
    in_maps = prepare_inputs(cfg, inp)
    res = run_bass_kernel_spmd(nc, in_maps, core_ids=list(range(NCORES)))
    Ls, Lq = cfg["Ls"], cfg["Lq"]
    ys = np.stack([res.results[c]["yT"][:, :Ls].T for c in range(NCORES)], axis=0)
    yp = np.stack([np.concatenate([res.results[4 * s_ + r]["yT"][:, Ls:].T for r in range(4)], axis=0) for s_ in range(2)], axis=0)
    return np.ascontiguousarray(yp, dtype=np.float32), np.ascontiguousarray(ys, dtype=np.float32)


def kernel(**inputs):
    return run(FULL_CFG, inputs)
```

```python
import numpy as np
from contextlib import ExitStack
import concourse.bass as bass
import concourse.mybir as mybir
from concourse.bass_utils import run_bass_kernel_spmd

F32 = mybir.dt.float32
BF16 = mybir.dt.bfloat16
AF = mybir.ActivationFunctionType
ALU = mybir.AluOpType

D = 1024
NH = 8
QL = 384
KVL = 256
DFF = 2816
INC = 4768
EPS = 1e-6
TB = 512
NCORES = 8
SAME_SYNC = True

FULL_CFG = dict(Ls=2048, Lq=2048, L=4)


class Tracker:
    COMPUTE = ("pe", "act", "dve", "pool")

    def __init__(self, nc, es):
        self.nc = nc
        self.es = es
        self.ops = {e: [] for e in ("pe", "act", "dve", "pool", "sp")}
        self.cnt = {e: 0 for e in self.COMPUTE}
        self.esem = {e: es.enter_context(nc.semaphore("c_" + e)) for e in self.COMPUTE}
        self.dsem = {}
        self.seen = {e: {} for e in self.ops}
        self.lastw = {}
        self.readers = {}
        self.nops = 0
        self.no_pool = False

    def _dsem(self, key):
        if key not in self.dsem:
            self.dsem[key] = [self.es.enter_context(self.nc.semaphore("d%d" % len(self.dsem))), 0]
        return self.dsem[key]

    def _resolve(self, tok):
        if tok[0] == "c":
            return self.esem[tok[1]], tok[2], ("c", tok[1])
        d = self._dsem(tok[1])
        return d[0], d[1], ("d", tok[1])

    def op(self, eng, fn, reads=(), writes=(), dma=None, inc=None):
        if eng == "pool" and dma is None and self.no_pool:
            eng = "dve"
        deps = set()
        for r in reads:
            if r in self.lastw:
                deps.add(self.lastw[r])
            if isinstance(r, tuple) and r[0] == "ps":
                for k, v in self.readers.get(r, {}).items():
                    if k != ("c", eng):
                        deps.add(k + (v,))
        for w in writes:
            if w in self.lastw:
                deps.add(self.lastw[w])
            for k, v in self.readers.get(w, {}).items():
                deps.add(k + (v,))
        waits = []
        for tok in deps:
            if tok[0] == "c" and tok[1] == eng and dma is None and (eng == "pe" or not SAME_SYNC):
                continue
            sem, val, sid = self._resolve(tok)
            if self.seen[eng].get(sid, 0) >= val:
                continue
            self.seen[eng][sid] = val
            waits.append((sem, val))
        if dma is not None:
            d = self._dsem(dma)
            step = 16 if inc is None else inc
            d[1] += step
            tok = ("d", dma, d[1])
            incr = (d[0], step)
        else:
            self.cnt[eng] += 1
            tok = ("c", eng, self.cnt[eng])
            incr = (self.esem[eng], 1)
        self.ops[eng].append((waits, fn, incr))
        self.nops += 1
        for r in reads:
            self.readers.setdefault(r, {})[tok[:2]] = tok[2]
        for w in writes:
            self.lastw[w] = tok
            self.readers[w] = {}
        return tok

    @staticmethod
    def _bg(key):
        return isinstance(key, str) and key.startswith("cast")

    def barrier(self):
        for eng in self.ops:
            waits = []
            for f in self.COMPUTE:
                if f == eng or self.cnt[f] == 0:
                    continue
                if self.seen[eng].get(("c", f), 0) < self.cnt[f]:
                    self.seen[eng][("c", f)] = self.cnt[f]
                    waits.append((self.esem[f], self.cnt[f]))
            for key, d in self.dsem.items():
                if self._bg(key):
                    continue
                if d[1] and self.seen[eng].get(("d", key), 0) < d[1]:
                    self.seen[eng][("d", key)] = d[1]
                    waits.append((d[0], d[1]))
            if waits:
                self.ops[eng].append((waits, None, None))
        self.lastw = {r: t for r, t in self.lastw.items() if t[0] == "d" and self._bg(t[1])}
        self.readers = {}

    def replay(self, name, e):
        for waits, fn, incr in self.ops[name]:
            for sem, val in waits:
                e.wait_ge(sem, val)
            if fn is not None:
                ins = fn(e)
                ins.then_inc(incr[0], incr[1])


class SbAlloc:
    BASE = 16512
    TOP = 229344

    def __init__(self, nc):
        self.nc = nc
        self.persist = self.BASE
        self.cur = self.BASE
        self.n = 0

    def _al(self, shape, dt, off):
        self.n += 1
        return self.nc.alloc_sbuf_tensor_at("sb%d" % self.n, list(shape), dt, offset=off)

    def alloc(self, shape, dt, persistent=False):
        esz = 4 if dt == F32 else 2
        nbytes = int(np.prod(shape[1:])) * esz
        nbytes = (nbytes + 31) // 32 * 32
        if persistent:
            assert self.cur == self.persist, "persistent allocs must come first"
            off = self.persist
            self.persist += nbytes
            self.cur = self.persist
        else:
            off = self.cur
            self.cur += nbytes
        assert self.cur <= self.TOP, "SBUF overflow %d" % self.cur
        return self._al(shape, dt, off)

    def phase(self):
        self.cur = self.persist


W_SPECS = [
    ("w_in", 1024, INC, "A"), ("w_kr4", 1024, 128, "A"), ("w_krrot4", 1024, 128, "A"),
    ("wq_nope", QL, 512, "A"), ("wq_rope", QL, 256, "A"), ("wq_rot", QL, 256, "A"),
    ("wkv_nope", KVL, 512, "A"), ("wkv_v", KVL, 512, "A"),
    ("w_o_a", 512, D, "B"), ("w_o_b", D, D, "B"), ("w_out", D, D, "B"),
    ("w_ffn_in", D, 2 * DFF, "B"), ("w_ffn_out", DFF, D, "B"),
]
V_EPS, V_N1G, V_N2G, V_SGUG, V_QAG, V_KVAG, V_GQN, V_GQR, V_GQRP, V_GKN, V_GKR, V_GKRP, V_BADA = range(13)


def vec_layout(L):
    off = {}
    o = 0
    for name, n in [(V_EPS, 1), (V_N1G, 8 * L), (V_N2G, 8 * L), (V_SGUG, 8 * L), (V_QAG, 3 * L), (V_KVAG, 2 * L),
                    (V_GQN, L), (V_GQR, L), (V_GQRP, L), (V_GKN, L), (V_GKR, L), (V_GKRP, L), (V_BADA, 48 * L)]:
        off[name] = o
        o += n
    return off, o


def build_program(cfg):
    Ls, Lq, L = cfg["Ls"], cfg["Lq"], cfg["L"]
    Lp = 4 * Lq
    NTOK = Ls + Lq
    nS, nP = Ls // TB, Lq // TB
    NB = nS + nP
    voff, NV = vec_layout(L)

    nc = bass.Bass("TRN2", target_bir_lowering=False)

    def din(name, shape, dt=F32):
        return nc.dram_tensor(name, list(shape), dt, kind="ExternalInput").ap()

    xT_in = din("xT", [D, NTOK])
    cT_in = din("cT", [128, 8, 2])
    cos_in = din("cos4", [128, NTOK])
    sin_in = din("sin4", [128, NTOK])
    vecs_in = din("vecs", [128, NV])
    consts_in = din("consts", [128, 7, 128])
    w_ada_in = din("w_ada", [L * D, 6 * D])
    wsT_in = din("w_sT", [L, 128, 8, 128])
    bsb_in = din("bsb", [L, 128, 8, 128])
    w_ext = {name: din(name, [L * K, N]) for name, K, N, _ in W_SPECS}
    yT_out = nc.dram_tensor("yT", [D, NTOK], F32, kind="ExternalOutput").ap()

    def dscr(name, shape, dt):
        return nc.dram_tensor(name, list(shape), dt)

    w_bf = {name: dscr(name + "_b", [L * K, N], BF16) for name, K, N, _ in W_SPECS}
    WK = {name: K for name, K, N, _ in W_SPECS}
    xT_d = dscr("xT_d", [D, NTOK], F32)
    hT_d = dscr("hT_d", [D, NTOK], BF16)
    qT_d = dscr("qT_d", [NB * 768, TB], BF16)
    oT_d = dscr("oT_d", [512, NTOK], BF16)
    kT_S = dscr("kT_S", [nS * 768, TB], BF16)
    v_S = dscr("v_S", [Ls, 512], BF16)
    kT_Pin = [dscr("kT_Pin%d" % b, [NH * 96, TB], BF16) for b in range(nP)]
    kT_Pall = [dscr("kT_Pall%d" % b, [4 * NH * 96, TB], BF16) for b in range(nP)]
    v_Pin = [dscr("v_Pin%d" % b, [TB, 512], BF16) for b in range(nP)]
    v_Pall = [dscr("v_Pall%d" % b, [4 * TB, 512], BF16) for b in range(nP)]

    es = ExitStack()
    T = Tracker(nc, es)
    T.no_pool = bool(cfg.get("no_pool"))
    sb = SbAlloc(nc)
    PS = [es.enter_context(nc.psum_tensor("ps%d" % i, [128, 512], F32)) for i in range(8)]
    ps_rr = [0]

    def ps_next():
        i = ps_rr[0]
        ps_rr[0] = (i + 1) % 8
        return i

    def dma(eng, out, in_, reads, writes, key):
        T.op(eng, lambda e, o=out, i=in_: e.dma_start(out=o, in_=i), reads=reads, writes=writes, dma=key)

    def mm(ps_ap, pairs, reads, writes):
        def fn(e, ps_ap=ps_ap, pairs=pairs):
            n = len(pairs)
            ins = None
            for i, (l, r) in enumerate(pairs):
                ins = e.matmul(ps_ap, l, r, start=(i == 0), stop=(i == n - 1))
            return ins
        T.op("pe", fn, reads=reads, writes=writes)

    def act(out, in_, func, reads, writes, **kw):
        T.op("act", lambda e, o=out, i=in_, f=func, kw=kw: e.activation(out=o, in_=i, func=f, **kw), reads=reads, writes=writes)

    def tt(out, in0, in1, op, reads, writes, eng="dve"):
        T.op(eng, lambda e, o=out, a=in0, b=in1, op=op: e.tensor_tensor(out=o, in0=a, in1=b, op=op), reads=reads, writes=writes)

    def stt(out, in0, scalar, in1, op0, op1, reads, writes, eng="dve"):
        T.op(eng, lambda e, o=out, a=in0, s=scalar, b=in1, p0=op0, p1=op1:
             e.scalar_tensor_tensor(out=o, in0=a, scalar=s, in1=b, op0=p0, op1=p1), reads=reads, writes=writes)

    def ts(out, in0, s1, s2, op0, op1, reads, writes, eng="dve"):
        if s2 is None:
            T.op(eng, lambda e, o=out, a=in0, s1=s1, p0=op0: e.tensor_scalar(out=o, in0=a, scalar1=s1, scalar2=None, op0=p0),
                 reads=reads, writes=writes)
        else:
            T.op(eng, lambda e, o=out, a=in0, s1=s1, s2=s2, p0=op0, p1=op1:
                 e.tensor_scalar(out=o, in0=a, scalar1=s1, scalar2=s2, op0=p0, op1=p1), reads=reads, writes=writes)

    def recip(out, in_, reads, writes):
        T.op("dve", lambda e, o=out, i=in_: e.reciprocal(out=o, in_=i), reads=reads, writes=writes)

    def memset(ap, val, writes, eng="dve"):
        T.op(eng, lambda e, a=ap, v=val: e.memset(a, v), writes=writes)

    vecs = sb.alloc([128, NV], F32, True)
    cst = sb.alloc([128, 7, 128], BF16, True)
    ones_f = sb.alloc([128, 64], F32, True)
    mods = sb.alloc([128, L, 48, 2], F32, True)
    A1 = sb.alloc([128, L, 2, 8], F32, True)
    A2 = sb.alloc([128, L, 2, 8], F32, True)
    silc = sb.alloc([128, 8, 2], BF16, True)
    eps_ap = vecs[:, voff[V_EPS]:voff[V_EPS] + 1]

    def vcol(kind, idx):
        o = voff[kind] + idx
        return vecs[:, o:o + 1]

    ONES, NN, RN0, RN1, NR0, NR1, RR = [cst[:, i, :] for i in range(7)]
    RN = [RN0, RN1]
    NR = [NR0, NR1]

    dma("sp", vecs[:], vecs_in, [], ["vecs"], "ld_misc")
    dma("pool", cst[:], consts_in, [], ["cst"], "ld_cst")
    memset(ones_f[:], 1.0, ["ones_f"])

    def cast_weights(l, grp):
        for name, K, N, g in W_SPECS:
            if g != grp:
                continue
            for rb in range(0, K, 128):
                r0 = l * K + rb
                dma("pool", w_bf[name][r0:r0 + 128, :], w_ext[name][r0:r0 + 128, :], [], [("wb", name, l, rb // 128)],
                    "cast%s%d" % (grp, l))

    def wkeys(name, l):
        return [("wb", name, l, i) for i in range(WK[name] // 128)]

    def wsrc(name, l, c0, n):
        K = WK[name]
        return w_bf[name][l * K:(l + 1) * K, c0:c0 + n].rearrange("(kc p) n -> p kc n", p=128)

    cast_weights(0, "A")

    cTt = sb.alloc([128, 8, 2], F32)
    WA0 = [sb.alloc([128, 8, 512], BF16) for _ in range(2)]
    dma("sp", cTt[:], cT_in, [], ["cTt"], "ld_misc")
    act(silc[:], cTt[:], AF.Silu, ["cTt"], ["silc"])

    def mods_load(l, jg, WA):
        s_ = jg % 2
        src = w_ada_in[l * D:(l + 1) * D, jg * 512:(jg + 1) * 512].rearrange("(kc p) n -> p kc n", p=128)
        dma("pool", WA[s_][:], src, [], [("WA", s_)], "ld_wa%d" % s_)

    def mods_compute(l, jg, WA, banks=None):
        s_ = jg % 2
        for jj in range(4):
            j = jg * 4 + jj
            pi = ps_next() if banks is None else banks[jj % len(banks)]
            mm(PS[pi][:, 0:2], [(WA[s_][:, kc, jj * 128:(jj + 1) * 128], silc[:, kc, :]) for kc in range(8)],
               [("WA", s_), "silc"], [("ps", pi)])
            ts(mods[:, l, j, :], PS[pi][:, 0:2], vcol(V_BADA, l * 48 + j), None, ALU.add, None,
               [("ps", pi), "vecs"], [("mods", l, j)])

    def mods_finish(l):
        for s_ in range(2):
            for (Ax, gk, m0) in ((A1, V_N1G, 8), (A2, V_N2G, 32)):
                ts(Ax[:, l, s_, :], mods[:, l, m0:m0 + 8, s_], 1.0, None, ALU.add, None,
                   [("mods", l, j) for j in range(m0, m0 + 8)], [("A", id(Ax), l, s_)])
                tt(Ax[:, l, s_, :], Ax[:, l, s_, :], vecs[:, voff[gk] + 8 * l:voff[gk] + 8 * l + 8], ALU.mult,
                   [("A", id(Ax), l, s_), "vecs"], [("A", id(Ax), l, s_)])

    mods_load(0, 0, WA0)
    for jg in range(12):
        if jg + 1 < 12:
            mods_load(0, jg + 1, WA0)
        mods_compute(0, jg, WA0)
    mods_finish(0)
    cast_weights(0, "B")
    T.barrier()
    STOP = cfg.get("stop")

    def mod_ap(l, m, c, s):
        return mods[:, l, m * 8 + c, s:s + 1]

    def tok0(b):
        return b * TB

    def seg(b):
        return 0 if b < nS else 1

    def rstd_from_psum(rs_ap, pi, n, rs_key):
        act(rs_ap, PS[pi][:], AF.Ln, [("ps", pi), "vecs"], [rs_key], bias=eps_ap, scale=1.0 / n)
        act(rs_ap, rs_ap, AF.Exp, [rs_key], [rs_key], scale=-0.5)

    def rms_stats(sq_list, lhs_list, n, rs_ap, rs_key, reads):
        pi = ps_next()
        mm(PS[pi][:], list(zip(lhs_list, sq_list)), reads + ["cst"], [("ps", pi)])
        rstd_from_psum(rs_ap, pi, n, rs_key)

    def norm_block(Xb, xkey, Hb, hkey, SQ, RS, Ax, l, s, shm, NT):
        for c in range(8):
            if c % 2 == 0:
                tt(SQ[:, c, :], Xb[:, c, :], Xb[:, c, :], ALU.mult, [(xkey, c)], [("SQ", c)])
            else:
                act(SQ[:, c, :], Xb[:, c, :], AF.Square, [(xkey, c)], [("SQ", c)])
        rms_stats([SQ[:, c, :] for c in range(8)], [ONES] * 8, D, RS[:], "RS", [("SQ", c) for c in range(8)])
        for c in range(8):
            i = c % 2
            tt(NT[i][:], Xb[:, c, :], RS[:], ALU.mult, [(xkey, c), "RS"], [("NT", i)])
            act(Hb[:, c, :], NT[i][:], AF.Identity, [("NT", i), ("A", id(Ax), l, s), ("mods", l, shm * 8 + c)], [(hkey, c)],
                scale=Ax[:, l, s, c:c + 1], bias=mod_ap(l, shm, c, s))

    PHASES = {}
    for l in range(L if STOP != "prologue" else 0):
        x_src = xT_in if l == 0 else xT_d
        x_dst = yT_out if l == L - 1 else xT_d
        last = (l == L - 1)

        if "ph1" not in PHASES:
            sb.phase()
            ns = {}
            ns["X"] = [sb.alloc([128, 8, TB], F32) for _ in range(2)]
            ns["H"] = [sb.alloc([128, 8, TB], BF16) for _ in range(2)]
            ns["SQ"] = sb.alloc([128, 8, TB], BF16)
            ns["RS"] = sb.alloc([128, TB], F32)
            ns["W1in"] = sb.alloc([128, 8, 640], BF16)
            ns["Wkr"] = sb.alloc([128, 8, 128], BF16)
            ns["Wkrr"] = sb.alloc([128, 8, 128], BF16)
            ns["Wqn"] = sb.alloc([128, 3, 512], BF16)
            ns["Wqr"] = sb.alloc([128, 3, 256], BF16)
            ns["Wqt"] = sb.alloc([128, 3, 256], BF16)
            ns["Wkn"] = sb.alloc([128, 2, 512], BF16)
            ns["Wv"] = sb.alloc([128, 2, 512], BF16)
            ns["QN"] = sb.alloc([128, 3, TB], BF16)
            ns["KVN"] = sb.alloc([128, 2, TB], BF16)
            ns["QO"] = [sb.alloc([128, 6, TB], BF16) for _ in range(2)]
            ns["KO"] = [sb.alloc([128, 6, TB], BF16) for _ in range(2)]
            ns["VO"] = [sb.alloc([128, 4, 512], BF16) for _ in range(2)]
            ns["TK"] = sb.alloc([128, TB], F32)
            ns["SQRK"] = sb.alloc([128, TB], BF16)
            ns["COS"] = [sb.alloc([128, TB], F32) for _ in range(2)]
            ns["SIN"] = [sb.alloc([128, TB], F32) for _ in range(2)]
            ns["CQ"] = sb.alloc([128, TB], F32)
            ns["SQT"] = sb.alloc([128, TB], F32)
            ns["CK"] = sb.alloc([128, TB], F32)
            ns["SK"] = sb.alloc([128, TB], F32)
            ns["SQH"] = [sb.alloc([128, TB], BF16) for _ in range(3)]
            ns["RSH"] = [sb.alloc([128, TB], F32) for _ in range(3)]
            ns["T1"] = sb.alloc([128, TB], F32)
            ns["T2"] = sb.alloc([128, TB], F32)
            ns["NTT"] = [sb.alloc([128, TB], F32) for _ in range(2)]
            PHASES["ph1"] = ns
        ns = PHASES["ph1"]
        X = ns["X"]
        H = ns["H"]
        SQ = ns["SQ"]
        RS = ns["RS"]
        W1in = ns["W1in"]
        Wkr = ns["Wkr"]
        Wkrr = ns["Wkrr"]
        Wqn = ns["Wqn"]
        Wqr = ns["Wqr"]
        Wqt = ns["Wqt"]
        Wkn = ns["Wkn"]
        Wv = ns["Wv"]
        QN = ns["QN"]
        KVN = ns["KVN"]
        QO = ns["QO"]
        KO = ns["KO"]
        VO = ns["VO"]
        TK = ns["TK"]
        SQRK = ns["SQRK"]
        COS = ns["COS"]
        SIN = ns["SIN"]
        CQ = ns["CQ"]
        SQT = ns["SQT"]
        CK = ns["CK"]
        SK = ns["SK"]
        SQH = ns["SQH"]
        RSH = ns["RSH"]
        T1 = ns["T1"]
        T2 = ns["T2"]
        NTT = ns["NTT"]

        for (buf, name, c0, n) in ((W1in, "w_in", 0, 640), (Wkr, "w_kr4", 0, 128), (Wkrr, "w_krrot4", 0, 128),
                                   (Wqn, "wq_nope", 0, 512), (Wqr, "wq_rope", 0, 256), (Wqt, "wq_rot", 0, 256),
                                   (Wkn, "wkv_nope", 0, 512), (Wv, "wkv_v", 0, 512)):
            dma("sp", buf[:], wsrc(name, l, c0, n), wkeys(name, l), [("W1", name)], "ld_w1")
        W1R = [("W1", n) for n in ("w_in", "w_kr4", "w_krrot4", "wq_nope", "wq_rope", "wq_rot", "wkv_nope", "wkv_v")]

        order = list(range(nS, NB)) + list(range(nS))

        def p1_loads(i):
            b = order[i]
            xs = i % 2
            t0 = tok0(b)
            dma("sp", X[xs][:], x_src[:, t0:t0 + TB].rearrange("(c p) t -> p c t", p=128), [("xd", b)],
                [(("X", xs), c) for c in range(8)], "ld_x%d" % xs)
            dma("sp", COS[xs][:], cos_in[:, t0:t0 + TB], [], [("COS", xs)], "ld_cs%d" % xs)
            dma("sp", SIN[xs][:], sin_in[:, t0:t0 + TB], [], [("SIN", xs)], "ld_cs%d" % xs)

        def head_half(hh, PSn, PSr_sq_key, sqr_ap, rope_src_fn, OUT, okey, gn, kdst_fn, rope_keys):
            for i in range(2):
                act(SQH[i][:], PS[PSn[i]][:], AF.Square, [("ps", PSn[i])], [("SQH", i)])
            for i in range(2):
                pi = ps_next()
                mm(PS[pi][:], [(NN, SQH[i][:]), (RN[i], sqr_ap)], [("SQH", i), PSr_sq_key, "cst"], [("ps", pi)])
                rstd_from_psum(RSH[i][:], pi, 96, ("RSH", i))
            pi = ps_next()
            mm(PS[pi][:], [(NR[0], SQH[0][:]), (NR[1], SQH[1][:]), (RR, sqr_ap)],
               [("SQH", 0), ("SQH", 1), PSr_sq_key, "cst"], [("ps", pi)])
            rstd_from_psum(RSH[2][:], pi, 96, ("RSH", 2))
            for i in range(2):
                jn = 2 * hh + i
                stt(OUT[:, jn, :], PS[PSn[i]][:], vcol(gn, l), RSH[i][:], ALU.mult, ALU.mult,
                    [("ps", PSn[i]), ("RSH", i), "vecs"], [(okey, jn)])
            rap, rkeys = rope_src_fn(hh)
            tt(OUT[:, 4 + hh, :], rap, RSH[2][:], ALU.mult, rkeys + [("RSH", 2)] + rope_keys, [(okey, 4 + hh)])
            kdst_fn(hh)

        for i, b in enumerate(order):
            if i == 0:
                p1_loads(0)
            if i + 1 < NB:
                p1_loads(i + 1)
            xs = i % 2
            s = seg(b)
            t0 = tok0(b)
            Xb, Hb = X[xs], H[xs]
            QOb, KOb, VOb = QO[xs], KO[xs], VO[xs]
            qok, kok, vok = "QO%d" % xs, "KO%d" % xs, "VO%d" % xs
            xkey, hkey = ("X", xs), ("H", xs)
            norm_block(Xb, xkey, Hb, hkey, SQ, RS, A1, l, s, 0, NTT)
            dma("pool", hT_d[:, t0:t0 + TB].rearrange("(c p) t -> p c t", p=128), Hb[:],
                [(hkey, c) for c in range(8)], [("hd", b)], "st_h%d" % xs)
            hreads = [(hkey, c) for c in range(8)] + W1R
            ts(CQ[:], COS[xs][:], vcol(V_GQR, l), None, ALU.mult, None, [("COS", xs), "vecs"], ["CQ"])
            ts(SQT[:], SIN[xs][:], vcol(V_GQRP, l), None, ALU.mult, None, [("SIN", xs), "vecs"], ["SQT"])
            ts(CK[:], COS[xs][:], vcol(V_GKR, l), None, ALU.mult, None, [("COS", xs), "vecs"], ["CK"])
            ts(SK[:], SIN[xs][:], vcol(V_GKRP, l), None, ALU.mult, None, [("SIN", xs), "vecs"], ["SK"])

            def latent(c0, nch, n, gk, OUTN, okey):
                pis = []
                for j in range(nch):
                    pi = ps_next()
                    pis.append(pi)
                    mm(PS[pi][:], [(W1in[:, kc, c0 + j * 128:c0 + (j + 1) * 128], Hb[:, kc, :]) for kc in range(8)],
                       hreads, [("ps", pi)])
                    act(SQ[:, j, :], PS[pi][:], AF.Square, [("ps", pi)], [("SQ", j)])
                rms_stats([SQ[:, j, :] for j in range(nch)], [ONES] * nch, n, RS[:], "RS", [("SQ", j) for j in range(nch)])
                for j in range(nch):
                    stt(OUTN[:, j, :], PS[pis[j]][:], vcol(gk, l * nch + j), RS[:], ALU.mult, ALU.mult,
                        [("ps", pis[j]), "RS", "vecs"], [(okey, j)])

            latent(0, 3, QL, V_QAG, QN, "QN")
            latent(QL, 2, KVL, V_KVAG, KVN, "KVN")
            qn_r = [("QN", j) for j in range(3)] + W1R
            kvn_r = [("KVN", j) for j in range(2)] + W1R

            for t4 in range(4):
                pi = ps_next()
                mm(PS[pi][:], [(KVN[:, kc, t4 * 128:(t4 + 1) * 128], Wv[:, kc, :]) for kc in range(2)], kvn_r, [("ps", pi)])
                act(VOb[:, t4, :], PS[pi][:], AF.Copy, [("ps", pi)], [(vok, t4)])
            bp = b - nS
            vdst = (v_S if s == 0 else v_Pin[bp])
            tl = t0 if s == 0 else 0
            dma("pool", vdst[tl:tl + TB, :].rearrange("(a p) c -> p a c", p=128), VOb[:],
                [(vok, a) for a in range(4)], [("vdst", s)], "st_v%d" % xs)

            pk = ps_next()
            mm(PS[pk][:], [(Wkr[:, kc, :], Hb[:, kc, :]) for kc in range(8)], hreads, [("ps", pk)])
            pkr = ps_next()
            mm(PS[pkr][:], [(Wkrr[:, kc, :], Hb[:, kc, :]) for kc in range(8)], hreads, [("ps", pkr)])
            act(SQRK[:], PS[pk][:], AF.Square, [("ps", pk)], ["SQRK"])
            tt(TK[:], PS[pk][:], CK[:], ALU.mult, [("ps", pk), "CK"], ["TK"])
            tt(T2[:], PS[pkr][:], SK[:], ALU.mult, [("ps", pkr), "SK"], ["T2"])
            tt(TK[:], TK[:], T2[:], ALU.add, ["TK", "T2"], ["TK"], eng="pool")

            kdst = kT_S if s == 0 else kT_Pin[bp]
            for hh in range(2):
                pn = []
                for i2 in range(2):
                    jn = 2 * hh + i2
                    pi = ps_next()
                    pn.append(pi)
                    mm(PS[pi][:], [(Wqn[:, kc, jn * 128:(jn + 1) * 128], QN[:, kc, :]) for kc in range(3)], qn_r, [("ps", pi)])
                pr = ps_next()
                mm(PS[pr][:], [(Wqr[:, kc, hh * 128:(hh + 1) * 128], QN[:, kc, :]) for kc in range(3)], qn_r, [("ps", pr)])
                pt = ps_next()
                mm(PS[pt][:], [(Wqt[:, kc, hh * 128:(hh + 1) * 128], QN[:, kc, :]) for kc in range(3)], qn_r, [("ps", pt)])
                act(SQH[2][:], PS[pr][:], AF.Square, [("ps", pr)], [("SQH", 2)])

                def q_rope(hh_, pr=pr, pt=pt):
                    tt(T1[:], PS[pr][:], CQ[:], ALU.mult, [("ps", pr), "CQ"], ["T1"])
                    tt(T2[:], PS[pt][:], SQT[:], ALU.mult, [("ps", pt), "SQT"], ["T2"])
                    tt(T1[:], T1[:], T2[:], ALU.add, ["T1", "T2"], ["T1"], eng="pool")
                    return T1[:], ["T1"]

                head_half(hh, pn, ("SQH", 2), SQH[2][:], q_rope, QOb, qok, V_GQN, lambda hh_: None, [])
                pn = []
                for i2 in range(2):
                    jn = 2 * hh + i2
                    pi = ps_next()
                    pn.append(pi)
                    mm(PS[pi][:], [(Wkn[:, kc, jn * 128:(jn + 1) * 128], KVN[:, kc, :]) for kc in range(2)], kvn_r, [("ps", pi)])

                def k_rope(hh_):
                    return TK[:], ["TK"]

                head_half(hh, pn, "SQRK", SQRK[:], k_rope, KOb, kok, V_GKN, lambda hh_: None, [])
            if not cfg.get("no_qkstore"):
                dma("pool", qT_d[b * 768:(b + 1) * 768, :].rearrange("(j p) t -> p j t", p=128), QOb[:],
                    [(qok, j) for j in range(6)], [("qkdst", "q")], "st_q%d" % xs)
                krows = kT_S[b * 768:(b + 1) * 768, :] if s == 0 else kT_Pin[bp][:, :]
                dma("pool", krows.rearrange("(j p) t -> p j t", p=128), KOb[:],
                    [(kok, j) for j in range(6)], [("qkdst", "k%d" % s)], "st_k%d" % xs)

            if s == 1 and not cfg.get("no_cc"):
                for ci, (src, dst, rk) in enumerate(((kT_Pin[bp], kT_Pall[bp], ("qkdst", "k1")), (v_Pin[bp], v_Pall[bp], ("vdst", 1)))):
                    T.op("pool", lambda e, src=src, dst=dst: e.collective_compute(
                        "AllGather", ALU.bypass, replica_groups=[[0, 1, 2, 3], [4, 5, 6, 7]],
                        ins=[src.ap().opt()], outs=[dst.ap().opt()]),
                        reads=[rk], writes=[("gath", ci, bp)], dma="cc%d_%d_%d" % (l, ci, bp), inc=1)
        T.barrier()
        if STOP == "pass1":
            break

        LkMax = max(Ls, Lp)
        if "ph2" not in PHASES:
            sb.phase()
            ns = {}
            ns["KT"] = [sb.alloc([96, LkMax], BF16) for _ in range(2)]
            ns["VP"] = [sb.alloc([128, LkMax // 128, 65], BF16) for _ in range(2)]
            ns["QT"] = [sb.alloc([96, max(Ls, Lq)], BF16) for _ in range(2)]
            ns["PT"] = [sb.alloc([128, TB], BF16) for _ in range(4)]
            ns["OSB"] = [sb.alloc([128, TB], F32) for _ in range(2)]
            ns["OTH"] = [sb.alloc([64, TB], BF16) for _ in range(2)]
            ns["WAa"] = [sb.alloc([128, 8, 512], BF16) for _ in range(2)]
            PHASES["ph2"] = ns
        ns = PHASES["ph2"]
        KT = ns["KT"]
        VP = ns["VP"]
        QT = ns["QT"]
        PT = ns["PT"]
        OSB = ns["OSB"]
        OTH = ns["OTH"]
        WAa = ns["WAa"]
        bg = []
        if l + 1 < L:
            cast_weights(l + 1, "A")
            cast_weights(l + 1, "B")
            for i8 in range(8):
                def step(i8=i8):
                    for k in (2 * i8 - 2, 2 * i8 - 1):
                        if 0 <= k < 12:
                            mods_compute(l + 1, k, WAa, banks=[7])
                    for k in (2 * i8, 2 * i8 + 1):
                        if k < 12:
                            mods_load(l + 1, k, WAa)
                    if i8 == 7:
                        mods_finish(l + 1)
                bg.append(step)
        for sl in range(2):
            memset(VP[sl][:, :, 64:65], 1.0, [("VP", sl)])
        scale = 96 ** -0.5
        hcount = 0
        for s in range(2):
            Lk = Ls if s == 0 else Lp
            Lseg = Ls if s == 0 else Lq
            tbase = 0 if s == 0 else Ls
            nkb = Lk // 128
            nqb = Lseg // TB

            def att_loads(h, sl, s=s, Lk=Lk, Lseg=Lseg, tbase=tbase, nkb=nkb):
                rn = (h // 2) * 128 + 64 * (h % 2)
                rr = (4 + h // 4) * 128 + 32 * (h % 4)
                parts = ((0, 64, rn), (64, 32, rr))
                if s == 0:
                    kv = kT_S.ap().rearrange("(b r) t -> r b t", r=768)
                    for (p0, np_, r0) in parts:
                        dma("sp", KT[sl][p0:p0 + np_, 0:Lk].rearrange("d (b t) -> d b t", t=TB), kv[r0:r0 + np_, :, :],
                            [("qkdst", "k0")], [("KT", sl)], "ld_kt%d" % sl)
                    vsrc = v_S[:, h * 64:(h + 1) * 64].rearrange("(kb p) c -> p kb c", p=128)
                    dma("sp", VP[sl][:, 0:nkb, 0:64], vsrc, [("vdst", 0)], [("VP", sl)], "ld_vp%d" % sl)
                else:
                    for bq in range(nP):
                        kv = kT_Pall[bq].ap().rearrange("(r c) t -> c r t", r=4)
                        for (p0, np_, r0) in parts:
                            dma("sp", KT[sl][p0:p0 + np_, bq * 4 * TB:(bq + 1) * 4 * TB].rearrange("d (r t) -> d r t", r=4),
                                kv[r0:r0 + np_, :, :], [("gath", 0, bq)], [("KT", sl)], "ld_kt%d" % sl)
                        vsrc = v_Pall[bq][:, h * 64:(h + 1) * 64].rearrange("(kb p) c -> p kb c", p=128)
                        dma("sp", VP[sl][:, bq * 16:(bq + 1) * 16, 0:64], vsrc, [("gath", 1, bq)], [("VP", sl)], "ld_vp%d" % sl)
                blk0 = 0 if s == 0 else nS
                nblk = Lseg // TB
                qv = qT_d.ap().rearrange("(b r) t -> r b t", r=768)
                for (p0, np_, r0) in parts:
                    dma("sp", QT[sl][p0:p0 + np_, 0:Lseg].rearrange("d (b t) -> d b t", t=TB), qv[r0:r0 + np_, blk0:blk0 + nblk, :],
                        [("qkdst", "q")], [("QT", sl)], "ld_qt%d" % sl)

            att_loads(0, hcount % 2)
            for h in range(NH):
                sl = hcount % 2
                hcount += 1
                if h + 1 < NH:
                    att_loads(h + 1, hcount % 2)
                if bg and s == 1:
                    bg.pop(0)()
                for qb in range(nqb):
                    po = 5 + (qb % 2)
                    steps = list(range(nkb))
                    spi = {}

                    def qk(kb, sl=sl, qb=qb):
                        pi = kb % 5
                        spi[kb] = pi
                        mm(PS[pi][:], [(KT[sl][:, kb * 128:(kb + 1) * 128], QT[sl][:, qb * TB:(qb + 1) * TB])],
                           [("KT", sl), ("QT", sl)], [("ps", pi)])

                    qk(0)
                    if nkb > 1:
                        qk(1)
                    for kb in steps:
                        pi = spi[kb]
                        pt_i = kb % 4
                        act(PT[pt_i][:], PS[pi][:], AF.Exp, [("ps", pi)], [("PT", pt_i)], scale=scale)
                        if kb + 2 < nkb:
                            qk(kb + 2)
                        T.op("pe", lambda e, po=po, sl=sl, kb=kb, pt_i=pt_i, nkb=nkb:
                             e.matmul(PS[po][0:65, :], VP[sl][:, kb, :], PT[pt_i][:], start=(kb == 0), stop=(kb == nkb - 1)),
                             reads=[("VP", sl), ("PT", pt_i)], writes=[("ps", po)])
                    ob = qb % 2
                    T.op("dve", lambda e, ob=ob, po=po: e.tensor_copy(out=OSB[ob][0:65, :], in_=PS[po][0:65, :]),
                         reads=[("ps", po)], writes=[("OSB", ob)])
                    recip(OSB[ob][64:65, :], OSB[ob][64:65, :], [("OSB", ob)], [("OSB", ob)])
                    mm(PS[7][0:64, :], [(ones_f[64:65, 0:64], OSB[ob][64:65, :])], [("OSB", ob), "ones_f"], [("ps", 7)])
                    tt(OTH[ob][:], OSB[ob][0:64, :], PS[7][0:64, :], ALU.mult, [("OSB", ob), ("ps", 7)], [("OTH", ob)])
                    tq = tbase + qb * TB
                    dma("pool", oT_d[h * 64:(h + 1) * 64, tq:tq + TB], OTH[ob][:], [("OTH", ob)], [("od",)], "st_o%d" % ob)
        while bg:
            bg.pop(0)()
        T.barrier()
        if STOP == "attn":
            break

        if "ph3" not in PHASES:
            sb.phase()
            ns = {}
            ns["X"] = [sb.alloc([128, 8, TB], F32) for _ in range(2)]
            ns["H"] = [sb.alloc([128, 8, TB], BF16) for _ in range(2)]
            ns["OT"] = [sb.alloc([128, 4, TB], BF16) for _ in range(2)]
            ns["US"] = sb.alloc([128, 8, TB], BF16)
            ns["BIGF"] = sb.alloc([128, 4096], F32)
            ns["VM"] = sb.alloc([128, 4096], BF16)
            ns["SH"] = sb.alloc([128, 8, TB], BF16)
            ns["SGT"] = sb.alloc([128, 4, TB], F32)
            ns["ACTT"] = sb.alloc([128, 22, TB], BF16)
            ns["SQ"] = sb.alloc([128, 8, TB], BF16)
            ns["RS"] = sb.alloc([128, TB], F32)
            ns["WS"] = [sb.alloc([128, 4096], BF16) for _ in range(3)]
            ns["BSB"] = sb.alloc([128, 8, 128], F32)
            ns["WST"] = sb.alloc([128, 8, 128], BF16)
            ns["SSV"] = sb.alloc([128, 4], F32)
            ns["JUNK"] = sb.alloc([128, 1024], BF16)
            ns["TMP"] = [sb.alloc([128, TB], F32) for _ in range(2)]
            PHASES["ph3"] = ns
        ns = PHASES["ph3"]
        X = ns["X"]
        H = ns["H"]
        OT = ns["OT"]
        US = ns["US"]
        BIGF = ns["BIGF"]
        VM = ns["VM"]
        SH = ns["SH"]
        SGT = ns["SGT"]
        ACTT = ns["ACTT"]
        SQ = ns["SQ"]
        RS = ns["RS"]
        WS = ns["WS"]
        BSB = ns["BSB"]
        WST = ns["WST"]
        SSV = ns["SSV"]
        JUNK = ns["JUNK"]
        TMP = ns["TMP"]
        GV = BIGF[:].rearrange("p (a b) -> p a b", a=4)
        MT = BIGF[:].rearrange("p (a b) -> p a b", a=8)
        VH = VM[:].rearrange("p (a b) -> p a b", a=4)
        MG = VM[:].rearrange("p (a b) -> p a b", a=8)
        dma("sp", BSB[:], bsb_in[l], [], ["BSB"], "ld_misc")
        dma("pool", WST[:], wsT_in[l], [], ["WST"], "ld_cst")
        tmp_rr = [0]

        def tmp_next():
            i = tmp_rr[0]
            tmp_rr[0] = 1 - i
            return i

        ws_rr = [0]
        pending = []

        def p2_loads(i):
            b = i
            xs = i % 2
            t0 = tok0(b)
            dma("sp", X[xs][:], x_src[:, t0:t0 + TB].rearrange("(c p) t -> p c t", p=128), [("xd", b)],
                [(("X", xs), c) for c in range(8)], "ld_x%d" % xs)
            dma("sp", H[xs][:], hT_d[:, t0:t0 + TB].rearrange("(c p) t -> p c t", p=128), [("hd", b)],
                [(("H", xs), c) for c in range(8)], "ld_h%d" % xs)
            dma("sp", OT[xs][:], oT_d[:, t0:t0 + TB].rearrange("(c p) t -> p c t", p=128), [("od",)],
                [("OT", xs)], "ld_o%d" % xs)

        for b in range(NB):
            if b == 0:
                p2_loads(0)
            xs = b % 2
            s = seg(b)
            t0 = tok0(b)
            Xb, Hb, OTb = X[xs], H[xs], OT[xs]
            xkey, hkey = ("X", xs), ("H", xs)
            hreads = [(hkey, c) for c in range(8)]

            steps = []

            def add(name, c0, n, fn):
                steps.append((name, c0, n, WK[name] // 128, fn))

            def f_zu(W, wkey, g):
                for jj in range(4):
                    c = g * 4 + jj
                    pi = ps_next()
                    mm(PS[pi][:], [(W[:, kc, jj * 128:(jj + 1) * 128], Hb[:, kc, :]) for kc in range(8)], hreads + [wkey], [("ps", pi)])
                    act(US[:, c, :], PS[pi][:], AF.Gelu_apprx_tanh, [("ps", pi)], [("US", c)])
            for g in range(2):
                add("w_in", 672 + g * 512, 512, lambda W, wkey, g=g: f_zu(W, wkey, g))

            def f_zv(W, wkey, half):
                for t4 in range(4):
                    pi = ps_next()
                    mm(PS[pi][:], [(Hb[:, kc, t4 * 128:(t4 + 1) * 128], W[:, kc, :]) for kc in range(8)], hreads + [wkey], [("ps", pi)])
                    act(GV[:, t4, half * 512:(half + 1) * 512], PS[pi][:], AF.Gelu_apprx_tanh, [("ps", pi)], [("BIGF", 2 * t4 + half)])
                if half == 1:
                    memset(SSV[:], 0.0, [("SSV", t4) for t4 in range(4)])
                    for t4 in range(4):
                        act(JUNK[:], GV[:, t4, :], AF.Square, [("BIGF", 2 * t4), ("BIGF", 2 * t4 + 1)], ["JUNK", ("SSV", t4)],
                            accum_out=SSV[:, t4:t4 + 1])
                    act(SSV[:], SSV[:], AF.Sqrt, [("SSV", t4) for t4 in range(4)] + ["vecs"], [("SSV", t4) for t4 in range(4)],
                        bias=eps_ap, scale=1.0 / D)
                    recip(SSV[:], SSV[:], [("SSV", t4) for t4 in range(4)], [("SSV", t4) for t4 in range(4)])
                    for t4 in range(4):
                        ts(VH[:, t4, :], GV[:, t4, :], SSV[:, t4:t4 + 1], None, ALU.mult, None,
                           [("BIGF", 2 * t4), ("BIGF", 2 * t4 + 1), ("SSV", t4)], [("VM", 2 * t4), ("VM", 2 * t4 + 1)])
                    for g in range(8):
                        pi = ps_next()

                        def fmix(e, pi=pi, g=g):
                            ins = None
                            for t4 in range(4):
                                ins = e.matmul(PS[pi][:, t4 * 128:(t4 + 1) * 128], VH[:, t4, g * 128:(g + 1) * 128], WST[:, g, :],
                                               start=True, stop=True)
                            return ins
                        T.op("pe", fmix, reads=[("VM", k) for k in range(8)] + ["WST"], writes=[("ps", pi)])
                        ti = tmp_next()
                        for t4 in range(4):
                            stt(TMP[ti][:, t4 * 128:(t4 + 1) * 128], PS[pi][:, t4 * 128:(t4 + 1) * 128], vcol(V_SGUG, l * 8 + g),
                                BSB[:, g, :], ALU.mult, ALU.add, [("ps", pi), "BSB", "vecs"], [("TMP", ti)])
                        tt(SH[:, g, :], TMP[ti][:], US[:, g, :], ALU.mult, [("TMP", ti), ("US", g)], [("SH", g)], eng="pool")
            for half in range(2):
                add("w_in", 1696 + half * 512, 512, lambda W, wkey, half=half: f_zv(W, wkey, half))

            def f_gate(W, wkey, g):
                for jj in range(4):
                    c = g * 4 + jj
                    pi = ps_next()
                    mm(PS[pi][:], [(W[:, kc, jj * 128:(jj + 1) * 128], Hb[:, kc, :]) for kc in range(8)], hreads + [wkey], [("ps", pi)])
                    act(US[:, c, :], PS[pi][:], AF.Sigmoid, [("ps", pi)], [("US", c)])

            def f_ob(W, wkey, g):
                for jj in range(4):
                    c = g * 4 + jj
                    pi = ps_next()
                    mm(PS[pi][:], [(W[:, kc, jj * 128:(jj + 1) * 128], SH[:, kc, :]) for kc in range(8)],
                       [("SH", k) for k in range(8)] + [wkey], [("ps", pi)])
                    tt(MT[:, c, :], PS[pi][:], US[:, c, :], ALU.mult, [("ps", pi), ("US", c)], [("BIGF", c)])

            def f_oa(W, wkey):
                for c in range(8):
                    pi = ps_next()
                    mm(PS[pi][:], [(W[:, kc, c * 128:(c + 1) * 128], OTb[:, kc, :]) for kc in range(4)], [("OT", xs), wkey], [("ps", pi)])
                    ti = tmp_next()
                    tt(TMP[ti][:], PS[pi][:], US[:, c, :], ALU.mult, [("ps", pi), ("US", c)], [("TMP", ti)])
                    tt(MG[:, c, :], TMP[ti][:], MT[:, c, :], ALU.add, [("TMP", ti), ("BIGF", c)], [("VM", c)], eng="pool")

            def f_out(W, wkey, g):
                for jj in range(4):
                    c = g * 4 + jj
                    pi = ps_next()
                    mm(PS[pi][:], [(W[:, kc, jj * 128:(jj + 1) * 128], MG[:, kc, :]) for kc in range(8)],
                       [("VM", k) for k in range(8)] + [wkey], [("ps", pi)])
                    stt(Xb[:, c, :], PS[pi][:], mod_ap(l, 2, c, s), Xb[:, c, :], ALU.mult, ALU.add,
                        [("ps", pi), (xkey, c), ("mods", l, 16 + c)], [(xkey, c)])
                if g == 1:
                    norm_block(Xb, xkey, SH, "SH", SQ, RS, A2, l, s, 3, TMP)

            for g in range(2):
                add("w_in", 3744 + g * 512, 512, lambda W, wkey, g=g: f_gate(W, wkey, g))
            for g in range(2):
                add("w_o_b", g * 512, 512, lambda W, wkey, g=g: f_ob(W, wkey, g))
            for g in range(2):
                add("w_in", 2720 + g * 512, 512, lambda W, wkey, g=g: f_gate(W, wkey, g))
            add("w_o_a", 0, 1024, lambda W, wkey: f_oa(W, wkey))
            for g in range(2):
                add("w_out", g * 512, 512, lambda W, wkey, g=g: f_out(W, wkey, g))

            h2reads = [("SH", k) for k in range(8)]

            def f_gatef(W, wkey, j0, nch):
                for jj in range(nch):
                    pi = ps_next()
                    mm(PS[pi][:], [(W[:, kc, jj * 128:(jj + 1) * 128], SH[:, kc, :]) for kc in range(8)], h2reads + [wkey], [("ps", pi)])
                    act(SGT[:, jj, :], PS[pi][:], AF.Silu, [("ps", pi)], [("SGT", jj)])

            def f_upf(W, wkey, j0, nch):
                for jj in range(nch):
                    pi = ps_next()
                    mm(PS[pi][:], [(W[:, kc, jj * 128:(jj + 1) * 128], SH[:, kc, :]) for kc in range(8)], h2reads + [wkey], [("ps", pi)])
                    tt(ACTT[:, j0 + jj, :], PS[pi][:], SGT[:, jj, :], ALU.mult, [("ps", pi), ("SGT", jj)], [("ACTT", j0 + jj)])

            for j0 in range(0, 22, 4):
                nch = min(4, 22 - j0)
                add("w_ffn_in", DFF + j0 * 128, nch * 128, lambda W, wkey, j0=j0, nch=nch: f_gatef(W, wkey, j0, nch))
                add("w_ffn_in", j0 * 128, nch * 128, lambda W, wkey, j0=j0, nch=nch: f_upf(W, wkey, j0, nch))

            def f_fo(W, wkey, c):
                pi = ps_next()
                mm(PS[pi][:], [(W[:, kc, :], ACTT[:, kc, :]) for kc in range(22)], [("ACTT", k) for k in range(22)] + [wkey], [("ps", pi)])
                stt(Xb[:, c, :], PS[pi][:], mod_ap(l, 5, c, s), Xb[:, c, :], ALU.mult, ALU.add,
                    [("ps", pi), (xkey, c), ("mods", l, 40 + c)], [(xkey, c)])
            for c in range(8):
                add("w_ffn_out", c * 128, 128, lambda W, wkey, c=c: f_fo(W, wkey, c))

            nst = len(steps)
            views = {}

            def issue_load(k):
                name, c0, n, kcn, fn = steps[k]
                sl = ws_rr[0]
                ws_rr[0] = (sl + 1) % 3
                view = WS[sl][:, 0:kcn * n].rearrange("p (k n) -> p k n", k=kcn)
                dma("sp", view, wsrc(name, l, c0, n), wkeys(name, l), [("WS", sl)], "ld_ws%d" % sl)
                views[k] = (view, ("WS", sl))

            issue_load(0)
            issue_load(1)
            for k in range(nst):
                if k + 2 < nst:
                    issue_load(k + 2)
                if k == 4 and b + 1 < NB:
                    p2_loads(b + 1)
                view, wkey = views[k]
                steps[k][4](view, wkey)
            dma("pool", x_dst[:, t0:t0 + TB].rearrange("(c p) t -> p c t", p=128), Xb[:],
                [(xkey, c) for c in range(8)], [("xd", b)], "st_x%d" % xs)
        T.barrier()

    with nc.Block() as block:
        @block.tensor
        def _(e):
            T.replay("pe", e)

        @block.scalar
        def _(e):
            T.replay("act", e)

        @block.vector
        def _(e):
            T.replay("dve", e)

        @block.gpsimd
        def _(e):
            T.replay("pool", e)

        @block.sync
        def _(e):
            T.replay("sp", e)
    es.close()
    return nc


def rope_tables(pos):
    inv = (10000.0 ** (-np.arange(0, 32, 2, dtype=np.float32) / np.float32(32))).astype(np.float32)
    ang = pos.astype(np.float32)[:, None] * inv[None, :]
    return np.cos(ang).astype(np.float32), np.sin(ang).astype(np.float32)


def make_consts():
    k = np.arange(128)[:, None]
    m = np.arange(128)[None, :]
    ones = np.ones((128, 128), np.float32)
    nn = (k // 64 == m // 64)
    rn = [(k // 32 == 2 * i + m // 64) for i in range(2)]
    nr = [(2 * i + k // 64 == m // 32) for i in range(2)]
    rr = (k // 32 == m // 32)
    mats = [ones, nn, rn[0], rn[1], nr[0], nr[1], rr]
    return np.ascontiguousarray(np.stack([np.asarray(x, np.float32) for x in mats], axis=1))


def prepare_inputs(cfg, inp):
    Ls, Lq, L = cfg["Ls"], cfg["Lq"], cfg["L"]
    voff, NV = vec_layout(L)
    f = lambda a: np.ascontiguousarray(np.asarray(a, dtype=np.float32))
    w_in = f(inp["w_in"])[:L]
    w_q_b = f(inp["w_q_b"])[:L]
    w_kv_b = f(inp["w_kv_b"])[:L]
    shared = {}
    shared["w_in"] = w_in.reshape(L * D, INC)
    kr = w_in[:, :, 640:672]
    krrot = np.concatenate([kr[:, :, 16:32], kr[:, :, 0:16]], axis=2)
    shared["w_kr4"] = f(np.tile(kr, (1, 1, 4))).reshape(L * D, 128)
    shared["w_krrot4"] = f(np.tile(krrot, (1, 1, 4))).reshape(L * D, 128)
    wq = w_q_b.reshape(L, QL, NH, 96)
    shared["wq_nope"] = f(wq[:, :, :, 0:64]).reshape(L * QL, 512)
    shared["wq_rope"] = f(wq[:, :, :, 64:96]).reshape(L * QL, 256)
    shared["wq_rot"] = f(np.concatenate([wq[:, :, :, 80:96], wq[:, :, :, 64:80]], axis=3)).reshape(L * QL, 256)
    wkv = w_kv_b.reshape(L, KVL, NH, 128)
    shared["wkv_nope"] = f(wkv[:, :, :, 0:64]).reshape(L * KVL, 512)
    shared["wkv_v"] = f(wkv[:, :, :, 64:128]).reshape(L * KVL, 512)
    shared["w_o_a"] = f(inp["w_o_a"])[:L].reshape(L * 512, D)
    shared["w_o_b"] = f(inp["w_o_b"])[:L].reshape(L * D, D)
    shared["w_out"] = f(inp["w_out"])[:L].reshape(L * D, D)
    shared["w_ffn_in"] = f(inp["w_ffn_in"])[:L].reshape(L * D, 2 * DFF)
    shared["w_ffn_out"] = f(inp["w_ffn_out"])[:L].reshape(L * DFF, D)
    shared["w_ada"] = f(inp["w_ada"])[:L].reshape(L * D, 6 * D)
    shared["w_sT"] = f(np.transpose(f(inp["w_s"])[:L], (0, 3, 1, 2)))
    shared["bsb"] = f(np.broadcast_to(f(inp["b_s"])[:L][:, None, :, :], (L, 128, 8, 128)))
    shared["consts"] = make_consts()
    vecs = np.zeros((128, NV), np.float32)
    vecs[:, voff[V_EPS]] = EPS
    p = np.arange(128)

    def fm(a, n):
        return f(a)[:L].reshape(L, n, 128).transpose(2, 0, 1).reshape(128, L * n)
    vecs[:, voff[V_N1G]:voff[V_N1G] + 8 * L] = fm(inp["norm1_g"], 8)
    vecs[:, voff[V_N2G]:voff[V_N2G] + 8 * L] = fm(inp["norm2_g"], 8)
    vecs[:, voff[V_SGUG]:voff[V_SGUG] + 8 * L] = fm(inp["sgu_norm_g"], 8)
    vecs[:, voff[V_QAG]:voff[V_QAG] + 3 * L] = fm(inp["q_a_norm_g"], 3)
    vecs[:, voff[V_KVAG]:voff[V_KVAG] + 2 * L] = fm(inp["kv_a_norm_g"], 2)
    vecs[:, voff[V_BADA]:voff[V_BADA] + 48 * L] = fm(inp["b_ada"], 48)
    qg = f(inp["q_norm_g"])[:L]
    kg = f(inp["k_norm_g"])[:L]
    for (g, vn, vr, vrp) in ((qg, V_GQN, V_GQR, V_GQRP), (kg, V_GKN, V_GKR, V_GKRP)):
        vecs[:, voff[vn]:voff[vn] + L] = g[:, p % 64].T
        vecs[:, voff[vr]:voff[vr] + L] = g[:, 64 + p % 32].T
        vecs[:, voff[vrp]:voff[vrp] + L] = g[:, 64 + (p % 32 + 16) % 32].T
    shared["vecs"] = vecs
    xs = f(inp["x_sample"])
    xp = f(inp["x_prompt"])
    cs = f(inp["c_sample"])
    cp = f(inp["c_prompt"])
    fidx = (p % 32) % 16
    sgn = np.where((p % 32) < 16, -1.0, 1.0).astype(np.float32)
    in_maps = []
    for c in range(NCORES):
        ps_, r = c // 4, c % 4
        m = dict(shared)
        m["xT"] = np.ascontiguousarray(np.concatenate([xs[c, :Ls].T, xp[ps_, r * Lq:(r + 1) * Lq].T], axis=1))
        m["cT"] = np.ascontiguousarray(np.stack([cs[c].reshape(8, 128).T, cp[ps_].reshape(8, 128).T], axis=2))
        pos = np.concatenate([np.arange(Ls), r * Lq + np.arange(Lq)])
        cos, sin = rope_tables(pos)
        m["cos4"] = np.ascontiguousarray(cos.T[fidx, :])
        m["sin4"] = np.ascontiguousarray(sin.T[fidx, :] * sgn[:, None])
        in_maps.append(m)
    return in_maps


_PROG_CACHE = {}


def run(cfg, inp):
    key = (cfg["Ls"], cfg["Lq"], cfg["L"], cfg.get("stop"), cfg.get("no_cc"), cfg.get("no_pool"), cfg.get("no_qkstore"))
    if key not in _PROG_CACHE:
        _PROG_CACHE[key] = build_program(cfg)
    nc = _PROG_CACHE[key]
    in_maps = prepare_inputs(cfg, inp)
    res = run_bass_kernel_spmd(nc, in_maps, core_ids=list(range(NCORES)))
    Ls, Lq = cfg["Ls"], cfg["Lq"]
    ys = np.stack([res.results[c]["yT"][:, :Ls].T for c in range(NCORES)], axis=0)
    yp = np.stack([np.concatenate([res.results[4 * s_ + r]["yT"][:, Ls:].T for r in range(4)], axis=0) for s_ in range(2)], axis=0)
    return np.ascontiguousarray(yp, dtype=np.float32), np.ascontiguousarray(ys, dtype=np.float32)


def kernel(**inputs):
    return run(FULL_CFG, inputs)
```

```python
import numpy as np
from contextlib import ExitStack
import concourse.bass as bass
import concourse.mybir as mybir
from concourse.bass_utils import run_bass_kernel_spmd

F32 = mybir.dt.float32
BF16 = mybir.dt.bfloat16
AF = mybir.ActivationFunctionType
ALU = mybir.AluOpType

D = 1024
NH = 8
QL = 384
KVL = 256
DFF = 2816
INC = 4768
EPS = 1e-6
TB = 512
NCORES = 8
SAME_SYNC = True

FULL_CFG = dict(Ls=2048, Lq=2048, L=4)


class Tracker:
    COMPUTE = ("pe", "act", "dve", "pool")

    def __init__(self, nc, es):
        self.nc = nc
        self.es = es
        self.ops = {e: [] for e in ("pe", "act", "dve", "pool", "sp")}
        self.cnt = {e: 0 for e in self.COMPUTE}
        self.esem = {e: es.enter_context(nc.semaphore("c_" + e)) for e in self.COMPUTE}
        self.dsem = {}
        self.seen = {e: {} for e in self.ops}
        self.lastw = {}
        self.readers = {}
        self.nops = 0
        self.no_pool = False

    def _dsem(self, key):
        if key not in self.dsem:
            self.dsem[key] = [self.es.enter_context(self.nc.semaphore("d%d" % len(self.dsem))), 0]
        return self.dsem[key]

    def _resolve(self, tok):
        if tok[0] == "c":
            return self.esem[tok[1]], tok[2], ("c", tok[1])
        d = self._dsem(tok[1])
        return d[0], d[1], ("d", tok[1])

    def op(self, eng, fn, reads=(), writes=(), dma=None, inc=None):
        if eng == "pool" and dma is None and self.no_pool:
            eng = "dve"
        deps = set()
        for r in reads:
            if r in self.lastw:
                deps.add(self.lastw[r])
            if isinstance(r, tuple) and r[0] == "ps":
                for k, v in self.readers.get(r, {}).items():
                    if k != ("c", eng):
                        deps.add(k + (v,))
        for w in writes:
            if w in self.lastw:
                deps.add(self.lastw[w])
            for k, v in self.readers.get(w, {}).items():
                deps.add(k + (v,))
        waits = []
        for tok in deps:
            if tok[0] == "c" and tok[1] == eng and dma is None and (eng == "pe" or not SAME_SYNC):
                continue
            sem, val, sid = self._resolve(tok)
            if self.seen[eng].get(sid, 0) >= val:
                continue
            self.seen[eng][sid] = val
            waits.append((sem, val))
        if dma is not None:
            d = self._dsem(dma)
            step = 16 if inc is None else inc
            d[1] += step
            tok = ("d", dma, d[1])
            incr = (d[0], step)
        else:
            self.cnt[eng] += 1
            tok = ("c", eng, self.cnt[eng])
            incr = (self.esem[eng], 1)
        self.ops[eng].append((waits, fn, incr))
        self.nops += 1
        for r in reads:
            self.readers.setdefault(r, {})[tok[:2]] = tok[2]
        for w in writes:
            self.lastw[w] = tok
            self.readers[w] = {}
        return tok

    @staticmethod
    def _bg(key):
        return isinstance(key, str) and key.startswith("cast")

    def barrier(self):
        for eng in self.ops:
            waits = []
            for f in self.COMPUTE:
                if f == eng or self.cnt[f] == 0:
                    continue
                if self.seen[eng].get(("c", f), 0) < self.cnt[f]:
                    self.seen[eng][("c", f)] = self.cnt[f]
                    waits.append((self.esem[f], self.cnt[f]))
            for key, d in self.dsem.items():
                if self._bg(key):
                    continue
                if d[1] and self.seen[eng].get(("d", key), 0) < d[1]:
                    self.seen[eng][("d", key)] = d[1]
                    waits.append((d[0], d[1]))
            if waits:
                self.ops[eng].append((waits, None, None))
        self.lastw = {r: t for r, t in self.lastw.items() if t[0] == "d" and self._bg(t[1])}
        self.readers = {}

    def replay(self, name, e):
        for waits, fn, incr in self.ops[name]:
            for sem, val in waits:
                e.wait_ge(sem, val)
            if fn is not None:
                ins = fn(e)
                ins.then_inc(incr[0], incr[1])


class SbAlloc:
    BASE = 16512
    TOP = 229344

    def __init__(self, nc):
        self.nc = nc
        self.persist = self.BASE
        self.cur = self.BASE
        self.n = 0

    def _al(self, shape, dt, off):
        self.n += 1
        return self.nc.alloc_sbuf_tensor_at("sb%d" % self.n, list(shape), dt, offset=off)

    def alloc(self, shape, dt, persistent=False):
        esz = 4 if dt == F32 else 2
        nbytes = int(np.prod(shape[1:])) * esz
        nbytes = (nbytes + 31) // 32 * 32
        if persistent:
            assert self.cur == self.persist, "persistent allocs must come first"
            off = self.persist
            self.persist += nbytes
            self.cur = self.persist
        else:
            off = self.cur
            self.cur += nbytes
        assert self.cur <= self.TOP, "SBUF overflow %d" % self.cur
        return self._al(shape, dt, off)

    def phase(self):
        self.cur = self.persist


W_SPECS = [
    ("w_in", 1024, INC, "A"), ("w_kr4", 1024, 128, "A"), ("w_krrot4", 1024, 128, "A"),
    ("wq_nope", QL, 512, "A"), ("wq_rope", QL, 256, "A"), ("wq_rot", QL, 256, "A"),
    ("wkv_nope", KVL, 512, "A"), ("wkv_v", KVL, 512, "A"),
    ("w_o_a", 512, D, "B"), ("w_o_b", D, D, "B"), ("w_out", D, D, "B"),
    ("w_ffn_in", D, 2 * DFF, "B"), ("w_ffn_out", DFF, D, "B"),
]
V_EPS, V_N1G, V_N2G, V_SGUG, V_QAG, V_KVAG, V_GQN, V_GQR, V_GQRP, V_GKN, V_GKR, V_GKRP, V_BADA = range(13)


def vec_layout(L):
    off = {}
    o = 0
    for name, n in [(V_EPS, 1), (V_N1G, 8 * L), (V_N2G, 8 * L), (V_SGUG, 8 * L), (V_QAG, 3 * L), (V_KVAG, 2 * L),
                    (V_GQN, L), (V_GQR, L), (V_GQRP, L), (V_GKN, L), (V_GKR, L), (V_GKRP, L), (V_BADA, 48 * L)]:
        off[name] = o
        o += n
    return off, o


def build_program(cfg):
    Ls, Lq, L = cfg["Ls"], cfg["Lq"], cfg["L"]
    Lp = 4 * Lq
    NTOK = Ls + Lq
    nS, nP = Ls // TB, Lq // TB
    NB = nS + nP
    voff, NV = vec_layout(L)

    nc = bass.Bass("TRN2", target_bir_lowering=False)

    def din(name, shape, dt=F32):
        return nc.dram_tensor(name, list(shape), dt, kind="ExternalInput").ap()

    xT_in = din("xT", [D, NTOK])
    cT_in = din("cT", [128, 8, 2])
    cos_in = din("cos4", [128, NTOK])
    sin_in = din("sin4", [128, NTOK])
    vecs_in = din("vecs", [128, NV])
    consts_in = din("consts", [128, 7, 128])
    w_ada_in = din("w_ada", [L * D, 6 * D])
    wsT_in = din("w_sT", [L, 128, 8, 128])
    bsb_in = din("bsb", [L, 128, 8, 128])
    w_ext = {name: din(name, [L * K, N]) for name, K, N, _ in W_SPECS}
    yT_out = nc.dram_tensor("yT", [D, NTOK], F32, kind="ExternalOutput").ap()

    def dscr(name, shape, dt):
        return nc.dram_tensor(name, list(shape), dt)

    w_bf = {name: dscr(name + "_b", [L * K, N], BF16) for name, K, N, _ in W_SPECS}
    WK = {name: K for name, K, N, _ in W_SPECS}
    xT_d = dscr("xT_d", [D, NTOK], F32)
    hT_d = dscr("hT_d", [D, NTOK], BF16)
    qT_d = dscr("qT_d", [NB * 768, TB], BF16)
    oT_d = dscr("oT_d", [512, NTOK], BF16)
    kT_S = dscr("kT_S", [nS * 768, TB], BF16)
    v_S = dscr("v_S", [Ls, 512], BF16)
    kT_Pin = [dscr("kT_Pin%d" % b, [NH * 96, TB], BF16) for b in range(nP)]
    kT_Pall = [dscr("kT_Pall%d" % b, [4 * NH * 96, TB], BF16) for b in range(nP)]
    v_Pin = [dscr("v_Pin%d" % b, [TB, 512], BF16) for b in range(nP)]
    v_Pall = [dscr("v_Pall%d" % b, [4 * TB, 512], BF16) for b in range(nP)]

    es = ExitStack()
    T = Tracker(nc, es)
    T.no_pool = bool(cfg.get("no_pool", True))
    sb = SbAlloc(nc)
    PS = [es.enter_context(nc.psum_tensor("ps%d" % i, [128, 512], F32)) for i in range(8)]
    ps_rr = [0]

    def ps_next():
        i = ps_rr[0]
        ps_rr[0] = (i + 1) % 8
        return i

    def dma(eng, out, in_, reads, writes, key):
        T.op(eng, lambda e, o=out, i=in_: e.dma_start(out=o, in_=i), reads=reads, writes=writes, dma=key)

    def mm(ps_ap, pairs, reads, writes):
        def fn(e, ps_ap=ps_ap, pairs=pairs):
            n = len(pairs)
            ins = None
            for i, (l, r) in enumerate(pairs):
                ins = e.matmul(ps_ap, l, r, start=(i == 0), stop=(i == n - 1))
            return ins
        T.op("pe", fn, reads=reads, writes=writes)

    def act(out, in_, func, reads, writes, **kw):
        T.op("act", lambda e, o=out, i=in_, f=func, kw=kw: e.activation(out=o, in_=i, func=f, **kw), reads=reads, writes=writes)

    def tt(out, in0, in1, op, reads, writes, eng="dve"):
        T.op(eng, lambda e, o=out, a=in0, b=in1, op=op: e.tensor_tensor(out=o, in0=a, in1=b, op=op), reads=reads, writes=writes)

    def stt(out, in0, scalar, in1, op0, op1, reads, writes, eng="dve"):
        T.op(eng, lambda e, o=out, a=in0, s=scalar, b=in1, p0=op0, p1=op1:
             e.scalar_tensor_tensor(out=o, in0=a, scalar=s, in1=b, op0=p0, op1=p1), reads=reads, writes=writes)

    def ts(out, in0, s1, s2, op0, op1, reads, writes, eng="dve"):
        if s2 is None:
            T.op(eng, lambda e, o=out, a=in0, s1=s1, p0=op0: e.tensor_scalar(out=o, in0=a, scalar1=s1, scalar2=None, op0=p0),
                 reads=reads, writes=writes)
        else:
            T.op(eng, lambda e, o=out, a=in0, s1=s1, s2=s2, p0=op0, p1=op1:
                 e.tensor_scalar(out=o, in0=a, scalar1=s1, scalar2=s2, op0=p0, op1=p1), reads=reads, writes=writes)

    def recip(out, in_, reads, writes):
        T.op("dve", lambda e, o=out, i=in_: e.reciprocal(out=o, in_=i), reads=reads, writes=writes)

    def memset(ap, val, writes, eng="dve"):
        T.op(eng, lambda e, a=ap, v=val: e.memset(a, v), writes=writes)

    vecs = sb.alloc([128, NV], F32, True)
    cst = sb.alloc([128, 7, 128], BF16, True)
    ones_f = sb.alloc([128, 64], F32, True)
    mods = sb.alloc([128, L, 48, 2], F32, True)
    A1 = sb.alloc([128, L, 2, 8], F32, True)
    A2 = sb.alloc([128, L, 2, 8], F32, True)
    silc = sb.alloc([128, 8, 2], BF16, True)
    eps_ap = vecs[:, voff[V_EPS]:voff[V_EPS] + 1]

    def vcol(kind, idx):
        o = voff[kind] + idx
        return vecs[:, o:o + 1]

    ONES, NN, RN0, RN1, NR0, NR1, RR = [cst[:, i, :] for i in range(7)]
    RN = [RN0, RN1]
    NR = [NR0, NR1]

    dma("sp", vecs[:], vecs_in, [], ["vecs"], "ld_misc")
    dma("pool", cst[:], consts_in, [], ["cst"], "ld_cst")
    memset(ones_f[:], 1.0, ["ones_f"])

    def cast_weights(l, grp):
        for name, K, N, g in W_SPECS:
            if g != grp:
                continue
            for rb in range(0, K, 128):
                r0 = l * K + rb
                dma("pool", w_bf[name][r0:r0 + 128, :], w_ext[name][r0:r0 + 128, :], [], [("wb", name, l, rb // 128)],
                    "cast%s%d" % (grp, l))

    def wkeys(name, l):
        return [("wb", name, l, i) for i in range(WK[name] // 128)]

    def wsrc(name, l, c0, n):
        K = WK[name]
        return w_bf[name][l * K:(l + 1) * K, c0:c0 + n].rearrange("(kc p) n -> p kc n", p=128)

    cast_weights(0, "A")

    cTt = sb.alloc([128, 8, 2], F32)
    WA0 = [sb.alloc([128, 8, 512], BF16) for _ in range(2)]
    dma("sp", cTt[:], cT_in, [], ["cTt"], "ld_misc")
    act(silc[:], cTt[:], AF.Silu, ["cTt"], ["silc"])

    def mods_load(l, jg, WA):
        s_ = jg % 2
        src = w_ada_in[l * D:(l + 1) * D, jg * 512:(jg + 1) * 512].rearrange("(kc p) n -> p kc n", p=128)
        dma("pool", WA[s_][:], src, [], [("WA", s_)], "ld_wa%d" % s_)

    def mods_compute(l, jg, WA, banks=None):
        s_ = jg % 2
        for jj in range(4):
            j = jg * 4 + jj
            pi = ps_next() if banks is None else banks[jj % len(banks)]
            mm(PS[pi][:, 0:2], [(WA[s_][:, kc, jj * 128:(jj + 1) * 128], silc[:, kc, :]) for kc in range(8)],
               [("WA", s_), "silc"], [("ps", pi)])
            ts(mods[:, l, j, :], PS[pi][:, 0:2], vcol(V_BADA, l * 48 + j), None, ALU.add, None,
               [("ps", pi), "vecs"], [("mods", l, j)])

    def mods_finish(l):
        for s_ in range(2):
            for (Ax, gk, m0) in ((A1, V_N1G, 8), (A2, V_N2G, 32)):
                ts(Ax[:, l, s_, :], mods[:, l, m0:m0 + 8, s_], 1.0, None, ALU.add, None,
                   [("mods", l, j) for j in range(m0, m0 + 8)], [("A", id(Ax), l, s_)])
                tt(Ax[:, l, s_, :], Ax[:, l, s_, :], vecs[:, voff[gk] + 8 * l:voff[gk] + 8 * l + 8], ALU.mult,
                   [("A", id(Ax), l, s_), "vecs"], [("A", id(Ax), l, s_)])

    mods_load(0, 0, WA0)
    for jg in range(12):
        if jg + 1 < 12:
            mods_load(0, jg + 1, WA0)
        mods_compute(0, jg, WA0)
    mods_finish(0)
    cast_weights(0, "B")
    T.barrier()
    STOP = cfg.get("stop")

    def mod_ap(l, m, c, s):
        return mods[:, l, m * 8 + c, s:s + 1]

    def tok0(b):
        return b * TB

    def seg(b):
        return 0 if b < nS else 1

    def rstd_from_psum(rs_ap, pi, n, rs_key):
        act(rs_ap, PS[pi][:], AF.Ln, [("ps", pi), "vecs"], [rs_key], bias=eps_ap, scale=1.0 / n)
        act(rs_ap, rs_ap, AF.Exp, [rs_key], [rs_key], scale=-0.5)

    def rms_stats(sq_list, lhs_list, n, rs_ap, rs_key, reads):
        pi = ps_next()
        mm(PS[pi][:], list(zip(lhs_list, sq_list)), reads + ["cst"], [("ps", pi)])
        rstd_from_psum(rs_ap, pi, n, rs_key)

    def norm_block(Xb, xkey, Hb, hkey, SQ, RS, Ax, l, s, shm, NT):
        for c in range(8):
            if c % 2 == 0:
                tt(SQ[:, c, :], Xb[:, c, :], Xb[:, c, :], ALU.mult, [(xkey, c)], [("SQ", c)])
            else:
                act(SQ[:, c, :], Xb[:, c, :], AF.Square, [(xkey, c)], [("SQ", c)])
        rms_stats([SQ[:, c, :] for c in range(8)], [ONES] * 8, D, RS[:], "RS", [("SQ", c) for c in range(8)])
        for c in range(8):
            i = c % 2
            tt(NT[i][:], Xb[:, c, :], RS[:], ALU.mult, [(xkey, c), "RS"], [("NT", i)])
            act(Hb[:, c, :], NT[i][:], AF.Identity, [("NT", i), ("A", id(Ax), l, s), ("mods", l, shm * 8 + c)], [(hkey, c)],
                scale=Ax[:, l, s, c:c + 1], bias=mod_ap(l, shm, c, s))

    PHASES = {}
    for l in range(L if STOP != "prologue" else 0):
        x_src = xT_in if l == 0 else xT_d
        x_dst = yT_out if l == L - 1 else xT_d
        last = (l == L - 1)

        if "ph1" not in PHASES:
            sb.phase()
            ns = {}
            ns["X"] = [sb.alloc([128, 8, TB], F32) for _ in range(2)]
            ns["H"] = [sb.alloc([128, 8, TB], BF16) for _ in range(2)]
            ns["SQ"] = sb.alloc([128, 8, TB], BF16)
            ns["RS"] = sb.alloc([128, TB], F32)
            ns["W1in"] = sb.alloc([128, 8, 640], BF16)
            ns["Wkr"] = sb.alloc([128, 8, 128], BF16)
            ns["Wkrr"] = sb.alloc([128, 8, 128], BF16)
            ns["Wqn"] = sb.alloc([128, 3, 512], BF16)
            ns["Wqr"] = sb.alloc([128, 3, 256], BF16)
            ns["Wqt"] = sb.alloc([128, 3, 256], BF16)
            ns["Wkn"] = sb.alloc([128, 2, 512], BF16)
            ns["Wv"] = sb.alloc([128, 2, 512], BF16)
            ns["QN"] = sb.alloc([128, 3, TB], BF16)
            ns["KVN"] = sb.alloc([128, 2, TB], BF16)
            ns["QO"] = [sb.alloc([128, 6, TB], BF16) for _ in range(2)]
            ns["KO"] = [sb.alloc([128, 6, TB], BF16) for _ in range(2)]
            ns["VO"] = [sb.alloc([128, 4, 512], BF16) for _ in range(2)]
            ns["TK"] = sb.alloc([128, TB], F32)
            ns["SQRK"] = sb.alloc([128, TB], BF16)
            ns["COS"] = [sb.alloc([128, TB], F32) for _ in range(2)]
            ns["SIN"] = [sb.alloc([128, TB], F32) for _ in range(2)]
            ns["CQ"] = sb.alloc([128, TB], F32)
            ns["SQT"] = sb.alloc([128, TB], F32)
            ns["CK"] = sb.alloc([128, TB], F32)
            ns["SK"] = sb.alloc([128, TB], F32)
            ns["SQH"] = [sb.alloc([128, TB], BF16) for _ in range(3)]
            ns["RSH"] = [sb.alloc([128, TB], F32) for _ in range(3)]
            ns["T1"] = sb.alloc([128, TB], F32)
            ns["T2"] = sb.alloc([128, TB], F32)
            ns["NTT"] = [sb.alloc([128, TB], F32) for _ in range(2)]
            PHASES["ph1"] = ns
        ns = PHASES["ph1"]
        X = ns["X"]
        H = ns["H"]
        SQ = ns["SQ"]
        RS = ns["RS"]
        W1in = ns["W1in"]
        Wkr = ns["Wkr"]
        Wkrr = ns["Wkrr"]
        Wqn = ns["Wqn"]
        Wqr = ns["Wqr"]
        Wqt = ns["Wqt"]
        Wkn = ns["Wkn"]
        Wv = ns["Wv"]
        QN = ns["QN"]
        KVN = ns["KVN"]
        QO = ns["QO"]
        KO = ns["KO"]
        VO = ns["VO"]
        TK = ns["TK"]
        SQRK = ns["SQRK"]
        COS = ns["COS"]
        SIN = ns["SIN"]
        CQ = ns["CQ"]
        SQT = ns["SQT"]
        CK = ns["CK"]
        SK = ns["SK"]
        SQH = ns["SQH"]
        RSH = ns["RSH"]
        T1 = ns["T1"]
        T2 = ns["T2"]
        NTT = ns["NTT"]

        for (buf, name, c0, n) in ((W1in, "w_in", 0, 640), (Wkr, "w_kr4", 0, 128), (Wkrr, "w_krrot4", 0, 128),
                                   (Wqn, "wq_nope", 0, 512), (Wqr, "wq_rope", 0, 256), (Wqt, "wq_rot", 0, 256),
                                   (Wkn, "wkv_nope", 0, 512), (Wv, "wkv_v", 0, 512)):
            dma("sp", buf[:], wsrc(name, l, c0, n), wkeys(name, l), [("W1", name)], "ld_w1")
        W1R = [("W1", n) for n in ("w_in", "w_kr4", "w_krrot4", "wq_nope", "wq_rope", "wq_rot", "wkv_nope", "wkv_v")]

        order = list(range(nS, NB)) + list(range(nS))

        def p1_loads(i):
            b = order[i]
            xs = i % 2
            t0 = tok0(b)
            dma("sp", X[xs][:], x_src[:, t0:t0 + TB].rearrange("(c p) t -> p c t", p=128), [("xd", b)],
                [(("X", xs), c) for c in range(8)], "ld_x%d" % xs)
            dma("sp", COS[xs][:], cos_in[:, t0:t0 + TB], [], [("COS", xs)], "ld_cs%d" % xs)
            dma("sp", SIN[xs][:], sin_in[:, t0:t0 + TB], [], [("SIN", xs)], "ld_cs%d" % xs)

        def head_half(hh, PSn, PSr_sq_key, sqr_ap, rope_src_fn, OUT, okey, gn, kdst_fn, rope_keys):
            for i in range(2):
                act(SQH[i][:], PS[PSn[i]][:], AF.Square, [("ps", PSn[i])], [("SQH", i)])
            for i in range(2):
                pi = ps_next()
                mm(PS[pi][:], [(NN, SQH[i][:]), (RN[i], sqr_ap)], [("SQH", i), PSr_sq_key, "cst"], [("ps", pi)])
                rstd_from_psum(RSH[i][:], pi, 96, ("RSH", i))
            pi = ps_next()
            mm(PS[pi][:], [(NR[0], SQH[0][:]), (NR[1], SQH[1][:]), (RR, sqr_ap)],
               [("SQH", 0), ("SQH", 1), PSr_sq_key, "cst"], [("ps", pi)])
            rstd_from_psum(RSH[2][:], pi, 96, ("RSH", 2))
            for i in range(2):
                jn = 2 * hh + i
                stt(OUT[:, jn, :], PS[PSn[i]][:], vcol(gn, l), RSH[i][:], ALU.mult, ALU.mult,
                    [("ps", PSn[i]), ("RSH", i), "vecs"], [(okey, jn)])
            rap, rkeys = rope_src_fn(hh)
            tt(OUT[:, 4 + hh, :], rap, RSH[2][:], ALU.mult, rkeys + [("RSH", 2)] + rope_keys, [(okey, 4 + hh)])
            kdst_fn(hh)

        for i, b in enumerate(order):
            if i == 0:
                p1_loads(0)
            if i + 1 < NB:
                p1_loads(i + 1)
            xs = i % 2
            s = seg(b)
            t0 = tok0(b)
            Xb, Hb = X[xs], H[xs]
            QOb, KOb, VOb = QO[xs], KO[xs], VO[xs]
            qok, kok, vok = "QO%d" % xs, "KO%d" % xs, "VO%d" % xs
            xkey, hkey = ("X", xs), ("H", xs)
            norm_block(Xb, xkey, Hb, hkey, SQ, RS, A1, l, s, 0, NTT)
            dma("pool", hT_d[:, t0:t0 + TB].rearrange("(c p) t -> p c t", p=128), Hb[:],
                [(hkey, c) for c in range(8)], [("hd", b)], "st_h%d" % xs)
            hreads = [(hkey, c) for c in range(8)] + W1R
            ts(CQ[:], COS[xs][:], vcol(V_GQR, l), None, ALU.mult, None, [("COS", xs), "vecs"], ["CQ"])
            ts(SQT[:], SIN[xs][:], vcol(V_GQRP, l), None, ALU.mult, None, [("SIN", xs), "vecs"], ["SQT"])
            ts(CK[:], COS[xs][:], vcol(V_GKR, l), None, ALU.mult, None, [("COS", xs), "vecs"], ["CK"])
            ts(SK[:], SIN[xs][:], vcol(V_GKRP, l), None, ALU.mult, None, [("SIN", xs), "vecs"], ["SK"])

            def latent(c0, nch, n, gk, OUTN, okey):
                pis = []
                for j in range(nch):
                    pi = ps_next()
                    pis.append(pi)
                    mm(PS[pi][:], [(W1in[:, kc, c0 + j * 128:c0 + (j + 1) * 128], Hb[:, kc, :]) for kc in range(8)],
                       hreads, [("ps", pi)])
                    act(SQ[:, j, :], PS[pi][:], AF.Square, [("ps", pi)], [("SQ", j)])
                rms_stats([SQ[:, j, :] for j in range(nch)], [ONES] * nch, n, RS[:], "RS", [("SQ", j) for j in range(nch)])
                for j in range(nch):
                    stt(OUTN[:, j, :], PS[pis[j]][:], vcol(gk, l * nch + j), RS[:], ALU.mult, ALU.mult,
                        [("ps", pis[j]), "RS", "vecs"], [(okey, j)])

            latent(0, 3, QL, V_QAG, QN, "QN")
            latent(QL, 2, KVL, V_KVAG, KVN, "KVN")
            qn_r = [("QN", j) for j in range(3)] + W1R
            kvn_r = [("KVN", j) for j in range(2)] + W1R

            for t4 in range(4):
                pi = ps_next()
                mm(PS[pi][:], [(KVN[:, kc, t4 * 128:(t4 + 1) * 128], Wv[:, kc, :]) for kc in range(2)], kvn_r, [("ps", pi)])
                act(VOb[:, t4, :], PS[pi][:], AF.Copy, [("ps", pi)], [(vok, t4)])
            bp = b - nS
            vdst = (v_S if s == 0 else v_Pin[bp])
            tl = t0 if s == 0 else 0
            dma("pool", vdst[tl:tl + TB, :].rearrange("(a p) c -> p a c", p=128), VOb[:],
                [(vok, a) for a in range(4)], [("vdst", s)], "st_v%d" % xs)

            pk = ps_next()
            mm(PS[pk][:], [(Wkr[:, kc, :], Hb[:, kc, :]) for kc in range(8)], hreads, [("ps", pk)])
            pkr = ps_next()
            mm(PS[pkr][:], [(Wkrr[:, kc, :], Hb[:, kc, :]) for kc in range(8)], hreads, [("ps", pkr)])
            act(SQRK[:], PS[pk][:], AF.Square, [("ps", pk)], ["SQRK"])
            tt(TK[:], PS[pk][:], CK[:], ALU.mult, [("ps", pk), "CK"], ["TK"])
            tt(T2[:], PS[pkr][:], SK[:], ALU.mult, [("ps", pkr), "SK"], ["T2"])
            tt(TK[:], TK[:], T2[:], ALU.add, ["TK", "T2"], ["TK"], eng="pool")

            kdst = kT_S if s == 0 else kT_Pin[bp]
            for hh in range(2):
                pn = []
                for i2 in range(2):
                    jn = 2 * hh + i2
                    pi = ps_next()
                    pn.append(pi)
                    mm(PS[pi][:], [(Wqn[:, kc, jn * 128:(jn + 1) * 128], QN[:, kc, :]) for kc in range(3)], qn_r, [("ps", pi)])
                pr = ps_next()
                mm(PS[pr][:], [(Wqr[:, kc, hh * 128:(hh + 1) * 128], QN[:, kc, :]) for kc in range(3)], qn_r, [("ps", pr)])
                pt = ps_next()
                mm(PS[pt][:], [(Wqt[:, kc, hh * 128:(hh + 1) * 128], QN[:, kc, :]) for kc in range(3)], qn_r, [("ps", pt)])
                act(SQH[2][:], PS[pr][:], AF.Square, [("ps", pr)], [("SQH", 2)])

                def q_rope(hh_, pr=pr, pt=pt):
                    tt(T1[:], PS[pr][:], CQ[:], ALU.mult, [("ps", pr), "CQ"], ["T1"])
                    tt(T2[:], PS[pt][:], SQT[:], ALU.mult, [("ps", pt), "SQT"], ["T2"])
                    tt(T1[:], T1[:], T2[:], ALU.add, ["T1", "T2"], ["T1"], eng="pool")
                    return T1[:], ["T1"]

                head_half(hh, pn, ("SQH", 2), SQH[2][:], q_rope, QOb, qok, V_GQN, lambda hh_: None, [])
                pn = []
                for i2 in range(2):
                    jn = 2 * hh + i2
                    pi = ps_next()
                    pn.append(pi)
                    mm(PS[pi][:], [(Wkn[:, kc, jn * 128:(jn + 1) * 128], KVN[:, kc, :]) for kc in range(2)], kvn_r, [("ps", pi)])

                def k_rope(hh_):
                    return TK[:], ["TK"]

                head_half(hh, pn, "SQRK", SQRK[:], k_rope, KOb, kok, V_GKN, lambda hh_: None, [])
            if not cfg.get("no_qkstore"):
                dma("pool", qT_d[b * 768:(b + 1) * 768, :].rearrange("(j p) t -> p j t", p=128), QOb[:],
                    [(qok, j) for j in range(6)], [("qkdst", "q")], "st_q%d" % xs)
                krows = kT_S[b * 768:(b + 1) * 768, :] if s == 0 else kT_Pin[bp][:, :]
                dma("pool", krows.rearrange("(j p) t -> p j t", p=128), KOb[:],
                    [(kok, j) for j in range(6)], [("qkdst", "k%d" % s)], "st_k%d" % xs)

            if s == 1 and not cfg.get("no_cc"):
                for ci, (src, dst, rk) in enumerate(((kT_Pin[bp], kT_Pall[bp], ("qkdst", "k1")), (v_Pin[bp], v_Pall[bp], ("vdst", 1)))):
                    T.op("pool", lambda e, src=src, dst=dst: e.collective_compute(
                        "AllGather", ALU.bypass, replica_groups=[[0, 1, 2, 3], [4, 5, 6, 7]],
                        ins=[src.ap().opt()], outs=[dst.ap().opt()]),
                        reads=[rk], writes=[("gath", ci, bp)], dma="cc%d_%d_%d" % (l, ci, bp), inc=1)
        T.barrier()
        if STOP == "pass1":
            break

        LkMax = max(Ls, Lp)
        if "ph2" not in PHASES:
            sb.phase()
            ns = {}
            ns["KT"] = [sb.alloc([96, LkMax], BF16) for _ in range(2)]
            ns["VP"] = [sb.alloc([128, LkMax // 128, 65], BF16) for _ in range(2)]
            ns["QT"] = [sb.alloc([96, max(Ls, Lq)], BF16) for _ in range(2)]
            ns["PT"] = [sb.alloc([128, TB], BF16) for _ in range(4)]
            ns["OSB"] = [sb.alloc([128, TB], F32) for _ in range(2)]
            ns["OTH"] = [sb.alloc([64, TB], BF16) for _ in range(2)]
            ns["WAa"] = [sb.alloc([128, 8, 512], BF16) for _ in range(2)]
            PHASES["ph2"] = ns
        ns = PHASES["ph2"]
        KT = ns["KT"]
        VP = ns["VP"]
        QT = ns["QT"]
        PT = ns["PT"]
        OSB = ns["OSB"]
        OTH = ns["OTH"]
        WAa = ns["WAa"]
        bg = []
        if l + 1 < L:
            cast_weights(l + 1, "A")
            cast_weights(l + 1, "B")
            for i8 in range(8):
                def step(i8=i8):
                    for k in (2 * i8 - 2, 2 * i8 - 1):
                        if 0 <= k < 12:
                            mods_compute(l + 1, k, WAa, banks=[7])
                    for k in (2 * i8, 2 * i8 + 1):
                        if k < 12:
                            mods_load(l + 1, k, WAa)
                    if i8 == 7:
                        mods_finish(l + 1)
                bg.append(step)
        for sl in range(2):
            memset(VP[sl][:, :, 64:65], 1.0, [("VP", sl)])
        scale = 96 ** -0.5
        hcount = 0
        for s in range(2):
            Lk = Ls if s == 0 else Lp
            Lseg = Ls if s == 0 else Lq
            tbase = 0 if s == 0 else Ls
            nkb = Lk // 128
            nqb = Lseg // TB

            def att_loads(h, sl, s=s, Lk=Lk, Lseg=Lseg, tbase=tbase, nkb=nkb):
                rn = (h // 2) * 128 + 64 * (h % 2)
                rr = (4 + h // 4) * 128 + 32 * (h % 4)
                parts = ((0, 64, rn), (64, 32, rr))
                if s == 0:
                    kv = kT_S.ap().rearrange("(b r) t -> r b t", r=768)
                    for (p0, np_, r0) in parts:
                        dma("sp", KT[sl][p0:p0 + np_, 0:Lk].rearrange("d (b t) -> d b t", t=TB), kv[r0:r0 + np_, :, :],
                            [("qkdst", "k0")], [("KT", sl)], "ld_kt%d" % sl)
                    vsrc = v_S[:, h * 64:(h + 1) * 64].rearrange("(kb p) c -> p kb c", p=128)
                    dma("sp", VP[sl][:, 0:nkb, 0:64], vsrc, [("vdst", 0)], [("VP", sl)], "ld_vp%d" % sl)
                else:
                    for bq in range(nP):
                        kv = kT_Pall[bq].ap().rearrange("(r c) t -> c r t", r=4)
                        for (p0, np_, r0) in parts:
                            dma("sp", KT[sl][p0:p0 + np_, bq * 4 * TB:(bq + 1) * 4 * TB].rearrange("d (r t) -> d r t", r=4),
                                kv[r0:r0 + np_, :, :], [("gath", 0, bq)], [("KT", sl)], "ld_kt%d" % sl)
                        vsrc = v_Pall[bq][:, h * 64:(h + 1) * 64].rearrange("(kb p) c -> p kb c", p=128)
                        dma("sp", VP[sl][:, bq * 16:(bq + 1) * 16, 0:64], vsrc, [("gath", 1, bq)], [("VP", sl)], "ld_vp%d" % sl)
                blk0 = 0 if s == 0 else nS
                nblk = Lseg // TB
                qv = qT_d.ap().rearrange("(b r) t -> r b t", r=768)
                for (p0, np_, r0) in parts:
                    dma("sp", QT[sl][p0:p0 + np_, 0:Lseg].rearrange("d (b t) -> d b t", t=TB), qv[r0:r0 + np_, blk0:blk0 + nblk, :],
                        [("qkdst", "q")], [("QT", sl)], "ld_qt%d" % sl)

            att_loads(0, hcount % 2)
            for h in range(NH):
                sl = hcount % 2
                hcount += 1
                if h + 1 < NH:
                    att_loads(h + 1, hcount % 2)
                if bg and s == 1:
                    bg.pop(0)()
                for qb in range(nqb):
                    po = 5 + (qb % 2)
                    steps = list(range(nkb))
                    spi = {}

                    def qk(kb, sl=sl, qb=qb):
                        pi = kb % 5
                        spi[kb] = pi
                        mm(PS[pi][:], [(KT[sl][:, kb * 128:(kb + 1) * 128], QT[sl][:, qb * TB:(qb + 1) * TB])],
                           [("KT", sl), ("QT", sl)], [("ps", pi)])

                    qk(0)
                    if nkb > 1:
                        qk(1)
                    for kb in steps:
                        pi = spi[kb]
                        pt_i = kb % 4
                        act(PT[pt_i][:], PS[pi][:], AF.Exp, [("ps", pi)], [("PT", pt_i)], scale=scale)
                        if kb + 2 < nkb:
                            qk(kb + 2)
                        T.op("pe", lambda e, po=po, sl=sl, kb=kb, pt_i=pt_i, nkb=nkb:
                             e.matmul(PS[po][0:65, :], VP[sl][:, kb, :], PT[pt_i][:], start=(kb == 0), stop=(kb == nkb - 1)),
                             reads=[("VP", sl), ("PT", pt_i)], writes=[("ps", po)])
                    ob = qb % 2
                    T.op("dve", lambda e, ob=ob, po=po: e.tensor_copy(out=OSB[ob][0:65, :], in_=PS[po][0:65, :]),
                         reads=[("ps", po)], writes=[("OSB", ob)])
                    recip(OSB[ob][64:65, :], OSB[ob][64:65, :], [("OSB", ob)], [("OSB", ob)])
                    mm(PS[7][0:64, :], [(ones_f[64:65, 0:64], OSB[ob][64:65, :])], [("OSB", ob), "ones_f"], [("ps", 7)])
                    tt(OTH[ob][:], OSB[ob][0:64, :], PS[7][0:64, :], ALU.mult, [("OSB", ob), ("ps", 7)], [("OTH", ob)])
                    tq = tbase + qb * TB
                    dma("pool", oT_d[h * 64:(h + 1) * 64, tq:tq + TB], OTH[ob][:], [("OTH", ob)], [("od",)], "st_o%d" % ob)
        while bg:
            bg.pop(0)()
        T.barrier()
        if STOP == "attn":
            break

        if "ph3" not in PHASES:
            sb.phase()
            ns = {}
            ns["X"] = [sb.alloc([128, 8, TB], F32) for _ in range(2)]
            ns["H"] = [sb.alloc([128, 8, TB], BF16) for _ in range(2)]
            ns["OT"] = [sb.alloc([128, 4, TB], BF16) for _ in range(2)]
            ns["US"] = sb.alloc([128, 8, TB], BF16)
            ns["BIGF"] = sb.alloc([128, 4096], F32)
            ns["VM"] = sb.alloc([128, 4096], BF16)
            ns["SH"] = sb.alloc([128, 8, TB], BF16)
            ns["SGT"] = sb.alloc([128, 4, TB], F32)
            ns["ACTT"] = sb.alloc([128, 22, TB], BF16)
            ns["SQ"] = sb.alloc([128, 8, TB], BF16)
            ns["RS"] = sb.alloc([128, TB], F32)
            ns["WS"] = [sb.alloc([128, 4096], BF16) for _ in range(3)]
            ns["BSB"] = sb.alloc([128, 8, 128], F32)
            ns["WST"] = sb.alloc([128, 8, 128], BF16)
            ns["SSV"] = sb.alloc([128, 4], F32)
            ns["JUNK"] = sb.alloc([128, 1024], BF16)
            ns["TMP"] = [sb.alloc([128, TB], F32) for _ in range(2)]
            PHASES["ph3"] = ns
        ns = PHASES["ph3"]
        X = ns["X"]
        H = ns["H"]
        OT = ns["OT"]
        US = ns["US"]
        BIGF = ns["BIGF"]
        VM = ns["VM"]
        SH = ns["SH"]
        SGT = ns["SGT"]
        ACTT = ns["ACTT"]
        SQ = ns["SQ"]
        RS = ns["RS"]
        WS = ns["WS"]
        BSB = ns["BSB"]
        WST = ns["WST"]
        SSV = ns["SSV"]
        JUNK = ns["JUNK"]
        TMP = ns["TMP"]
        GV = BIGF[:].rearrange("p (a b) -> p a b", a=4)
        MT = BIGF[:].rearrange("p (a b) -> p a b", a=8)
        VH = VM[:].rearrange("p (a b) -> p a b", a=4)
        MG = VM[:].rearrange("p (a b) -> p a b", a=8)
        dma("sp", BSB[:], bsb_in[l], [], ["BSB"], "ld_misc")
        dma("pool", WST[:], wsT_in[l], [], ["WST"], "ld_cst")
        tmp_rr = [0]

        def tmp_next():
            i = tmp_rr[0]
            tmp_rr[0] = 1 - i
            return i

        ws_rr = [0]
        pending = []

        def p2_loads(i):
            b = i
            xs = i % 2
            t0 = tok0(b)
            dma("sp", X[xs][:], x_src[:, t0:t0 + TB].rearrange("(c p) t -> p c t", p=128), [("xd", b)],
                [(("X", xs), c) for c in range(8)], "ld_x%d" % xs)
            dma("sp", H[xs][:], hT_d[:, t0:t0 + TB].rearrange("(c p) t -> p c t", p=128), [("hd", b)],
                [(("H", xs), c) for c in range(8)], "ld_h%d" % xs)
            dma("sp", OT[xs][:], oT_d[:, t0:t0 + TB].rearrange("(c p) t -> p c t", p=128), [("od",)],
                [("OT", xs)], "ld_o%d" % xs)

        for b in range(NB):
            if b == 0:
                p2_loads(0)
            xs = b % 2
            s = seg(b)
            t0 = tok0(b)
            Xb, Hb, OTb = X[xs], H[xs], OT[xs]
            xkey, hkey = ("X", xs), ("H", xs)
            hreads = [(hkey, c) for c in range(8)]

            steps = []

            def add(name, c0, n, fn):
                steps.append((name, c0, n, WK[name] // 128, fn))

            def f_zu(W, wkey, g):
                for jj in range(4):
                    c = g * 4 + jj
                    pi = ps_next()
                    mm(PS[pi][:], [(W[:, kc, jj * 128:(jj + 1) * 128], Hb[:, kc, :]) for kc in range(8)], hreads + [wkey], [("ps", pi)])
                    act(US[:, c, :], PS[pi][:], AF.Gelu_apprx_tanh, [("ps", pi)], [("US", c)])
            for g in range(2):
                add("w_in", 672 + g * 512, 512, lambda W, wkey, g=g: f_zu(W, wkey, g))

            def f_zv(W, wkey, half):
                for t4 in range(4):
                    pi = ps_next()
                    mm(PS[pi][:], [(Hb[:, kc, t4 * 128:(t4 + 1) * 128], W[:, kc, :]) for kc in range(8)], hreads + [wkey], [("ps", pi)])
                    act(GV[:, t4, half * 512:(half + 1) * 512], PS[pi][:], AF.Gelu_apprx_tanh, [("ps", pi)], [("BIGF", 2 * t4 + half)])
                if half == 1:
                    memset(SSV[:], 0.0, [("SSV", t4) for t4 in range(4)])
                    for t4 in range(4):
                        act(JUNK[:], GV[:, t4, :], AF.Square, [("BIGF", 2 * t4), ("BIGF", 2 * t4 + 1)], ["JUNK", ("SSV", t4)],
                            accum_out=SSV[:, t4:t4 + 1])
                    act(SSV[:], SSV[:], AF.Sqrt, [("SSV", t4) for t4 in range(4)] + ["vecs"], [("SSV", t4) for t4 in range(4)],
                        bias=eps_ap, scale=1.0 / D)
                    recip(SSV[:], SSV[:], [("SSV", t4) for t4 in range(4)], [("SSV", t4) for t4 in range(4)])
                    for t4 in range(4):
                        ts(VH[:, t4, :], GV[:, t4, :], SSV[:, t4:t4 + 1], None, ALU.mult, None,
                           [("BIGF", 2 * t4), ("BIGF", 2 * t4 + 1), ("SSV", t4)], [("VM", 2 * t4), ("VM", 2 * t4 + 1)])
                    for g in range(8):
                        pi = ps_next()

                        def fmix(e, pi=pi, g=g):
                            ins = None
                            for t4 in range(4):
                                ins = e.matmul(PS[pi][:, t4 * 128:(t4 + 1) * 128], VH[:, t4, g * 128:(g + 1) * 128], WST[:, g, :],
                                               start=True, stop=True)
                            return ins
                        T.op("pe", fmix, reads=[("VM", k) for k in range(8)] + ["WST"], writes=[("ps", pi)])
                        ti = tmp_next()
                        for t4 in range(4):
                            stt(TMP[ti][:, t4 * 128:(t4 + 1) * 128], PS[pi][:, t4 * 128:(t4 + 1) * 128], vcol(V_SGUG, l * 8 + g),
                                BSB[:, g, :], ALU.mult, ALU.add, [("ps", pi), "BSB", "vecs"], [("TMP", ti)])
                        tt(SH[:, g, :], TMP[ti][:], US[:, g, :], ALU.mult, [("TMP", ti), ("US", g)], [("SH", g)], eng="pool")
            for half in range(2):
                add("w_in", 1696 + half * 512, 512, lambda W, wkey, half=half: f_zv(W, wkey, half))

            def f_gate(W, wkey, g):
                for jj in range(4):
                    c = g * 4 + jj
                    pi = ps_next()
                    mm(PS[pi][:], [(W[:, kc, jj * 128:(jj + 1) * 128], Hb[:, kc, :]) for kc in range(8)], hreads + [wkey], [("ps", pi)])
                    act(US[:, c, :], PS[pi][:], AF.Sigmoid, [("ps", pi)], [("US", c)])

            def f_ob(W, wkey, g):
                for jj in range(4):
                    c = g * 4 + jj
                    pi = ps_next()
                    mm(PS[pi][:], [(W[:, kc, jj * 128:(jj + 1) * 128], SH[:, kc, :]) for kc in range(8)],
                       [("SH", k) for k in range(8)] + [wkey], [("ps", pi)])
                    tt(MT[:, c, :], PS[pi][:], US[:, c, :], ALU.mult, [("ps", pi), ("US", c)], [("BIGF", c)])

            def f_oa(W, wkey):
                for c in range(8):
                    pi = ps_next()
                    mm(PS[pi][:], [(W[:, kc, c * 128:(c + 1) * 128], OTb[:, kc, :]) for kc in range(4)], [("OT", xs), wkey], [("ps", pi)])
                    ti = tmp_next()
                    tt(TMP[ti][:], PS[pi][:], US[:, c, :], ALU.mult, [("ps", pi), ("US", c)], [("TMP", ti)])
                    tt(MG[:, c, :], TMP[ti][:], MT[:, c, :], ALU.add, [("TMP", ti), ("BIGF", c)], [("VM", c)], eng="pool")

            def f_out(W, wkey, g):
                for jj in range(4):
                    c = g * 4 + jj
                    pi = ps_next()
                    mm(PS[pi][:], [(W[:, kc, jj * 128:(jj + 1) * 128], MG[:, kc, :]) for kc in range(8)],
                       [("VM", k) for k in range(8)] + [wkey], [("ps", pi)])
                    stt(Xb[:, c, :], PS[pi][:], mod_ap(l, 2, c, s), Xb[:, c, :], ALU.mult, ALU.add,
                        [("ps", pi), (xkey, c), ("mods", l, 16 + c)], [(xkey, c)])
                if g == 1:
                    norm_block(Xb, xkey, SH, "SH", SQ, RS, A2, l, s, 3, TMP)

            for g in range(2):
                add("w_in", 3744 + g * 512, 512, lambda W, wkey, g=g: f_gate(W, wkey, g))
            for g in range(2):
                add("w_o_b", g * 512, 512, lambda W, wkey, g=g: f_ob(W, wkey, g))
            for g in range(2):
                add("w_in", 2720 + g * 512, 512, lambda W, wkey, g=g: f_gate(W, wkey, g))
            add("w_o_a", 0, 1024, lambda W, wkey: f_oa(W, wkey))
            for g in range(2):
                add("w_out", g * 512, 512, lambda W, wkey, g=g: f_out(W, wkey, g))

            h2reads = [("SH", k) for k in range(8)]

            def f_gatef(W, wkey, j0, nch):
                for jj in range(nch):
                    pi = ps_next()
                    mm(PS[pi][:], [(W[:, kc, jj * 128:(jj + 1) * 128], SH[:, kc, :]) for kc in range(8)], h2reads + [wkey], [("ps", pi)])
                    act(SGT[:, jj, :], PS[pi][:], AF.Silu, [("ps", pi)], [("SGT", jj)])

            def f_upf(W, wkey, j0, nch):
                for jj in range(nch):
                    pi = ps_next()
                    mm(PS[pi][:], [(W[:, kc, jj * 128:(jj + 1) * 128], SH[:, kc, :]) for kc in range(8)], h2reads + [wkey], [("ps", pi)])
                    tt(ACTT[:, j0 + jj, :], PS[pi][:], SGT[:, jj, :], ALU.mult, [("ps", pi), ("SGT", jj)], [("ACTT", j0 + jj)])

            for j0 in range(0, 22, 4):
                nch = min(4, 22 - j0)
                add("w_ffn_in", DFF + j0 * 128, nch * 128, lambda W, wkey, j0=j0, nch=nch: f_gatef(W, wkey, j0, nch))
                add("w_ffn_in", j0 * 128, nch * 128, lambda W, wkey, j0=j0, nch=nch: f_upf(W, wkey, j0, nch))

            def f_fo(W, wkey, c):
                pi = ps_next()
                mm(PS[pi][:], [(W[:, kc, :], ACTT[:, kc, :]) for kc in range(22)], [("ACTT", k) for k in range(22)] + [wkey], [("ps", pi)])
                stt(Xb[:, c, :], PS[pi][:], mod_ap(l, 5, c, s), Xb[:, c, :], ALU.mult, ALU.add,
                    [("ps", pi), (xkey, c), ("mods", l, 40 + c)], [(xkey, c)])
            for c in range(8):
                add("w_ffn_out", c * 128, 128, lambda W, wkey, c=c: f_fo(W, wkey, c))

            nst = len(steps)
            views = {}

            def issue_load(k):
                name, c0, n, kcn, fn = steps[k]
                sl = ws_rr[0]
                ws_rr[0] = (sl + 1) % 3
                view = WS[sl][:, 0:kcn * n].rearrange("p (k n) -> p k n", k=kcn)
                dma("sp", view, wsrc(name, l, c0, n), wkeys(name, l), [("WS", sl)], "ld_ws%d" % sl)
                views[k] = (view, ("WS", sl))

            issue_load(0)
            issue_load(1)
            for k in range(nst):
                if k + 2 < nst:
                    issue_load(k + 2)
                if k == 4 and b + 1 < NB:
                    p2_loads(b + 1)
                view, wkey = views[k]
                steps[k][4](view, wkey)
            dma("pool", x_dst[:, t0:t0 + TB].rearrange("(c p) t -> p c t", p=128), Xb[:],
                [(xkey, c) for c in range(8)], [("xd", b)], "st_x%d" % xs)
        T.barrier()

    with nc.Block() as block:
        @block.tensor
        def _(e):
            T.replay("pe", e)

        @block.scalar
        def _(e):
            T.replay("act", e)

        @block.vector
        def _(e):
            T.replay("dve", e)

        @block.gpsimd
        def _(e):
            T.replay("pool", e)

        @block.sync
        def _(e):
            T.replay("sp", e)
    es.close()
    return nc


def rope_tables(pos):
    inv = (10000.0 ** (-np.arange(0, 32, 2, dtype=np.float32) / np.float32(32))).astype(np.float32)
    ang = pos.astype(np.float32)[:, None] * inv[None, :]
    return np.cos(ang).astype(np.float32), np.sin(ang).astype(np.float32)


def make_consts():
    k = np.arange(128)[:, None]
    m = np.arange(128)[None, :]
    ones = np.ones((128, 128), np.float32)
    nn = (k // 64 == m // 64)
    rn = [(k // 32 == 2 * i + m // 64) for i in range(2)]
    nr = [(2 * i + k // 64 == m // 32) for i in range(2)]
    rr = (k // 32 == m // 32)
    mats = [ones, nn, rn[0], rn[1], nr[0], nr[1], rr]
    return np.ascontiguousarray(np.stack([np.asarray(x, np.float32) for x in mats], axis=1))


def prepare_inputs(cfg, inp):
    Ls, Lq, L = cfg["Ls"], cfg["Lq"], cfg["L"]
    voff, NV = vec_layout(L)
    f = lambda a: np.ascontiguousarray(np.asarray(a, dtype=np.float32))
    w_in = f(inp["w_in"])[:L]
    w_q_b = f(inp["w_q_b"])[:L]
    w_kv_b = f(inp["w_kv_b"])[:L]
    shared = {}
    shared["w_in"] = w_in.reshape(L * D, INC)
    kr = w_in[:, :, 640:672]
    krrot = np.concatenate([kr[:, :, 16:32], kr[:, :, 0:16]], axis=2)
    shared["w_kr4"] = f(np.tile(kr, (1, 1, 4))).reshape(L * D, 128)
    shared["w_krrot4"] = f(np.tile(krrot, (1, 1, 4))).reshape(L * D, 128)
    wq = w_q_b.reshape(L, QL, NH, 96)
    shared["wq_nope"] = f(wq[:, :, :, 0:64]).reshape(L * QL, 512)
    shared["wq_rope"] = f(wq[:, :, :, 64:96]).reshape(L * QL, 256)
    shared["wq_rot"] = f(np.concatenate([wq[:, :, :, 80:96], wq[:, :, :, 64:80]], axis=3)).reshape(L * QL, 256)
    wkv = w_kv_b.reshape(L, KVL, NH, 128)
    shared["wkv_nope"] = f(wkv[:, :, :, 0:64]).reshape(L * KVL, 512)
    shared["wkv_v"] = f(wkv[:, :, :, 64:128]).reshape(L * KVL, 512)
    shared["w_o_a"] = f(inp["w_o_a"])[:L].reshape(L * 512, D)
    shared["w_o_b"] = f(inp["w_o_b"])[:L].reshape(L * D, D)
    shared["w_out"] = f(inp["w_out"])[:L].reshape(L * D, D)
    shared["w_ffn_in"] = f(inp["w_ffn_in"])[:L].reshape(L * D, 2 * DFF)
    shared["w_ffn_out"] = f(inp["w_ffn_out"])[:L].reshape(L * DFF, D)
    shared["w_ada"] = f(inp["w_ada"])[:L].reshape(L * D, 6 * D)
    shared["w_sT"] = f(np.transpose(f(inp["w_s"])[:L], (0, 3, 1, 2)))
    shared["bsb"] = f(np.broadcast_to(f(inp["b_s"])[:L][:, None, :, :], (L, 128, 8, 128)))
    shared["consts"] = make_consts()
    vecs = np.zeros((128, NV), np.float32)
    vecs[:, voff[V_EPS]] = EPS
    p = np.arange(128)

    def fm(a, n):
        return f(a)[:L].reshape(L, n, 128).transpose(2, 0, 1).reshape(128, L * n)
    vecs[:, voff[V_N1G]:voff[V_N1G] + 8 * L] = fm(inp["norm1_g"], 8)
    vecs[:, voff[V_N2G]:voff[V_N2G] + 8 * L] = fm(inp["norm2_g"], 8)
    vecs[:, voff[V_SGUG]:voff[V_SGUG] + 8 * L] = fm(inp["sgu_norm_g"], 8)
    vecs[:, voff[V_QAG]:voff[V_QAG] + 3 * L] = fm(inp["q_a_norm_g"], 3)
    vecs[:, voff[V_KVAG]:voff[V_KVAG] + 2 * L] = fm(inp["kv_a_norm_g"], 2)
    vecs[:, voff[V_BADA]:voff[V_BADA] + 48 * L] = fm(inp["b_ada"], 48)
    qg = f(inp["q_norm_g"])[:L]
    kg = f(inp["k_norm_g"])[:L]
    for (g, vn, vr, vrp) in ((qg, V_GQN, V_GQR, V_GQRP), (kg, V_GKN, V_GKR, V_GKRP)):
        vecs[:, voff[vn]:voff[vn] + L] = g[:, p % 64].T
        vecs[:, voff[vr]:voff[vr] + L] = g[:, 64 + p % 32].T
        vecs[:, voff[vrp]:voff[vrp] + L] = g[:, 64 + (p % 32 + 16) % 32].T
    shared["vecs"] = vecs
    xs = f(inp["x_sample"])
    xp = f(inp["x_prompt"])
    cs = f(inp["c_sample"])
    cp = f(inp["c_prompt"])
    fidx = (p % 32) % 16
    sgn = np.where((p % 32) < 16, -1.0, 1.0).astype(np.float32)
    in_maps = []
    for c in range(NCORES):
        ps_, r = c // 4, c % 4
        m = dict(shared)
        m["xT"] = np.ascontiguousarray(np.concatenate([xs[c, :Ls].T, xp[ps_, r * Lq:(r + 1) * Lq].T], axis=1))
        m["cT"] = np.ascontiguousarray(np.stack([cs[c].reshape(8, 128).T, cp[ps_].reshape(8, 128).T], axis=2))
        pos = np.concatenate([np.arange(Ls), r * Lq + np.arange(Lq)])
        cos, sin = rope_tables(pos)
        m["cos4"] = np.ascontiguousarray(cos.T[fidx, :])
        m["sin4"] = np.ascontiguousarray(sin.T[fidx, :] * sgn[:, None])
        in_maps.append(m)
    return in_maps


_PROG_CACHE = {}


def run(cfg, inp):
    key = (cfg["Ls"], cfg["Lq"], cfg["L"], cfg.get("stop"), cfg.get("no_cc"), cfg.get("no_pool"), cfg.get("no_qkstore"))
    if key not in _PROG_CACHE:
        _PROG_CACHE[key] = build_program(cfg)
    nc = _PROG_CACHE[key]
    in_maps = prepare_inputs(cfg, inp)
    res = run_bass_kernel_spmd(nc, in_maps, core_ids=list(range(NCORES)))
    Ls, Lq = cfg["Ls"], cfg["Lq"]
    ys = np.stack([res.results[c]["yT"][:, :Ls].T for c in range(NCORES)], axis=0)
    yp = np.stack([np.concatenate([res.results[4 * s_ + r]["yT"][:, Ls:].T for r in range(4)], axis=0) for s_ in range(2)], axis=0)
    return np.ascontiguousarray(yp, dtype=np.float32), np.ascontiguousarray(ys, dtype=np.float32)


def kernel(**inputs):
    return run(FULL_CFG, inputs)
```
